# Optimizing a Trainium2 kernel written in Bass

```python
import math
import jax, jax.numpy as jnp
from jax import lax
import numpy as np

D_MODEL = 1024
BATCH = 4
SEQ = 8192
DEPTH = 2

HEAD_DIM = 64
CONV_CH = 256
CONV_K = 31
SC_CH = 256
SC_K = 3
SWA_Q_HEADS = 4
SWA_KV_HEADS = 2
WINDOW = 128
BLOCK = 128
FOX_HEADS = 4
N_BUCKETS = 32
MAX_DISTANCE = 128
D_FF = 2816
N_BRANCH = 4
EPS = 1e-6
NEG_INF = -1e30

SWA_COLS = (SWA_Q_HEADS + 2 * SWA_KV_HEADS) * HEAD_DIM
FOX_COLS = 3 * FOX_HEADS * HEAD_DIM + FOX_HEADS
_SIZES = (CONV_CH, CONV_CH,
          SC_CH, SC_CH, SC_CH,
          SWA_Q_HEADS * HEAD_DIM, SWA_KV_HEADS * HEAD_DIM, SWA_KV_HEADS * HEAD_DIM,
          FOX_HEADS * HEAD_DIM, FOX_HEADS * HEAD_DIM, FOX_HEADS * HEAD_DIM, FOX_HEADS,
          N_BRANCH * D_MODEL)
SPLITS = tuple(int(s) for s in np.cumsum(_SIZES)[:-1])
IN_COLS = int(sum(_SIZES))

kernel_name = "hybrid_gated_conformer_shortconv_swa_fox"


def _rmsnorm(x, g):
    xf = x.astype(jnp.float32)
    y = xf * lax.rsqrt(jnp.mean(xf * xf, axis=-1, keepdims=True) + EPS)
    return (y * g.astype(jnp.float32)).astype(x.dtype)


def _layernorm(x, g, b):
    xf = x.astype(jnp.float32)
    mu = jnp.mean(xf, axis=-1, keepdims=True)
    var = jnp.mean(jnp.square(xf - mu), axis=-1, keepdims=True)
    y = (xf - mu) * lax.rsqrt(var + EPS)
    return (y * g.astype(jnp.float32) + b.astype(jnp.float32)).astype(x.dtype)


def _swiglu(x, w_gate, w_up, w_down):
    return (jax.nn.silu(x @ w_gate) * (x @ w_up)) @ w_down


def _causal_depthwise_conv(x, w):
    k = w.shape[0]
    return lax.conv_general_dilated(
        x, w[:, None, :].astype(x.dtype), window_strides=(1,), padding=[(k - 1, 0)],
        dimension_numbers=("NWC", "WIO", "NWC"), feature_group_count=x.shape[-1])


def _t5_bucket(dist):
    max_exact = N_BUCKETS // 2
    d = jnp.maximum(dist, 1).astype(jnp.float32)
    large = max_exact + (jnp.log(d / max_exact) / math.log(MAX_DISTANCE / max_exact)
                         * (N_BUCKETS - max_exact)).astype(jnp.int32)
    large = jnp.minimum(large, N_BUCKETS - 1)
    return jnp.where(dist < max_exact, dist, large)


def _sliding_window_attention(q, k, v, sink, rel_bias):
    b, t, hq, dh = q.shape
    hkv = k.shape[2]
    grp = hq // hkv
    nb = t // BLOCK
    qb = q.reshape(b, nb, BLOCK, hkv, grp, dh)
    kb = k.reshape(b, nb, BLOCK, hkv, dh)
    vb = v.reshape(b, nb, BLOCK, hkv, dh)
    pad = ((0, 0), (1, 0), (0, 0), (0, 0), (0, 0))
    kk = jnp.concatenate([jnp.pad(kb, pad)[:, :-1], kb], axis=2)
    vv = jnp.concatenate([jnp.pad(vb, pad)[:, :-1], vb], axis=2)
    s = jnp.einsum("bnqhgd,bnkhd->bnhgqk", qb, kk,
                   preferred_element_type=jnp.float32) * (dh ** -0.5)
    qi = jnp.arange(BLOCK)[:, None] + BLOCK
    ki = jnp.arange(2 * BLOCK)[None, :]
    dist = qi - ki
    local_ok = (dist >= 0) & (dist < WINDOW)
    blk_ok = (jnp.arange(nb)[:, None, None] > 0) | (ki[None] >= BLOCK)
    mask = local_ok[None] & blk_ok
    bias = rel_bias[_t5_bucket(jnp.maximum(dist, 0))]
    bias = bias.transpose(2, 0, 1).reshape(hkv, grp, BLOCK, 2 * BLOCK).astype(jnp.float32)
    s = jnp.where(mask[None, :, None, None], s + bias, NEG_INF)
    sk = sink.astype(jnp.float32).reshape(hkv, grp)[:, :, None, None]
    m = jnp.maximum(jnp.max(s, axis=-1, keepdims=True), sk)
    p = jnp.exp(s - m)
    p = p / (jnp.sum(p, axis=-1, keepdims=True) + jnp.exp(sk - m))
    o = jnp.einsum("bnhgqk,bnkhd->bnqhgd", p.astype(v.dtype), vv)
    return o.reshape(b, t, hq * dh)


def _forgetting_attention(q, k, v, log_f):
    b, t, h, dh = q.shape
    nb = t // BLOCK
    cum = jnp.cumsum(log_f, axis=1)
    cum_k = cum.transpose(0, 2, 1)
    qb = q.reshape(b, nb, BLOCK, h, dh).transpose(1, 0, 2, 3, 4)
    cq = cum.reshape(b, nb, BLOCK, h).transpose(1, 0, 3, 2)
    kpos = jnp.arange(t)

    def one_block(args):
        q_blk, c_blk, bi = args
        s = jnp.einsum("bqhd,bkhd->bhqk", q_blk, k,
                       preferred_element_type=jnp.float32) * (dh ** -0.5)
        s = s + (c_blk[..., :, None] - cum_k[:, :, None, :])
        qpos = bi * BLOCK + jnp.arange(BLOCK)
        s = jnp.where(kpos[None, :] <= qpos[:, None], s, NEG_INF)
        p = jax.nn.softmax(s, axis=-1)
        return jnp.einsum("bhqk,bkhd->bqhd", p.astype(v.dtype), v)

    o = lax.map(one_block, (qb, cq, jnp.arange(nb)))
    return o.transpose(1, 0, 2, 3, 4).reshape(b, t, h * dh)


def _normal(k, shape, scale):
    return scale * jax.random.normal(k, shape, jnp.float32)


def setup_inputs(seed: int = 0) -> dict:
    key = jax.random.key(seed)
    ks = jax.random.split(key, 32)
    L, D = DEPTH, D_MODEL
    return {
        "x": _normal(ks[0], (BATCH, SEQ, D), 1.0),
        "rel_bias": _normal(ks[1], (N_BUCKETS, SWA_Q_HEADS), 0.5),
        "ffn1_norm": 1.0 + _normal(ks[2], (L, D), 0.05),
        "ffn1_w_gate": _normal(ks[3], (L, D, D_FF), D ** -0.5),
        "ffn1_w_up": _normal(ks[4], (L, D, D_FF), D ** -0.5),
        "ffn1_w_down": _normal(ks[5], (L, D_FF, D), D_FF ** -0.5),
        "mix_norm": 1.0 + _normal(ks[6], (L, D), 0.05),
        "w_in": _normal(ks[7], (L, D, IN_COLS), D ** -0.5),
        "b_forget": 3.0 + _normal(ks[8], (L, FOX_HEADS), 0.5),
        "conf_dw": _normal(ks[9], (L, CONV_K, CONV_CH), CONV_K ** -0.5),
        "conf_dw_b": _normal(ks[10], (L, CONV_CH), 0.02),
        "conf_ln_g": 1.0 + _normal(ks[11], (L, CONV_CH), 0.05),
        "conf_ln_b": _normal(ks[12], (L, CONV_CH), 0.02),
        "conf_w_out": _normal(ks[13], (L, CONV_CH, D), CONV_CH ** -0.5),
        "sc_conv": _normal(ks[14], (L, SC_K, SC_CH), SC_K ** -0.5),
        "sc_w_out": _normal(ks[15], (L, SC_CH, D), SC_CH ** -0.5),
        "swa_q_norm": 1.0 + _normal(ks[16], (L, HEAD_DIM), 0.05),
        "swa_k_norm": 1.0 + _normal(ks[17], (L, HEAD_DIM), 0.05),
        "swa_sink": _normal(ks[18], (L, SWA_Q_HEADS), 0.5),
        "swa_w_o": _normal(ks[19], (L, SWA_Q_HEADS * HEAD_DIM, D), (SWA_Q_HEADS * HEAD_DIM) ** -0.5),
        "fox_q_norm": 1.0 + _normal(ks[20], (L, HEAD_DIM), 0.05),
        "fox_k_norm": 1.0 + _normal(ks[21], (L, HEAD_DIM), 0.05),
        "fox_w_o": _normal(ks[22], (L, FOX_HEADS * HEAD_DIM, D), (FOX_HEADS * HEAD_DIM) ** -0.5),
        "w_out": _normal(ks[23], (L, D, D), D ** -0.5),
        "ffn2_norm": 1.0 + _normal(ks[24], (L, D), 0.05),
        "ffn2_w_gate": _normal(ks[25], (L, D, D_FF), D ** -0.5),
        "ffn2_w_up": _normal(ks[26], (L, D, D_FF), D ** -0.5),
        "ffn2_w_down": _normal(ks[27], (L, D_FF, D), D_FF ** -0.5),
    }


def reference(x, rel_bias, ffn1_norm, ffn1_w_gate, ffn1_w_up, ffn1_w_down, mix_norm, w_in,
              b_forget, conf_dw, conf_dw_b, conf_ln_g, conf_ln_b, conf_w_out, sc_conv, sc_w_out,
              swa_q_norm, swa_k_norm, swa_sink, swa_w_o, fox_q_norm, fox_k_norm, fox_w_o, w_out,
              ffn2_norm, ffn2_w_gate, ffn2_w_up, ffn2_w_down):
    bsz, t = x.shape[0], x.shape[1]
    for l in range(DEPTH):
        x = x + 0.5 * _swiglu(_rmsnorm(x, ffn1_norm[l]), ffn1_w_gate[l], ffn1_w_up[l], ffn1_w_down[l])

        h = _rmsnorm(x, mix_norm[l])
        z = h @ w_in[l]
        (c_a, c_b, s_b, s_c, s_x, a_q, a_k, a_v,
         f_q, f_k, f_v, f_f, z_gate) = jnp.split(z, SPLITS, axis=-1)

        u = c_a * jax.nn.sigmoid(c_b)
        u = _causal_depthwise_conv(u, conf_dw[l]) + conf_dw_b[l]
        u = jax.nn.silu(_layernorm(u, conf_ln_g[l], conf_ln_b[l]))
        p_conf = u @ conf_w_out[l]

        p_sc = (s_b * _causal_depthwise_conv(s_c * s_x, sc_conv[l])) @ sc_w_out[l]

        q = _rmsnorm(a_q.reshape(bsz, t, SWA_Q_HEADS, HEAD_DIM), swa_q_norm[l])
        k = _rmsnorm(a_k.reshape(bsz, t, SWA_KV_HEADS, HEAD_DIM), swa_k_norm[l])
        v = a_v.reshape(bsz, t, SWA_KV_HEADS, HEAD_DIM)
        p_swa = _sliding_window_attention(q, k, v, swa_sink[l], rel_bias) @ swa_w_o[l]

        q2 = _rmsnorm(f_q.reshape(bsz, t, FOX_HEADS, HEAD_DIM), fox_q_norm[l])
        k2 = _rmsnorm(f_k.reshape(bsz, t, FOX_HEADS, HEAD_DIM), fox_k_norm[l])
        v2 = f_v.reshape(bsz, t, FOX_HEADS, HEAD_DIM)
        log_f = jax.nn.log_sigmoid(f_f.astype(jnp.float32) + b_forget[l].astype(jnp.float32))
        p_fox = _forgetting_attention(q2, k2, v2, log_f) @ fox_w_o[l]

        g = jax.nn.sigmoid(z_gate).reshape(bsz, t, N_BRANCH, D_MODEL)
        merged = (g[:, :, 0] * p_conf + g[:, :, 1] * p_sc
                  + g[:, :, 2] * p_swa + g[:, :, 3] * p_fox)
        x = x + merged @ w_out[l]

        x = x + 0.5 * _swiglu(_rmsnorm(x, ffn2_norm[l]), ffn2_w_gate[l], ffn2_w_up[l], ffn2_w_down[l])
    return x
```

```python
from contextlib import ExitStack

import numpy as np

import concourse.bass as bass
import concourse.mybir as mybir
from concourse.bass_utils import run_bass_kernel_spmd

F32 = mybir.dt.float32
BF16 = mybir.dt.bfloat16
AF = mybir.ActivationFunctionType
ALU = mybir.AluOpType

D = 1024
DFF = 2816
NF = DFF // 128
L = 2
TT = 512
NS = TT // 128
EPS = 1e-6
NEG = -30000.0
N_CORES = 8

WIN_TILES = ([(0, 128), (256, 128), (128, 128), (384, 128), (512, 128), (640, 128),
              (768, 128), (1024, 128), (896, 128), (1152, 128)]
             + [(1280, 128), (1408, 128), (1536, 128), (1664, 128)]
             + [(1792, 128), (1920, 128), (2048, 128), (2176, 128), (2304, 128), (2432, 128), (2560, 4)]
             + [(2564 + i * 1024 + m * 128, 128) for m in range(8) for i in range(4)])
NWT = len(WIN_TILES)


class Res:
    __slots__ = ("name", "lw", "rd")

    def __init__(self, name=""):
        self.name = name
        self.lw = None
        self.rd = []


class Op:
    __slots__ = ("eng", "fn", "deps", "sig", "signo", "dma", "dj")


ENGS = ["pe", "act", "dve", "pool", "sp"]
NSC = 4
NSD = 8


class Sched:
    def __init__(self, nc, dry=False):
        self.nc = nc
        self.dry = dry
        self.ops = {e: [] for e in ENGS}

    def op(self, eng, fn, reads=(), writes=(), dma=False):
        if self.dry:
            return None
        o = Op()
        o.eng, o.fn, o.dma, o.sig = eng, fn, dma, False
        deps = {}

        def add(d, raw):
            if d is None:
                return
            if d.dma or dma or d.eng != eng or (raw and eng != "pe"):
                deps[id(d)] = d

        for r in reads:
            add(r.lw, True)
        for w in writes:
            add(w.lw, False)
            for x in w.rd:
                add(x, False)
        o.deps = list(deps.values())
        for d in o.deps:
            d.sig = True
        for r in reads:
            r.rd.append(o)
        for w in writes:
            w.lw = o
            w.rd = []
        self.ops[eng].append(o)
        return o

    def dma(self, eng, out, in_, reads=(), writes=()):
        return self.op(eng, lambda e: e.dma_start(out=out, in_=in_), reads=reads, writes=writes, dma=True)

    def emit(self):
        nc = self.nc
        for e in ENGS:
            c = 0
            j = 0
            for o in self.ops[e]:
                if o.dma:
                    o.dj = j
                    j += 1
                elif o.sig:
                    c += 1
                    o.signo = c
        with ExitStack() as st:
            csem = {e: [st.enter_context(nc.semaphore(f"c_{e}_{i}")) for i in range(NSC)]
                    for e in ENGS if e != "sp"}
            dsem = {e: [st.enter_context(nc.semaphore(f"d_{e}_{i}")) for i in range(NSD)]
                    for e in ENGS if e != "pe"}
            block = st.enter_context(nc.Block())

            def body(eng, e):
                cw = {}
                dw = {}

                def wait_dma(q, dj):
                    slot, val = dj % NSD, 16 * (dj // NSD + 1)
                    if dw.get((q, slot), 0) < val:
                        eng.wait_ge(dsem[q][slot], val)
                        dw[(q, slot)] = val

                for o in self.ops[e]:
                    for d in o.deps:
                        if d.dma:
                            wait_dma(d.eng, d.dj)
                        elif cw.get(d.eng, 0) < d.signo:
                            s = d.signo - 1
                            eng.wait_ge(csem[d.eng][s % NSC], s // NSC + 1)
                            cw[d.eng] = d.signo
                    if o.dma and o.dj >= NSD:
                        wait_dma(e, o.dj - NSD)
                    ins = o.fn(eng)
                    if o.dma:
                        ins.then_inc(dsem[e][o.dj % NSD], 16)
                    elif o.sig:
                        ins.then_inc(csem[e][(o.signo - 1) % NSC], 1)
                n = sum(1 for o in self.ops[e] if o.dma)
                for dj in range(max(0, n - NSD), n):
                    wait_dma(e, dj)

            names = {"pe": "tensor", "act": "scalar", "dve": "vector", "pool": "gpsimd", "sp": "sync"}
            for e in ENGS:
                getattr(block, names[e])(lambda eng, e=e: body(eng, e))


class Stream:
    def __init__(self, S, slots, res, plan):
        self.S, self.slots, self.res, self.plan = S, slots, res, plan
        self.rec = []
        self.pos = 0
        self.issued = 0

    def next(self, src):
        R = len(self.slots)
        if self.plan is None:
            self.rec.append(src)
            return self.slots[0], self.res[0]
        while self.issued < min(len(self.plan), self.pos + R):
            k = self.issued % R
            self.S.dma("pool", self.slots[k][:], self.plan[self.issued], writes=[self.res[k]])
            self.issued += 1
        k = self.pos % R
        self.pos += 1
        return self.slots[k], self.res[k]


class Bank:
    def __init__(self, ap, name):
        self.ap = ap
        self.res = Res(name)


class Rot:
    def __init__(self, items):
        self.items = items
        self.i = 0

    def get(self):
        b = self.items[self.i % len(self.items)]
        self.i += 1
        return b


def build(NTOK, stages=3):
    NT = NTOK // TT
    nc = bass.Bass("TRN2", target_bir_lowering=False)

    def din(name, shape, dt=F32):
        return nc.dram_tensor(name, list(shape), dt, kind="ExternalInput").ap()

    xin = din("xin", [NTOK, D])
    wg = din("wg", [L, 2, NF, 128, 8, 128])
    wu = din("wu", [L, 2, NF, 128, 8, 128])
    wd2 = din("wd2", [L, 2, 2, NF, 128, 512])
    win = din("win", [L, NWT, 128, 8, 128])
    wbr = din("wbr", [L, 8, 128, 12, 128])
    wo2 = din("wo2", [L, 2, 8, 128, 512])
    gains = din("gains", [L, 3, 128, D])
    cdw = din("cdw", [L, 128, 2, 31])
    cvec = din("cvec", [L, 128, 2, 3])
    scw = din("scw", [L, 128, 2, 3])
    qkg = din("qkg", [L, 64, 4])
    bfg = din("bfg", [L, 4, 1])
    sink = din("sink", [L, 65, 4])
    bm = din("bm", [128, 4, 256])
    identin = din("identin", [128, 128])
    yout = nc.dram_tensor("yout", [NTOK, D], F32, kind="ExternalOutput").ap()
    KF = nc.dram_tensor("KF", [L, 70, 4, NTOK], BF16, kind="Internal").ap()
    VF = nc.dram_tensor("VF", [L, NTOK, 4 * 65], BF16, kind="Internal").ap()
    QFs = nc.dram_tensor("QFs", [3, 4, TT], BF16, kind="Internal").ap()

    with ExitStack() as st:
        def sb(name, shape, dt):
            return st.enter_context(nc.sbuf_tensor(name, list(shape), dt))

        x = sb("x", [128, NS, D], F32)
        r_x = [Res(f"x{s}") for s in range(NS)]
        h = sb("h", [128, NS, D], BF16)
        r_h = [Res(f"h{s}") for s in range(NS)]
        hT = sb("hT", [128, 8, TT], BF16)
        r_hT = Res("hT")
        act = sb("act", [128, NF, TT], BF16)
        r_act = [Res(f"act{f}") for f in range(NF)]
        RA, RD, RR = 6, 8, 2
        wA = [sb(f"wA{i}", [128, 8, 128], BF16) for i in range(RA)]
        rA = [Res(f"wA{i}") for i in range(RA)]
        wD = [sb(f"wD{i}", [128, 512], BF16) for i in range(RD)]
        rD = [Res(f"wD{i}") for i in range(RD)]
        wR = [sb(f"wR{i}", [128, 12, 128], BF16) for i in range(RR)]
        rR = [Res(f"wR{i}") for i in range(RR)]
        gb = sb("gb", [128, D], F32)
        r_gb = Res("gb")
        ss = sb("ss", [128, 8], F32)
        r_ss = Res("ss")
        sg = [sb(f"sg{i}", [128, TT], F32) for i in range(2)]
        r_sg = [Res(f"sg{i}") for i in range(2)]
        ident = sb("ident", [128, 128], BF16)
        r_ident = Res("ident")
        ones = sb("ones", [128, 128], F32)
        r_ones = Res("ones")
        o64 = sb("o64", [64, 64], F32)
        r_o64 = Res("o64")
        uext = sb("uext", [128, 2, 30 + TT], F32)
        r_uext = Res("uext")
        ucar = [sb(f"ucar{l}", [128, 2, 30], F32) for l in range(L)]
        r_ucar = [Res(f"ucar{l}") for l in range(L)]
        sxext = sb("sxext", [128, 2, 2 + TT], F32)
        r_sxext = Res("sxext")
        sxcar = [sb(f"sxcar{l}", [128, 2, 2], F32) for l in range(L)]
        r_sxcar = [Res(f"sxcar{l}") for l in range(L)]
        sbt = sb("sbt", [128, 2, TT], F32)
        r_sbt = Res("sbt")
        cacc = sb("cacc", [128, 2, TT], F32)
        r_cacc = [Res("cacc0"), Res("cacc1")]
        csq = sb("csq", [128, 2, TT], F32)
        r_csq = Res("csq")
        lnm = sb("lnm", [128, TT], F32)
        r_lnm = Res("lnm")
        lnv = sb("lnv", [128, TT], F32)
        r_lnv = Res("lnv")
        uT = sb("uT", [128, 2, TT], BF16)
        r_uT = Res("uT")
        scT = sb("scT", [128, 2, TT], BF16)
        r_scT = Res("scT")
        cdw_t = sb("cdw_t", [128, 2, 31], F32)
        cvec_t = sb("cvec_t", [128, 2, 3], F32)
        scw_t = sb("scw_t", [128, 2, 3], F32)
        qkg_t = sb("qkg_t", [64, 4], F32)
        bfg_t = sb("bfg_t", [4, 1], F32)
        sink_t = sb("sink_t", [65, 4], F32)
        r_small = Res("small")
        bm_t = sb("bm_t", [128, 4, 256], F32)
        r_bm = Res("bm")
        qs = sb("qs", [64, 4, TT], BF16)
        r_qs = Res("qs")
        ksx = sb("ksx", [64, 2, 128 + TT], BF16)
        r_ksx = Res("ksx")
        kscar = [sb(f"kscar{l}", [64, 2, 128], BF16) for l in range(L)]
        r_kscar = [Res(f"kscar{l}") for l in range(L)]
        vsx = sb("vsx", [128, NS + 1, 2, 65], BF16)
        r_vsx = Res("vsx")
        vscar = [sb(f"vscar{l}", [128, 2, 65], BF16) for l in range(L)]
        r_vscar = [Res(f"vscar{l}") for l in range(L)]
        pT = [sb(f"pT{i}", [128, TT], BF16) for i in range(4)]
        r_pT = [Res(f"pT{i}") for i in range(4)]
        oa = sb("oa", [65, TT], F32)
        r_oa = Res("oa")
        rden = sb("rden", [65, TT], F32)
        r_rden = Res("rden")
        onT = [sb(f"onT{i}", [64, 4, TT], BF16) for i in range(2)]
        r_onT = [Res("onT0"), Res("onT1")]
        qf = sb("qf", [70, 4, TT], BF16)
        r_qf = Res("qf")
        kf = sb("kf", [64, 4, TT], BF16)
        r_kf = Res("kf")
        vf = sb("vf", [128, NS, 4, 65], BF16)
        r_vf = Res("vf")
        kblk = [sb(f"kblk{i}", [70, 4, TT], BF16) for i in range(2)]
        r_kblk = [Res(f"kblk{i}") for i in range(2)]
        vblk = [sb(f"vblk{i}", [128, NS, 4 * 65], BF16) for i in range(2)]
        r_vblk = [Res(f"vblk{i}") for i in range(2)]
        fe = sb("fe", [4, TT], F32)
        r_fe = Res("fe")
        fG = sb("fG", [4, TT], F32)
        r_fG = Res("fG")
        fones = sb("fones", [4, TT], F32)
        r_fones = Res("fones")
        gsp = sb("gsp", [4, 3, TT], BF16)
        r_gs = Res("gs")
        monesb = sb("monesb", [4, 3, TT], BF16)
        r_monesb = Res("monesb")
        fcar = [sb(f"fcar{l}", [4, 1], F32) for l in range(L)]
        r_fcar = [Res(f"fcar{l}") for l in range(L)]
        macc = sb("macc", [128, TT], F32)
        r_macc = Res("macc")
        mtmp = [sb(f"mtmp{i}", [128, TT], F32) for i in range(2)]
        r_mtmp = [Res(f"mtmp{i}") for i in range(2)]
        mT = sb("mT", [128, 8, TT], BF16)
        r_mT = Res("mT")
        banks = [Bank(st.enter_context(nc.psum_tensor(f"ps{i}", [128, 512], F32)), f"ps{i}") for i in range(7)]
        ptb = Bank(st.enter_context(nc.psum_tensor("ptb", [128, 1024], BF16)), "ptb")

        def program(S, plans):
            WA = Stream(S, wA, rA, plans[0])
            WD = Stream(S, wD, rD, plans[1])
            WR = Stream(S, wR, rR, plans[2])
            rot = Rot(banks)
            rot_hi = Rot(banks[4:7])
            RKF = [[Res(f"KF{l}_{j}") for j in range(NT)] for l in range(L)]
            r_QFs = Res("QFs")
            kbi = [0]
            pti = [0]
            sgi = [0]

            def ACT(fn, reads, writes):
                S.op("act", fn, reads, writes)

            def DVE(fn, reads, writes):
                S.op("dve", fn, reads, writes)

            def PE(fn, reads, writes):
                S.op("pe", fn, reads, writes)

            def POOL(fn, reads, writes):
                S.op("pool", fn, reads, writes)

            S.dma("pool", ident[:], identin, writes=[r_ident])
            DVE(lambda e: e.memset(ones[:], 1.0), [], [r_ones])
            DVE(lambda e: e.memset(o64[:], 1.0 / 64.0), [], [r_o64])
            DVE(lambda e: e.memset(fones[:], 1.0), [], [r_fones])
            DVE(lambda e: e.memset(monesb[:], -1.0), [], [r_monesb])
            DVE(lambda e: e.memset(qf[:], 1.0), [], [r_qf])
            S.dma("sp", bm_t[:], bm, writes=[r_bm])
            DVE(lambda e: e.memset(vsx[:], 1.0), [], [r_vsx])
            DVE(lambda e: e.memset(vf[:], 1.0), [], [r_vf])
            for l in range(L):
                DVE(lambda e, l=l: e.memset(fcar[l][:], 0.0), [], [r_fcar[l]])

            def rmsnorm_hT(gain_src):
                S.dma("sp", gb[:], gain_src, writes=[r_gb])
                for s in range(NS):
                    ACT(lambda e, s=s: e.activation(out=h[:, s, :], in_=x[:, s, :], func=AF.Square,
                                                    accum_out=ss[:, s:s + 1]),
                        [r_x[s]], [r_h[s], r_ss])
                ACT(lambda e: e.activation(out=ss[:, 4:8], in_=ss[:, 0:4], func=AF.Ln, scale=1.0 / D, bias=EPS),
                    [r_ss], [r_ss])
                ACT(lambda e: e.activation(out=ss[:, 4:8], in_=ss[:, 4:8], func=AF.Exp, scale=-0.5),
                    [r_ss], [r_ss])
                for s in range(NS):
                    DVE(lambda e, s=s: e.scalar_tensor_tensor(out=h[:, s, :], in0=x[:, s, :],
                                                              scalar=ss[:, 4 + s:5 + s], in1=gb[:],
                                                              op0=ALU.mult, op1=ALU.mult),
                        [r_x[s], r_ss, r_gb], [r_h[s]])
                for s in range(NS):
                    for c in range(8):
                        PE(lambda e, s=s, c=c: e.transpose(out=ptb.ap[:, c * 128:(c + 1) * 128],
                                                           in_=h[:, s, c * 128:(c + 1) * 128], identity=ident[:]),
                           [r_h[s], r_ident], [ptb.res])
                    ACT(lambda e, s=s: e.copy(out=hT[:, :, s * 128:(s + 1) * 128],
                                              in_=ptb.ap[:, :].rearrange("p (c t) -> p c t", c=8)),
                        [ptb.res], [r_hT])

            def resid_proj(src, src_res_of, nk, tile_src, scale):
                for n in range(2):
                    accs = [rot.get() for _ in range(NS)]
                    for kc in range(nk):
                        t, r = WD.next(tile_src(n, kc))
                        for s in range(NS):
                            PE(lambda e, s=s, kc=kc, t=t, a=accs[s]: e.matmul(
                                a.ap[:, :], lhsT=src[:, kc, s * 128:(s + 1) * 128], rhs=t[:],
                                start=(kc == 0), stop=(kc == nk - 1)), [src_res_of(kc), r], [accs[s].res])
                    for s in range(NS):
                        DVE(lambda e, s=s, n=n, a=accs[s]: e.scalar_tensor_tensor(
                            out=x[:, s, n * 512:(n + 1) * 512], in0=a.ap[:, :], scalar=scale,
                            in1=x[:, s, n * 512:(n + 1) * 512], op0=ALU.mult, op1=ALU.add),
                            [accs[s].res, r_x[s]], [r_x[s]])

            def ffn(l, k):
                rmsnorm_hT(gains[l, 2 * k])
                for f in range(NF):
                    tg, rg = WA.next(wg[l, k, f])
                    bg = rot.get()
                    for c in range(8):
                        PE(lambda e, c=c, tg=tg, bg=bg: e.matmul(bg.ap[:, :], lhsT=tg[:, c, :], rhs=hT[:, c, :],
                                                                 start=(c == 0), stop=(c == 7)),
                           [rg, r_hT], [bg.res])
                    tu, ru = WA.next(wu[l, k, f])
                    bu = rot.get()
                    for c in range(8):
                        PE(lambda e, c=c, tu=tu, bu=bu: e.matmul(bu.ap[:, :], lhsT=tu[:, c, :], rhs=hT[:, c, :],
                                                                 start=(c == 0), stop=(c == 7)),
                           [ru, r_hT], [bu.res])
                    i = sgi[0] % 2
                    sgi[0] += 1
                    ACT(lambda e, i=i, bg=bg: e.activation(out=sg[i][:], in_=bg.ap[:, :], func=AF.Silu),
                        [bg.res], [r_sg[i]])
                    DVE(lambda e, i=i, f=f, bu=bu: e.tensor_tensor(out=act[:, f, :], in0=sg[i][:], in1=bu.ap[:, :],
                                                                   op=ALU.mult),
                        [r_sg[i], bu.res], [r_act[f]])
                resid_proj(act, lambda f: r_act[f], NF, lambda n, f: wd2[l, k, n, f], 0.5)

            def proj128(l, wi):
                t, r = WA.next(win[l, wi])
                b = rot.get()
                for c in range(8):
                    PE(lambda e, c=c, t=t, b=b: e.matmul(b.ap[:, :], lhsT=t[:, c, :], rhs=hT[:, c, :],
                                                         start=(c == 0), stop=(c == 7)), [r, r_hT], [b.res])
                return b

            def head_norm(src_ap, src_res, gcol, out_ap, out_res):
                ACT(lambda e: e.activation(out=lnm[0:64, :], in_=src_ap, func=AF.Square), [src_res], [r_lnm])
                b = rot.get()
                PE(lambda e, b=b: e.matmul(b.ap[0:64, :], lhsT=o64[:], rhs=lnm[0:64, :], start=True, stop=True),
                   [r_lnm, r_o64], [b.res])
                ACT(lambda e, b=b: e.activation(out=lnv[0:64, :], in_=b.ap[0:64, :], func=AF.Ln, bias=EPS),
                    [b.res], [r_lnv])
                ACT(lambda e: e.activation(out=lnv[0:64, :], in_=lnv[0:64, :], func=AF.Exp, scale=-0.5),
                    [r_lnv], [r_lnv])
                DVE(lambda e: e.scalar_tensor_tensor(out=out_ap, in0=src_ap, scalar=qkg_t[:, gcol:gcol + 1],
                                                     in1=lnv[0:64, :], op0=ALU.mult, op1=ALU.mult),
                    [src_res, r_lnv, r_small], [out_res])

            def heads_tile(l, wi, specs):
                t, r = WA.next(win[l, wi])
                for hh in range(2):
                    b = rot.get()
                    for c in range(8):
                        PE(lambda e, c=c, t=t, b=b, hh=hh: e.matmul(
                            b.ap[0:64, :], lhsT=t[:, c, hh * 64:(hh + 1) * 64], rhs=hT[:, c, :],
                            start=(c == 0), stop=(c == 7)), [r, r_hT], [b.res])
                    gcol, out_ap, out_res = specs[hh]
                    head_norm(b.ap[0:64, :], b.res, gcol, out_ap, out_res)

            def v_tile(l, wi, dst, dst_res, chunk_off, h0):
                t, r = WA.next(win[l, wi])
                for s in range(NS):
                    b = rot.get()
                    for c in range(8):
                        PE(lambda e, c=c, t=t, b=b, s=s: e.matmul(
                            b.ap[:, 0:128], lhsT=hT[:, c, s * 128:(s + 1) * 128], rhs=t[:, c, :],
                            start=(c == 0), stop=(c == 7)), [r, r_hT], [b.res])
                    ACT(lambda e, b=b, s=s: e.copy(out=dst[:, chunk_off + s, h0:h0 + 2, 0:64],
                                                   in_=b.ap[:, 0:128].rearrange("p (h d) -> p h d", h=2)),
                        [b.res], [dst_res])

            def attn_norm(acc, extra_den, out_ap, out_res):
                ACT(lambda e: e.copy(out=oa[:], in_=acc.ap[0:65, :]), [acc.res], [r_oa])
                if extra_den is not None:
                    DVE(lambda e: e.tensor_scalar(out=oa[64:65, :], in0=oa[64:65, :], scalar1=extra_den,
                                                  scalar2=None, op0=ALU.add), [r_oa, r_small], [r_oa])
                ACT(lambda e: e.activation(out=rden[64:65, :], in_=oa[64:65, :], func=AF.Ln), [r_oa], [r_rden])
                ACT(lambda e: e.activation(out=rden[64:65, :], in_=rden[64:65, :], func=AF.Exp, scale=-1.0),
                    [r_rden], [r_rden])
                b = rot_hi.get()
                PE(lambda e, b=b: e.matmul(b.ap[0:64, :], lhsT=ones[64:65, 0:64], rhs=rden[64:65, :],
                                           start=True, stop=True), [r_rden, r_ones], [b.res])
                DVE(lambda e, b=b: e.tensor_tensor(out=out_ap, in0=oa[0:64, :], in1=b.ap[0:64, :], op=ALU.mult),
                    [r_oa, b.res], [out_res])

            def mixer(l, j):
                t0 = j * TT
                S.dma("sp", cdw_t[:], cdw[l], writes=[r_small])
                S.dma("sp", cvec_t[:], cvec[l], writes=[r_small])
                S.dma("sp", scw_t[:], scw[l], writes=[r_small])
                S.dma("sp", qkg_t[:], qkg[l], writes=[r_small])
                S.dma("sp", bfg_t[:], bfg[l], writes=[r_small])
                S.dma("sp", sink_t[:], sink[l], writes=[r_small])
                DVE(lambda e: e.tensor_scalar(out=qkg_t[:, 0:1], in0=qkg_t[:, 0:1], scalar1=0.125, scalar2=None,
                                              op0=ALU.mult), [r_small], [r_small])
                DVE(lambda e: e.tensor_scalar(out=qkg_t[:, 2:3], in0=qkg_t[:, 2:3], scalar1=0.125, scalar2=None,
                                              op0=ALU.mult), [r_small], [r_small])
                ACT(lambda e: e.activation(out=sink_t[64:65, :], in_=sink_t[64:65, :], func=AF.Exp),
                    [r_small], [r_small])
                DVE(lambda e: e.tensor_scalar(out=bfg_t[:], in0=bfg_t[:], scalar1=-1.0, scalar2=None,
                                              op0=ALU.mult), [r_small], [r_small])
                rmsnorm_hT(gains[l, 1])

                if j == 0:
                    DVE(lambda e: e.memset(uext[:, :, 0:30], 0.0), [], [r_uext])
                    DVE(lambda e: e.memset(sxext[:, :, 0:2], 0.0), [], [r_sxext])
                else:
                    DVE(lambda e: e.tensor_copy(out=uext[:, :, 0:30], in_=ucar[l][:]), [r_ucar[l]], [r_uext])
                    DVE(lambda e: e.tensor_copy(out=sxext[:, :, 0:2], in_=sxcar[l][:]), [r_sxcar[l]], [r_sxext])
                for ch in range(2):
                    ba = proj128(l, 2 * ch)
                    bb = proj128(l, 2 * ch + 1)
                    i = sgi[0] % 2
                    sgi[0] += 1
                    ACT(lambda e, i=i, bb=bb: e.activation(out=sg[i][:], in_=bb.ap[:, :], func=AF.Sigmoid),
                        [bb.res], [r_sg[i]])
                    DVE(lambda e, i=i, ba=ba, ch=ch: e.tensor_tensor(out=uext[:, ch, 30:30 + TT], in0=sg[i][:],
                                                                     in1=ba.ap[:, :], op=ALU.mult),
                        [r_sg[i], ba.res], [r_uext])
                for ch in range(2):
                    b = proj128(l, 4 + ch)
                    ACT(lambda e, b=b, ch=ch: e.copy(out=sbt[:, ch, :], in_=b.ap[:, :]), [b.res], [r_sbt])
                for ch in range(2):
                    bc = proj128(l, 6 + 2 * ch)
                    bx = proj128(l, 7 + 2 * ch)
                    i = sgi[0] % 2
                    sgi[0] += 1
                    ACT(lambda e, i=i, bc=bc: e.copy(out=sg[i][:], in_=bc.ap[:, :]), [bc.res], [r_sg[i]])
                    DVE(lambda e, i=i, bx=bx, ch=ch: e.tensor_tensor(out=sxext[:, ch, 2:2 + TT], in0=sg[i][:],
                                                                     in1=bx.ap[:, :], op=ALU.mult),
                        [r_sg[i], bx.res], [r_sxext])
                DVE(lambda e: e.tensor_copy(out=ucar[l][:], in_=uext[:, :, TT:TT + 30]), [r_uext], [r_ucar[l]])
                DVE(lambda e: e.tensor_copy(out=sxcar[l][:], in_=sxext[:, :, TT:TT + 2]), [r_sxext], [r_sxcar[l]])

                for kk in range(31):
                    for ch in range(2):
                        if kk == 0:
                            DVE(lambda e, ch=ch: e.tensor_scalar(
                                out=cacc[:, ch, :], in0=uext[:, ch, 0:TT], scalar1=cdw_t[:, ch, 0:1],
                                scalar2=cvec_t[:, ch, 0:1], op0=ALU.mult, op1=ALU.add),
                                [r_uext, r_small], [r_cacc[ch]])
                        else:
                            DVE(lambda e, ch=ch, kk=kk: e.scalar_tensor_tensor(
                                out=cacc[:, ch, :], in0=uext[:, ch, kk:kk + TT], scalar=cdw_t[:, ch, kk:kk + 1],
                                in1=cacc[:, ch, :], op0=ALU.mult, op1=ALU.add),
                                [r_uext, r_small, r_cacc[ch]], [r_cacc[ch]])
                for ch in range(2):
                    ACT(lambda e, ch=ch: e.activation(out=csq[:, ch, :], in_=cacc[:, ch, :], func=AF.Square),
                        [r_cacc[ch]], [r_csq])
                b1 = rot.get()
                for ch in range(2):
                    PE(lambda e, ch=ch, b1=b1: e.matmul(b1.ap[:, :], lhsT=ones[:], rhs=cacc[:, ch, :],
                                                        start=(ch == 0), stop=(ch == 1)),
                       [r_cacc[ch], r_ones], [b1.res])
                b2 = rot.get()
                for ch in range(2):
                    PE(lambda e, ch=ch, b2=b2: e.matmul(b2.ap[:, :], lhsT=ones[:], rhs=csq[:, ch, :],
                                                        start=(ch == 0), stop=(ch == 1)),
                       [r_csq, r_ones], [b2.res])
                DVE(lambda e: e.tensor_scalar(out=lnm[:], in0=b1.ap[:, :], scalar1=1.0 / 256.0, scalar2=None,
                                              op0=ALU.mult), [b1.res], [r_lnm])
                DVE(lambda e: e.tensor_tensor(out=lnv[:], in0=lnm[:], in1=lnm[:], op=ALU.mult), [r_lnm], [r_lnv])
                DVE(lambda e: e.scalar_tensor_tensor(out=lnv[:], in0=b2.ap[:, :], scalar=1.0 / 256.0, in1=lnv[:],
                                                     op0=ALU.mult, op1=ALU.subtract), [b2.res, r_lnv], [r_lnv])
                DVE(lambda e: e.tensor_scalar(out=lnv[:], in0=lnv[:], scalar1=0.0, scalar2=None, op0=ALU.max),
                    [r_lnv], [r_lnv])
                ACT(lambda e: e.activation(out=lnv[:], in_=lnv[:], func=AF.Ln, bias=EPS), [r_lnv], [r_lnv])
                ACT(lambda e: e.activation(out=lnv[:], in_=lnv[:], func=AF.Exp, scale=-0.5), [r_lnv], [r_lnv])
                for ch in range(2):
                    DVE(lambda e, ch=ch: e.tensor_tensor(out=cacc[:, ch, :], in0=cacc[:, ch, :], in1=lnm[:],
                                                         op=ALU.subtract), [r_cacc[ch], r_lnm], [r_cacc[ch]])
                    DVE(lambda e, ch=ch: e.tensor_tensor(out=cacc[:, ch, :], in0=cacc[:, ch, :], in1=lnv[:],
                                                         op=ALU.mult), [r_cacc[ch], r_lnv], [r_cacc[ch]])
                    ACT(lambda e, ch=ch: e.activation(out=uT[:, ch, :], in_=cacc[:, ch, :], func=AF.Silu,
                                                      scale=cvec_t[:, ch, 1:2], bias=cvec_t[:, ch, 2:3]),
                        [r_cacc[ch], r_small], [r_uT])
                for ch in range(2):
                    DVE(lambda e, ch=ch: e.tensor_scalar(out=csq[:, ch, :], in0=sxext[:, ch, 0:TT],
                                                         scalar1=scw_t[:, ch, 0:1], scalar2=None, op0=ALU.mult),
                        [r_sxext, r_small], [r_csq])
                    for kk in (1, 2):
                        DVE(lambda e, ch=ch, kk=kk: e.scalar_tensor_tensor(
                            out=csq[:, ch, :], in0=sxext[:, ch, kk:kk + TT], scalar=scw_t[:, ch, kk:kk + 1],
                            in1=csq[:, ch, :], op0=ALU.mult, op1=ALU.add), [r_sxext, r_small, r_csq], [r_csq])
                    DVE(lambda e, ch=ch: e.tensor_tensor(out=scT[:, ch, :], in0=csq[:, ch, :], in1=sbt[:, ch, :],
                                                         op=ALU.mult), [r_csq, r_sbt], [r_scT])

                if j > 0:
                    DVE(lambda e: e.tensor_copy(out=ksx[:, :, 0:128], in_=kscar[l][:]), [r_kscar[l]], [r_ksx])
                    DVE(lambda e: e.tensor_copy(out=vsx[:, 0, :, :], in_=vscar[l][:]), [r_vscar[l]], [r_vsx])
                heads_tile(l, 10, [(0, qs[:, 0, :], r_qs), (0, qs[:, 1, :], r_qs)])
                heads_tile(l, 11, [(0, qs[:, 2, :], r_qs), (0, qs[:, 3, :], r_qs)])
                heads_tile(l, 12, [(1, ksx[:, 0, 128:128 + TT], r_ksx), (1, ksx[:, 1, 128:128 + TT], r_ksx)])
                v_tile(l, 13, vsx, r_vsx, 1, 0)
                DVE(lambda e: e.tensor_copy(out=kscar[l][:], in_=ksx[:, :, TT:TT + 128]), [r_ksx], [r_kscar[l]])
                DVE(lambda e: e.tensor_copy(out=vscar[l][:], in_=vsx[:, NS, :, :]), [r_vsx], [r_vscar[l]])
                for hq in range(4):
                    hk = hq // 2
                    acc = banks[hq % 4]
                    for pr in range(2):
                        bS = rot_hi.get()
                        i = sgi[0] % 2
                        sgi[0] += 1
                        for s2 in range(2):
                            s = 2 * pr + s2
                            for part in range(2):
                                if j == 0 and s == 0 and part == 0:
                                    continue
                                PE(lambda e, s=s, s2=s2, part=part, bS=bS, hk=hk, hq=hq: e.matmul(
                                    bS.ap[:, s2 * 256 + part * 128: s2 * 256 + part * 128 + 128],
                                    lhsT=ksx[:, hk, (s + part) * 128:(s + part + 1) * 128],
                                    rhs=qs[:, hq, s * 128:(s + 1) * 128], start=True, stop=True),
                                   [r_ksx, r_qs], [bS.res])
                            DVE(lambda e, bS=bS, i=i, hq=hq, s2=s2: e.tensor_tensor(
                                out=sg[i][:, s2 * 256:(s2 + 1) * 256], in0=bS.ap[:, s2 * 256:(s2 + 1) * 256],
                                in1=bm_t[:, hq, :], op=ALU.add), [bS.res, r_bm], [r_sg[i]])
                        pi = pti[0] % 4
                        pti[0] += 1
                        ACT(lambda e, i=i, pi=pi: e.activation(out=pT[pi][:], in_=sg[i][:], func=AF.Exp),
                            [r_sg[i]], [r_pT[pi]])
                        for s2 in range(2):
                            s = 2 * pr + s2
                            first = True
                            for part in range(2):
                                if j == 0 and s == 0 and part == 0:
                                    continue
                                PE(lambda e, s=s, s2=s2, part=part, pi=pi, hk=hk, acc=acc, first=first: e.matmul(
                                    acc.ap[0:65, s * 128:(s + 1) * 128], lhsT=vsx[:, s + part, hk, :],
                                    rhs=pT[pi][:, s2 * 256 + part * 128: s2 * 256 + part * 128 + 128],
                                    start=first, stop=(part == 1)), [r_vsx, r_pT[pi]], [acc.res])
                                first = False
                    attn_norm(acc, sink_t[64:65, hq:hq + 1], onT[0][:, hq, :], r_onT[0])

                heads_tile(l, 14, [(2, qf[0:64, 0, :], r_qf), (2, qf[0:64, 1, :], r_qf)])
                heads_tile(l, 15, [(2, qf[0:64, 2, :], r_qf), (2, qf[0:64, 3, :], r_qf)])
                heads_tile(l, 16, [(3, kf[:, 0, :], r_kf), (3, kf[:, 1, :], r_kf)])
                heads_tile(l, 17, [(3, kf[:, 2, :], r_kf), (3, kf[:, 3, :], r_kf)])
                v_tile(l, 18, vf, r_vf, 0, 0)
                v_tile(l, 19, vf, r_vf, 0, 2)
                t, r = WA.next(win[l, 20])
                bF = rot.get()
                for c in range(8):
                    PE(lambda e, c=c, t=t, bF=bF: e.matmul(bF.ap[0:4, :], lhsT=t[:, c, 0:4], rhs=hT[:, c, :],
                                                           start=(c == 0), stop=(c == 7)), [r, r_hT], [bF.res])
                ACT(lambda e: e.activation(out=fe[:], in_=bF.ap[0:4, :], func=AF.Exp, scale=-1.0,
                                           bias=bfg_t[:, 0:1]), [bF.res, r_small], [r_fe])
                ACT(lambda e: e.activation(out=fe[:], in_=fe[:], func=AF.Ln, bias=1.0), [r_fe], [r_fe])
                DVE(lambda e: e.tensor_tensor_scan(out=fG[:], data0=fones[:], data1=fe[:], initial=fcar[l][:, 0:1],
                                                   op0=ALU.mult, op1=ALU.add), [r_fe, r_fones, r_fcar[l]], [r_fG])
                DVE(lambda e: e.tensor_copy(out=fcar[l][:], in_=fG[:, TT - 1:TT]), [r_fG], [r_fcar[l]])
                DVE(lambda e: e.tensor_copy(out=gsp[:, 0, :], in_=fG[:]), [r_fG], [r_gs])
                DVE(lambda e: e.tensor_tensor(out=fe[:], in0=fG[:], in1=gsp[:, 0, :], op=ALU.subtract),
                    [r_fG, r_gs], [r_fe])
                DVE(lambda e: e.tensor_copy(out=gsp[:, 1, :], in_=fe[:]), [r_fe], [r_gs])
                DVE(lambda e: e.tensor_tensor(out=fe[:], in0=fe[:], in1=gsp[:, 1, :], op=ALU.subtract),
                    [r_fe, r_gs], [r_fe])
                DVE(lambda e: e.tensor_copy(out=gsp[:, 2, :], in_=fe[:]), [r_fe], [r_gs])
                rKF = RKF[l][j]
                S.dma("sp", KF[l, 0:64, :, t0:t0 + TT], kf[:], reads=[r_kf], writes=[rKF])
                for hh in range(4):
                    S.dma("sp", KF[l, 64:67, hh, t0:t0 + TT].rearrange("(o a) t -> o a t", o=1),
                          monesb[hh:hh + 1, :, :], reads=[r_monesb], writes=[rKF])
                    S.dma("sp", KF[l, 67:70, hh, t0:t0 + TT].rearrange("(o a) t -> o a t", o=1),
                          gsp[hh:hh + 1, :, :], reads=[r_gs], writes=[rKF])
                S.dma("sp", VF[l, t0:t0 + TT, :].rearrange("(s p) e -> p s e", p=128),
                      vf[:].rearrange("p s h e -> p s (h e)"), reads=[r_vf], writes=[rKF])
                for hh in range(4):
                    S.dma("sp", QFs[:, hh, :].rearrange("(o a) t -> o a t", o=1), gsp[hh:hh + 1, :, :],
                          reads=[r_gs], writes=[r_QFs])
                S.dma("sp", qf[64:67, :, :], QFs, reads=[r_QFs], writes=[r_qf])
                accs = banks[0:4]
                for kb in range(j + 1):
                    i = kbi[0] % 2
                    kbi[0] += 1
                    S.dma("sp", kblk[i][:], KF[l, :, :, kb * TT:(kb + 1) * TT], reads=[RKF[l][kb]],
                          writes=[r_kblk[i]])
                    S.dma("sp", vblk[i][:], VF[l, kb * TT:(kb + 1) * TT, :].rearrange("(s p) e -> p s e", p=128),
                          reads=[RKF[l][kb]], writes=[r_vblk[i]])
                    for hh in range(4):
                        for c in range(NS):
                            bS = rot_hi.get()
                            PE(lambda e, i=i, hh=hh, c=c, bS=bS: e.matmul(
                                bS.ap[:, :], lhsT=kblk[i][:, hh, c * 128:(c + 1) * 128], rhs=qf[:, hh, :],
                                start=True, stop=True), [r_kblk[i], r_qf], [bS.res])
                            pi = pti[0] % 4
                            pti[0] += 1
                            ACT(lambda e, pi=pi, bS=bS: e.activation(out=pT[pi][:], in_=bS.ap[:, :], func=AF.Exp),
                                [bS.res], [r_pT[pi]])
                            if kb == j:
                                POOL(lambda e, pi=pi, c=c: e.affine_select(
                                    out=pT[pi][:], in_=pT[pi][:], pattern=[[1, TT]], compare_op=ALU.is_ge,
                                    fill=0.0, base=-c * 128, channel_multiplier=-1), [r_pT[pi]], [r_pT[pi]])
                            PE(lambda e, i=i, hh=hh, c=c, pi=pi, kb=kb: e.matmul(
                                accs[hh].ap[0:65, :], lhsT=vblk[i][:, c, hh * 65:(hh + 1) * 65], rhs=pT[pi][:],
                                start=(kb == 0 and c == 0), stop=(kb == j and c == NS - 1)),
                               [r_vblk[i], r_pT[pi]], [accs[hh].res])
                for hh in range(4):
                    attn_norm(accs[hh], None, onT[1][:, hh, :], r_onT[1])

                brsrc = [(uT, r_uT), (scT, r_scT)]
                for m in range(8):
                    tr, rr = WR.next(wbr[l, m])
                    for br in range(4):
                        bp = rot.get()
                        if br < 2:
                            src, rs = brsrc[br]
                            for ch in range(2):
                                PE(lambda e, br=br, ch=ch, bp=bp, src=src, tr=tr: e.matmul(
                                    bp.ap[:, :], lhsT=tr[:, 2 * br + ch, :], rhs=src[:, ch, :],
                                    start=(ch == 0), stop=(ch == 1)), [rr, rs], [bp.res])
                        else:
                            a = br - 2
                            for hh in range(4):
                                PE(lambda e, a=a, hh=hh, bp=bp, tr=tr: e.matmul(
                                    bp.ap[:, :], lhsT=tr[0:64, 4 + 4 * a + hh, :],
                                    rhs=onT[a][:, hh, :], start=(hh == 0), stop=(hh == 3)),
                                   [rr, r_onT[a]], [bp.res])
                        bgt = proj128(l, 21 + 4 * m + br)
                        i = sgi[0] % 2
                        sgi[0] += 1
                        ACT(lambda e, i=i, bgt=bgt: e.activation(out=sg[i][:], in_=bgt.ap[:, :], func=AF.Sigmoid),
                            [bgt.res], [r_sg[i]])
                        if br == 0:
                            DVE(lambda e, i=i, bp=bp: e.tensor_tensor(out=macc[:], in0=sg[i][:], in1=bp.ap[:, :],
                                                                      op=ALU.mult), [r_sg[i], bp.res], [r_macc])
                        else:
                            k2 = br % 2
                            DVE(lambda e, i=i, bp=bp, k2=k2: e.tensor_tensor(out=mtmp[k2][:], in0=sg[i][:],
                                                                             in1=bp.ap[:, :], op=ALU.mult),
                                [r_sg[i], bp.res], [r_mtmp[k2]])
                            if br < 3:
                                POOL(lambda e, k2=k2: e.tensor_tensor(out=macc[:], in0=macc[:], in1=mtmp[k2][:],
                                                                      op=ALU.add), [r_macc, r_mtmp[k2]], [r_macc])
                            else:
                                POOL(lambda e, k2=k2, m=m: e.tensor_tensor(out=mT[:, m, :], in0=macc[:],
                                                                           in1=mtmp[k2][:], op=ALU.add),
                                     [r_macc, r_mtmp[k2]], [r_mT])
                resid_proj(mT, lambda m: r_mT, 8, lambda n, m: wo2[l, n, m], 1.0)

            for j in range(NT):
                t0 = j * TT
                S.dma("sp", x[:], xin[t0:t0 + TT, :].rearrange("(s p) d -> p s d", p=128), writes=r_x)
                for l in range(L):
                    if stages >= 1:
                        ffn(l, 0)
                    if stages >= 2:
                        mixer(l, j)
                    if stages >= 3:
                        ffn(l, 1)
                S.dma("sp", yout[t0:t0 + TT, :].rearrange("(s p) d -> p s d", p=128), x[:], reads=r_x)
            return [WA.rec, WD.rec, WR.rec]

        plans = program(Sched(nc, dry=True), [None, None, None])
        S = Sched(nc)
        program(S, plans)
        S.emit()
    return nc


def _t5_bucket(dist):
    max_exact = 16
    d = np.maximum(dist, 1).astype(np.float32)
    large = max_exact + (np.log(d / np.float32(max_exact)) / np.float32(np.log(128 / max_exact))
                         * np.float32(32 - max_exact)).astype(np.int32)
    large = np.minimum(large, 31)
    return np.where(dist < max_exact, dist, large)


def host_prep(inp):
    f = lambda a: np.ascontiguousarray(np.asarray(a, dtype=np.float32))

    def wtile(w, col0, n):
        out = np.zeros((128, 8, 128), np.float32)
        out[:, :, :n] = w[:, col0:col0 + n].reshape(8, 128, n).transpose(1, 0, 2)
        return out

    shared = {}
    gates = [np.asarray(inp["ffn1_w_gate"]), np.asarray(inp["ffn2_w_gate"])]
    ups = [np.asarray(inp["ffn1_w_up"]), np.asarray(inp["ffn2_w_up"])]
    downs = [np.asarray(inp["ffn1_w_down"]), np.asarray(inp["ffn2_w_down"])]

    def ftile(w):
        return w.reshape(8, 128, NF, 128).transpose(2, 1, 0, 3)

    def dtile(w):
        return w.reshape(NF, 128, 2, 512).transpose(2, 0, 1, 3)

    shared["wg"] = f(np.stack([np.stack([ftile(gates[k][l]) for k in range(2)]) for l in range(L)]))
    shared["wu"] = f(np.stack([np.stack([ftile(ups[k][l]) for k in range(2)]) for l in range(L)]))
    shared["wd2"] = f(np.stack([np.stack([dtile(downs[k][l]) for k in range(2)]) for l in range(L)]))
    w_in = np.asarray(inp["w_in"])
    shared["win"] = f(np.stack([np.stack([wtile(w_in[l], c0, n) for (c0, n) in WIN_TILES]) for l in range(L)]))
    cwo, swo = np.asarray(inp["conf_w_out"]), np.asarray(inp["sc_w_out"])
    awo, fwo = np.asarray(inp["swa_w_o"]), np.asarray(inp["fox_w_o"])
    wbr = np.zeros((L, 8, 128, 12, 128), np.float32)
    for l in range(L):
        for m in range(8):
            cs = slice(m * 128, (m + 1) * 128)
            for ch in range(2):
                wbr[l, m, :, ch, :] = cwo[l][ch * 128:(ch + 1) * 128, cs]
                wbr[l, m, :, 2 + ch, :] = swo[l][ch * 128:(ch + 1) * 128, cs]
            for hh in range(4):
                wbr[l, m, 0:64, 4 + hh, :] = awo[l][hh * 64:(hh + 1) * 64, cs]
                wbr[l, m, 0:64, 8 + hh, :] = fwo[l][hh * 64:(hh + 1) * 64, cs]
    shared["wbr"] = wbr
    shared["wo2"] = f(np.stack([np.asarray(inp["w_out"])[l].reshape(8, 128, 2, 512).transpose(2, 0, 1, 3)
                                for l in range(L)]))
    gn = [np.asarray(inp["ffn1_norm"]), np.asarray(inp["mix_norm"]), np.asarray(inp["ffn2_norm"])]
    shared["gains"] = f(np.stack([np.stack([np.broadcast_to(gn[i][l][None, :], (128, D)) for i in range(3)])
                                  for l in range(L)]))
    shared["cdw"] = f(np.stack([np.asarray(inp["conf_dw"])[l].T.reshape(2, 128, 31).transpose(1, 0, 2)
                                for l in range(L)]))
    cv = [np.asarray(inp["conf_dw_b"]), np.asarray(inp["conf_ln_g"]), np.asarray(inp["conf_ln_b"])]
    shared["cvec"] = f(np.stack([np.stack([cv[i][l].reshape(2, 128).T for i in range(3)], axis=-1)
                                 for l in range(L)]))
    shared["scw"] = f(np.stack([np.asarray(inp["sc_conv"])[l].T.reshape(2, 128, 3).transpose(1, 0, 2)
                                for l in range(L)]))
    qk = [np.asarray(inp["swa_q_norm"]), np.asarray(inp["swa_k_norm"]),
          np.asarray(inp["fox_q_norm"]), np.asarray(inp["fox_k_norm"])]
    shared["qkg"] = f(np.stack([np.stack([qk[i][l] for i in range(4)], axis=-1) for l in range(L)]))
    shared["bfg"] = f(np.asarray(inp["b_forget"]).reshape(L, 4, 1))
    sk = np.zeros((L, 65, 4), np.float32)
    sk[:, 64, :] = np.asarray(inp["swa_sink"])
    shared["sink"] = sk
    rb = np.asarray(inp["rel_bias"], dtype=np.float32)
    i = np.arange(128)[:, None]
    jq = np.arange(128)[None, :]
    bmt = np.full((128, 4, 256), NEG, np.float32)
    d_prev = jq + 128 - i
    ok_prev = d_prev <= 127
    d_cur = jq - i
    ok_cur = d_cur >= 0
    bk_prev = _t5_bucket(np.clip(d_prev, 0, 127))
    bk_cur = _t5_bucket(np.clip(d_cur, 0, 127))
    for hq in range(4):
        bmt[:, hq, 0:128] = np.where(ok_prev, rb[bk_prev, hq], np.float32(NEG))
        bmt[:, hq, 128:256] = np.where(ok_cur, rb[bk_cur, hq], np.float32(NEG))
    shared["bm"] = bmt
    shared["identin"] = np.eye(128, dtype=np.float32)
    return shared


_NC_CACHE = {}


def kernel(**inputs):
    x = np.asarray(inputs["x"], dtype=np.float32)
    B, T, _ = x.shape
    shared = host_prep(inputs)
    if T not in _NC_CACHE:
        _NC_CACHE[T] = build(T)
    nc = _NC_CACHE[T]
    in_maps = []
    for c in range(N_CORES):
        m = dict(shared)
        m["xin"] = np.ascontiguousarray(x[c % B])
        in_maps.append(m)
    res = run_bass_kernel_spmd(nc, in_maps, core_ids=list(range(N_CORES)))
    return np.stack([np.asarray(res.results[b]["yout"], dtype=np.float32) for b in range(B)], axis=0)
```

```python
from contextlib import ExitStack

import numpy as np

import concourse.bass as bass
import concourse.mybir as mybir
from concourse.bass_utils import run_bass_kernel_spmd

F32 = mybir.dt.float32
BF16 = mybir.dt.bfloat16
AF = mybir.ActivationFunctionType
ALU = mybir.AluOpType

D = 1024
DFF = 2816
NF = DFF // 128
L = 2
TT = 512
NS = TT // 128
EPS = 1e-6
NEG = -30000.0
N_CORES = 8

WIN_TILES = ([(0, 128), (256, 128), (128, 128), (384, 128), (512, 128), (640, 128),
              (768, 128), (1024, 128), (896, 128), (1152, 128)]
             + [(1280, 128), (1408, 128), (1536, 128), (1664, 128)]
             + [(1792, 128), (1920, 128), (2048, 128), (2176, 128), (2304, 128), (2432, 128), (2560, 4)]
             + [(2564 + i * 1024 + m * 128, 128) for m in range(8) for i in range(4)])
NWT = len(WIN_TILES)


class Res:
    __slots__ = ("name", "lw", "rd")

    def __init__(self, name=""):
        self.name = name
        self.lw = None
        self.rd = {}


class Op:
    __slots__ = ("eng", "fn", "deps", "sig", "signo", "dma", "dj", "idx", "cc")


ENGS = ["pe", "act", "dve", "pool", "sp"]
NSC = 4
NSD = 8


def _dkey(o):
    return ("d", id(o)) if o.dma else ("c", o.eng)


def _dput(d, o):
    k = _dkey(o)
    cur = d.get(k)
    if cur is None or o.dma or o.idx > cur.idx:
        d[k] = o


class Sched:
    def __init__(self, nc, dry=False):
        self.nc = nc
        self.dry = dry
        self.ops = {e: [] for e in ENGS}

    def op(self, eng, fn, reads=(), writes=(), dma=False, cc=None):
        if self.dry:
            return None
        o = Op()
        o.eng, o.fn, o.dma, o.sig, o.cc = eng, fn, dma, False, cc
        o.idx = len(self.ops[eng])
        deps = {}

        def add(d, raw):
            if d is None:
                return
            if d.dma or dma or d.eng != eng or (raw and eng != "pe"):
                _dput(deps, d)

        for r in reads:
            add(r.lw, True)
        for w in writes:
            add(w.lw, False)
            for x in w.rd.values():
                add(x, False)
        o.deps = list(deps.values())
        for d in o.deps:
            d.sig = True
        for r in reads:
            _dput(r.rd, o)
        for w in writes:
            w.lw = o
            w.rd = {}
        self.ops[eng].append(o)
        return o

    def dma(self, eng, out, in_, reads=(), writes=(), **kw):
        return self.op(eng, lambda e: e.dma_start(out=out, in_=in_, **kw), reads=reads, writes=writes, dma=True)

    def emit(self):
        nc = self.nc
        for e in ENGS:
            c = 0
            j = 0
            for o in self.ops[e]:
                if o.cc is not None:
                    continue
                if o.dma:
                    o.dj = j
                    j += 1
                elif o.sig:
                    c += 1
                    o.signo = c
        with ExitStack() as st:
            csem = {e: [st.enter_context(nc.semaphore(f"c_{e}_{i}")) for i in range(NSC)]
                    for e in ENGS if e != "sp"}
            dsem = {e: [st.enter_context(nc.semaphore(f"d_{e}_{i}")) for i in range(NSD)]
                    for e in ENGS if e != "pe"}
            ncc = sum(1 for e in ENGS for o in self.ops[e] if o.cc is not None)
            ccsem = [st.enter_context(nc.semaphore(f"cc_{i}")) for i in range(ncc)]
            block = st.enter_context(nc.Block())

            def body(eng, e):
                cw = {}
                dw = {}

                def wait_dma(q, dj):
                    slot, val = dj % NSD, 16 * (dj // NSD + 1)
                    if dw.get((q, slot), 0) < val:
                        eng.wait_ge(dsem[q][slot], val)
                        dw[(q, slot)] = val

                for o in self.ops[e]:
                    for d in o.deps:
                        if d.cc is not None:
                            if dw.get(("cc", d.cc), 0) < 1:
                                eng.wait_ge(ccsem[d.cc], 1)
                                dw[("cc", d.cc)] = 1
                        elif d.dma:
                            wait_dma(d.eng, d.dj)
                        elif cw.get(d.eng, 0) < d.signo:
                            s = d.signo - 1
                            eng.wait_ge(csem[d.eng][s % NSC], s // NSC + 1)
                            cw[d.eng] = d.signo
                    if o.cc is not None:
                        o.fn(eng).then_inc(ccsem[o.cc])
                        continue
                    if o.dma and o.dj >= NSD:
                        wait_dma(e, o.dj - NSD)
                    ins = o.fn(eng)
                    if o.dma:
                        ins.then_inc(dsem[e][o.dj % NSD], 16)
                    elif o.sig:
                        ins.then_inc(csem[e][(o.signo - 1) % NSC], 1)
                n = sum(1 for o in self.ops[e] if o.dma and o.cc is None)
                for dj in range(max(0, n - NSD), n):
                    wait_dma(e, dj)

            names = {"pe": "tensor", "act": "scalar", "dve": "vector", "pool": "gpsimd", "sp": "sync"}
            for e in ENGS:
                getattr(block, names[e])(lambda eng, e=e: body(eng, e))


class Stream:
    def __init__(self, S, slots, res, plan):
        self.S, self.slots, self.res, self.plan = S, slots, res, plan
        self.rec = []
        self.pos = 0
        self.issued = 0

    def next(self, src):
        R = len(self.slots)
        if self.plan is None:
            self.rec.append(src)
            return self.slots[0], self.res[0]
        while self.issued < min(len(self.plan), self.pos + R):
            k = self.issued % R
            self.S.dma("pool", self.slots[k][:], self.plan[self.issued], writes=[self.res[k]])
            self.issued += 1
        k = self.pos % R
        self.pos += 1
        return self.slots[k], self.res[k]


class Bank:
    def __init__(self, ap, name):
        self.ap = ap
        self.res = Res(name)


class Rot:
    def __init__(self, items):
        self.items = items
        self.i = 0

    def get(self):
        b = self.items[self.i % len(self.items)]
        self.i += 1
        return b


def build(NTOK, stages=3):
    NT = NTOK // TT
    nc = bass.Bass("TRN2", target_bir_lowering=False)

    def din(name, shape, dt=F32):
        return nc.dram_tensor(name, list(shape), dt, kind="ExternalInput").ap()

    xin = din("xin", [NTOK, D])
    wg = din("wg", [L, 2, NF, 128, 8, 128])
    wu = din("wu", [L, 2, NF, 128, 8, 128])
    wd2 = din("wd2", [L, 2, 2, NF, 128, 512])
    win = din("win", [L, NWT, 128, 8, 128])
    wbr = din("wbr", [L, 8, 128, 12, 128])
    wo2 = din("wo2", [L, 2, 8, 128, 512])
    gains = din("gains", [L, 3, 128, D])
    cdw = din("cdw", [L, 128, 2, 31])
    cvec = din("cvec", [L, 128, 2, 3])
    scw = din("scw", [L, 128, 2, 3])
    qkg = din("qkg", [L, 64, 4])
    bfg = din("bfg", [L, 4, 1])
    sink = din("sink", [L, 65, 4])
    bm = din("bm", [128, 4, 256])
    identin = din("identin", [128, 128])
    flagin = din("flagin", [128, 1])
    yout = nc.dram_tensor("yout", [NTOK, D], F32, kind="ExternalOutput").ap()

    def dscr(name, shape, dt):
        return nc.dram_tensor(name, list(shape), dt, kind="Internal")

    Xs = dscr("Xs", [NTOK, D], F32).ap()
    FU = dscr("FU", [NT, 128, 2, TT], F32).ap()
    FSX = dscr("FSX", [NT, 128, 2, TT], F32).ap()
    FSB = dscr("FSB", [NT, 128, 2, TT], F32).ap()
    FQS = dscr("FQS", [NT, 64, 4, TT], BF16).ap()
    FKS = dscr("FKS", [NT, 64, 2, TT], BF16).ap()
    FVS = dscr("FVS", [NT, 128, NS, 2 * 65], BF16).ap()
    FQF = dscr("FQF", [NT, 64, 4, TT], BF16).ap()
    FG = dscr("FG", [NT, 4, TT], F32).ap()
    QFs = dscr("QFs", [2, 3, 4, TT], BF16).ap()
    NVC = max(1, NTOK // 1024)
    VC = NTOK // NVC
    KFh = [[dscr(f"KF{l}_{hh}", [70, NTOK], BF16) for hh in range(4)] for l in range(L)]
    VFh = [[dscr(f"VF{l}_{q}", [VC, 4 * 65], BF16) for q in range(NVC)] for l in range(L)]
    HFh = [dscr(f"HF{l}", [128, 128], F32) for l in range(L)]
    HBh = [dscr(f"HB{l}", [128, 512], BF16) for l in range(L)]
    GKFh = [[dscr(f"GKF{l}_{hh}", [2 * 70, NTOK], BF16) for hh in range(4)] for l in range(L)]
    GVFh = [[dscr(f"GVF{l}_{q}", [2 * VC, 4 * 65], BF16) for q in range(NVC)] for l in range(L)]
    GHFh = [dscr(f"GHF{l}", [2 * 128, 128], F32) for l in range(L)]
    GHBh = [dscr(f"GHB{l}", [2 * 128, 512], BF16) for l in range(L)]
    KF = [[t.ap() for t in row] for row in KFh]
    VF = [[t.ap() for t in row] for row in VFh]
    HF = [t.ap() for t in HFh]
    HB = [t.ap() for t in HBh]
    GKF = [[t.ap().rearrange("(g r) t -> g r t", g=2) for t in row] for row in GKFh]
    GVF = [[t.ap().rearrange("(g n) e -> g n e", g=2) for t in row] for row in GVFh]
    GHF = [t.ap().rearrange("(g p) e -> g p e", g=2) for t in GHFh]
    GHB = [t.ap().rearrange("(g p) e -> g p e", g=2) for t in GHBh]

    with ExitStack() as st:
        def sb(name, shape, dt):
            return st.enter_context(nc.sbuf_tensor(name, list(shape), dt))

        x = sb("x", [128, NS, D], F32)
        r_x = [Res(f"x{s}") for s in range(NS)]
        h = sb("h", [128, NS, D], BF16)
        r_h = [Res(f"h{s}") for s in range(NS)]
        hT = sb("hT", [128, 8, TT], BF16)
        r_hT = Res("hT")
        act = sb("act", [128, NF, TT], BF16)
        r_act = [Res(f"act{f}") for f in range(NF)]
        RA, RD, RR = 6, 8, 2
        wA = [sb(f"wA{i}", [128, 8, 128], BF16) for i in range(RA)]
        rA = [Res(f"wA{i}") for i in range(RA)]
        wD = [sb(f"wD{i}", [128, 512], BF16) for i in range(RD)]
        rD = [Res(f"wD{i}") for i in range(RD)]
        wR = [sb(f"wR{i}", [128, 12, 128], BF16) for i in range(RR)]
        rR = [Res(f"wR{i}") for i in range(RR)]
        gb = sb("gb", [128, D], F32)
        r_gb = Res("gb")
        ss = sb("ss", [128, 8], F32)
        r_ss = Res("ss")
        sg = [sb(f"sg{i}", [128, TT], F32) for i in range(2)]
        r_sg = [Res(f"sg{i}") for i in range(2)]
        ident = sb("ident", [128, 128], BF16)
        r_ident = Res("ident")
        ones = sb("ones", [128, 128], F32)
        r_ones = Res("ones")
        o64 = sb("o64", [64, 64], F32)
        r_o64 = Res("o64")
        uext = sb("uext", [128, 2, 30 + TT], F32)
        r_uext = Res("uext")
        sxext = sb("sxext", [128, 2, 2 + TT], F32)
        r_sxext = Res("sxext")
        sbt = sb("sbt", [128, 2, TT], F32)
        r_sbt = Res("sbt")
        cacc = sb("cacc", [128, 2, TT], F32)
        r_cacc = [Res("cacc0"), Res("cacc1")]
        csq = sb("csq", [128, 2, TT], F32)
        r_csq = Res("csq")
        lnm = sb("lnm", [128, TT], F32)
        r_lnm = Res("lnm")
        lnv = sb("lnv", [128, TT], F32)
        r_lnv = Res("lnv")
        uT = sb("uT", [128, 2, TT], BF16)
        r_uT = Res("uT")
        scT = sb("scT", [128, 2, TT], BF16)
        r_scT = Res("scT")
        cdw_t = sb("cdw_t", [128, 2, 31], F32)
        cvec_t = sb("cvec_t", [128, 2, 3], F32)
        scw_t = sb("scw_t", [128, 2, 3], F32)
        qkg_t = sb("qkg_t", [64, 4], F32)
        bfg_t = sb("bfg_t", [4, 1], F32)
        sink_t = sb("sink_t", [65, 4], F32)
        r_small = Res("small")
        bm_t = sb("bm_t", [128, 4, 256], F32)
        r_bm = Res("bm")
        qs = sb("qs", [64, 4, TT], BF16)
        r_qs = Res("qs")
        ksx = sb("ksx", [64, 2, 128 + TT], BF16)
        r_ksx = Res("ksx")
        vsx = sb("vsx", [128, NS + 1, 2, 65], BF16)
        r_vsx = Res("vsx")
        pT = [sb(f"pT{i}", [128, TT], BF16) for i in range(4)]
        r_pT = [Res(f"pT{i}") for i in range(4)]
        oa = sb("oa", [65, TT], F32)
        r_oa = Res("oa")
        rden = sb("rden", [65, TT], F32)
        r_rden = Res("rden")
        onT = [sb(f"onT{i}", [64, 4, TT], BF16) for i in range(2)]
        r_onT = [Res("onT0"), Res("onT1")]
        qf = sb("qf", [70, 4, TT], BF16)
        r_qf = Res("qf")
        qfx = sb("qfx", [70, 4, TT], BF16)
        r_qfx = Res("qfx")
        flag_t = sb("flag_t", [128, 1], F32)
        r_flag = Res("flag")
        tot_t = sb("tot_t", [4, 1], F32)
        r_tot = Res("tot")
        kf = sb("kf", [64, 4, TT], BF16)
        r_kf = Res("kf")
        vf = sb("vf", [128, NS, 4, 65], BF16)
        r_vf = Res("vf")
        kblk = [sb(f"kblk{i}", [70, 4, TT], BF16) for i in range(2)]
        r_kblk = [Res(f"kblk{i}") for i in range(2)]
        vblk = [sb(f"vblk{i}", [128, NS, 4 * 65], BF16) for i in range(2)]
        r_vblk = [Res(f"vblk{i}") for i in range(2)]
        fe = sb("fe", [4, TT], F32)
        r_fe = Res("fe")
        fG = sb("fG", [4, TT], F32)
        r_fG = Res("fG")
        fones = sb("fones", [4, TT], F32)
        r_fones = Res("fones")
        gsp = sb("gsp", [4, 3, TT], BF16)
        r_gs = Res("gs")
        monesb = sb("monesb", [4, 3, TT], BF16)
        r_monesb = Res("monesb")
        fcar = [sb(f"fcar{l}", [4, 1], F32) for l in range(L)]
        r_fcar = [Res(f"fcar{l}") for l in range(L)]
        macc = sb("macc", [128, TT], F32)
        r_macc = Res("macc")
        mtmp = [sb(f"mtmp{i}", [128, TT], F32) for i in range(2)]
        r_mtmp = [Res(f"mtmp{i}") for i in range(2)]
        mT = sb("mT", [128, 8, TT], BF16)
        r_mT = Res("mT")
        banks = [Bank(st.enter_context(nc.psum_tensor(f"ps{i}", [128, 512], F32)), f"ps{i}") for i in range(7)]
        ptb = Bank(st.enter_context(nc.psum_tensor("ptb", [128, 1024], BF16)), "ptb")

        def program(S, plans):
            WA = Stream(S, wA, rA, plans[0])
            WD = Stream(S, wD, rD, plans[1])
            WR = Stream(S, wR, rR, plans[2])
            rot = Rot(banks)
            rot_hi = Rot(banks[4:7])
            RKF = [[Res(f"KF{l}_{j}") for j in range(NT)] for l in range(L)]
            RX = [Res(f"X{j}") for j in range(NT)]
            RF = [Res(f"F{j}") for j in range(NT)]
            r_HAL = [Res(f"HAL{l}") for l in range(L)]
            r_G = [Res(f"G{l}") for l in range(L)]
            r_QFs = Res("QFs")
            ccn = [0]
            kbi = [0]
            pti = [0]
            sgi = [0]

            def ACT(fn, reads, writes):
                S.op("act", fn, reads, writes)

            def DVE(fn, reads, writes):
                S.op("dve", fn, reads, writes)

            def PE(fn, reads, writes):
                S.op("pe", fn, reads, writes)

            def POOL(fn, reads, writes):
                S.op("pool", fn, reads, writes)

            S.dma("pool", ident[:], identin, writes=[r_ident])
            DVE(lambda e: e.memset(ones[:], 1.0), [], [r_ones])
            DVE(lambda e: e.memset(o64[:], 1.0 / 64.0), [], [r_o64])
            DVE(lambda e: e.memset(fones[:], 1.0), [], [r_fones])
            DVE(lambda e: e.memset(monesb[:], -1.0), [], [r_monesb])
            DVE(lambda e: e.memset(qf[:], 1.0), [], [r_qf])
            DVE(lambda e: e.memset(qfx[:], 1.0), [], [r_qfx])
            S.dma("sp", flag_t[:], flagin, writes=[r_flag])
            S.dma("sp", bm_t[:], bm, writes=[r_bm])
            DVE(lambda e: e.memset(vsx[:], 1.0), [], [r_vsx])
            DVE(lambda e: e.memset(vf[:], 1.0), [], [r_vf])
            for l in range(L):
                DVE(lambda e, l=l: e.memset(fcar[l][:], 0.0), [], [r_fcar[l]])

            def rmsnorm_hT(gain_src):
                S.dma("sp", gb[:], gain_src, writes=[r_gb])
                for s in range(NS):
                    ACT(lambda e, s=s: e.activation(out=h[:, s, :], in_=x[:, s, :], func=AF.Square,
                                                    accum_out=ss[:, s:s + 1]),
                        [r_x[s]], [r_h[s], r_ss])
                ACT(lambda e: e.activation(out=ss[:, 4:8], in_=ss[:, 0:4], func=AF.Ln, scale=1.0 / D, bias=EPS),
                    [r_ss], [r_ss])
                ACT(lambda e: e.activation(out=ss[:, 4:8], in_=ss[:, 4:8], func=AF.Exp, scale=-0.5),
                    [r_ss], [r_ss])
                for s in range(NS):
                    DVE(lambda e, s=s: e.scalar_tensor_tensor(out=h[:, s, :], in0=x[:, s, :],
                                                              scalar=ss[:, 4 + s:5 + s], in1=gb[:],
                                                              op0=ALU.mult, op1=ALU.mult),
                        [r_x[s], r_ss, r_gb], [r_h[s]])
                for s in range(NS):
                    for c in range(8):
                        PE(lambda e, s=s, c=c: e.transpose(out=ptb.ap[:, c * 128:(c + 1) * 128],
                                                           in_=h[:, s, c * 128:(c + 1) * 128], identity=ident[:]),
                           [r_h[s], r_ident], [ptb.res])
                    ACT(lambda e, s=s: e.copy(out=hT[:, :, s * 128:(s + 1) * 128],
                                              in_=ptb.ap[:, :].rearrange("p (c t) -> p c t", c=8)),
                        [ptb.res], [r_hT])

            def resid_proj(src, src_res_of, nk, tile_src, scale):
                for n in range(2):
                    accs = [rot.get() for _ in range(NS)]
                    for kc in range(nk):
                        t, r = WD.next(tile_src(n, kc))
                        for s in range(NS):
                            PE(lambda e, s=s, kc=kc, t=t, a=accs[s]: e.matmul(
                                a.ap[:, :], lhsT=src[:, kc, s * 128:(s + 1) * 128], rhs=t[:],
                                start=(kc == 0), stop=(kc == nk - 1)), [src_res_of(kc), r], [accs[s].res])
                    for s in range(NS):
                        DVE(lambda e, s=s, n=n, a=accs[s]: e.scalar_tensor_tensor(
                            out=x[:, s, n * 512:(n + 1) * 512], in0=a.ap[:, :], scalar=scale,
                            in1=x[:, s, n * 512:(n + 1) * 512], op0=ALU.mult, op1=ALU.add),
                            [accs[s].res, r_x[s]], [r_x[s]])

            def ffn(l, k):
                rmsnorm_hT(gains[l, 2 * k])
                for f in range(NF):
                    tg, rg = WA.next(wg[l, k, f])
                    bg = rot.get()
                    for c in range(8):
                        PE(lambda e, c=c, tg=tg, bg=bg: e.matmul(bg.ap[:, :], lhsT=tg[:, c, :], rhs=hT[:, c, :],
                                                                 start=(c == 0), stop=(c == 7)),
                           [rg, r_hT], [bg.res])
                    tu, ru = WA.next(wu[l, k, f])
                    bu = rot.get()
                    for c in range(8):
                        PE(lambda e, c=c, tu=tu, bu=bu: e.matmul(bu.ap[:, :], lhsT=tu[:, c, :], rhs=hT[:, c, :],
                                                                 start=(c == 0), stop=(c == 7)),
                           [ru, r_hT], [bu.res])
                    i = sgi[0] % 2
                    sgi[0] += 1
                    ACT(lambda e, i=i, bg=bg: e.activation(out=sg[i][:], in_=bg.ap[:, :], func=AF.Silu),
                        [bg.res], [r_sg[i]])
                    DVE(lambda e, i=i, f=f, bu=bu: e.tensor_tensor(out=act[:, f, :], in0=sg[i][:], in1=bu.ap[:, :],
                                                                   op=ALU.mult),
                        [r_sg[i], bu.res], [r_act[f]])
                resid_proj(act, lambda f: r_act[f], NF, lambda n, f: wd2[l, k, n, f], 0.5)

            def proj128(l, wi):
                t, r = WA.next(win[l, wi])
                b = rot.get()
                for c in range(8):
                    PE(lambda e, c=c, t=t, b=b: e.matmul(b.ap[:, :], lhsT=t[:, c, :], rhs=hT[:, c, :],
                                                         start=(c == 0), stop=(c == 7)), [r, r_hT], [b.res])
                return b

            def head_norm(src_ap, src_res, gcol, out_ap, out_res):
                ACT(lambda e: e.activation(out=lnm[0:64, :], in_=src_ap, func=AF.Square), [src_res], [r_lnm])
                b = rot.get()
                PE(lambda e, b=b: e.matmul(b.ap[0:64, :], lhsT=o64[:], rhs=lnm[0:64, :], start=True, stop=True),
                   [r_lnm, r_o64], [b.res])
                ACT(lambda e, b=b: e.activation(out=lnv[0:64, :], in_=b.ap[0:64, :], func=AF.Ln, bias=EPS),
                    [b.res], [r_lnv])
                ACT(lambda e: e.activation(out=lnv[0:64, :], in_=lnv[0:64, :], func=AF.Exp, scale=-0.5),
                    [r_lnv], [r_lnv])
                DVE(lambda e: e.scalar_tensor_tensor(out=out_ap, in0=src_ap, scalar=qkg_t[:, gcol:gcol + 1],
                                                     in1=lnv[0:64, :], op0=ALU.mult, op1=ALU.mult),
                    [src_res, r_lnv, r_small], [out_res])

            def heads_tile(l, wi, specs):
                t, r = WA.next(win[l, wi])
                for hh in range(2):
                    b = rot.get()
                    for c in range(8):
                        PE(lambda e, c=c, t=t, b=b, hh=hh: e.matmul(
                            b.ap[0:64, :], lhsT=t[:, c, hh * 64:(hh + 1) * 64], rhs=hT[:, c, :],
                            start=(c == 0), stop=(c == 7)), [r, r_hT], [b.res])
                    gcol, out_ap, out_res = specs[hh]
                    head_norm(b.ap[0:64, :], b.res, gcol, out_ap, out_res)

            def v_tile(l, wi, dst, dst_res, chunk_off, h0):
                t, r = WA.next(win[l, wi])
                for s in range(NS):
                    b = rot.get()
                    for c in range(8):
                        PE(lambda e, c=c, t=t, b=b, s=s: e.matmul(
                            b.ap[:, 0:128], lhsT=hT[:, c, s * 128:(s + 1) * 128], rhs=t[:, c, :],
                            start=(c == 0), stop=(c == 7)), [r, r_hT], [b.res])
                    ACT(lambda e, b=b, s=s: e.copy(out=dst[:, chunk_off + s, h0:h0 + 2, 0:64],
                                                   in_=b.ap[:, 0:128].rearrange("p (h d) -> p h d", h=2)),
                        [b.res], [dst_res])

            def attn_norm(acc, extra_den, out_ap, out_res):
                ACT(lambda e: e.copy(out=oa[:], in_=acc.ap[0:65, :]), [acc.res], [r_oa])
                if extra_den is not None:
                    DVE(lambda e: e.tensor_scalar(out=oa[64:65, :], in0=oa[64:65, :], scalar1=extra_den,
                                                  scalar2=None, op0=ALU.add), [r_oa, r_small], [r_oa])
                ACT(lambda e: e.activation(out=rden[64:65, :], in_=oa[64:65, :], func=AF.Ln), [r_oa], [r_rden])
                ACT(lambda e: e.activation(out=rden[64:65, :], in_=rden[64:65, :], func=AF.Exp, scale=-1.0),
                    [r_rden], [r_rden])
                b = rot_hi.get()
                PE(lambda e, b=b: e.matmul(b.ap[0:64, :], lhsT=ones[64:65, 0:64], rhs=rden[64:65, :],
                                           start=True, stop=True), [r_rden, r_ones], [b.res])
                DVE(lambda e, b=b: e.tensor_tensor(out=out_ap, in0=oa[0:64, :], in1=b.ap[0:64, :], op=ALU.mult),
                    [r_oa, b.res], [out_res])

            def load_small(l):
                S.dma("sp", cdw_t[:], cdw[l], writes=[r_small])
                S.dma("sp", cvec_t[:], cvec[l], writes=[r_small])
                S.dma("sp", scw_t[:], scw[l], writes=[r_small])
                S.dma("sp", qkg_t[:], qkg[l], writes=[r_small])
                S.dma("sp", bfg_t[:], bfg[l], writes=[r_small])
                S.dma("sp", sink_t[:], sink[l], writes=[r_small])
                DVE(lambda e: e.tensor_scalar(out=qkg_t[:, 0:1], in0=qkg_t[:, 0:1], scalar1=0.125, scalar2=None,
                                              op0=ALU.mult), [r_small], [r_small])
                DVE(lambda e: e.tensor_scalar(out=qkg_t[:, 2:3], in0=qkg_t[:, 2:3], scalar1=0.125, scalar2=None,
                                              op0=ALU.mult), [r_small], [r_small])
                ACT(lambda e: e.activation(out=sink_t[64:65, :], in_=sink_t[64:65, :], func=AF.Exp),
                    [r_small], [r_small])
                DVE(lambda e: e.tensor_scalar(out=bfg_t[:], in0=bfg_t[:], scalar1=-1.0, scalar2=None,
                                              op0=ALU.mult), [r_small], [r_small])

            def split3(src, src_res):
                DVE(lambda e: e.tensor_copy(out=gsp[:, 0, :], in_=src[:]), [src_res], [r_gs])
                DVE(lambda e: e.tensor_tensor(out=fe[:], in0=src[:], in1=gsp[:, 0, :], op=ALU.subtract),
                    [src_res, r_gs], [r_fe])
                DVE(lambda e: e.tensor_copy(out=gsp[:, 1, :], in_=fe[:]), [r_fe], [r_gs])
                DVE(lambda e: e.tensor_tensor(out=fe[:], in0=fe[:], in1=gsp[:, 1, :], op=ALU.subtract),
                    [r_fe, r_gs], [r_fe])
                DVE(lambda e: e.tensor_copy(out=gsp[:, 2, :], in_=fe[:]), [r_fe], [r_gs])

            def front(l, j):
                t0 = j * TT
                last = (j == NT - 1)
                load_small(l)
                rmsnorm_hT(gains[l, 1])
                for ch in range(2):
                    ba = proj128(l, 2 * ch)
                    bb = proj128(l, 2 * ch + 1)
                    i = sgi[0] % 2
                    sgi[0] += 1
                    ACT(lambda e, i=i, bb=bb: e.activation(out=sg[i][:], in_=bb.ap[:, :], func=AF.Sigmoid),
                        [bb.res], [r_sg[i]])
                    DVE(lambda e, i=i, ba=ba, ch=ch: e.tensor_tensor(out=uext[:, ch, 30:30 + TT], in0=sg[i][:],
                                                                     in1=ba.ap[:, :], op=ALU.mult),
                        [r_sg[i], ba.res], [r_uext])
                for ch in range(2):
                    b = proj128(l, 4 + ch)
                    ACT(lambda e, b=b, ch=ch: e.copy(out=sbt[:, ch, :], in_=b.ap[:, :]), [b.res], [r_sbt])
                for ch in range(2):
                    bc = proj128(l, 6 + 2 * ch)
                    bx = proj128(l, 7 + 2 * ch)
                    i = sgi[0] % 2
                    sgi[0] += 1
                    ACT(lambda e, i=i, bc=bc: e.copy(out=sg[i][:], in_=bc.ap[:, :]), [bc.res], [r_sg[i]])
                    DVE(lambda e, i=i, bx=bx, ch=ch: e.tensor_tensor(out=sxext[:, ch, 2:2 + TT], in0=sg[i][:],
                                                                     in1=bx.ap[:, :], op=ALU.mult),
                        [r_sg[i], bx.res], [r_sxext])
                S.dma("sp", FU[j], uext[:, :, 30:30 + TT], reads=[r_uext], writes=[RF[j]])
                S.dma("sp", FSX[j], sxext[:, :, 2:2 + TT], reads=[r_sxext], writes=[RF[j]])
                S.dma("sp", FSB[j], sbt[:], reads=[r_sbt], writes=[RF[j]])
                if last:
                    S.dma("sp", HF[l][:, 0:60].rearrange("p (c k) -> p c k", c=2), uext[:, :, TT:TT + 30],
                          reads=[r_uext], writes=[r_HAL[l]])
                    S.dma("sp", HF[l][:, 60:64].rearrange("p (c k) -> p c k", c=2), sxext[:, :, TT:TT + 2],
                          reads=[r_sxext], writes=[r_HAL[l]])
                heads_tile(l, 10, [(0, qs[:, 0, :], r_qs), (0, qs[:, 1, :], r_qs)])
                heads_tile(l, 11, [(0, qs[:, 2, :], r_qs), (0, qs[:, 3, :], r_qs)])
                heads_tile(l, 12, [(1, ksx[:, 0, 128:128 + TT], r_ksx), (1, ksx[:, 1, 128:128 + TT], r_ksx)])
                v_tile(l, 13, vsx, r_vsx, 1, 0)
                S.dma("sp", FQS[j], qs[:], reads=[r_qs], writes=[RF[j]])
                S.dma("sp", FKS[j], ksx[:, :, 128:128 + TT], reads=[r_ksx], writes=[RF[j]])
                S.dma("sp", FVS[j], vsx[:, 1:NS + 1, :, :].rearrange("p s h e -> p s (h e)"), reads=[r_vsx],
                      writes=[RF[j]])
                if last:
                    S.dma("sp", HB[l][0:64, 0:256].rearrange("p (c k) -> p c k", c=2), ksx[:, :, TT:TT + 128],
                          reads=[r_ksx], writes=[r_HAL[l]])
                    S.dma("sp", HB[l][:, 256:386], vsx[:, NS, :, :].rearrange("p h e -> p (h e)"),
                          reads=[r_vsx], writes=[r_HAL[l]])
                heads_tile(l, 14, [(2, qf[0:64, 0, :], r_qf), (2, qf[0:64, 1, :], r_qf)])
                heads_tile(l, 15, [(2, qf[0:64, 2, :], r_qf), (2, qf[0:64, 3, :], r_qf)])
                heads_tile(l, 16, [(3, kf[:, 0, :], r_kf), (3, kf[:, 1, :], r_kf)])
                heads_tile(l, 17, [(3, kf[:, 2, :], r_kf), (3, kf[:, 3, :], r_kf)])
                v_tile(l, 18, vf, r_vf, 0, 0)
                v_tile(l, 19, vf, r_vf, 0, 2)
                t, r = WA.next(win[l, 20])
                bF = rot.get()
                for c in range(8):
                    PE(lambda e, c=c, t=t, bF=bF: e.matmul(bF.ap[0:4, :], lhsT=t[:, c, 0:4], rhs=hT[:, c, :],
                                                           start=(c == 0), stop=(c == 7)), [r, r_hT], [bF.res])
                ACT(lambda e: e.activation(out=fe[:], in_=bF.ap[0:4, :], func=AF.Exp, scale=-1.0,
                                           bias=bfg_t[:, 0:1]), [bF.res, r_small], [r_fe])
                ACT(lambda e: e.activation(out=fe[:], in_=fe[:], func=AF.Ln, bias=1.0), [r_fe], [r_fe])
                if j == 0:
                    DVE(lambda e: e.memset(fcar[l][:], 0.0), [], [r_fcar[l]])
                DVE(lambda e: e.tensor_tensor_scan(out=fG[:], data0=fones[:], data1=fe[:], initial=fcar[l][:, 0:1],
                                                   op0=ALU.mult, op1=ALU.add), [r_fe, r_fones, r_fcar[l]], [r_fG])
                DVE(lambda e: e.tensor_copy(out=fcar[l][:], in_=fG[:, TT - 1:TT]), [r_fG], [r_fcar[l]])
                split3(fG, r_fG)
                S.dma("sp", FQF[j], qf[0:64, :, :], reads=[r_qf], writes=[RF[j]])
                S.dma("sp", FG[j], fG[:], reads=[r_fG], writes=[RF[j]])
                rKF = RKF[l][j]
                for hh in range(4):
                    S.dma("sp", KF[l][hh][0:64, t0:t0 + TT], kf[:, hh, :], reads=[r_kf], writes=[rKF])
                    S.dma("sp", KF[l][hh][64:67, t0:t0 + TT].rearrange("(o a) t -> o a t", o=1),
                          monesb[hh:hh + 1, :, :], reads=[r_monesb], writes=[rKF])
                    S.dma("sp", KF[l][hh][67:70, t0:t0 + TT].rearrange("(o a) t -> o a t", o=1),
                          gsp[hh:hh + 1, :, :], reads=[r_gs], writes=[rKF])
                vq, vo = t0 // VC, t0 % VC
                S.dma("sp", VF[l][vq][vo:vo + TT, :].rearrange("(s p) e -> p s e", p=128),
                      vf[:].rearrange("p s h e -> p s (h e)"), reads=[r_vf], writes=[rKF])
                if last:
                    S.dma("sp", HF[l][0:4, 64:65], fcar[l][:], reads=[r_fcar[l]], writes=[r_HAL[l]],
                          allow_slow_non_contiguous=True)

            def exchange(l):
                srcs = ([(KFh[l][hh], GKFh[l][hh]) for hh in range(4)]
                        + [(VFh[l][q], GVFh[l][q]) for q in range(NVC)]
                        + [(HFh[l], GHFh[l]), (HBh[l], GHBh[l])])
                for (a, g) in srcs:
                    k = ccn[0]
                    ccn[0] += 1
                    S.op("pool", lambda e, a=a, g=g: e.collective_compute(
                        "AllGather", ALU.bypass, replica_groups=[[0, 1], [2, 3], [4, 5], [6, 7]],
                        ins=[a.ap().opt()], outs=[g.ap().opt()]),
                        reads=RKF[l] + [r_HAL[l]], writes=[r_G[l]], dma=True, cc=k)

            def back(l, j):
                t0 = j * TT
                load_small(l)
                rmsnorm_hT(gains[l, 1])
                S.dma("sp", uext[:, :, 30:30 + TT], FU[j], reads=[RF[j]], writes=[r_uext])
                S.dma("sp", sxext[:, :, 2:2 + TT], FSX[j], reads=[RF[j]], writes=[r_sxext])
                S.dma("sp", sbt[:], FSB[j], reads=[RF[j]], writes=[r_sbt])
                if j == 0:
                    S.dma("sp", uext[:, :, 0:30], GHF[l][0, :, 0:60].rearrange("p (c k) -> p c k", c=2),
                          reads=[r_G[l]], writes=[r_uext])
                    S.dma("sp", sxext[:, :, 0:2], GHF[l][0, :, 60:64].rearrange("p (c k) -> p c k", c=2),
                          reads=[r_G[l]], writes=[r_sxext])
                    S.dma("sp", tot_t[:], GHF[l][0, 0:4, 64:65], reads=[r_G[l]], writes=[r_tot],
                          allow_slow_non_contiguous=True)
                    DVE(lambda e: e.tensor_scalar(out=uext[:, :, 0:30], in0=uext[:, :, 0:30],
                                                  scalar1=flag_t[:, 0:1], scalar2=None, op0=ALU.mult),
                        [r_uext, r_flag], [r_uext])
                    DVE(lambda e: e.tensor_scalar(out=sxext[:, :, 0:2], in0=sxext[:, :, 0:2],
                                                  scalar1=flag_t[:, 0:1], scalar2=None, op0=ALU.mult),
                        [r_sxext, r_flag], [r_sxext])
                else:
                    S.dma("sp", uext[:, :, 0:30], FU[j - 1][:, :, TT - 30:TT], reads=[RF[j - 1]], writes=[r_uext])
                    S.dma("sp", sxext[:, :, 0:2], FSX[j - 1][:, :, TT - 2:TT], reads=[RF[j - 1]], writes=[r_sxext])
                for kk in range(31):
                    for ch in range(2):
                        if kk == 0:
                            DVE(lambda e, ch=ch: e.tensor_scalar(
                                out=cacc[:, ch, :], in0=uext[:, ch, 0:TT], scalar1=cdw_t[:, ch, 0:1],
                                scalar2=cvec_t[:, ch, 0:1], op0=ALU.mult, op1=ALU.add),
                                [r_uext, r_small], [r_cacc[ch]])
                        else:
                            DVE(lambda e, ch=ch, kk=kk: e.scalar_tensor_tensor(
                                out=cacc[:, ch, :], in0=uext[:, ch, kk:kk + TT], scalar=cdw_t[:, ch, kk:kk + 1],
                                in1=cacc[:, ch, :], op0=ALU.mult, op1=ALU.add),
                                [r_uext, r_small, r_cacc[ch]], [r_cacc[ch]])
                for ch in range(2):
                    ACT(lambda e, ch=ch: e.activation(out=csq[:, ch, :], in_=cacc[:, ch, :], func=AF.Square),
                        [r_cacc[ch]], [r_csq])
                b1 = rot.get()
                for ch in range(2):
                    PE(lambda e, ch=ch, b1=b1: e.matmul(b1.ap[:, :], lhsT=ones[:], rhs=cacc[:, ch, :],
                                                        start=(ch == 0), stop=(ch == 1)),
                       [r_cacc[ch], r_ones], [b1.res])
                b2 = rot.get()
                for ch in range(2):
                    PE(lambda e, ch=ch, b2=b2: e.matmul(b2.ap[:, :], lhsT=ones[:], rhs=csq[:, ch, :],
                                                        start=(ch == 0), stop=(ch == 1)),
                       [r_csq, r_ones], [b2.res])
                DVE(lambda e: e.tensor_scalar(out=lnm[:], in0=b1.ap[:, :], scalar1=1.0 / 256.0, scalar2=None,
                                              op0=ALU.mult), [b1.res], [r_lnm])
                DVE(lambda e: e.tensor_tensor(out=lnv[:], in0=lnm[:], in1=lnm[:], op=ALU.mult), [r_lnm], [r_lnv])
                DVE(lambda e: e.scalar_tensor_tensor(out=lnv[:], in0=b2.ap[:, :], scalar=1.0 / 256.0, in1=lnv[:],
                                                     op0=ALU.mult, op1=ALU.subtract), [b2.res, r_lnv], [r_lnv])
                DVE(lambda e: e.tensor_scalar(out=lnv[:], in0=lnv[:], scalar1=0.0, scalar2=None, op0=ALU.max),
                    [r_lnv], [r_lnv])
                ACT(lambda e: e.activation(out=lnv[:], in_=lnv[:], func=AF.Ln, bias=EPS), [r_lnv], [r_lnv])
                ACT(lambda e: e.activation(out=lnv[:], in_=lnv[:], func=AF.Exp, scale=-0.5), [r_lnv], [r_lnv])
                for ch in range(2):
                    DVE(lambda e, ch=ch: e.tensor_tensor(out=cacc[:, ch, :], in0=cacc[:, ch, :], in1=lnm[:],
                                                         op=ALU.subtract), [r_cacc[ch], r_lnm], [r_cacc[ch]])
                    DVE(lambda e, ch=ch: e.tensor_tensor(out=cacc[:, ch, :], in0=cacc[:, ch, :], in1=lnv[:],
                                                         op=ALU.mult), [r_cacc[ch], r_lnv], [r_cacc[ch]])
                    ACT(lambda e, ch=ch: e.activation(out=uT[:, ch, :], in_=cacc[:, ch, :], func=AF.Silu,
                                                      scale=cvec_t[:, ch, 1:2], bias=cvec_t[:, ch, 2:3]),
                        [r_cacc[ch], r_small], [r_uT])
                for ch in range(2):
                    DVE(lambda e, ch=ch: e.tensor_scalar(out=csq[:, ch, :], in0=sxext[:, ch, 0:TT],
                                                         scalar1=scw_t[:, ch, 0:1], scalar2=None, op0=ALU.mult),
                        [r_sxext, r_small], [r_csq])
                    for kk in (1, 2):
                        DVE(lambda e, ch=ch, kk=kk: e.scalar_tensor_tensor(
                            out=csq[:, ch, :], in0=sxext[:, ch, kk:kk + TT], scalar=scw_t[:, ch, kk:kk + 1],
                            in1=csq[:, ch, :], op0=ALU.mult, op1=ALU.add), [r_sxext, r_small, r_csq], [r_csq])
                    DVE(lambda e, ch=ch: e.tensor_tensor(out=scT[:, ch, :], in0=csq[:, ch, :], in1=sbt[:, ch, :],
                                                         op=ALU.mult), [r_csq, r_sbt], [r_scT])

                S.dma("sp", qs[:], FQS[j], reads=[RF[j]], writes=[r_qs])
                S.dma("sp", ksx[:, :, 128:128 + TT], FKS[j], reads=[RF[j]], writes=[r_ksx])
                S.dma("sp", vsx[:, 1:NS + 1, :, :].rearrange("p s h e -> p s (h e)"), FVS[j], reads=[RF[j]],
                      writes=[r_vsx])
                if j == 0:
                    S.dma("sp", ksx[:, :, 0:128], GHB[l][0, 0:64, 0:256].rearrange("p (c k) -> p c k", c=2),
                          reads=[r_G[l]], writes=[r_ksx])
                    S.dma("sp", vsx[:, 0, :, :].rearrange("p h e -> p (h e)"), GHB[l][0, :, 256:386],
                          reads=[r_G[l]], writes=[r_vsx])
                    DVE(lambda e: e.tensor_scalar(out=vsx[:, 0, :, :], in0=vsx[:, 0, :, :],
                                                  scalar1=flag_t[:, 0:1], scalar2=None, op0=ALU.mult),
                        [r_vsx, r_flag], [r_vsx])
                else:
                    S.dma("sp", ksx[:, :, 0:128], FKS[j - 1][:, :, TT - 128:TT], reads=[RF[j - 1]], writes=[r_ksx])
                    S.dma("sp", vsx[:, 0, :, :].rearrange("p h e -> p (h e)"), FVS[j - 1][:, NS - 1, :],
                          reads=[RF[j - 1]], writes=[r_vsx])
                for hq in range(4):
                    hk = hq // 2
                    acc = banks[hq % 4]
                    for pr in range(2):
                        bS = rot_hi.get()
                        i = sgi[0] % 2
                        sgi[0] += 1
                        for s2 in range(2):
                            s = 2 * pr + s2
                            for part in range(2):
                                PE(lambda e, s=s, s2=s2, part=part, bS=bS, hk=hk, hq=hq: e.matmul(
                                    bS.ap[:, s2 * 256 + part * 128: s2 * 256 + part * 128 + 128],
                                    lhsT=ksx[:, hk, (s + part) * 128:(s + part + 1) * 128],
                                    rhs=qs[:, hq, s * 128:(s + 1) * 128], start=True, stop=True),
                                   [r_ksx, r_qs], [bS.res])
                            DVE(lambda e, bS=bS, i=i, hq=hq, s2=s2: e.tensor_tensor(
                                out=sg[i][:, s2 * 256:(s2 + 1) * 256], in0=bS.ap[:, s2 * 256:(s2 + 1) * 256],
                                in1=bm_t[:, hq, :], op=ALU.add), [bS.res, r_bm], [r_sg[i]])
                        pi = pti[0] % 4
                        pti[0] += 1
                        ACT(lambda e, i=i, pi=pi: e.activation(out=pT[pi][:], in_=sg[i][:], func=AF.Exp),
                            [r_sg[i]], [r_pT[pi]])
                        for s2 in range(2):
                            s = 2 * pr + s2
                            first = True
                            for part in range(2):
                                PE(lambda e, s=s, s2=s2, part=part, pi=pi, hk=hk, acc=acc, first=first: e.matmul(
                                    acc.ap[0:65, s * 128:(s + 1) * 128], lhsT=vsx[:, s + part, hk, :],
                                    rhs=pT[pi][:, s2 * 256 + part * 128: s2 * 256 + part * 128 + 128],
                                    start=first, stop=(part == 1)), [r_vsx, r_pT[pi]], [acc.res])
                                first = False
                    attn_norm(acc, sink_t[64:65, hq:hq + 1], onT[0][:, hq, :], r_onT[0])

                S.dma("sp", qf[0:64, :, :], FQF[j], reads=[RF[j]], writes=[r_qf])
                S.dma("sp", qfx[0:64, :, :], FQF[j], reads=[RF[j]], writes=[r_qfx])
                S.dma("sp", fG[:], FG[j], reads=[RF[j]], writes=[r_fG])
                split3(fG, r_fG)
                for hh in range(4):
                    S.dma("sp", QFs[0, :, hh, :].rearrange("(o a) t -> o a t", o=1), gsp[hh:hh + 1, :, :],
                          reads=[r_gs], writes=[r_QFs])
                S.dma("sp", qf[64:67, :, :], QFs[0], reads=[r_QFs], writes=[r_qf])
                DVE(lambda e: e.tensor_scalar(out=fG[:], in0=fG[:], scalar1=tot_t[:, 0:1], scalar2=None,
                                              op0=ALU.add), [r_fG, r_tot], [r_fG])
                split3(fG, r_fG)
                for hh in range(4):
                    S.dma("sp", QFs[1, :, hh, :].rearrange("(o a) t -> o a t", o=1), gsp[hh:hh + 1, :, :],
                          reads=[r_gs], writes=[r_QFs])
                S.dma("sp", qfx[64:67, :, :], QFs[1], reads=[r_QFs], writes=[r_qfx])
                accs = banks[0:4]
                nkb = NT + j + 1
                for kk in range(nkb):
                    cross = kk < NT
                    kb = kk if cross else kk - NT
                    i = kbi[0] % 2
                    kbi[0] += 1
                    vq, vo = (kb * TT) // VC, (kb * TT) % VC
                    if cross:
                        for hh in range(4):
                            S.dma("sp", kblk[i][:, hh, :], GKF[l][hh][0, :, kb * TT:(kb + 1) * TT],
                                  reads=[r_G[l]], writes=[r_kblk[i]])
                        S.dma("sp", vblk[i][:],
                              GVF[l][vq][0, vo:vo + TT, :].rearrange("(s p) e -> p s e", p=128),
                              reads=[r_G[l]], writes=[r_vblk[i]])
                        POOL(lambda e, i=i: e.tensor_scalar(out=vblk[i][:], in0=vblk[i][:], scalar1=flag_t[:, 0:1],
                                                            scalar2=None, op0=ALU.mult),
                             [r_vblk[i], r_flag], [r_vblk[i]])
                        qsrc, rq = qfx, r_qfx
                    else:
                        for hh in range(4):
                            S.dma("sp", kblk[i][:, hh, :], KF[l][hh][:, kb * TT:(kb + 1) * TT],
                                  reads=[RKF[l][kb]], writes=[r_kblk[i]])
                        S.dma("sp", vblk[i][:],
                              VF[l][vq][vo:vo + TT, :].rearrange("(s p) e -> p s e", p=128),
                              reads=[RKF[l][kb]], writes=[r_vblk[i]])
                        qsrc, rq = qf, r_qf
                    diag = (not cross) and kb == j
                    for hh in range(4):
                        for c in range(NS):
                            bS = rot_hi.get()
                            PE(lambda e, i=i, hh=hh, c=c, bS=bS, qsrc=qsrc: e.matmul(
                                bS.ap[:, :], lhsT=kblk[i][:, hh, c * 128:(c + 1) * 128], rhs=qsrc[:, hh, :],
                                start=True, stop=True), [r_kblk[i], rq], [bS.res])
                            pi = pti[0] % 4
                            pti[0] += 1
                            ACT(lambda e, pi=pi, bS=bS: e.activation(out=pT[pi][:], in_=bS.ap[:, :], func=AF.Exp),
                                [bS.res], [r_pT[pi]])
                            if diag:
                                POOL(lambda e, pi=pi, c=c: e.affine_select(
                                    out=pT[pi][:], in_=pT[pi][:], pattern=[[1, TT]], compare_op=ALU.is_ge,
                                    fill=0.0, base=-c * 128, channel_multiplier=-1), [r_pT[pi]], [r_pT[pi]])
                            PE(lambda e, i=i, hh=hh, c=c, pi=pi, kk=kk: e.matmul(
                                accs[hh].ap[0:65, :], lhsT=vblk[i][:, c, hh * 65:(hh + 1) * 65], rhs=pT[pi][:],
                                start=(kk == 0 and c == 0), stop=(kk == nkb - 1 and c == NS - 1)),
                               [r_vblk[i], r_pT[pi]], [accs[hh].res])
                for hh in range(4):
                    attn_norm(accs[hh], None, onT[1][:, hh, :], r_onT[1])

                brsrc = [(uT, r_uT), (scT, r_scT)]
                for m in range(8):
                    tr, rr = WR.next(wbr[l, m])
                    for br in range(4):
                        bp = rot.get()
                        if br < 2:
                            src, rs = brsrc[br]
                            for ch in range(2):
                                PE(lambda e, br=br, ch=ch, bp=bp, src=src, tr=tr: e.matmul(
                                    bp.ap[:, :], lhsT=tr[:, 2 * br + ch, :], rhs=src[:, ch, :],
                                    start=(ch == 0), stop=(ch == 1)), [rr, rs], [bp.res])
                        else:
                            a = br - 2
                            for hh in range(4):
                                PE(lambda e, a=a, hh=hh, bp=bp, tr=tr: e.matmul(
                                    bp.ap[:, :], lhsT=tr[0:64, 4 + 4 * a + hh, :],
                                    rhs=onT[a][:, hh, :], start=(hh == 0), stop=(hh == 3)),
                                   [rr, r_onT[a]], [bp.res])
                        bgt = proj128(l, 21 + 4 * m + br)
                        i = sgi[0] % 2
                        sgi[0] += 1
                        ACT(lambda e, i=i, bgt=bgt: e.activation(out=sg[i][:], in_=bgt.ap[:, :], func=AF.Sigmoid),
                            [bgt.res], [r_sg[i]])
                        if br == 0:
                            DVE(lambda e, i=i, bp=bp: e.tensor_tensor(out=macc[:], in0=sg[i][:], in1=bp.ap[:, :],
                                                                      op=ALU.mult), [r_sg[i], bp.res], [r_macc])
                        else:
                            k2 = br % 2
                            DVE(lambda e, i=i, bp=bp, k2=k2: e.tensor_tensor(out=mtmp[k2][:], in0=sg[i][:],
                                                                             in1=bp.ap[:, :], op=ALU.mult),
                                [r_sg[i], bp.res], [r_mtmp[k2]])
                            if br < 3:
                                POOL(lambda e, k2=k2: e.tensor_tensor(out=macc[:], in0=macc[:], in1=mtmp[k2][:],
                                                                      op=ALU.add), [r_macc, r_mtmp[k2]], [r_macc])
                            else:
                                POOL(lambda e, k2=k2, m=m: e.tensor_tensor(out=mT[:, m, :], in0=macc[:],
                                                                           in1=mtmp[k2][:], op=ALU.add),
                                     [r_macc, r_mtmp[k2]], [r_mT])
                resid_proj(mT, lambda m: r_mT, 8, lambda n, m: wo2[l, n, m], 1.0)

            for l in range(L):
                for j in range(NT):
                    t0 = j * TT
                    src = xin if l == 0 else Xs
                    S.dma("sp", x[:], src[t0:t0 + TT, :].rearrange("(s p) d -> p s d", p=128),
                          reads=([RX[j]] if l > 0 else []), writes=r_x)
                    ffn(l, 0)
                    S.dma("sp", Xs[t0:t0 + TT, :].rearrange("(s p) d -> p s d", p=128), x[:], reads=r_x,
                          writes=[RX[j]])
                    front(l, j)
                exchange(l)
                for j in range(NT):
                    t0 = j * TT
                    S.dma("sp", x[:], Xs[t0:t0 + TT, :].rearrange("(s p) d -> p s d", p=128), reads=[RX[j]],
                          writes=r_x)
                    back(l, j)
                    ffn(l, 1)
                    dst = Xs if l < L - 1 else yout
                    S.dma("sp", dst[t0:t0 + TT, :].rearrange("(s p) d -> p s d", p=128), x[:], reads=r_x,
                          writes=[RX[j]])
            return [WA.rec, WD.rec, WR.rec]

        plans = program(Sched(nc, dry=True), [None, None, None])
        S = Sched(nc)
        program(S, plans)
        S.emit()
    return nc


def _t5_bucket(dist):
    max_exact = 16
    d = np.maximum(dist, 1).astype(np.float32)
    large = max_exact + (np.log(d / np.float32(max_exact)) / np.float32(np.log(128 / max_exact))
                         * np.float32(32 - max_exact)).astype(np.int32)
    large = np.minimum(large, 31)
    return np.where(dist < max_exact, dist, large)


def host_prep(inp):
    f = lambda a: np.ascontiguousarray(np.asarray(a, dtype=np.float32))

    def wtile(w, col0, n):
        out = np.zeros((128, 8, 128), np.float32)
        out[:, :, :n] = w[:, col0:col0 + n].reshape(8, 128, n).transpose(1, 0, 2)
        return out

    shared = {}
    gates = [np.asarray(inp["ffn1_w_gate"]), np.asarray(inp["ffn2_w_gate"])]
    ups = [np.asarray(inp["ffn1_w_up"]), np.asarray(inp["ffn2_w_up"])]
    downs = [np.asarray(inp["ffn1_w_down"]), np.asarray(inp["ffn2_w_down"])]

    def ftile(w):
        return w.reshape(8, 128, NF, 128).transpose(2, 1, 0, 3)

    def dtile(w):
        return w.reshape(NF, 128, 2, 512).transpose(2, 0, 1, 3)

    shared["wg"] = f(np.stack([np.stack([ftile(gates[k][l]) for k in range(2)]) for l in range(L)]))
    shared["wu"] = f(np.stack([np.stack([ftile(ups[k][l]) for k in range(2)]) for l in range(L)]))
    shared["wd2"] = f(np.stack([np.stack([dtile(downs[k][l]) for k in range(2)]) for l in range(L)]))
    w_in = np.asarray(inp["w_in"])
    shared["win"] = f(np.stack([np.stack([wtile(w_in[l], c0, n) for (c0, n) in WIN_TILES]) for l in range(L)]))
    cwo, swo = np.asarray(inp["conf_w_out"]), np.asarray(inp["sc_w_out"])
    awo, fwo = np.asarray(inp["swa_w_o"]), np.asarray(inp["fox_w_o"])
    wbr = np.zeros((L, 8, 128, 12, 128), np.float32)
    for l in range(L):
        for m in range(8):
            cs = slice(m * 128, (m + 1) * 128)
            for ch in range(2):
                wbr[l, m, :, ch, :] = cwo[l][ch * 128:(ch + 1) * 128, cs]
                wbr[l, m, :, 2 + ch, :] = swo[l][ch * 128:(ch + 1) * 128, cs]
            for hh in range(4):
                wbr[l, m, 0:64, 4 + hh, :] = awo[l][hh * 64:(hh + 1) * 64, cs]
                wbr[l, m, 0:64, 8 + hh, :] = fwo[l][hh * 64:(hh + 1) * 64, cs]
    shared["wbr"] = wbr
    shared["wo2"] = f(np.stack([np.asarray(inp["w_out"])[l].reshape(8, 128, 2, 512).transpose(2, 0, 1, 3)
                                for l in range(L)]))
    gn = [np.asarray(inp["ffn1_norm"]), np.asarray(inp["mix_norm"]), np.asarray(inp["ffn2_norm"])]
    shared["gains"] = f(np.stack([np.stack([np.broadcast_to(gn[i][l][None, :], (128, D)) for i in range(3)])
                                  for l in range(L)]))
    shared["cdw"] = f(np.stack([np.asarray(inp["conf_dw"])[l].T.reshape(2, 128, 31).transpose(1, 0, 2)
                                for l in range(L)]))
    cv = [np.asarray(inp["conf_dw_b"]), np.asarray(inp["conf_ln_g"]), np.asarray(inp["conf_ln_b"])]
    shared["cvec"] = f(np.stack([np.stack([cv[i][l].reshape(2, 128).T for i in range(3)], axis=-1)
                                 for l in range(L)]))
    shared["scw"] = f(np.stack([np.asarray(inp["sc_conv"])[l].T.reshape(2, 128, 3).transpose(1, 0, 2)
                                for l in range(L)]))
    qk = [np.asarray(inp["swa_q_norm"]), np.asarray(inp["swa_k_norm"]),
          np.asarray(inp["fox_q_norm"]), np.asarray(inp["fox_k_norm"])]
    shared["qkg"] = f(np.stack([np.stack([qk[i][l] for i in range(4)], axis=-1) for l in range(L)]))
    shared["bfg"] = f(np.asarray(inp["b_forget"]).reshape(L, 4, 1))
    sk = np.zeros((L, 65, 4), np.float32)
    sk[:, 64, :] = np.asarray(inp["swa_sink"])
    shared["sink"] = sk
    rb = np.asarray(inp["rel_bias"], dtype=np.float32)
    i = np.arange(128)[:, None]
    jq = np.arange(128)[None, :]
    bmt = np.full((128, 4, 256), NEG, np.float32)
    d_prev = jq + 128 - i
    ok_prev = d_prev <= 127
    d_cur = jq - i
    ok_cur = d_cur >= 0
    bk_prev = _t5_bucket(np.clip(d_prev, 0, 127))
    bk_cur = _t5_bucket(np.clip(d_cur, 0, 127))
    for hq in range(4):
        bmt[:, hq, 0:128] = np.where(ok_prev, rb[bk_prev, hq], np.float32(NEG))
        bmt[:, hq, 128:256] = np.where(ok_cur, rb[bk_cur, hq], np.float32(NEG))
    shared["bm"] = bmt
    shared["identin"] = np.eye(128, dtype=np.float32)
    return shared


_NC_CACHE = {}


def kernel(**inputs):
    x = np.asarray(inputs["x"], dtype=np.float32)
    B, T, _ = x.shape
    ntok = T // 2
    shared = host_prep(inputs)
    if ntok not in _NC_CACHE:
        _NC_CACHE[ntok] = build(ntok)
    nc = _NC_CACHE[ntok]
    in_maps = []
    for c in range(N_CORES):
        b, half = c // 2, c % 2
        m = dict(shared)
        m["xin"] = np.ascontiguousarray(x[b, half * ntok:(half + 1) * ntok])
        m["flagin"] = np.full((128, 1), float(half), np.float32)
        in_maps.append(m)
    res = run_bass_kernel_spmd(nc, in_maps, core_ids=list(range(N_CORES)))
    out = np.empty((B, T, D), np.float32)
    for c in range(N_CORES):
        b, half = c // 2, c % 2
        out[b, half * ntok:(half + 1) * ntok] = np.asarray(res.results[c]["yout"], dtype=np.float32)
    return out
```

```python
from contextlib import ExitStack

import numpy as np

import concourse.bass as bass
import concourse.mybir as mybir
from concourse.bass_utils import run_bass_kernel_spmd

F32 = mybir.dt.float32
BF16 = mybir.dt.bfloat16
AF = mybir.ActivationFunctionType
ALU = mybir.AluOpType

D = 1024
DFF = 2816
NF = DFF // 128
L = 2
TT = 512
NS = TT // 128
EPS = 1e-6
NEG = -30000.0
N_CORES = 8

WIN_TILES = ([(0, 128), (256, 128), (128, 128), (384, 128), (512, 128), (640, 128),
              (768, 128), (1024, 128), (896, 128), (1152, 128)]
             + [(1280, 128), (1408, 128), (1536, 128), (1664, 128)]
             + [(1792, 128), (1920, 128), (2048, 128), (2176, 128), (2304, 128), (2432, 128), (2560, 4)]
             + [(2564 + i * 1024 + m * 128, 128) for m in range(8) for i in range(4)])
NWT = len(WIN_TILES)


class Res:
    __slots__ = ("name", "lw", "rd")

    def __init__(self, name=""):
        self.name = name
        self.lw = None
        self.rd = {}


class Op:
    __slots__ = ("eng", "fn", "deps", "sig", "signo", "dma", "dj", "idx", "cc")


ENGS = ["pe", "act", "dve", "pool", "sp"]
NSC = 4
NSD = 8


def _dkey(o):
    return ("d", id(o)) if o.dma else ("c", o.eng)


def _dput(d, o):
    k = _dkey(o)
    cur = d.get(k)
    if cur is None or o.dma or o.idx > cur.idx:
        d[k] = o


class Sched:
    def __init__(self, nc, dry=False):
        self.nc = nc
        self.dry = dry
        self.ops = {e: [] for e in ENGS}

    def op(self, eng, fn, reads=(), writes=(), dma=False, cc=None):
        if self.dry:
            return None
        o = Op()
        o.eng, o.fn, o.dma, o.sig, o.cc = eng, fn, dma, False, cc
        o.idx = len(self.ops[eng])
        deps = {}

        def add(d, raw):
            if d is None:
                return
            if d.dma or dma or d.eng != eng or (raw and eng != "pe"):
                _dput(deps, d)

        for r in reads:
            add(r.lw, True)
        for w in writes:
            add(w.lw, False)
            for x in w.rd.values():
                add(x, False)
        o.deps = list(deps.values())
        for d in o.deps:
            d.sig = True
        for r in reads:
            _dput(r.rd, o)
        for w in writes:
            w.lw = o
            w.rd = {}
        self.ops[eng].append(o)
        return o

    def dma(self, eng, out, in_, reads=(), writes=(), **kw):
        return self.op(eng, lambda e: e.dma_start(out=out, in_=in_, **kw), reads=reads, writes=writes, dma=True)

    def emit(self):
        nc = self.nc
        for e in ENGS:
            c = 0
            j = 0
            for o in self.ops[e]:
                if o.cc is not None:
                    continue
                if o.dma:
                    o.dj = j
                    j += 1
                elif o.sig:
                    c += 1
                    o.signo = c
        with ExitStack() as st:
            csem = {e: [st.enter_context(nc.semaphore(f"c_{e}_{i}")) for i in range(NSC)]
                    for e in ENGS if e != "sp"}
            dsem = {e: [st.enter_context(nc.semaphore(f"d_{e}_{i}")) for i in range(NSD)]
                    for e in ENGS if e != "pe"}
            ncc = sum(1 for e in ENGS for o in self.ops[e] if o.cc is not None)
            ccsem = [st.enter_context(nc.semaphore(f"cc_{i}")) for i in range(ncc)]
            block = st.enter_context(nc.Block())

            def body(eng, e):
                cw = {}
                dw = {}

                def wait_dma(q, dj):
                    slot, val = dj % NSD, 16 * (dj // NSD + 1)
                    if dw.get((q, slot), 0) < val:
                        eng.wait_ge(dsem[q][slot], val)
                        dw[(q, slot)] = val

                for o in self.ops[e]:
                    for d in o.deps:
                        if d.cc is not None:
                            if dw.get(("cc", d.cc), 0) < 1:
                                eng.wait_ge(ccsem[d.cc], 1)
                                dw[("cc", d.cc)] = 1
                        elif d.dma:
                            wait_dma(d.eng, d.dj)
                        elif cw.get(d.eng, 0) < d.signo:
                            s = d.signo - 1
                            eng.wait_ge(csem[d.eng][s % NSC], s // NSC + 1)
                            cw[d.eng] = d.signo
                    if o.cc is not None:
                        o.fn(eng).then_inc(ccsem[o.cc])
                        continue
                    if o.dma and o.dj >= NSD:
                        wait_dma(e, o.dj - NSD)
                    ins = o.fn(eng)
                    if o.dma:
                        ins.then_inc(dsem[e][o.dj % NSD], 16)
                    elif o.sig:
                        ins.then_inc(csem[e][(o.signo - 1) % NSC], 1)
                n = sum(1 for o in self.ops[e] if o.dma and o.cc is None)
                for dj in range(max(0, n - NSD), n):
                    wait_dma(e, dj)

            names = {"pe": "tensor", "act": "scalar", "dve": "vector", "pool": "gpsimd", "sp": "sync"}
            for e in ENGS:
                getattr(block, names[e])(lambda eng, e=e: body(eng, e))


class Stream:
    def __init__(self, S, slots, res, plan):
        self.S, self.slots, self.res, self.plan = S, slots, res, plan
        self.rec = []
        self.pos = 0
        self.issued = 0

    def next(self, src):
        R = len(self.slots)
        if self.plan is None:
            self.rec.append(src)
            return self.slots[0], self.res[0]
        while self.issued < min(len(self.plan), self.pos + R):
            k = self.issued % R
            self.S.dma("pool", self.slots[k][:], self.plan[self.issued], writes=[self.res[k]])
            self.issued += 1
        k = self.pos % R
        self.pos += 1
        return self.slots[k], self.res[k]


class Bank:
    def __init__(self, ap, name):
        self.ap = ap
        self.res = Res(name)


class Rot:
    def __init__(self, items):
        self.items = items
        self.i = 0

    def get(self):
        b = self.items[self.i % len(self.items)]
        self.i += 1
        return b


def build(NTOK, stages=3):
    NT = NTOK // TT
    nc = bass.Bass("TRN2", target_bir_lowering=False)

    def din(name, shape, dt=F32):
        return nc.dram_tensor(name, list(shape), dt, kind="ExternalInput").ap()

    xin = din("xin", [NTOK, D])
    wg = din("wg", [L, 2, NF, 128, 8, 128])
    wu = din("wu", [L, 2, NF, 128, 8, 128])
    wd2 = din("wd2", [L, 2, 2, NF, 128, 512])
    win = din("win", [L, NWT, 128, 8, 128])
    wbr = din("wbr", [L, 8, 128, 12, 128])
    wo2 = din("wo2", [L, 2, 8, 128, 512])
    gains = din("gains", [L, 3, 128, D])
    cdw = din("cdw", [L, 128, 2, 31])
    cvec = din("cvec", [L, 128, 2, 3])
    scw = din("scw", [L, 128, 2, 3])
    qkg = din("qkg", [L, 64, 4])
    bfg = din("bfg", [L, 4, 1])
    sink = din("sink", [L, 65, 4])
    bm = din("bm", [128, 4, 256])
    identin = din("identin", [128, 128])
    flagin = din("flagin", [128, 1])
    yout = nc.dram_tensor("yout", [NTOK, D], F32, kind="ExternalOutput").ap()

    def dscr(name, shape, dt):
        return nc.dram_tensor(name, list(shape), dt, kind="Internal")

    Xs = dscr("Xs", [NTOK, D], F32).ap()
    FU = dscr("FU", [NT, 128, 2, TT], F32).ap()
    FSX = dscr("FSX", [NT, 128, 2, TT], F32).ap()
    FSB = dscr("FSB", [NT, 128, 2, TT], F32).ap()
    FQS = dscr("FQS", [NT, 64, 4, TT], BF16).ap()
    FKS = dscr("FKS", [NT, 64, 2, TT], BF16).ap()
    FVS = dscr("FVS", [NT, 128, NS, 2 * 65], BF16).ap()
    FQF = dscr("FQF", [NT, 64, 4, TT], BF16).ap()
    FG = dscr("FG", [NT, 4, TT], F32).ap()
    QFs = dscr("QFs", [2, 3, 4, TT], BF16).ap()
    NVC = max(1, NTOK // 1024)
    VC = NTOK // NVC
    KFh = [[dscr(f"KF{l}_{hh}", [70, NTOK], BF16) for hh in range(4)] for l in range(L)]
    VFh = [[dscr(f"VF{l}_{q}", [VC, 4 * 65], BF16) for q in range(NVC)] for l in range(L)]
    HFh = [dscr(f"HF{l}", [128, 128], F32) for l in range(L)]
    HBh = [dscr(f"HB{l}", [128, 512], BF16) for l in range(L)]
    GKFh = [[dscr(f"GKF{l}_{hh}", [2 * 70, NTOK], BF16) for hh in range(4)] for l in range(L)]
    GVFh = [[dscr(f"GVF{l}_{q}", [2 * VC, 4 * 65], BF16) for q in range(NVC)] for l in range(L)]
    GHFh = [dscr(f"GHF{l}", [2 * 128, 128], F32) for l in range(L)]
    GHBh = [dscr(f"GHB{l}", [2 * 128, 512], BF16) for l in range(L)]
    KF = [[t.ap() for t in row] for row in KFh]
    VF = [[t.ap() for t in row] for row in VFh]
    HF = [t.ap() for t in HFh]
    HB = [t.ap() for t in HBh]
    GKF = [[t.ap().rearrange("(g r) t -> g r t", g=2) for t in row] for row in GKFh]
    GVF = [[t.ap().rearrange("(g n) e -> g n e", g=2) for t in row] for row in GVFh]
    GHF = [t.ap().rearrange("(g p) e -> g p e", g=2) for t in GHFh]
    GHB = [t.ap().rearrange("(g p) e -> g p e", g=2) for t in GHBh]

    with ExitStack() as st:
        def sb(name, shape, dt):
            return st.enter_context(nc.sbuf_tensor(name, list(shape), dt))

        x = sb("x", [128, NS, D], F32)
        r_x = [Res(f"x{s}") for s in range(NS)]
        h = sb("h", [128, NS, D], BF16)
        r_h = [Res(f"h{s}") for s in range(NS)]
        hT = sb("hT", [128, 8, TT], BF16)
        r_hT = Res("hT")
        act = sb("act", [128, NF, TT], BF16)
        r_act = [Res(f"act{f}") for f in range(NF)]
        RA, RD, RR = 6, 8, 2
        wA = [sb(f"wA{i}", [128, 8, 128], BF16) for i in range(RA)]
        rA = [Res(f"wA{i}") for i in range(RA)]
        wD = [sb(f"wD{i}", [128, 512], BF16) for i in range(RD)]
        rD = [Res(f"wD{i}") for i in range(RD)]
        wR = [sb(f"wR{i}", [128, 12, 128], BF16) for i in range(RR)]
        rR = [Res(f"wR{i}") for i in range(RR)]
        gb = sb("gb", [128, D], F32)
        r_gb = Res("gb")
        ss = sb("ss", [128, 8], F32)
        r_ss = Res("ss")
        sg = [sb(f"sg{i}", [128, TT], F32) for i in range(2)]
        r_sg = [Res(f"sg{i}") for i in range(2)]
        ident = sb("ident", [128, 128], BF16)
        r_ident = Res("ident")
        ones = sb("ones", [128, 128], F32)
        r_ones = Res("ones")
        o64 = sb("o64", [64, 64], F32)
        r_o64 = Res("o64")
        uext = sb("uext", [128, 2, 30 + TT], F32)
        r_uext = Res("uext")
        sxext = sb("sxext", [128, 2, 2 + TT], F32)
        r_sxext = Res("sxext")
        sbt = sb("sbt", [128, 2, TT], F32)
        r_sbt = Res("sbt")
        cacc = sb("cacc", [128, 2, TT], F32)
        r_cacc = [Res("cacc0"), Res("cacc1")]
        csq = sb("csq", [128, 2, TT], F32)
        r_csq = Res("csq")
        lnm = sb("lnm", [128, TT], F32)
        r_lnm = Res("lnm")
        lnv = sb("lnv", [128, TT], F32)
        r_lnv = Res("lnv")
        uT = sb("uT", [128, 2, TT], BF16)
        r_uT = Res("uT")
        scT = sb("scT", [128, 2, TT], BF16)
        r_scT = Res("scT")
        cdw_t = sb("cdw_t", [128, 2, 31], F32)
        cvec_t = sb("cvec_t", [128, 2, 3], F32)
        scw_t = sb("scw_t", [128, 2, 3], F32)
        qkg_t = sb("qkg_t", [64, 4], F32)
        bfg_t = sb("bfg_t", [4, 1], F32)
        sink_t = sb("sink_t", [65, 4], F32)
        r_small = Res("small")
        bm_t = sb("bm_t", [128, 4, 256], F32)
        r_bm = Res("bm")
        qs = sb("qs", [64, 4, TT], BF16)
        r_qs = Res("qs")
        ksx = sb("ksx", [64, 2, 128 + TT], BF16)
        r_ksx = Res("ksx")
        vsx = sb("vsx", [128, NS + 1, 2, 65], BF16)
        r_vsx = Res("vsx")
        pT = [sb(f"pT{i}", [128, TT], BF16) for i in range(4)]
        r_pT = [Res(f"pT{i}") for i in range(4)]
        oa = sb("oa", [65, TT], F32)
        r_oa = Res("oa")
        rden = sb("rden", [65, TT], F32)
        r_rden = Res("rden")
        onT = [sb(f"onT{i}", [64, 4, TT], BF16) for i in range(2)]
        r_onT = [Res("onT0"), Res("onT1")]
        qf = sb("qf", [128, 4, TT], BF16)
        r_qf = Res("qf")
        qfx = sb("qfx", [128, 4, TT], BF16)
        r_qfx = Res("qfx")
        flag_t = sb("flag_t", [128, 1], F32)
        r_flag = Res("flag")
        tot_t = sb("tot_t", [4, 1], F32)
        r_tot = Res("tot")
        kf = sb("kf", [64, 4, TT], BF16)
        r_kf = Res("kf")
        vf = sb("vf", [128, NS, 4, 65], BF16)
        r_vf = Res("vf")
        kblk = [sb(f"kblk{i}", [128, 4, TT], BF16) for i in range(2)]
        r_kblk = [Res(f"kblk{i}") for i in range(2)]
        vblk = [sb(f"vblk{i}", [128, NS, 324], BF16) for i in range(2)]
        r_vblk = [Res(f"vblk{i}") for i in range(2)]
        fe = sb("fe", [4, TT], F32)
        r_fe = Res("fe")
        fG = sb("fG", [4, TT], F32)
        r_fG = Res("fG")
        fones = sb("fones", [4, TT], F32)
        r_fones = Res("fones")
        gsp = sb("gsp", [4, 3, TT], BF16)
        r_gs = Res("gs")
        monesb = sb("monesb", [4, 3, TT], BF16)
        r_monesb = Res("monesb")
        fcar = [sb(f"fcar{l}", [4, 1], F32) for l in range(L)]
        r_fcar = [Res(f"fcar{l}") for l in range(L)]
        macc = sb("macc", [128, TT], F32)
        r_macc = Res("macc")
        mtmp = [sb(f"mtmp{i}", [128, TT], F32) for i in range(2)]
        r_mtmp = [Res(f"mtmp{i}") for i in range(2)]
        mT = sb("mT", [128, 8, TT], BF16)
        r_mT = Res("mT")
        banks = [Bank(st.enter_context(nc.psum_tensor(f"ps{i}", [128, 512], F32)), f"ps{i}") for i in range(7)]
        ptb = Bank(st.enter_context(nc.psum_tensor("ptb", [128, 1024], BF16)), "ptb")

        def program(S, plans):
            WA = Stream(S, wA, rA, plans[0])
            WD = Stream(S, wD, rD, plans[1])
            WR = Stream(S, wR, rR, plans[2])
            rot = Rot(banks)
            rot_hi = Rot(banks[4:7])
            RKF = [[Res(f"KF{l}_{j}") for j in range(NT)] for l in range(L)]
            RX = [Res(f"X{j}") for j in range(NT)]
            RF = [Res(f"F{j}") for j in range(NT)]
            r_HAL = [Res(f"HAL{l}") for l in range(L)]
            r_G = [Res(f"G{l}") for l in range(L)]
            r_QFs = Res("QFs")
            ccn = [0]
            kbi = [0]
            pti = [0]
            sgi = [0]

            def ACT(fn, reads, writes):
                S.op("act", fn, reads, writes)

            def DVE(fn, reads, writes):
                S.op("dve", fn, reads, writes)

            def PE(fn, reads, writes):
                S.op("pe", fn, reads, writes)

            def POOL(fn, reads, writes):
                S.op("pool", fn, reads, writes)

            S.dma("pool", ident[:], identin, writes=[r_ident])
            DVE(lambda e: e.memset(ones[:], 1.0), [], [r_ones])
            DVE(lambda e: e.memset(o64[:], 1.0 / 64.0), [], [r_o64])
            DVE(lambda e: e.memset(fones[:], 1.0), [], [r_fones])
            DVE(lambda e: e.memset(monesb[:], -1.0), [], [r_monesb])
            DVE(lambda e: e.memset(qf[:], 0.0), [], [r_qf])
            DVE(lambda e: e.memset(qf[64:70, :, :], 1.0), [], [r_qf])
            DVE(lambda e: e.memset(qfx[:], 0.0), [], [r_qfx])
            DVE(lambda e: e.memset(qfx[64:70, :, :], 1.0), [], [r_qfx])
            for i in range(2):
                DVE(lambda e, i=i: e.memset(kblk[i][:], 0.0), [], [r_kblk[i]])
                DVE(lambda e, i=i: e.memset(vblk[i][:], 0.0), [], [r_vblk[i]])
            S.dma("sp", flag_t[:], flagin, writes=[r_flag])
            S.dma("sp", bm_t[:], bm, writes=[r_bm])
            DVE(lambda e: e.memset(vsx[:], 1.0), [], [r_vsx])
            DVE(lambda e: e.memset(vf[:], 1.0), [], [r_vf])
            for l in range(L):
                DVE(lambda e, l=l: e.memset(fcar[l][:], 0.0), [], [r_fcar[l]])

            def rmsnorm_hT(gain_src):
                S.dma("sp", gb[:], gain_src, writes=[r_gb])
                for s in range(NS):
                    ACT(lambda e, s=s: e.activation(out=h[:, s, :], in_=x[:, s, :], func=AF.Square,
                                                    accum_out=ss[:, s:s + 1]),
                        [r_x[s]], [r_h[s], r_ss])
                ACT(lambda e: e.activation(out=ss[:, 4:8], in_=ss[:, 0:4], func=AF.Ln, scale=1.0 / D, bias=EPS),
                    [r_ss], [r_ss])
                ACT(lambda e: e.activation(out=ss[:, 4:8], in_=ss[:, 4:8], func=AF.Exp, scale=-0.5),
                    [r_ss], [r_ss])
                for s in range(NS):
                    DVE(lambda e, s=s: e.scalar_tensor_tensor(out=h[:, s, :], in0=x[:, s, :],
                                                              scalar=ss[:, 4 + s:5 + s], in1=gb[:],
                                                              op0=ALU.mult, op1=ALU.mult),
                        [r_x[s], r_ss, r_gb], [r_h[s]])
                for s in range(NS):
                    for c in range(8):
                        PE(lambda e, s=s, c=c: e.transpose(out=ptb.ap[:, c * 128:(c + 1) * 128],
                                                           in_=h[:, s, c * 128:(c + 1) * 128], identity=ident[:]),
                           [r_h[s], r_ident], [ptb.res])
                    ACT(lambda e, s=s: e.copy(out=hT[:, :, s * 128:(s + 1) * 128],
                                              in_=ptb.ap[:, :].rearrange("p (c t) -> p c t", c=8)),
                        [ptb.res], [r_hT])

            def resid_proj(src, src_res_of, nk, tile_src, scale):
                for n in range(2):
                    accs = [rot.get() for _ in range(NS)]
                    for kc in range(nk):
                        t, r = WD.next(tile_src(n, kc))
                        for s in range(NS):
                            PE(lambda e, s=s, kc=kc, t=t, a=accs[s]: e.matmul(
                                a.ap[:, :], lhsT=src[:, kc, s * 128:(s + 1) * 128], rhs=t[:],
                                start=(kc == 0), stop=(kc == nk - 1)), [src_res_of(kc), r], [accs[s].res])
                    for s in range(NS):
                        DVE(lambda e, s=s, n=n, a=accs[s]: e.scalar_tensor_tensor(
                            out=x[:, s, n * 512:(n + 1) * 512], in0=a.ap[:, :], scalar=scale,
                            in1=x[:, s, n * 512:(n + 1) * 512], op0=ALU.mult, op1=ALU.add),
                            [accs[s].res, r_x[s]], [r_x[s]])

            def ffn(l, k):
                rmsnorm_hT(gains[l, 2 * k])
                for f in range(NF):
                    tg, rg = WA.next(wg[l, k, f])
                    bg = rot.get()
                    for c in range(8):
                        PE(lambda e, c=c, tg=tg, bg=bg: e.matmul(bg.ap[:, :], lhsT=tg[:, c, :], rhs=hT[:, c, :],
                                                                 start=(c == 0), stop=(c == 7)),
                           [rg, r_hT], [bg.res])
                    tu, ru = WA.next(wu[l, k, f])
                    bu = rot.get()
                    for c in range(8):
                        PE(lambda e, c=c, tu=tu, bu=bu: e.matmul(bu.ap[:, :], lhsT=tu[:, c, :], rhs=hT[:, c, :],
                                                                 start=(c == 0), stop=(c == 7)),
                           [ru, r_hT], [bu.res])
                    i = sgi[0] % 2
                    sgi[0] += 1
                    ACT(lambda e, i=i, bg=bg: e.activation(out=sg[i][:], in_=bg.ap[:, :], func=AF.Silu),
                        [bg.res], [r_sg[i]])
                    DVE(lambda e, i=i, f=f, bu=bu: e.tensor_tensor(out=act[:, f, :], in0=sg[i][:], in1=bu.ap[:, :],
                                                                   op=ALU.mult),
                        [r_sg[i], bu.res], [r_act[f]])
                resid_proj(act, lambda f: r_act[f], NF, lambda n, f: wd2[l, k, n, f], 0.5)

            def proj128(l, wi):
                t, r = WA.next(win[l, wi])
                b = rot.get()
                for c in range(8):
                    PE(lambda e, c=c, t=t, b=b: e.matmul(b.ap[:, :], lhsT=t[:, c, :], rhs=hT[:, c, :],
                                                         start=(c == 0), stop=(c == 7)), [r, r_hT], [b.res])
                return b

            def head_norm(src_ap, src_res, gcol, out_ap, out_res):
                ACT(lambda e: e.activation(out=lnm[0:64, :], in_=src_ap, func=AF.Square), [src_res], [r_lnm])
                b = rot.get()
                PE(lambda e, b=b: e.matmul(b.ap[0:64, :], lhsT=o64[:], rhs=lnm[0:64, :], start=True, stop=True),
                   [r_lnm, r_o64], [b.res])
                ACT(lambda e, b=b: e.activation(out=lnv[0:64, :], in_=b.ap[0:64, :], func=AF.Ln, bias=EPS),
                    [b.res], [r_lnv])
                ACT(lambda e: e.activation(out=lnv[0:64, :], in_=lnv[0:64, :], func=AF.Exp, scale=-0.5),
                    [r_lnv], [r_lnv])
                DVE(lambda e: e.scalar_tensor_tensor(out=out_ap, in0=src_ap, scalar=qkg_t[:, gcol:gcol + 1],
                                                     in1=lnv[0:64, :], op0=ALU.mult, op1=ALU.mult),
                    [src_res, r_lnv, r_small], [out_res])

            def heads_tile(l, wi, specs):
                t, r = WA.next(win[l, wi])
                for hh in range(2):
                    b = rot.get()
                    for c in range(8):
                        PE(lambda e, c=c, t=t, b=b, hh=hh: e.matmul(
                            b.ap[0:64, :], lhsT=t[:, c, hh * 64:(hh + 1) * 64], rhs=hT[:, c, :],
                            start=(c == 0), stop=(c == 7)), [r, r_hT], [b.res])
                    gcol, out_ap, out_res = specs[hh]
                    head_norm(b.ap[0:64, :], b.res, gcol, out_ap, out_res)

            def v_tile(l, wi, dst, dst_res, chunk_off, h0):
                t, r = WA.next(win[l, wi])
                for s in range(NS):
                    b = rot.get()
                    for c in range(8):
                        PE(lambda e, c=c, t=t, b=b, s=s: e.matmul(
                            b.ap[:, 0:128], lhsT=hT[:, c, s * 128:(s + 1) * 128], rhs=t[:, c, :],
                            start=(c == 0), stop=(c == 7)), [r, r_hT], [b.res])
                    ACT(lambda e, b=b, s=s: e.copy(out=dst[:, chunk_off + s, h0:h0 + 2, 0:64],
                                                   in_=b.ap[:, 0:128].rearrange("p (h d) -> p h d", h=2)),
                        [b.res], [dst_res])

            def attn_norm(acc, extra_den, out_ap, out_res):
                ACT(lambda e: e.copy(out=oa[:], in_=acc.ap[0:65, :]), [acc.res], [r_oa])
                if extra_den is not None:
                    DVE(lambda e: e.tensor_scalar(out=oa[64:65, :], in0=oa[64:65, :], scalar1=extra_den,
                                                  scalar2=None, op0=ALU.add), [r_oa, r_small], [r_oa])
                ACT(lambda e: e.activation(out=rden[64:65, :], in_=oa[64:65, :], func=AF.Ln), [r_oa], [r_rden])
                ACT(lambda e: e.activation(out=rden[64:65, :], in_=rden[64:65, :], func=AF.Exp, scale=-1.0),
                    [r_rden], [r_rden])
                b = rot_hi.get()
                PE(lambda e, b=b: e.matmul(b.ap[0:64, :], lhsT=ones[64:65, 0:64], rhs=rden[64:65, :],
                                           start=True, stop=True), [r_rden, r_ones], [b.res])
                DVE(lambda e, b=b: e.tensor_tensor(out=out_ap, in0=oa[0:64, :], in1=b.ap[0:64, :], op=ALU.mult),
                    [r_oa, b.res], [out_res])

            def load_small(l):
                S.dma("sp", cdw_t[:], cdw[l], writes=[r_small])
                S.dma("sp", cvec_t[:], cvec[l], writes=[r_small])
                S.dma("sp", scw_t[:], scw[l], writes=[r_small])
                S.dma("sp", qkg_t[:], qkg[l], writes=[r_small])
                S.dma("sp", bfg_t[:], bfg[l], writes=[r_small])
                S.dma("sp", sink_t[:], sink[l], writes=[r_small])
                DVE(lambda e: e.tensor_scalar(out=qkg_t[:, 0:1], in0=qkg_t[:, 0:1], scalar1=0.125, scalar2=None,
                                              op0=ALU.mult), [r_small], [r_small])
                DVE(lambda e: e.tensor_scalar(out=qkg_t[:, 2:3], in0=qkg_t[:, 2:3], scalar1=0.125, scalar2=None,
                                              op0=ALU.mult), [r_small], [r_small])
                ACT(lambda e: e.activation(out=sink_t[64:65, :], in_=sink_t[64:65, :], func=AF.Exp),
                    [r_small], [r_small])
                DVE(lambda e: e.tensor_scalar(out=bfg_t[:], in0=bfg_t[:], scalar1=-1.0, scalar2=None,
                                              op0=ALU.mult), [r_small], [r_small])

            def split3(src, src_res):
                DVE(lambda e: e.tensor_copy(out=gsp[:, 0, :], in_=src[:]), [src_res], [r_gs])
                DVE(lambda e: e.tensor_tensor(out=fe[:], in0=src[:], in1=gsp[:, 0, :], op=ALU.subtract),
                    [src_res, r_gs], [r_fe])
                DVE(lambda e: e.tensor_copy(out=gsp[:, 1, :], in_=fe[:]), [r_fe], [r_gs])
                DVE(lambda e: e.tensor_tensor(out=fe[:], in0=fe[:], in1=gsp[:, 1, :], op=ALU.subtract),
                    [r_fe, r_gs], [r_fe])
                DVE(lambda e: e.tensor_copy(out=gsp[:, 2, :], in_=fe[:]), [r_fe], [r_gs])

            def front(l, j):
                t0 = j * TT
                last = (j == NT - 1)
                load_small(l)
                rmsnorm_hT(gains[l, 1])
                for ch in range(2):
                    ba = proj128(l, 2 * ch)
                    bb = proj128(l, 2 * ch + 1)
                    i = sgi[0] % 2
                    sgi[0] += 1
                    ACT(lambda e, i=i, bb=bb: e.activation(out=sg[i][:], in_=bb.ap[:, :], func=AF.Sigmoid),
                        [bb.res], [r_sg[i]])
                    DVE(lambda e, i=i, ba=ba, ch=ch: e.tensor_tensor(out=uext[:, ch, 30:30 + TT], in0=sg[i][:],
                                                                     in1=ba.ap[:, :], op=ALU.mult),
                        [r_sg[i], ba.res], [r_uext])
                for ch in range(2):
                    b = proj128(l, 4 + ch)
                    ACT(lambda e, b=b, ch=ch: e.copy(out=sbt[:, ch, :], in_=b.ap[:, :]), [b.res], [r_sbt])
                for ch in range(2):
                    bc = proj128(l, 6 + 2 * ch)
                    bx = proj128(l, 7 + 2 * ch)
                    i = sgi[0] % 2
                    sgi[0] += 1
                    ACT(lambda e, i=i, bc=bc: e.copy(out=sg[i][:], in_=bc.ap[:, :]), [bc.res], [r_sg[i]])
                    DVE(lambda e, i=i, bx=bx, ch=ch: e.tensor_tensor(out=sxext[:, ch, 2:2 + TT], in0=sg[i][:],
                                                                     in1=bx.ap[:, :], op=ALU.mult),
                        [r_sg[i], bx.res], [r_sxext])
                S.dma("sp", FU[j], uext[:, :, 30:30 + TT], reads=[r_uext], writes=[RF[j]])
                S.dma("sp", FSX[j], sxext[:, :, 2:2 + TT], reads=[r_sxext], writes=[RF[j]])
                S.dma("sp", FSB[j], sbt[:], reads=[r_sbt], writes=[RF[j]])
                if last:
                    S.dma("sp", HF[l][:, 0:60].rearrange("p (c k) -> p c k", c=2), uext[:, :, TT:TT + 30],
                          reads=[r_uext], writes=[r_HAL[l]])
                    S.dma("sp", HF[l][:, 60:64].rearrange("p (c k) -> p c k", c=2), sxext[:, :, TT:TT + 2],
                          reads=[r_sxext], writes=[r_HAL[l]])
                heads_tile(l, 10, [(0, qs[:, 0, :], r_qs), (0, qs[:, 1, :], r_qs)])
                heads_tile(l, 11, [(0, qs[:, 2, :], r_qs), (0, qs[:, 3, :], r_qs)])
                heads_tile(l, 12, [(1, ksx[:, 0, 128:128 + TT], r_ksx), (1, ksx[:, 1, 128:128 + TT], r_ksx)])
                v_tile(l, 13, vsx, r_vsx, 1, 0)
                S.dma("sp", FQS[j], qs[:], reads=[r_qs], writes=[RF[j]])
                S.dma("sp", FKS[j], ksx[:, :, 128:128 + TT], reads=[r_ksx], writes=[RF[j]])
                S.dma("sp", FVS[j], vsx[:, 1:NS + 1, :, :].rearrange("p s h e -> p s (h e)"), reads=[r_vsx],
                      writes=[RF[j]])
                if last:
                    S.dma("sp", HB[l][0:64, 0:256].rearrange("p (c k) -> p c k", c=2), ksx[:, :, TT:TT + 128],
                          reads=[r_ksx], writes=[r_HAL[l]])
                    S.dma("sp", HB[l][:, 256:386], vsx[:, NS, :, :].rearrange("p h e -> p (h e)"),
                          reads=[r_vsx], writes=[r_HAL[l]])
                heads_tile(l, 14, [(2, qf[0:64, 0, :], r_qf), (2, qf[0:64, 1, :], r_qf)])
                heads_tile(l, 15, [(2, qf[0:64, 2, :], r_qf), (2, qf[0:64, 3, :], r_qf)])
                heads_tile(l, 16, [(3, kf[:, 0, :], r_kf), (3, kf[:, 1, :], r_kf)])
                heads_tile(l, 17, [(3, kf[:, 2, :], r_kf), (3, kf[:, 3, :], r_kf)])
                v_tile(l, 18, vf, r_vf, 0, 0)
                v_tile(l, 19, vf, r_vf, 0, 2)
                t, r = WA.next(win[l, 20])
                bF = rot.get()
                for c in range(8):
                    PE(lambda e, c=c, t=t, bF=bF: e.matmul(bF.ap[0:4, :], lhsT=t[:, c, 0:4], rhs=hT[:, c, :],
                                                           start=(c == 0), stop=(c == 7)), [r, r_hT], [bF.res])
                ACT(lambda e: e.activation(out=fe[:], in_=bF.ap[0:4, :], func=AF.Exp, scale=-1.0,
                                           bias=bfg_t[:, 0:1]), [bF.res, r_small], [r_fe])
                ACT(lambda e: e.activation(out=fe[:], in_=fe[:], func=AF.Ln, bias=1.0), [r_fe], [r_fe])
                if j == 0:
                    DVE(lambda e: e.memset(fcar[l][:], 0.0), [], [r_fcar[l]])
                DVE(lambda e: e.tensor_tensor_scan(out=fG[:], data0=fones[:], data1=fe[:], initial=fcar[l][:, 0:1],
                                                   op0=ALU.mult, op1=ALU.add), [r_fe, r_fones, r_fcar[l]], [r_fG])
                DVE(lambda e: e.tensor_copy(out=fcar[l][:], in_=fG[:, TT - 1:TT]), [r_fG], [r_fcar[l]])
                split3(fG, r_fG)
                S.dma("sp", FQF[j], qf[0:64, :, :], reads=[r_qf], writes=[RF[j]])
                S.dma("sp", FG[j], fG[:], reads=[r_fG], writes=[RF[j]])
                rKF = RKF[l][j]
                for hh in range(4):
                    S.dma("sp", KF[l][hh][0:64, t0:t0 + TT], kf[:, hh, :], reads=[r_kf], writes=[rKF])
                    S.dma("sp", KF[l][hh][64:67, t0:t0 + TT].rearrange("(o a) t -> o a t", o=1),
                          monesb[hh:hh + 1, :, :], reads=[r_monesb], writes=[rKF])
                    S.dma("sp", KF[l][hh][67:70, t0:t0 + TT].rearrange("(o a) t -> o a t", o=1),
                          gsp[hh:hh + 1, :, :], reads=[r_gs], writes=[rKF])
                vq, vo = t0 // VC, t0 % VC
                S.dma("sp", VF[l][vq][vo:vo + TT, :].rearrange("(s p) e -> p s e", p=128),
                      vf[:].rearrange("p s h e -> p s (h e)"), reads=[r_vf], writes=[rKF])
                if last:
                    S.dma("sp", HF[l][0:4, 64:65], fcar[l][:], reads=[r_fcar[l]], writes=[r_HAL[l]],
                          allow_slow_non_contiguous=True)

            def exchange(l):
                srcs = ([(KFh[l][hh], GKFh[l][hh]) for hh in range(4)]
                        + [(VFh[l][q], GVFh[l][q]) for q in range(NVC)]
                        + [(HFh[l], GHFh[l]), (HBh[l], GHBh[l])])
                for (a, g) in srcs:
                    k = ccn[0]
                    ccn[0] += 1
                    S.op("pool", lambda e, a=a, g=g: e.collective_compute(
                        "AllGather", ALU.bypass, replica_groups=[[0, 1], [2, 3], [4, 5], [6, 7]],
                        ins=[a.ap().opt()], outs=[g.ap().opt()]),
                        reads=RKF[l] + [r_HAL[l]], writes=[r_G[l]], dma=True, cc=k)

            def back(l, j):
                t0 = j * TT
                load_small(l)
                rmsnorm_hT(gains[l, 1])
                S.dma("sp", uext[:, :, 30:30 + TT], FU[j], reads=[RF[j]], writes=[r_uext])
                S.dma("sp", sxext[:, :, 2:2 + TT], FSX[j], reads=[RF[j]], writes=[r_sxext])
                S.dma("sp", sbt[:], FSB[j], reads=[RF[j]], writes=[r_sbt])
                if j == 0:
                    S.dma("sp", uext[:, :, 0:30], GHF[l][0, :, 0:60].rearrange("p (c k) -> p c k", c=2),
                          reads=[r_G[l]], writes=[r_uext])
                    S.dma("sp", sxext[:, :, 0:2], GHF[l][0, :, 60:64].rearrange("p (c k) -> p c k", c=2),
                          reads=[r_G[l]], writes=[r_sxext])
                    S.dma("sp", tot_t[:], GHF[l][0, 0:4, 64:65], reads=[r_G[l]], writes=[r_tot],
                          allow_slow_non_contiguous=True)
                    DVE(lambda e: e.tensor_scalar(out=uext[:, :, 0:30], in0=uext[:, :, 0:30],
                                                  scalar1=flag_t[:, 0:1], scalar2=None, op0=ALU.mult),
                        [r_uext, r_flag], [r_uext])
                    DVE(lambda e: e.tensor_scalar(out=sxext[:, :, 0:2], in0=sxext[:, :, 0:2],
                                                  scalar1=flag_t[:, 0:1], scalar2=None, op0=ALU.mult),
                        [r_sxext, r_flag], [r_sxext])
                else:
                    S.dma("sp", uext[:, :, 0:30], FU[j - 1][:, :, TT - 30:TT], reads=[RF[j - 1]], writes=[r_uext])
                    S.dma("sp", sxext[:, :, 0:2], FSX[j - 1][:, :, TT - 2:TT], reads=[RF[j - 1]], writes=[r_sxext])
                for kk in range(31):
                    for ch in range(2):
                        if kk == 0:
                            DVE(lambda e, ch=ch: e.tensor_scalar(
                                out=cacc[:, ch, :], in0=uext[:, ch, 0:TT], scalar1=cdw_t[:, ch, 0:1],
                                scalar2=cvec_t[:, ch, 0:1], op0=ALU.mult, op1=ALU.add),
                                [r_uext, r_small], [r_cacc[ch]])
                        else:
                            DVE(lambda e, ch=ch, kk=kk: e.scalar_tensor_tensor(
                                out=cacc[:, ch, :], in0=uext[:, ch, kk:kk + TT], scalar=cdw_t[:, ch, kk:kk + 1],
                                in1=cacc[:, ch, :], op0=ALU.mult, op1=ALU.add),
                                [r_uext, r_small, r_cacc[ch]], [r_cacc[ch]])
                for ch in range(2):
                    ACT(lambda e, ch=ch: e.activation(out=csq[:, ch, :], in_=cacc[:, ch, :], func=AF.Square),
                        [r_cacc[ch]], [r_csq])
                b1 = rot.get()
                for ch in range(2):
                    PE(lambda e, ch=ch, b1=b1: e.matmul(b1.ap[:, :], lhsT=ones[:], rhs=cacc[:, ch, :],
                                                        start=(ch == 0), stop=(ch == 1)),
                       [r_cacc[ch], r_ones], [b1.res])
                b2 = rot.get()
                for ch in range(2):
                    PE(lambda e, ch=ch, b2=b2: e.matmul(b2.ap[:, :], lhsT=ones[:], rhs=csq[:, ch, :],
                                                        start=(ch == 0), stop=(ch == 1)),
                       [r_csq, r_ones], [b2.res])
                DVE(lambda e: e.tensor_scalar(out=lnm[:], in0=b1.ap[:, :], scalar1=1.0 / 256.0, scalar2=None,
                                              op0=ALU.mult), [b1.res], [r_lnm])
                DVE(lambda e: e.tensor_tensor(out=lnv[:], in0=lnm[:], in1=lnm[:], op=ALU.mult), [r_lnm], [r_lnv])
                DVE(lambda e: e.scalar_tensor_tensor(out=lnv[:], in0=b2.ap[:, :], scalar=1.0 / 256.0, in1=lnv[:],
                                                     op0=ALU.mult, op1=ALU.subtract), [b2.res, r_lnv], [r_lnv])
                DVE(lambda e: e.tensor_scalar(out=lnv[:], in0=lnv[:], scalar1=0.0, scalar2=None, op0=ALU.max),
                    [r_lnv], [r_lnv])
                ACT(lambda e: e.activation(out=lnv[:], in_=lnv[:], func=AF.Ln, bias=EPS), [r_lnv], [r_lnv])
                ACT(lambda e: e.activation(out=lnv[:], in_=lnv[:], func=AF.Exp, scale=-0.5), [r_lnv], [r_lnv])
                for ch in range(2):
                    DVE(lambda e, ch=ch: e.tensor_tensor(out=cacc[:, ch, :], in0=cacc[:, ch, :], in1=lnm[:],
                                                         op=ALU.subtract), [r_cacc[ch], r_lnm], [r_cacc[ch]])
                    DVE(lambda e, ch=ch: e.tensor_tensor(out=cacc[:, ch, :], in0=cacc[:, ch, :], in1=lnv[:],
                                                         op=ALU.mult), [r_cacc[ch], r_lnv], [r_cacc[ch]])
                    ACT(lambda e, ch=ch: e.activation(out=uT[:, ch, :], in_=cacc[:, ch, :], func=AF.Silu,
                                                      scale=cvec_t[:, ch, 1:2], bias=cvec_t[:, ch, 2:3]),
                        [r_cacc[ch], r_small], [r_uT])
                for ch in range(2):
                    DVE(lambda e, ch=ch: e.tensor_scalar(out=csq[:, ch, :], in0=sxext[:, ch, 0:TT],
                                                         scalar1=scw_t[:, ch, 0:1], scalar2=None, op0=ALU.mult),
                        [r_sxext, r_small], [r_csq])
                    for kk in (1, 2):
                        DVE(lambda e, ch=ch, kk=kk: e.scalar_tensor_tensor(
                            out=csq[:, ch, :], in0=sxext[:, ch, kk:kk + TT], scalar=scw_t[:, ch, kk:kk + 1],
                            in1=csq[:, ch, :], op0=ALU.mult, op1=ALU.add), [r_sxext, r_small, r_csq], [r_csq])
                    DVE(lambda e, ch=ch: e.tensor_tensor(out=scT[:, ch, :], in0=csq[:, ch, :], in1=sbt[:, ch, :],
                                                         op=ALU.mult), [r_csq, r_sbt], [r_scT])

                S.dma("sp", qs[:], FQS[j], reads=[RF[j]], writes=[r_qs])
                S.dma("sp", ksx[:, :, 128:128 + TT], FKS[j], reads=[RF[j]], writes=[r_ksx])
                S.dma("sp", vsx[:, 1:NS + 1, :, :].rearrange("p s h e -> p s (h e)"), FVS[j], reads=[RF[j]],
                      writes=[r_vsx])
                if j == 0:
                    S.dma("sp", ksx[:, :, 0:128], GHB[l][0, 0:64, 0:256].rearrange("p (c k) -> p c k", c=2),
                          reads=[r_G[l]], writes=[r_ksx])
                    S.dma("sp", vsx[:, 0, :, :].rearrange("p h e -> p (h e)"), GHB[l][0, :, 256:386],
                          reads=[r_G[l]], writes=[r_vsx])
                    DVE(lambda e: e.tensor_scalar(out=vsx[:, 0, :, :], in0=vsx[:, 0, :, :],
                                                  scalar1=flag_t[:, 0:1], scalar2=None, op0=ALU.mult),
                        [r_vsx, r_flag], [r_vsx])
                else:
                    S.dma("sp", ksx[:, :, 0:128], FKS[j - 1][:, :, TT - 128:TT], reads=[RF[j - 1]], writes=[r_ksx])
                    S.dma("sp", vsx[:, 0, :, :].rearrange("p h e -> p (h e)"), FVS[j - 1][:, NS - 1, :],
                          reads=[RF[j - 1]], writes=[r_vsx])
                for hq in range(4):
                    hk = hq // 2
                    acc = banks[hq % 4]
                    for pr in range(2):
                        bS = rot_hi.get()
                        i = sgi[0] % 2
                        sgi[0] += 1
                        for s2 in range(2):
                            s = 2 * pr + s2
                            for part in range(2):
                                PE(lambda e, s=s, s2=s2, part=part, bS=bS, hk=hk, hq=hq: e.matmul(
                                    bS.ap[:, s2 * 256 + part * 128: s2 * 256 + part * 128 + 128],
                                    lhsT=ksx[:, hk, (s + part) * 128:(s + part + 1) * 128],
                                    rhs=qs[:, hq, s * 128:(s + 1) * 128], start=True, stop=True),
                                   [r_ksx, r_qs], [bS.res])
                            DVE(lambda e, bS=bS, i=i, hq=hq, s2=s2: e.tensor_tensor(
                                out=sg[i][:, s2 * 256:(s2 + 1) * 256], in0=bS.ap[:, s2 * 256:(s2 + 1) * 256],
                                in1=bm_t[:, hq, :], op=ALU.add), [bS.res, r_bm], [r_sg[i]])
                        pi = pti[0] % 4
                        pti[0] += 1
                        ACT(lambda e, i=i, pi=pi: e.activation(out=pT[pi][:], in_=sg[i][:], func=AF.Exp),
                            [r_sg[i]], [r_pT[pi]])
                        for s2 in range(2):
                            s = 2 * pr + s2
                            first = True
                            for part in range(2):
                                PE(lambda e, s=s, s2=s2, part=part, pi=pi, hk=hk, acc=acc, first=first: e.matmul(
                                    acc.ap[0:65, s * 128:(s + 1) * 128], lhsT=vsx[:, s + part, hk, :],
                                    rhs=pT[pi][:, s2 * 256 + part * 128: s2 * 256 + part * 128 + 128],
                                    start=first, stop=(part == 1)), [r_vsx, r_pT[pi]], [acc.res])
                                first = False
                    attn_norm(acc, sink_t[64:65, hq:hq + 1], onT[0][:, hq, :], r_onT[0])

                S.dma("sp", qf[0:64, :, :], FQF[j], reads=[RF[j]], writes=[r_qf])
                S.dma("sp", qfx[0:64, :, :], FQF[j], reads=[RF[j]], writes=[r_qfx])
                S.dma("sp", fG[:], FG[j], reads=[RF[j]], writes=[r_fG])
                split3(fG, r_fG)
                for hh in range(4):
                    S.dma("sp", QFs[0, :, hh, :].rearrange("(o a) t -> o a t", o=1), gsp[hh:hh + 1, :, :],
                          reads=[r_gs], writes=[r_QFs])
                S.dma("sp", qf[64:67, :, :], QFs[0], reads=[r_QFs], writes=[r_qf])
                DVE(lambda e: e.tensor_scalar(out=fG[:], in0=fG[:], scalar1=tot_t[:, 0:1], scalar2=None,
                                              op0=ALU.add), [r_fG, r_tot], [r_fG])
                split3(fG, r_fG)
                for hh in range(4):
                    S.dma("sp", QFs[1, :, hh, :].rearrange("(o a) t -> o a t", o=1), gsp[hh:hh + 1, :, :],
                          reads=[r_gs], writes=[r_QFs])
                S.dma("sp", qfx[64:67, :, :], QFs[1], reads=[r_QFs], writes=[r_qfx])
                accs = banks[0:4]
                nkb = NT + j + 1
                jobs = [(kk, hh, c) for kk in range(nkb) for hh in range(4) for c in range(NS)]
                blk = {}

                def load_blk(kk):
                    cross = kk < NT
                    kb = kk if cross else kk - NT
                    i = kbi[0] % 2
                    kbi[0] += 1
                    vq, vo = (kb * TT) // VC, (kb * TT) % VC
                    if cross:
                        for hh in range(4):
                            S.dma("sp", kblk[i][0:70, hh, :], GKF[l][hh][0, :, kb * TT:(kb + 1) * TT],
                                  reads=[r_G[l]], writes=[r_kblk[i]])
                        S.dma("sp", vblk[i][:, :, 0:260],
                              GVF[l][vq][0, vo:vo + TT, :].rearrange("(s p) e -> p s e", p=128),
                              reads=[r_G[l]], writes=[r_vblk[i]])
                        DVE(lambda e, i=i: e.tensor_scalar(out=vblk[i][:, :, 0:260], in0=vblk[i][:, :, 0:260],
                                                           scalar1=flag_t[:, 0:1], scalar2=None, op0=ALU.mult),
                            [r_vblk[i], r_flag], [r_vblk[i]])
                    else:
                        for hh in range(4):
                            S.dma("sp", kblk[i][0:70, hh, :], KF[l][hh][:, kb * TT:(kb + 1) * TT],
                                  reads=[RKF[l][kb]], writes=[r_kblk[i]])
                        S.dma("sp", vblk[i][:, :, 0:260],
                              VF[l][vq][vo:vo + TT, :].rearrange("(s p) e -> p s e", p=128),
                              reads=[RKF[l][kb]], writes=[r_vblk[i]])
                    blk[kk] = (i, cross, (not cross) and kb == j)

                pis = {}

                def emit_S(job):
                    kk, hh, c = job
                    if kk not in blk:
                        load_blk(kk)
                    i, cross, diag = blk[kk]
                    qsrc, rq = (qfx, r_qfx) if cross else (qf, r_qf)
                    bS = rot_hi.get()
                    PE(lambda e, i=i, hh=hh, c=c, bS=bS, qsrc=qsrc: e.matmul(
                        bS.ap[:, :], lhsT=kblk[i][:, hh, c * 128:(c + 1) * 128], rhs=qsrc[:, hh, :],
                        start=True, stop=True), [r_kblk[i], rq], [bS.res])
                    pi = pti[0] % 4
                    pti[0] += 1
                    pis[job] = pi
                    ACT(lambda e, pi=pi, bS=bS: e.activation(out=pT[pi][:], in_=bS.ap[:, :], func=AF.Exp),
                        [bS.res], [r_pT[pi]])
                    if diag:
                        POOL(lambda e, pi=pi, c=c: e.affine_select(
                            out=pT[pi][:], in_=pT[pi][:], pattern=[[1, TT]], compare_op=ALU.is_ge,
                            fill=0.0, base=-c * 128, channel_multiplier=-1), [r_pT[pi]], [r_pT[pi]])

                def emit_PV(job):
                    kk, hh, c = job
                    i = blk[kk][0]
                    pi = pis[job]
                    PE(lambda e, i=i, hh=hh, c=c, pi=pi, kk=kk: e.matmul(
                        accs[hh].ap[:, :], lhsT=vblk[i][:, c, hh * 65:hh * 65 + 128], rhs=pT[pi][:],
                        start=(kk == 0 and c == 0), stop=(kk == nkb - 1 and c == NS - 1)),
                       [r_vblk[i], r_pT[pi]], [accs[hh].res])

                LA = 2
                for idx, job in enumerate(jobs):
                    emit_S(job)
                    if idx >= LA:
                        emit_PV(jobs[idx - LA])
                for job in jobs[len(jobs) - LA:]:
                    emit_PV(job)
                for hh in range(4):
                    attn_norm(accs[hh], None, onT[1][:, hh, :], r_onT[1])

                brsrc = [(uT, r_uT), (scT, r_scT)]
                for m in range(8):
                    tr, rr = WR.next(wbr[l, m])
                    for br in range(4):
                        bp = rot.get()
                        if br < 2:
                            src, rs = brsrc[br]
                            for ch in range(2):
                                PE(lambda e, br=br, ch=ch, bp=bp, src=src, tr=tr: e.matmul(
                                    bp.ap[:, :], lhsT=tr[:, 2 * br + ch, :], rhs=src[:, ch, :],
                                    start=(ch == 0), stop=(ch == 1)), [rr, rs], [bp.res])
                        else:
                            a = br - 2
                            for hh in range(4):
                                PE(lambda e, a=a, hh=hh, bp=bp, tr=tr: e.matmul(
                                    bp.ap[:, :], lhsT=tr[0:64, 4 + 4 * a + hh, :],
                                    rhs=onT[a][:, hh, :], start=(hh == 0), stop=(hh == 3)),
                                   [rr, r_onT[a]], [bp.res])
                        bgt = proj128(l, 21 + 4 * m + br)
                        i = sgi[0] % 2
                        sgi[0] += 1
                        ACT(lambda e, i=i, bgt=bgt: e.activation(out=sg[i][:], in_=bgt.ap[:, :], func=AF.Sigmoid),
                            [bgt.res], [r_sg[i]])
                        if br == 0:
                            DVE(lambda e, i=i, bp=bp: e.tensor_tensor(out=macc[:], in0=sg[i][:], in1=bp.ap[:, :],
                                                                      op=ALU.mult), [r_sg[i], bp.res], [r_macc])
                        else:
                            k2 = br % 2
                            DVE(lambda e, i=i, bp=bp, k2=k2: e.tensor_tensor(out=mtmp[k2][:], in0=sg[i][:],
                                                                             in1=bp.ap[:, :], op=ALU.mult),
                                [r_sg[i], bp.res], [r_mtmp[k2]])
                            if br < 3:
                                DVE(lambda e, k2=k2: e.tensor_tensor(out=macc[:], in0=macc[:], in1=mtmp[k2][:],
                                                                     op=ALU.add), [r_macc, r_mtmp[k2]], [r_macc])
                            else:
                                DVE(lambda e, k2=k2, m=m: e.tensor_tensor(out=mT[:, m, :], in0=macc[:],
                                                                          in1=mtmp[k2][:], op=ALU.add),
                                    [r_macc, r_mtmp[k2]], [r_mT])
                resid_proj(mT, lambda m: r_mT, 8, lambda n, m: wo2[l, n, m], 1.0)

            for l in range(L):
                for j in range(NT):
                    t0 = j * TT
                    src = xin if l == 0 else Xs
                    S.dma("sp", x[:], src[t0:t0 + TT, :].rearrange("(s p) d -> p s d", p=128),
                          reads=([RX[j]] if l > 0 else []), writes=r_x)
                    ffn(l, 0)
                    S.dma("sp", Xs[t0:t0 + TT, :].rearrange("(s p) d -> p s d", p=128), x[:], reads=r_x,
                          writes=[RX[j]])
                    front(l, j)
                exchange(l)
                for j in range(NT):
                    t0 = j * TT
                    S.dma("sp", x[:], Xs[t0:t0 + TT, :].rearrange("(s p) d -> p s d", p=128), reads=[RX[j]],
                          writes=r_x)
                    back(l, j)
                    ffn(l, 1)
                    dst = Xs if l < L - 1 else yout
                    S.dma("sp", dst[t0:t0 + TT, :].rearrange("(s p) d -> p s d", p=128), x[:], reads=r_x,
                          writes=[RX[j]])
            return [WA.rec, WD.rec, WR.rec]

        plans = program(Sched(nc, dry=True), [None, None, None])
        S = Sched(nc)
        program(S, plans)
        S.emit()
    return nc


def _t5_bucket(dist):
    max_exact = 16
    d = np.maximum(dist, 1).astype(np.float32)
    large = max_exact + (np.log(d / np.float32(max_exact)) / np.float32(np.log(128 / max_exact))
                         * np.float32(32 - max_exact)).astype(np.int32)
    large = np.minimum(large, 31)
    return np.where(dist < max_exact, dist, large)


def host_prep(inp):
    f = lambda a: np.ascontiguousarray(np.asarray(a, dtype=np.float32))

    def wtile(w, col0, n):
        out = np.zeros((128, 8, 128), np.float32)
        out[:, :, :n] = w[:, col0:col0 + n].reshape(8, 128, n).transpose(1, 0, 2)
        return out

    shared = {}
    gates = [np.asarray(inp["ffn1_w_gate"]), np.asarray(inp["ffn2_w_gate"])]
    ups = [np.asarray(inp["ffn1_w_up"]), np.asarray(inp["ffn2_w_up"])]
    downs = [np.asarray(inp["ffn1_w_down"]), np.asarray(inp["ffn2_w_down"])]

    def ftile(w):
        return w.reshape(8, 128, NF, 128).transpose(2, 1, 0, 3)

    def dtile(w):
        return w.reshape(NF, 128, 2, 512).transpose(2, 0, 1, 3)

    shared["wg"] = f(np.stack([np.stack([ftile(gates[k][l]) for k in range(2)]) for l in range(L)]))
    shared["wu"] = f(np.stack([np.stack([ftile(ups[k][l]) for k in range(2)]) for l in range(L)]))
    shared["wd2"] = f(np.stack([np.stack([dtile(downs[k][l]) for k in range(2)]) for l in range(L)]))
    w_in = np.asarray(inp["w_in"])
    shared["win"] = f(np.stack([np.stack([wtile(w_in[l], c0, n) for (c0, n) in WIN_TILES]) for l in range(L)]))
    cwo, swo = np.asarray(inp["conf_w_out"]), np.asarray(inp["sc_w_out"])
    awo, fwo = np.asarray(inp["swa_w_o"]), np.asarray(inp["fox_w_o"])
    wbr = np.zeros((L, 8, 128, 12, 128), np.float32)
    for l in range(L):
        for m in range(8):
            cs = slice(m * 128, (m + 1) * 128)
            for ch in range(2):
                wbr[l, m, :, ch, :] = cwo[l][ch * 128:(ch + 1) * 128, cs]
                wbr[l, m, :, 2 + ch, :] = swo[l][ch * 128:(ch + 1) * 128, cs]
            for hh in range(4):
                wbr[l, m, 0:64, 4 + hh, :] = awo[l][hh * 64:(hh + 1) * 64, cs]
                wbr[l, m, 0:64, 8 + hh, :] = fwo[l][hh * 64:(hh + 1) * 64, cs]
    shared["wbr"] = wbr
    shared["wo2"] = f(np.stack([np.asarray(inp["w_out"])[l].reshape(8, 128, 2, 512).transpose(2, 0, 1, 3)
                                for l in range(L)]))
    gn = [np.asarray(inp["ffn1_norm"]), np.asarray(inp["mix_norm"]), np.asarray(inp["ffn2_norm"])]
    shared["gains"] = f(np.stack([np.stack([np.broadcast_to(gn[i][l][None, :], (128, D)) for i in range(3)])
                                  for l in range(L)]))
    shared["cdw"] = f(np.stack([np.asarray(inp["conf_dw"])[l].T.reshape(2, 128, 31).transpose(1, 0, 2)
                                for l in range(L)]))
    cv = [np.asarray(inp["conf_dw_b"]), np.asarray(inp["conf_ln_g"]), np.asarray(inp["conf_ln_b"])]
    shared["cvec"] = f(np.stack([np.stack([cv[i][l].reshape(2, 128).T for i in range(3)], axis=-1)
                                 for l in range(L)]))
    shared["scw"] = f(np.stack([np.asarray(inp["sc_conv"])[l].T.reshape(2, 128, 3).transpose(1, 0, 2)
                                for l in range(L)]))
    qk = [np.asarray(inp["swa_q_norm"]), np.asarray(inp["swa_k_norm"]),
          np.asarray(inp["fox_q_norm"]), np.asarray(inp["fox_k_norm"])]
    shared["qkg"] = f(np.stack([np.stack([qk[i][l] for i in range(4)], axis=-1) for l in range(L)]))
    shared["bfg"] = f(np.asarray(inp["b_forget"]).reshape(L, 4, 1))
    sk = np.zeros((L, 65, 4), np.float32)
    sk[:, 64, :] = np.asarray(inp["swa_sink"])
    shared["sink"] = sk
    rb = np.asarray(inp["rel_bias"], dtype=np.float32)
    i = np.arange(128)[:, None]
    jq = np.arange(128)[None, :]
    bmt = np.full((128, 4, 256), NEG, np.float32)
    d_prev = jq + 128 - i
    ok_prev = d_prev <= 127
    d_cur = jq - i
    ok_cur = d_cur >= 0
    bk_prev = _t5_bucket(np.clip(d_prev, 0, 127))
    bk_cur = _t5_bucket(np.clip(d_cur, 0, 127))
    for hq in range(4):
        bmt[:, hq, 0:128] = np.where(ok_prev, rb[bk_prev, hq], np.float32(NEG))
        bmt[:, hq, 128:256] = np.where(ok_cur, rb[bk_cur, hq], np.float32(NEG))
    shared["bm"] = bmt
    shared["identin"] = np.eye(128, dtype=np.float32)
    return shared


_NC_CACHE = {}


def kernel(**inputs):
    x = np.asarray(inputs["x"], dtype=np.float32)
    B, T, _ = x.shape
    ntok = T // 2
    shared = host_prep(inputs)
    if ntok not in _NC_CACHE:
        _NC_CACHE[ntok] = build(ntok)
    nc = _NC_CACHE[ntok]
    in_maps = []
    for c in range(N_CORES):
        b, half = c // 2, c % 2
        m = dict(shared)
        m["xin"] = np.ascontiguousarray(x[b, half * ntok:(half + 1) * ntok])
        m["flagin"] = np.full((128, 1), float(half), np.float32)
        in_maps.append(m)
    res = run_bass_kernel_spmd(nc, in_maps, core_ids=list(range(N_CORES)))
    out = np.empty((B, T, D), np.float32)
    for c in range(N_CORES):
        b, half = c // 2, c % 2
        out[b, half * ntok:(half + 1) * ntok] = np.asarray(res.results[c]["yout"], dtype=np.float32)
    return out
```

```python
from contextlib import ExitStack

import numpy as np

import concourse.bass as bass
import concourse.mybir as mybir
from concourse.bass_utils import run_bass_kernel_spmd

F32 = mybir.dt.float32
BF16 = mybir.dt.bfloat16
AF = mybir.ActivationFunctionType
ALU = mybir.AluOpType

D = 1024
DFF = 2816
NF = DFF // 128
L = 2
TT = 512
NS = TT // 128
EPS = 1e-6
NEG = -30000.0
N_CORES = 8

WIN_TILES = ([(0, 128), (256, 128), (128, 128), (384, 128), (512, 128), (640, 128),
              (768, 128), (1024, 128), (896, 128), (1152, 128)]
             + [(1280, 128), (1408, 128), (1536, 128), (1664, 128)]
             + [(1792, 128), (1920, 128), (2048, 128), (2176, 128), (2304, 128), (2432, 128), (2560, 4)]
             + [(2564 + i * 1024 + m * 128, 128) for m in range(8) for i in range(4)])
NWT = len(WIN_TILES)


class Res:
    __slots__ = ("name", "lw", "rd")

    def __init__(self, name=""):
        self.name = name
        self.lw = None
        self.rd = {}


class Op:
    __slots__ = ("eng", "fn", "deps", "sig", "signo", "dma", "dj", "idx", "cc")


ENGS = ["pe", "act", "dve", "pool", "sp"]
NSC = 4
NSD = 8


def _dkey(o):
    return ("d", id(o)) if o.dma else ("c", o.eng)


def _dput(d, o):
    k = _dkey(o)
    cur = d.get(k)
    if cur is None or o.dma or o.idx > cur.idx:
        d[k] = o


class Sched:
    def __init__(self, nc, dry=False):
        self.nc = nc
        self.dry = dry
        self.ops = {e: [] for e in ENGS}

    def op(self, eng, fn, reads=(), writes=(), dma=False, cc=None):
        if self.dry:
            return None
        o = Op()
        o.eng, o.fn, o.dma, o.sig, o.cc = eng, fn, dma, False, cc
        o.idx = len(self.ops[eng])
        deps = {}

        def add(d, raw):
            if d is None:
                return
            if d.dma or dma or d.eng != eng or (raw and eng != "pe"):
                _dput(deps, d)

        for r in reads:
            add(r.lw, True)
        for w in writes:
            add(w.lw, False)
            for x in w.rd.values():
                add(x, False)
        o.deps = list(deps.values())
        for d in o.deps:
            d.sig = True
        for r in reads:
            _dput(r.rd, o)
        for w in writes:
            w.lw = o
            w.rd = {}
        self.ops[eng].append(o)
        return o

    def dma(self, eng, out, in_, reads=(), writes=(), **kw):
        return self.op(eng, lambda e: e.dma_start(out=out, in_=in_, **kw), reads=reads, writes=writes, dma=True)

    def emit(self):
        nc = self.nc
        for e in ENGS:
            c = 0
            j = 0
            for o in self.ops[e]:
                if o.cc is not None:
                    continue
                if o.dma:
                    o.dj = j
                    j += 1
                elif o.sig:
                    c += 1
                    o.signo = c
        with ExitStack() as st:
            csem = {e: [st.enter_context(nc.semaphore(f"c_{e}_{i}")) for i in range(NSC)]
                    for e in ENGS if e != "sp"}
            dsem = {e: [st.enter_context(nc.semaphore(f"d_{e}_{i}")) for i in range(NSD)]
                    for e in ENGS if e != "pe"}
            ncc = sum(1 for e in ENGS for o in self.ops[e] if o.cc is not None)
            ccsem = [st.enter_context(nc.semaphore(f"cc_{i}")) for i in range(ncc)]
            block = st.enter_context(nc.Block())

            def body(eng, e):
                cw = {}
                dw = {}

                def wait_dma(q, dj):
                    slot, val = dj % NSD, 16 * (dj // NSD + 1)
                    if dw.get((q, slot), 0) < val:
                        eng.wait_ge(dsem[q][slot], val)
                        dw[(q, slot)] = val

                for o in self.ops[e]:
                    for d in o.deps:
                        if d.cc is not None:
                            if dw.get(("cc", d.cc), 0) < 1:
                                eng.wait_ge(ccsem[d.cc], 1)
                                dw[("cc", d.cc)] = 1
                        elif d.dma:
                            wait_dma(d.eng, d.dj)
                        elif cw.get(d.eng, 0) < d.signo:
                            s = d.signo - 1
                            eng.wait_ge(csem[d.eng][s % NSC], s // NSC + 1)
                            cw[d.eng] = d.signo
                    if o.cc is not None:
                        o.fn(eng).then_inc(ccsem[o.cc])
                        continue
                    if o.dma and o.dj >= NSD:
                        wait_dma(e, o.dj - NSD)
                    ins = o.fn(eng)
                    if o.dma:
                        ins.then_inc(dsem[e][o.dj % NSD], 16)
                    elif o.sig:
                        ins.then_inc(csem[e][(o.signo - 1) % NSC], 1)
                n = sum(1 for o in self.ops[e] if o.dma and o.cc is None)
                for dj in range(max(0, n - NSD), n):
                    wait_dma(e, dj)

            names = {"pe": "tensor", "act": "scalar", "dve": "vector", "pool": "gpsimd", "sp": "sync"}
            for e in ENGS:
                getattr(block, names[e])(lambda eng, e=e: body(eng, e))


class Stream:
    def __init__(self, S, slots, res, plan):
        self.S, self.slots, self.res, self.plan = S, slots, res, plan
        self.rec = []
        self.pos = 0
        self.issued = 0

    def next(self, src):
        R = len(self.slots)
        if self.plan is None:
            self.rec.append(src)
            return self.slots[0], self.res[0]
        while self.issued < min(len(self.plan), self.pos + R):
            k = self.issued % R
            self.S.dma("pool", self.slots[k][:], self.plan[self.issued], writes=[self.res[k]])
            self.issued += 1
        k = self.pos % R
        self.pos += 1
        return self.slots[k], self.res[k]


class Bank:
    def __init__(self, ap, name):
        self.ap = ap
        self.res = Res(name)


class Rot:
    def __init__(self, items):
        self.items = items
        self.i = 0

    def get(self):
        b = self.items[self.i % len(self.items)]
        self.i += 1
        return b


def build(NTOK, stages=3):
    NT = NTOK // TT
    nc = bass.Bass("TRN2", target_bir_lowering=False)

    def din(name, shape, dt=F32):
        return nc.dram_tensor(name, list(shape), dt, kind="ExternalInput").ap()

    xin = din("xin", [NTOK, D])
    wg = din("wg", [L, 2, NF, 128, 8, 128])
    wu = din("wu", [L, 2, NF, 128, 8, 128])
    wd2 = din("wd2", [L, 2, 2, NF, 128, 512])
    win = din("win", [L, NWT, 128, 8, 128])
    wbr = din("wbr", [L, 8, 128, 12, 128])
    wo2 = din("wo2", [L, 2, 8, 128, 512])
    gains = din("gains", [L, 3, 128, D])
    cdw = din("cdw", [L, 128, 2, 31])
    cvec = din("cvec", [L, 128, 2, 3])
    scw = din("scw", [L, 128, 2, 3])
    qkg = din("qkg", [L, 64, 4])
    bfg = din("bfg", [L, 4, 1])
    sink = din("sink", [L, 65, 4])
    bm = din("bm", [128, 4, 256])
    identin = din("identin", [128, 128])
    flagin = din("flagin", [128, 1])
    maskin = din("maskin", [1, 4 * TT])
    yout = nc.dram_tensor("yout", [NTOK, D], F32, kind="ExternalOutput").ap()

    def dscr(name, shape, dt):
        return nc.dram_tensor(name, list(shape), dt, kind="Internal")

    Xs = dscr("Xs", [NTOK, D], F32).ap()
    FU = dscr("FU", [NT, 128, 2, TT], F32).ap()
    FSX = dscr("FSX", [NT, 128, 2, TT], F32).ap()
    FSB = dscr("FSB", [NT, 128, 2, TT], F32).ap()
    FQS = dscr("FQS", [NT, 64, 4, TT], BF16).ap()
    FKS = dscr("FKS", [NT, 64, 2, TT], BF16).ap()
    FVS = dscr("FVS", [NT, 128, NS, 2 * 65], BF16).ap()
    FQF = dscr("FQF", [NT, 64, 4, TT], BF16).ap()
    FG = dscr("FG", [NT, 4, TT], F32).ap()
    QFs = dscr("QFs", [2, 3, 4, TT], BF16).ap()
    NVC = max(1, NTOK // 1024)
    VC = NTOK // NVC
    KFh = [[dscr(f"KF{l}_{hh}", [70, NTOK], BF16) for hh in range(4)] for l in range(L)]
    VFh = [[dscr(f"VF{l}_{q}", [VC, 4 * 65], BF16) for q in range(NVC)] for l in range(L)]
    HFh = [dscr(f"HF{l}", [128, 128], F32) for l in range(L)]
    HBh = [dscr(f"HB{l}", [128, 512], BF16) for l in range(L)]
    GKFh = [[dscr(f"GKF{l}_{hh}", [2 * 70, NTOK], BF16) for hh in range(4)] for l in range(L)]
    GVFh = [[dscr(f"GVF{l}_{q}", [2 * VC, 4 * 65], BF16) for q in range(NVC)] for l in range(L)]
    GHFh = [dscr(f"GHF{l}", [2 * 128, 128], F32) for l in range(L)]
    GHBh = [dscr(f"GHB{l}", [2 * 128, 512], BF16) for l in range(L)]
    KF = [[t.ap() for t in row] for row in KFh]
    VF = [[t.ap() for t in row] for row in VFh]
    HF = [t.ap() for t in HFh]
    HB = [t.ap() for t in HBh]
    GKF = [[t.ap().rearrange("(g r) t -> g r t", g=2) for t in row] for row in GKFh]
    GVF = [[t.ap().rearrange("(g n) e -> g n e", g=2) for t in row] for row in GVFh]
    GHF = [t.ap().rearrange("(g p) e -> g p e", g=2) for t in GHFh]
    GHB = [t.ap().rearrange("(g p) e -> g p e", g=2) for t in GHBh]

    with ExitStack() as st:
        def sb(name, shape, dt):
            return st.enter_context(nc.sbuf_tensor(name, list(shape), dt))

        x = sb("x", [128, NS, D], F32)
        r_x = [Res(f"x{s}") for s in range(NS)]
        h = sb("h", [128, NS, D], BF16)
        r_h = [Res(f"h{s}") for s in range(NS)]
        hT = sb("hT", [128, 8, TT], BF16)
        r_hT = Res("hT")
        act = sb("act", [128, NF, TT], BF16)
        r_act = [Res(f"act{f}") for f in range(NF)]
        RA, RD, RR = 6, 8, 2
        wA = [sb(f"wA{i}", [128, 8, 128], BF16) for i in range(RA)]
        rA = [Res(f"wA{i}") for i in range(RA)]
        wD = [sb(f"wD{i}", [128, 512], BF16) for i in range(RD)]
        rD = [Res(f"wD{i}") for i in range(RD)]
        wR = [sb(f"wR{i}", [128, 12, 128], BF16) for i in range(RR)]
        rR = [Res(f"wR{i}") for i in range(RR)]
        gb = sb("gb", [128, D], F32)
        r_gb = Res("gb")
        ss = sb("ss", [128, 8], F32)
        r_ss = Res("ss")
        sg = [sb(f"sg{i}", [128, TT], F32) for i in range(2)]
        r_sg = [Res(f"sg{i}") for i in range(2)]
        ident = sb("ident", [128, 128], BF16)
        r_ident = Res("ident")
        ones = sb("ones", [128, 128], F32)
        r_ones = Res("ones")
        o64 = sb("o64", [64, 64], F32)
        r_o64 = Res("o64")
        uext = sb("uext", [128, 2, 30 + TT], F32)
        r_uext = Res("uext")
        sxext = sb("sxext", [128, 2, 2 + TT], F32)
        r_sxext = Res("sxext")
        sbt = sb("sbt", [128, 2, TT], F32)
        r_sbt = Res("sbt")
        cacc = sb("cacc", [128, 2, TT], F32)
        r_cacc = [Res("cacc0"), Res("cacc1")]
        csq = sb("csq", [128, 2, TT], F32)
        r_csq = Res("csq")
        lnm = sb("lnm", [128, TT], F32)
        r_lnm = Res("lnm")
        lnv = sb("lnv", [128, TT], F32)
        r_lnv = Res("lnv")
        uT = sb("uT", [128, 2, TT], BF16)
        r_uT = Res("uT")
        scT = sb("scT", [128, 2, TT], BF16)
        r_scT = Res("scT")
        cdw_t = sb("cdw_t", [128, 2, 31], F32)
        cvec_t = sb("cvec_t", [128, 2, 3], F32)
        scw_t = sb("scw_t", [128, 2, 3], F32)
        qkg_t = sb("qkg_t", [64, 4], F32)
        bfg_t = sb("bfg_t", [4, 1], F32)
        sink_t = sb("sink_t", [65, 4], F32)
        r_small = Res("small")
        bm_t = sb("bm_t", [128, 4, 256], F32)
        r_bm = Res("bm")
        qs = sb("qs", [64, 4, TT], BF16)
        r_qs = Res("qs")
        ksx = sb("ksx", [64, 2, 128 + TT], BF16)
        r_ksx = Res("ksx")
        vsx = sb("vsx", [128, NS + 1, 2, 65], BF16)
        r_vsx = Res("vsx")
        pT = [sb(f"pT{i}", [128, TT], BF16) for i in range(4)]
        r_pT = [Res(f"pT{i}") for i in range(4)]
        oa = sb("oa", [65, TT], F32)
        r_oa = Res("oa")
        rden = sb("rden", [65, TT], F32)
        r_rden = Res("rden")
        onT = [sb(f"onT{i}", [64, 4, TT], BF16) for i in range(2)]
        r_onT = [Res("onT0"), Res("onT1")]
        qf = sb("qf", [128, 4, TT], BF16)
        r_qf = Res("qf")
        qfx = sb("qfx", [128, 4, TT], BF16)
        r_qfx = Res("qfx")
        flag_t = sb("flag_t", [128, 1], F32)
        r_flag = Res("flag")
        tot_t = sb("tot_t", [4, 1], F32)
        r_tot = Res("tot")
        kf = sb("kf", [64, 4, TT], BF16)
        r_kf = Res("kf")
        vf = sb("vf", [128, NS, 4, 65], BF16)
        r_vf = Res("vf")
        kblk = [sb(f"kblk{i}", [128, 4, TT], BF16) for i in range(2)]
        r_kblk = [Res(f"kblk{i}") for i in range(2)]
        vblk = [sb(f"vblk{i}", [128, NS, 324], BF16) for i in range(2)]
        r_vblk = [Res(f"vblk{i}") for i in range(2)]
        fe = sb("fe", [4, TT], F32)
        r_fe = Res("fe")
        fG = sb("fG", [4, TT], F32)
        r_fG = Res("fG")
        fones = sb("fones", [4, TT], F32)
        r_fones = Res("fones")
        gsp = sb("gsp", [4, 3, TT], BF16)
        r_gs = Res("gs")
        monesb = sb("monesb", [4, 3, TT], BF16)
        r_monesb = Res("monesb")
        fcar = [sb(f"fcar{l}", [4, 1], F32) for l in range(L)]
        r_fcar = [Res(f"fcar{l}") for l in range(L)]
        macc = sb("macc", [128, TT], F32)
        r_macc = Res("macc")
        mtmp = [sb(f"mtmp{i}", [128, TT], F32) for i in range(2)]
        r_mtmp = [Res(f"mtmp{i}") for i in range(2)]
        mT = sb("mT", [128, 8, TT], BF16)
        r_mT = Res("mT")
        banks = [Bank(st.enter_context(nc.psum_tensor(f"ps{i}", [128, 512], F32)), f"ps{i}") for i in range(7)]
        ptb = Bank(st.enter_context(nc.psum_tensor("ptb", [128, 1024], BF16)), "ptb")

        def program(S, plans):
            WA = Stream(S, wA, rA, plans[0])
            WD = Stream(S, wD, rD, plans[1])
            WR = Stream(S, wR, rR, plans[2])
            rot = Rot(banks)
            rot_hi = Rot(banks[4:7])
            RKF = [[Res(f"KF{l}_{j}") for j in range(NT)] for l in range(L)]
            RX = [Res(f"X{j}") for j in range(NT)]
            RF = [Res(f"F{j}") for j in range(NT)]
            r_HAL = [Res(f"HAL{l}") for l in range(L)]
            r_G = [Res(f"G{l}") for l in range(L)]
            r_QFs = Res("QFs")
            ccn = [0]
            kbi = [0]
            pti = [0]
            sgi = [0]

            def ACT(fn, reads, writes):
                S.op("act", fn, reads, writes)

            def DVE(fn, reads, writes):
                S.op("dve", fn, reads, writes)

            def PE(fn, reads, writes):
                S.op("pe", fn, reads, writes)

            def POOL(fn, reads, writes):
                S.op("pool", fn, reads, writes)

            S.dma("pool", ident[:], identin, writes=[r_ident])
            DVE(lambda e: e.memset(ones[:], 1.0), [], [r_ones])
            DVE(lambda e: e.memset(o64[:], 1.0 / 64.0), [], [r_o64])
            DVE(lambda e: e.memset(fones[:], 1.0), [], [r_fones])
            DVE(lambda e: e.memset(monesb[:], -1.0), [], [r_monesb])
            DVE(lambda e: e.memset(qf[:], 0.0), [], [r_qf])
            DVE(lambda e: e.memset(qf[64:70, :, :], 1.0), [], [r_qf])
            DVE(lambda e: e.memset(qfx[:], 0.0), [], [r_qfx])
            DVE(lambda e: e.memset(qfx[64:70, :, :], 1.0), [], [r_qfx])
            DVE(lambda e: e.memset(qfx[64:71, :, :], 1.0), [], [r_qfx])
            for i in range(2):
                DVE(lambda e, i=i: e.memset(kblk[i][:], 0.0), [], [r_kblk[i]])
                DVE(lambda e, i=i: e.memset(vblk[i][:], 0.0), [], [r_vblk[i]])
                S.dma("pool", kblk[i][70:71, :, :], maskin.rearrange("o (h t) -> o h t", h=4), writes=[r_kblk[i]])
            S.dma("sp", flag_t[:], flagin, writes=[r_flag])
            S.dma("sp", bm_t[:], bm, writes=[r_bm])
            DVE(lambda e: e.memset(vsx[:], 1.0), [], [r_vsx])
            DVE(lambda e: e.memset(vf[:], 1.0), [], [r_vf])
            for l in range(L):
                DVE(lambda e, l=l: e.memset(fcar[l][:], 0.0), [], [r_fcar[l]])

            def rmsnorm_hT(gain_src):
                S.dma("sp", gb[:], gain_src, writes=[r_gb])
                for s in range(NS):
                    ACT(lambda e, s=s: e.activation(out=h[:, s, :], in_=x[:, s, :], func=AF.Square,
                                                    accum_out=ss[:, s:s + 1]),
                        [r_x[s]], [r_h[s], r_ss])
                ACT(lambda e: e.activation(out=ss[:, 4:8], in_=ss[:, 0:4], func=AF.Ln, scale=1.0 / D, bias=EPS),
                    [r_ss], [r_ss])
                ACT(lambda e: e.activation(out=ss[:, 4:8], in_=ss[:, 4:8], func=AF.Exp, scale=-0.5),
                    [r_ss], [r_ss])
                for s in range(NS):
                    DVE(lambda e, s=s: e.scalar_tensor_tensor(out=h[:, s, :], in0=x[:, s, :],
                                                              scalar=ss[:, 4 + s:5 + s], in1=gb[:],
                                                              op0=ALU.mult, op1=ALU.mult),
                        [r_x[s], r_ss, r_gb], [r_h[s]])
                for s in range(NS):
                    for c in range(8):
                        PE(lambda e, s=s, c=c: e.transpose(out=ptb.ap[:, c * 128:(c + 1) * 128],
                                                           in_=h[:, s, c * 128:(c + 1) * 128], identity=ident[:]),
                           [r_h[s], r_ident], [ptb.res])
                    ACT(lambda e, s=s: e.copy(out=hT[:, :, s * 128:(s + 1) * 128],
                                              in_=ptb.ap[:, :].rearrange("p (c t) -> p c t", c=8)),
                        [ptb.res], [r_hT])

            def resid_proj(src, src_res_of, nk, tile_src, scale):
                for n in range(2):
                    accs = [rot.get() for _ in range(NS)]
                    for kc in range(nk):
                        t, r = WD.next(tile_src(n, kc))
                        for s in range(NS):
                            PE(lambda e, s=s, kc=kc, t=t, a=accs[s]: e.matmul(
                                a.ap[:, :], lhsT=src[:, kc, s * 128:(s + 1) * 128], rhs=t[:],
                                start=(kc == 0), stop=(kc == nk - 1)), [src_res_of(kc), r], [accs[s].res])
                    for s in range(NS):
                        DVE(lambda e, s=s, n=n, a=accs[s]: e.scalar_tensor_tensor(
                            out=x[:, s, n * 512:(n + 1) * 512], in0=a.ap[:, :], scalar=scale,
                            in1=x[:, s, n * 512:(n + 1) * 512], op0=ALU.mult, op1=ALU.add),
                            [accs[s].res, r_x[s]], [r_x[s]])

            def ffn(l, k):
                rmsnorm_hT(gains[l, 2 * k])
                for f in range(NF):
                    tg, rg = WA.next(wg[l, k, f])
                    bg = rot.get()
                    for c in range(8):
                        PE(lambda e, c=c, tg=tg, bg=bg: e.matmul(bg.ap[:, :], lhsT=tg[:, c, :], rhs=hT[:, c, :],
                                                                 start=(c == 0), stop=(c == 7)),
                           [rg, r_hT], [bg.res])
                    tu, ru = WA.next(wu[l, k, f])
                    bu = rot.get()
                    for c in range(8):
                        PE(lambda e, c=c, tu=tu, bu=bu: e.matmul(bu.ap[:, :], lhsT=tu[:, c, :], rhs=hT[:, c, :],
                                                                 start=(c == 0), stop=(c == 7)),
                           [ru, r_hT], [bu.res])
                    i = sgi[0] % 2
                    sgi[0] += 1
                    ACT(lambda e, i=i, bg=bg: e.activation(out=sg[i][:], in_=bg.ap[:, :], func=AF.Silu),
                        [bg.res], [r_sg[i]])
                    DVE(lambda e, i=i, f=f, bu=bu: e.tensor_tensor(out=act[:, f, :], in0=sg[i][:], in1=bu.ap[:, :],
                                                                   op=ALU.mult),
                        [r_sg[i], bu.res], [r_act[f]])
                resid_proj(act, lambda f: r_act[f], NF, lambda n, f: wd2[l, k, n, f], 0.5)

            def proj128(l, wi):
                t, r = WA.next(win[l, wi])
                b = rot.get()
                for c in range(8):
                    PE(lambda e, c=c, t=t, b=b: e.matmul(b.ap[:, :], lhsT=t[:, c, :], rhs=hT[:, c, :],
                                                         start=(c == 0), stop=(c == 7)), [r, r_hT], [b.res])
                return b

            def head_norm(src_ap, src_res, gcol, out_ap, out_res):
                ACT(lambda e: e.activation(out=lnm[0:64, :], in_=src_ap, func=AF.Square), [src_res], [r_lnm])
                b = rot.get()
                PE(lambda e, b=b: e.matmul(b.ap[0:64, :], lhsT=o64[:], rhs=lnm[0:64, :], start=True, stop=True),
                   [r_lnm, r_o64], [b.res])
                ACT(lambda e, b=b: e.activation(out=lnv[0:64, :], in_=b.ap[0:64, :], func=AF.Ln, bias=EPS),
                    [b.res], [r_lnv])
                ACT(lambda e: e.activation(out=lnv[0:64, :], in_=lnv[0:64, :], func=AF.Exp, scale=-0.5),
                    [r_lnv], [r_lnv])
                DVE(lambda e: e.scalar_tensor_tensor(out=out_ap, in0=src_ap, scalar=qkg_t[:, gcol:gcol + 1],
                                                     in1=lnv[0:64, :], op0=ALU.mult, op1=ALU.mult),
                    [src_res, r_lnv, r_small], [out_res])

            def heads_tile(l, wi, specs):
                t, r = WA.next(win[l, wi])
                for hh in range(2):
                    b = rot.get()
                    for c in range(8):
                        PE(lambda e, c=c, t=t, b=b, hh=hh: e.matmul(
                            b.ap[0:64, :], lhsT=t[:, c, hh * 64:(hh + 1) * 64], rhs=hT[:, c, :],
                            start=(c == 0), stop=(c == 7)), [r, r_hT], [b.res])
                    gcol, out_ap, out_res = specs[hh]
                    head_norm(b.ap[0:64, :], b.res, gcol, out_ap, out_res)

            def v_tile(l, wi, dst, dst_res, chunk_off, h0):
                t, r = WA.next(win[l, wi])
                for s in range(NS):
                    b = rot.get()
                    for c in range(8):
                        PE(lambda e, c=c, t=t, b=b, s=s: e.matmul(
                            b.ap[:, 0:128], lhsT=hT[:, c, s * 128:(s + 1) * 128], rhs=t[:, c, :],
                            start=(c == 0), stop=(c == 7)), [r, r_hT], [b.res])
                    ACT(lambda e, b=b, s=s: e.copy(out=dst[:, chunk_off + s, h0:h0 + 2, 0:64],
                                                   in_=b.ap[:, 0:128].rearrange("p (h d) -> p h d", h=2)),
                        [b.res], [dst_res])

            def attn_norm(acc, extra_den, out_ap, out_res):
                ACT(lambda e: e.copy(out=oa[:], in_=acc.ap[0:65, :]), [acc.res], [r_oa])
                if extra_den is not None:
                    DVE(lambda e: e.tensor_scalar(out=oa[64:65, :], in0=oa[64:65, :], scalar1=extra_den,
                                                  scalar2=None, op0=ALU.add), [r_oa, r_small], [r_oa])
                ACT(lambda e: e.activation(out=rden[64:65, :], in_=oa[64:65, :], func=AF.Ln), [r_oa], [r_rden])
                ACT(lambda e: e.activation(out=rden[64:65, :], in_=rden[64:65, :], func=AF.Exp, scale=-1.0),
                    [r_rden], [r_rden])
                b = rot_hi.get()
                PE(lambda e, b=b: e.matmul(b.ap[0:64, :], lhsT=ones[64:65, 0:64], rhs=rden[64:65, :],
                                           start=True, stop=True), [r_rden, r_ones], [b.res])
                DVE(lambda e, b=b: e.tensor_tensor(out=out_ap, in0=oa[0:64, :], in1=b.ap[0:64, :], op=ALU.mult),
                    [r_oa, b.res], [out_res])

            def load_small(l):
                S.dma("sp", cdw_t[:], cdw[l], writes=[r_small])
                S.dma("sp", cvec_t[:], cvec[l], writes=[r_small])
                S.dma("sp", scw_t[:], scw[l], writes=[r_small])
                S.dma("sp", qkg_t[:], qkg[l], writes=[r_small])
                S.dma("sp", bfg_t[:], bfg[l], writes=[r_small])
                S.dma("sp", sink_t[:], sink[l], writes=[r_small])
                DVE(lambda e: e.tensor_scalar(out=qkg_t[:, 0:1], in0=qkg_t[:, 0:1], scalar1=0.125, scalar2=None,
                                              op0=ALU.mult), [r_small], [r_small])
                DVE(lambda e: e.tensor_scalar(out=qkg_t[:, 2:3], in0=qkg_t[:, 2:3], scalar1=0.125, scalar2=None,
                                              op0=ALU.mult), [r_small], [r_small])
                ACT(lambda e: e.activation(out=sink_t[64:65, :], in_=sink_t[64:65, :], func=AF.Exp),
                    [r_small], [r_small])
                DVE(lambda e: e.tensor_scalar(out=bfg_t[:], in0=bfg_t[:], scalar1=-1.0, scalar2=None,
                                              op0=ALU.mult), [r_small], [r_small])

            def split3(src, src_res):
                DVE(lambda e: e.tensor_copy(out=gsp[:, 0, :], in_=src[:]), [src_res], [r_gs])
                DVE(lambda e: e.tensor_tensor(out=fe[:], in0=src[:], in1=gsp[:, 0, :], op=ALU.subtract),
                    [src_res, r_gs], [r_fe])
                DVE(lambda e: e.tensor_copy(out=gsp[:, 1, :], in_=fe[:]), [r_fe], [r_gs])
                DVE(lambda e: e.tensor_tensor(out=fe[:], in0=fe[:], in1=gsp[:, 1, :], op=ALU.subtract),
                    [r_fe, r_gs], [r_fe])
                DVE(lambda e: e.tensor_copy(out=gsp[:, 2, :], in_=fe[:]), [r_fe], [r_gs])

            def front(l, j):
                t0 = j * TT
                last = (j == NT - 1)
                load_small(l)
                rmsnorm_hT(gains[l, 1])
                if j + 1 < NT:
                    src = xin if l == 0 else Xs
                    S.dma("act", x[:], src[t0 + TT:t0 + 2 * TT, :].rearrange("(s p) d -> p s d", p=128),
                          reads=([RX[j + 1]] if l > 0 else []), writes=r_x)
                for ch in range(2):
                    ba = proj128(l, 2 * ch)
                    bb = proj128(l, 2 * ch + 1)
                    i = sgi[0] % 2
                    sgi[0] += 1
                    ACT(lambda e, i=i, bb=bb: e.activation(out=sg[i][:], in_=bb.ap[:, :], func=AF.Sigmoid),
                        [bb.res], [r_sg[i]])
                    DVE(lambda e, i=i, ba=ba, ch=ch: e.tensor_tensor(out=uext[:, ch, 30:30 + TT], in0=sg[i][:],
                                                                     in1=ba.ap[:, :], op=ALU.mult),
                        [r_sg[i], ba.res], [r_uext])
                for ch in range(2):
                    b = proj128(l, 4 + ch)
                    ACT(lambda e, b=b, ch=ch: e.copy(out=sbt[:, ch, :], in_=b.ap[:, :]), [b.res], [r_sbt])
                for ch in range(2):
                    bc = proj128(l, 6 + 2 * ch)
                    bx = proj128(l, 7 + 2 * ch)
                    i = sgi[0] % 2
                    sgi[0] += 1
                    ACT(lambda e, i=i, bc=bc: e.copy(out=sg[i][:], in_=bc.ap[:, :]), [bc.res], [r_sg[i]])
                    DVE(lambda e, i=i, bx=bx, ch=ch: e.tensor_tensor(out=sxext[:, ch, 2:2 + TT], in0=sg[i][:],
                                                                     in1=bx.ap[:, :], op=ALU.mult),
                        [r_sg[i], bx.res], [r_sxext])
                S.dma("sp", FU[j], uext[:, :, 30:30 + TT], reads=[r_uext], writes=[RF[j]])
                S.dma("sp", FSX[j], sxext[:, :, 2:2 + TT], reads=[r_sxext], writes=[RF[j]])
                S.dma("sp", FSB[j], sbt[:], reads=[r_sbt], writes=[RF[j]])
                if last:
                    S.dma("sp", HF[l][:, 0:60].rearrange("p (c k) -> p c k", c=2), uext[:, :, TT:TT + 30],
                          reads=[r_uext], writes=[r_HAL[l]])
                    S.dma("sp", HF[l][:, 60:64].rearrange("p (c k) -> p c k", c=2), sxext[:, :, TT:TT + 2],
                          reads=[r_sxext], writes=[r_HAL[l]])
                heads_tile(l, 10, [(0, qs[:, 0, :], r_qs), (0, qs[:, 1, :], r_qs)])
                heads_tile(l, 11, [(0, qs[:, 2, :], r_qs), (0, qs[:, 3, :], r_qs)])
                heads_tile(l, 12, [(1, ksx[:, 0, 128:128 + TT], r_ksx), (1, ksx[:, 1, 128:128 + TT], r_ksx)])
                v_tile(l, 13, vsx, r_vsx, 1, 0)
                S.dma("sp", FQS[j], qs[:], reads=[r_qs], writes=[RF[j]])
                S.dma("sp", FKS[j], ksx[:, :, 128:128 + TT], reads=[r_ksx], writes=[RF[j]])
                S.dma("sp", FVS[j], vsx[:, 1:NS + 1, :, :].rearrange("p s h e -> p s (h e)"), reads=[r_vsx],
                      writes=[RF[j]])
                if last:
                    S.dma("sp", HB[l][0:64, 0:256].rearrange("p (c k) -> p c k", c=2), ksx[:, :, TT:TT + 128],
                          reads=[r_ksx], writes=[r_HAL[l]])
                    S.dma("sp", HB[l][:, 256:386], vsx[:, NS, :, :].rearrange("p h e -> p (h e)"),
                          reads=[r_vsx], writes=[r_HAL[l]])
                heads_tile(l, 14, [(2, qf[0:64, 0, :], r_qf), (2, qf[0:64, 1, :], r_qf)])
                heads_tile(l, 15, [(2, qf[0:64, 2, :], r_qf), (2, qf[0:64, 3, :], r_qf)])
                heads_tile(l, 16, [(3, kf[:, 0, :], r_kf), (3, kf[:, 1, :], r_kf)])
                heads_tile(l, 17, [(3, kf[:, 2, :], r_kf), (3, kf[:, 3, :], r_kf)])
                v_tile(l, 18, vf, r_vf, 0, 0)
                v_tile(l, 19, vf, r_vf, 0, 2)
                t, r = WA.next(win[l, 20])
                bF = rot.get()
                for c in range(8):
                    PE(lambda e, c=c, t=t, bF=bF: e.matmul(bF.ap[0:4, :], lhsT=t[:, c, 0:4], rhs=hT[:, c, :],
                                                           start=(c == 0), stop=(c == 7)), [r, r_hT], [bF.res])
                ACT(lambda e: e.activation(out=fe[:], in_=bF.ap[0:4, :], func=AF.Exp, scale=-1.0,
                                           bias=bfg_t[:, 0:1]), [bF.res, r_small], [r_fe])
                ACT(lambda e: e.activation(out=fe[:], in_=fe[:], func=AF.Ln, bias=1.0), [r_fe], [r_fe])
                if j == 0:
                    DVE(lambda e: e.memset(fcar[l][:], 0.0), [], [r_fcar[l]])
                DVE(lambda e: e.tensor_tensor_scan(out=fG[:], data0=fones[:], data1=fe[:], initial=fcar[l][:, 0:1],
                                                   op0=ALU.mult, op1=ALU.add), [r_fe, r_fones, r_fcar[l]], [r_fG])
                DVE(lambda e: e.tensor_copy(out=fcar[l][:], in_=fG[:, TT - 1:TT]), [r_fG], [r_fcar[l]])
                split3(fG, r_fG)
                S.dma("sp", FQF[j], qf[0:64, :, :], reads=[r_qf], writes=[RF[j]])
                S.dma("sp", FG[j], fG[:], reads=[r_fG], writes=[RF[j]])
                rKF = RKF[l][j]
                for hh in range(4):
                    S.dma("sp", KF[l][hh][0:64, t0:t0 + TT], kf[:, hh, :], reads=[r_kf], writes=[rKF])
                    S.dma("sp", KF[l][hh][64:67, t0:t0 + TT].rearrange("(o a) t -> o a t", o=1),
                          monesb[hh:hh + 1, :, :], reads=[r_monesb], writes=[rKF])
                    S.dma("sp", KF[l][hh][67:70, t0:t0 + TT].rearrange("(o a) t -> o a t", o=1),
                          gsp[hh:hh + 1, :, :], reads=[r_gs], writes=[rKF])
                vq, vo = t0 // VC, t0 % VC
                S.dma("sp", VF[l][vq][vo:vo + TT, :].rearrange("(s p) e -> p s e", p=128),
                      vf[:].rearrange("p s h e -> p s (h e)"), reads=[r_vf], writes=[rKF])
                if last:
                    S.dma("sp", HF[l][0:4, 64:65], fcar[l][:], reads=[r_fcar[l]], writes=[r_HAL[l]],
                          allow_slow_non_contiguous=True)

            def exchange(l):
                srcs = ([(KFh[l][hh], GKFh[l][hh]) for hh in range(4)]
                        + [(VFh[l][q], GVFh[l][q]) for q in range(NVC)]
                        + [(HFh[l], GHFh[l]), (HBh[l], GHBh[l])])
                for (a, g) in srcs:
                    k = ccn[0]
                    ccn[0] += 1
                    S.op("pool", lambda e, a=a, g=g: e.collective_compute(
                        "AllGather", ALU.bypass, replica_groups=[[0, 1], [2, 3], [4, 5], [6, 7]],
                        ins=[a.ap().opt()], outs=[g.ap().opt()]),
                        reads=RKF[l] + [r_HAL[l]], writes=[r_G[l]], dma=True, cc=k)

            def back(l, j):
                t0 = j * TT
                load_small(l)
                S.dma("sp", uext[:, :, 30:30 + TT], FU[j], reads=[RF[j]], writes=[r_uext])
                S.dma("sp", sxext[:, :, 2:2 + TT], FSX[j], reads=[RF[j]], writes=[r_sxext])
                S.dma("sp", sbt[:], FSB[j], reads=[RF[j]], writes=[r_sbt])
                if j == 0:
                    S.dma("sp", uext[:, :, 0:30], GHF[l][0, :, 0:60].rearrange("p (c k) -> p c k", c=2),
                          reads=[r_G[l]], writes=[r_uext])
                    S.dma("sp", sxext[:, :, 0:2], GHF[l][0, :, 60:64].rearrange("p (c k) -> p c k", c=2),
                          reads=[r_G[l]], writes=[r_sxext])
                    S.dma("sp", tot_t[:], GHF[l][0, 0:4, 64:65], reads=[r_G[l]], writes=[r_tot],
                          allow_slow_non_contiguous=True)
                    DVE(lambda e: e.tensor_scalar(out=uext[:, :, 0:30], in0=uext[:, :, 0:30],
                                                  scalar1=flag_t[:, 0:1], scalar2=None, op0=ALU.mult),
                        [r_uext, r_flag], [r_uext])
                    DVE(lambda e: e.tensor_scalar(out=sxext[:, :, 0:2], in0=sxext[:, :, 0:2],
                                                  scalar1=flag_t[:, 0:1], scalar2=None, op0=ALU.mult),
                        [r_sxext, r_flag], [r_sxext])
                else:
                    S.dma("sp", uext[:, :, 0:30], FU[j - 1][:, :, TT - 30:TT], reads=[RF[j - 1]], writes=[r_uext])
                    S.dma("sp", sxext[:, :, 0:2], FSX[j - 1][:, :, TT - 2:TT], reads=[RF[j - 1]], writes=[r_sxext])
                S.dma("sp", qf[0:64, :, :], FQF[j], reads=[RF[j]], writes=[r_qf])
                S.dma("sp", qfx[0:64, :, :], FQF[j], reads=[RF[j]], writes=[r_qfx])
                S.dma("sp", fG[:], FG[j], reads=[RF[j]], writes=[r_fG])
                split3(fG, r_fG)
                for hh in range(4):
                    S.dma("sp", QFs[0, :, hh, :].rearrange("(o a) t -> o a t", o=1), gsp[hh:hh + 1, :, :],
                          reads=[r_gs], writes=[r_QFs])
                S.dma("sp", qf[64:67, :, :], QFs[0], reads=[r_QFs], writes=[r_qf])
                DVE(lambda e: e.tensor_scalar(out=fG[:], in0=fG[:], scalar1=tot_t[:, 0:1], scalar2=None,
                                              op0=ALU.add), [r_fG, r_tot], [r_fG])
                split3(fG, r_fG)
                for hh in range(4):
                    S.dma("sp", QFs[1, :, hh, :].rearrange("(o a) t -> o a t", o=1), gsp[hh:hh + 1, :, :],
                          reads=[r_gs], writes=[r_QFs])
                S.dma("sp", qfx[64:67, :, :], QFs[1], reads=[r_QFs], writes=[r_qfx])
                for kk in range(31):
                    for ch in range(2):
                        if kk == 0:
                            DVE(lambda e, ch=ch: e.tensor_scalar(
                                out=cacc[:, ch, :], in0=uext[:, ch, 0:TT], scalar1=cdw_t[:, ch, 0:1],
                                scalar2=cvec_t[:, ch, 0:1], op0=ALU.mult, op1=ALU.add),
                                [r_uext, r_small], [r_cacc[ch]])
                        else:
                            DVE(lambda e, ch=ch, kk=kk: e.scalar_tensor_tensor(
                                out=cacc[:, ch, :], in0=uext[:, ch, kk:kk + TT], scalar=cdw_t[:, ch, kk:kk + 1],
                                in1=cacc[:, ch, :], op0=ALU.mult, op1=ALU.add),
                                [r_uext, r_small, r_cacc[ch]], [r_cacc[ch]])
                accs = banks[0:4]
                nkb = NT + j + 1
                jobs = [(kk, hh, c) for kk in range(nkb) for hh in range(4) for c in range(NS)]
                blk = {}

                def load_blk(kk):
                    cross = kk < NT
                    kb = kk if cross else kk - NT
                    i = kbi[0] % 2
                    kbi[0] += 1
                    vq, vo = (kb * TT) // VC, (kb * TT) % VC
                    if cross:
                        for hh in range(4):
                            S.dma("sp", kblk[i][0:70, hh, :], GKF[l][hh][0, :, kb * TT:(kb + 1) * TT],
                                  reads=[r_G[l]], writes=[r_kblk[i]])
                        S.dma("sp", vblk[i][:, :, 0:260],
                              GVF[l][vq][0, vo:vo + TT, :].rearrange("(s p) e -> p s e", p=128),
                              reads=[r_G[l]], writes=[r_vblk[i]])
                    else:
                        for hh in range(4):
                            S.dma("sp", kblk[i][0:70, hh, :], KF[l][hh][:, kb * TT:(kb + 1) * TT],
                                  reads=[RKF[l][kb]], writes=[r_kblk[i]])
                        S.dma("sp", vblk[i][:, :, 0:260],
                              VF[l][vq][vo:vo + TT, :].rearrange("(s p) e -> p s e", p=128),
                              reads=[RKF[l][kb]], writes=[r_vblk[i]])
                    blk[kk] = (i, cross, (not cross) and kb == j)

                pis = {}

                def emit_S(job):
                    kk, hh, c = job
                    if kk not in blk:
                        load_blk(kk)
                    i, cross, diag = blk[kk]
                    qsrc, rq = (qfx, r_qfx) if cross else (qf, r_qf)
                    bS = rot_hi.get()
                    PE(lambda e, i=i, hh=hh, c=c, bS=bS, qsrc=qsrc: e.matmul(
                        bS.ap[:, :], lhsT=kblk[i][:, hh, c * 128:(c + 1) * 128], rhs=qsrc[:, hh, :],
                        start=True, stop=True), [r_kblk[i], rq], [bS.res])
                    pi = pti[0] % 4
                    pti[0] += 1
                    pis[job] = pi
                    ACT(lambda e, pi=pi, bS=bS: e.activation(out=pT[pi][:], in_=bS.ap[:, :], func=AF.Exp),
                        [bS.res], [r_pT[pi]])
                    if diag:
                        POOL(lambda e, pi=pi, c=c: e.affine_select(
                            out=pT[pi][:], in_=pT[pi][:], pattern=[[1, TT]], compare_op=ALU.is_ge,
                            fill=0.0, base=-c * 128, channel_multiplier=-1), [r_pT[pi]], [r_pT[pi]])

                def emit_PV(job):
                    kk, hh, c = job
                    i = blk[kk][0]
                    pi = pis[job]
                    PE(lambda e, i=i, hh=hh, c=c, pi=pi, kk=kk: e.matmul(
                        accs[hh].ap[:, :], lhsT=vblk[i][:, c, hh * 65:hh * 65 + 128], rhs=pT[pi][:],
                        start=(kk == 0 and c == 0), stop=(kk == nkb - 1 and c == NS - 1)),
                       [r_vblk[i], r_pT[pi]], [accs[hh].res])

                LA = 2
                for idx, job in enumerate(jobs):
                    emit_S(job)
                    if idx >= LA:
                        emit_PV(jobs[idx - LA])
                for job in jobs[len(jobs) - LA:]:
                    emit_PV(job)
                for hh in range(4):
                    attn_norm(accs[hh], None, onT[1][:, hh, :], r_onT[1])

                rmsnorm_hT(gains[l, 1])
                for ch in range(2):
                    ACT(lambda e, ch=ch: e.activation(out=csq[:, ch, :], in_=cacc[:, ch, :], func=AF.Square),
                        [r_cacc[ch]], [r_csq])
                b1 = rot.get()
                for ch in range(2):
                    PE(lambda e, ch=ch, b1=b1: e.matmul(b1.ap[:, :], lhsT=ones[:], rhs=cacc[:, ch, :],
                                                        start=(ch == 0), stop=(ch == 1)),
                       [r_cacc[ch], r_ones], [b1.res])
                b2 = rot.get()
                for ch in range(2):
                    PE(lambda e, ch=ch, b2=b2: e.matmul(b2.ap[:, :], lhsT=ones[:], rhs=csq[:, ch, :],
                                                        start=(ch == 0), stop=(ch == 1)),
                       [r_csq, r_ones], [b2.res])
                DVE(lambda e: e.tensor_scalar(out=lnm[:], in0=b1.ap[:, :], scalar1=1.0 / 256.0, scalar2=None,
                                              op0=ALU.mult), [b1.res], [r_lnm])
                DVE(lambda e: e.tensor_tensor(out=lnv[:], in0=lnm[:], in1=lnm[:], op=ALU.mult), [r_lnm], [r_lnv])
                DVE(lambda e: e.scalar_tensor_tensor(out=lnv[:], in0=b2.ap[:, :], scalar=1.0 / 256.0, in1=lnv[:],
                                                     op0=ALU.mult, op1=ALU.subtract), [b2.res, r_lnv], [r_lnv])
                DVE(lambda e: e.tensor_scalar(out=lnv[:], in0=lnv[:], scalar1=0.0, scalar2=None, op0=ALU.max),
                    [r_lnv], [r_lnv])
                ACT(lambda e: e.activation(out=lnv[:], in_=lnv[:], func=AF.Ln, bias=EPS), [r_lnv], [r_lnv])
                ACT(lambda e: e.activation(out=lnv[:], in_=lnv[:], func=AF.Exp, scale=-0.5), [r_lnv], [r_lnv])
                for ch in range(2):
                    DVE(lambda e, ch=ch: e.tensor_tensor(out=cacc[:, ch, :], in0=cacc[:, ch, :], in1=lnm[:],
                                                         op=ALU.subtract), [r_cacc[ch], r_lnm], [r_cacc[ch]])
                    DVE(lambda e, ch=ch: e.tensor_tensor(out=cacc[:, ch, :], in0=cacc[:, ch, :], in1=lnv[:],
                                                         op=ALU.mult), [r_cacc[ch], r_lnv], [r_cacc[ch]])
                    ACT(lambda e, ch=ch: e.activation(out=uT[:, ch, :], in_=cacc[:, ch, :], func=AF.Silu,
                                                      scale=cvec_t[:, ch, 1:2], bias=cvec_t[:, ch, 2:3]),
                        [r_cacc[ch], r_small], [r_uT])
                for ch in range(2):
                    DVE(lambda e, ch=ch: e.tensor_scalar(out=csq[:, ch, :], in0=sxext[:, ch, 0:TT],
                                                         scalar1=scw_t[:, ch, 0:1], scalar2=None, op0=ALU.mult),
                        [r_sxext, r_small], [r_csq])
                    for kk in (1, 2):
                        DVE(lambda e, ch=ch, kk=kk: e.scalar_tensor_tensor(
                            out=csq[:, ch, :], in0=sxext[:, ch, kk:kk + TT], scalar=scw_t[:, ch, kk:kk + 1],
                            in1=csq[:, ch, :], op0=ALU.mult, op1=ALU.add), [r_sxext, r_small, r_csq], [r_csq])
                    DVE(lambda e, ch=ch: e.tensor_tensor(out=scT[:, ch, :], in0=csq[:, ch, :], in1=sbt[:, ch, :],
                                                         op=ALU.mult), [r_csq, r_sbt], [r_scT])

                S.dma("sp", qs[:], FQS[j], reads=[RF[j]], writes=[r_qs])
                S.dma("sp", ksx[:, :, 128:128 + TT], FKS[j], reads=[RF[j]], writes=[r_ksx])
                S.dma("sp", vsx[:, 1:NS + 1, :, :].rearrange("p s h e -> p s (h e)"), FVS[j], reads=[RF[j]],
                      writes=[r_vsx])
                if j == 0:
                    S.dma("sp", ksx[:, :, 0:128], GHB[l][0, 0:64, 0:256].rearrange("p (c k) -> p c k", c=2),
                          reads=[r_G[l]], writes=[r_ksx])
                    S.dma("sp", vsx[:, 0, :, :].rearrange("p h e -> p (h e)"), GHB[l][0, :, 256:386],
                          reads=[r_G[l]], writes=[r_vsx])
                    DVE(lambda e: e.tensor_scalar(out=vsx[:, 0, :, :], in0=vsx[:, 0, :, :],
                                                  scalar1=flag_t[:, 0:1], scalar2=None, op0=ALU.mult),
                        [r_vsx, r_flag], [r_vsx])
                else:
                    S.dma("sp", ksx[:, :, 0:128], FKS[j - 1][:, :, TT - 128:TT], reads=[RF[j - 1]], writes=[r_ksx])
                    S.dma("sp", vsx[:, 0, :, :].rearrange("p h e -> p (h e)"), FVS[j - 1][:, NS - 1, :],
                          reads=[RF[j - 1]], writes=[r_vsx])
                for hq in range(4):
                    hk = hq // 2
                    acc = banks[hq % 4]
                    for pr in range(2):
                        bS = rot_hi.get()
                        i = sgi[0] % 2
                        sgi[0] += 1
                        for s2 in range(2):
                            s = 2 * pr + s2
                            for part in range(2):
                                PE(lambda e, s=s, s2=s2, part=part, bS=bS, hk=hk, hq=hq: e.matmul(
                                    bS.ap[:, s2 * 256 + part * 128: s2 * 256 + part * 128 + 128],
                                    lhsT=ksx[:, hk, (s + part) * 128:(s + part + 1) * 128],
                                    rhs=qs[:, hq, s * 128:(s + 1) * 128], start=True, stop=True),
                                   [r_ksx, r_qs], [bS.res])
                            DVE(lambda e, bS=bS, i=i, hq=hq, s2=s2: e.tensor_tensor(
                                out=sg[i][:, s2 * 256:(s2 + 1) * 256], in0=bS.ap[:, s2 * 256:(s2 + 1) * 256],
                                in1=bm_t[:, hq, :], op=ALU.add), [bS.res, r_bm], [r_sg[i]])
                        pi = pti[0] % 4
                        pti[0] += 1
                        ACT(lambda e, i=i, pi=pi: e.activation(out=pT[pi][:], in_=sg[i][:], func=AF.Exp),
                            [r_sg[i]], [r_pT[pi]])
                        for s2 in range(2):
                            s = 2 * pr + s2
                            first = True
                            for part in range(2):
                                PE(lambda e, s=s, s2=s2, part=part, pi=pi, hk=hk, acc=acc, first=first: e.matmul(
                                    acc.ap[0:65, s * 128:(s + 1) * 128], lhsT=vsx[:, s + part, hk, :],
                                    rhs=pT[pi][:, s2 * 256 + part * 128: s2 * 256 + part * 128 + 128],
                                    start=first, stop=(part == 1)), [r_vsx, r_pT[pi]], [acc.res])
                                first = False
                    attn_norm(acc, sink_t[64:65, hq:hq + 1], onT[0][:, hq, :], r_onT[0])

                brsrc = [(uT, r_uT), (scT, r_scT)]
                for m in range(8):
                    tr, rr = WR.next(wbr[l, m])
                    for br in range(4):
                        bp = rot.get()
                        if br < 2:
                            src, rs = brsrc[br]
                            for ch in range(2):
                                PE(lambda e, br=br, ch=ch, bp=bp, src=src, tr=tr: e.matmul(
                                    bp.ap[:, :], lhsT=tr[:, 2 * br + ch, :], rhs=src[:, ch, :],
                                    start=(ch == 0), stop=(ch == 1)), [rr, rs], [bp.res])
                        else:
                            a = br - 2
                            for hh in range(4):
                                PE(lambda e, a=a, hh=hh, bp=bp, tr=tr: e.matmul(
                                    bp.ap[:, :], lhsT=tr[0:64, 4 + 4 * a + hh, :],
                                    rhs=onT[a][:, hh, :], start=(hh == 0), stop=(hh == 3)),
                                   [rr, r_onT[a]], [bp.res])
                        bgt = proj128(l, 21 + 4 * m + br)
                        i = sgi[0] % 2
                        sgi[0] += 1
                        ACT(lambda e, i=i, bgt=bgt: e.activation(out=sg[i][:], in_=bgt.ap[:, :], func=AF.Sigmoid),
                            [bgt.res], [r_sg[i]])
                        if br == 0:
                            DVE(lambda e, i=i, bp=bp: e.tensor_tensor(out=macc[:], in0=sg[i][:], in1=bp.ap[:, :],
                                                                      op=ALU.mult), [r_sg[i], bp.res], [r_macc])
                        else:
                            k2 = br % 2
                            DVE(lambda e, i=i, bp=bp, k2=k2: e.tensor_tensor(out=mtmp[k2][:], in0=sg[i][:],
                                                                             in1=bp.ap[:, :], op=ALU.mult),
                                [r_sg[i], bp.res], [r_mtmp[k2]])
                            if br < 3:
                                DVE(lambda e, k2=k2: e.tensor_tensor(out=macc[:], in0=macc[:], in1=mtmp[k2][:],
                                                                     op=ALU.add), [r_macc, r_mtmp[k2]], [r_macc])
                            else:
                                DVE(lambda e, k2=k2, m=m: e.tensor_tensor(out=mT[:, m, :], in0=macc[:],
                                                                          in1=mtmp[k2][:], op=ALU.add),
                                    [r_macc, r_mtmp[k2]], [r_mT])
                resid_proj(mT, lambda m: r_mT, 8, lambda n, m: wo2[l, n, m], 1.0)

            for l in range(L):
                for j in range(NT):
                    t0 = j * TT
                    src = xin if l == 0 else Xs
                    if j == 0:
                        S.dma("sp", x[:], src[t0:t0 + TT, :].rearrange("(s p) d -> p s d", p=128),
                              reads=([RX[j]] if l > 0 else []), writes=r_x)
                    ffn(l, 0)
                    S.dma("sp", Xs[t0:t0 + TT, :].rearrange("(s p) d -> p s d", p=128), x[:], reads=r_x,
                          writes=[RX[j]])
                    front(l, j)
                exchange(l)
                for j in range(NT):
                    t0 = j * TT
                    S.dma("sp", x[:], Xs[t0:t0 + TT, :].rearrange("(s p) d -> p s d", p=128), reads=[RX[j]],
                          writes=r_x)
                    back(l, j)
                    ffn(l, 1)
                    dst = Xs if l < L - 1 else yout
                    S.dma("sp", dst[t0:t0 + TT, :].rearrange("(s p) d -> p s d", p=128), x[:], reads=r_x,
                          writes=[RX[j]])
            return [WA.rec, WD.rec, WR.rec]

        plans = program(Sched(nc, dry=True), [None, None, None])
        S = Sched(nc)
        program(S, plans)
        S.emit()
    return nc


def _t5_bucket(dist):
    max_exact = 16
    d = np.maximum(dist, 1).astype(np.float32)
    large = max_exact + (np.log(d / np.float32(max_exact)) / np.float32(np.log(128 / max_exact))
                         * np.float32(32 - max_exact)).astype(np.int32)
    large = np.minimum(large, 31)
    return np.where(dist < max_exact, dist, large)


def host_prep(inp):
    f = lambda a: np.ascontiguousarray(np.asarray(a, dtype=np.float32))

    def wtile(w, col0, n):
        out = np.zeros((128, 8, 128), np.float32)
        out[:, :, :n] = w[:, col0:col0 + n].reshape(8, 128, n).transpose(1, 0, 2)
        return out

    shared = {}
    gates = [np.asarray(inp["ffn1_w_gate"]), np.asarray(inp["ffn2_w_gate"])]
    ups = [np.asarray(inp["ffn1_w_up"]), np.asarray(inp["ffn2_w_up"])]
    downs = [np.asarray(inp["ffn1_w_down"]), np.asarray(inp["ffn2_w_down"])]

    def ftile(w):
        return w.reshape(8, 128, NF, 128).transpose(2, 1, 0, 3)

    def dtile(w):
        return w.reshape(NF, 128, 2, 512).transpose(2, 0, 1, 3)

    shared["wg"] = f(np.stack([np.stack([ftile(gates[k][l]) for k in range(2)]) for l in range(L)]))
    shared["wu"] = f(np.stack([np.stack([ftile(ups[k][l]) for k in range(2)]) for l in range(L)]))
    shared["wd2"] = f(np.stack([np.stack([dtile(downs[k][l]) for k in range(2)]) for l in range(L)]))
    w_in = np.asarray(inp["w_in"])
    shared["win"] = f(np.stack([np.stack([wtile(w_in[l], c0, n) for (c0, n) in WIN_TILES]) for l in range(L)]))
    cwo, swo = np.asarray(inp["conf_w_out"]), np.asarray(inp["sc_w_out"])
    awo, fwo = np.asarray(inp["swa_w_o"]), np.asarray(inp["fox_w_o"])
    wbr = np.zeros((L, 8, 128, 12, 128), np.float32)
    for l in range(L):
        for m in range(8):
            cs = slice(m * 128, (m + 1) * 128)
            for ch in range(2):
                wbr[l, m, :, ch, :] = cwo[l][ch * 128:(ch + 1) * 128, cs]
                wbr[l, m, :, 2 + ch, :] = swo[l][ch * 128:(ch + 1) * 128, cs]
            for hh in range(4):
                wbr[l, m, 0:64, 4 + hh, :] = awo[l][hh * 64:(hh + 1) * 64, cs]
                wbr[l, m, 0:64, 8 + hh, :] = fwo[l][hh * 64:(hh + 1) * 64, cs]
    shared["wbr"] = wbr
    shared["wo2"] = f(np.stack([np.asarray(inp["w_out"])[l].reshape(8, 128, 2, 512).transpose(2, 0, 1, 3)
                                for l in range(L)]))
    gn = [np.asarray(inp["ffn1_norm"]), np.asarray(inp["mix_norm"]), np.asarray(inp["ffn2_norm"])]
    shared["gains"] = f(np.stack([np.stack([np.broadcast_to(gn[i][l][None, :], (128, D)) for i in range(3)])
                                  for l in range(L)]))
    shared["cdw"] = f(np.stack([np.asarray(inp["conf_dw"])[l].T.reshape(2, 128, 31).transpose(1, 0, 2)
                                for l in range(L)]))
    cv = [np.asarray(inp["conf_dw_b"]), np.asarray(inp["conf_ln_g"]), np.asarray(inp["conf_ln_b"])]
    shared["cvec"] = f(np.stack([np.stack([cv[i][l].reshape(2, 128).T for i in range(3)], axis=-1)
                                 for l in range(L)]))
    shared["scw"] = f(np.stack([np.asarray(inp["sc_conv"])[l].T.reshape(2, 128, 3).transpose(1, 0, 2)
                                for l in range(L)]))
    qk = [np.asarray(inp["swa_q_norm"]), np.asarray(inp["swa_k_norm"]),
          np.asarray(inp["fox_q_norm"]), np.asarray(inp["fox_k_norm"])]
    shared["qkg"] = f(np.stack([np.stack([qk[i][l] for i in range(4)], axis=-1) for l in range(L)]))
    shared["bfg"] = f(np.asarray(inp["b_forget"]).reshape(L, 4, 1))
    sk = np.zeros((L, 65, 4), np.float32)
    sk[:, 64, :] = np.asarray(inp["swa_sink"])
    shared["sink"] = sk
    rb = np.asarray(inp["rel_bias"], dtype=np.float32)
    i = np.arange(128)[:, None]
    jq = np.arange(128)[None, :]
    bmt = np.full((128, 4, 256), NEG, np.float32)
    d_prev = jq + 128 - i
    ok_prev = d_prev <= 127
    d_cur = jq - i
    ok_cur = d_cur >= 0
    bk_prev = _t5_bucket(np.clip(d_prev, 0, 127))
    bk_cur = _t5_bucket(np.clip(d_cur, 0, 127))
    for hq in range(4):
        bmt[:, hq, 0:128] = np.where(ok_prev, rb[bk_prev, hq], np.float32(NEG))
        bmt[:, hq, 128:256] = np.where(ok_cur, rb[bk_cur, hq], np.float32(NEG))
    shared["bm"] = bmt
    shared["identin"] = np.eye(128, dtype=np.float32)
    return shared


_NC_CACHE = {}


def kernel(**inputs):
    x = np.asarray(inputs["x"], dtype=np.float32)
    B, T, _ = x.shape
    ntok = T // 2
    shared = host_prep(inputs)
    if ntok not in _NC_CACHE:
        _NC_CACHE[ntok] = build(ntok)
    nc = _NC_CACHE[ntok]
    in_maps = []
    for c in range(N_CORES):
        b, half = c // 2, c % 2
        m = dict(shared)
        m["xin"] = np.ascontiguousarray(x[b, half * ntok:(half + 1) * ntok])
        m["flagin"] = np.full((128, 1), float(half), np.float32)
        m["maskin"] = np.full((1, 4 * TT), 0.0 if half else NEG, np.float32)
        in_maps.append(m)
    res = run_bass_kernel_spmd(nc, in_maps, core_ids=list(range(N_CORES)))
    out = np.empty((B, T, D), np.float32)
    for c in range(N_CORES):
        b, half = c // 2, c % 2
        out[b, half * ntok:(half + 1) * ntok] = np.asarray(res.results[c]["yout"], dtype=np.float32)
    return out
```

```python
from contextlib import ExitStack

import numpy as np

import concourse.bass as bass
import concourse.mybir as mybir
from concourse.bass_utils import run_bass_kernel_spmd

F32 = mybir.dt.float32
BF16 = mybir.dt.bfloat16
AF = mybir.ActivationFunctionType
ALU = mybir.AluOpType

D = 1024
DFF = 2816
NF = DFF // 128
L = 2
TT = 512
NS = TT // 128
EPS = 1e-6
NEG = -30000.0
N_CORES = 8

WIN_TILES = ([(0, 128), (256, 128), (128, 128), (384, 128), (512, 128), (640, 128),
              (768, 128), (1024, 128), (896, 128), (1152, 128)]
             + [(1280, 128), (1408, 128), (1536, 128), (1664, 128)]
             + [(1792, 128), (1920, 128), (2048, 128), (2176, 128), (2304, 128), (2432, 128), (2560, 4)]
             + [(2564 + i * 1024 + m * 128, 128) for m in range(8) for i in range(4)])
NWT = len(WIN_TILES)


class Res:
    __slots__ = ("name", "lw", "rd")

    def __init__(self, name=""):
        self.name = name
        self.lw = None
        self.rd = {}


class Op:
    __slots__ = ("eng", "fn", "deps", "sig", "signo", "dma", "dj", "idx", "cc")


ENGS = ["pe", "act", "dve", "pool", "sp"]
NSC = 4
NSD = 8


def _dkey(o):
    return ("d", id(o)) if o.dma else ("c", o.eng)


def _dput(d, o):
    k = _dkey(o)
    cur = d.get(k)
    if cur is None or o.dma or o.idx > cur.idx:
        d[k] = o


class Sched:
    def __init__(self, nc, dry=False):
        self.nc = nc
        self.dry = dry
        self.ops = {e: [] for e in ENGS}

    def op(self, eng, fn, reads=(), writes=(), dma=False, cc=None):
        if self.dry:
            return None
        o = Op()
        o.eng, o.fn, o.dma, o.sig, o.cc = eng, fn, dma, False, cc
        o.idx = len(self.ops[eng])
        deps = {}

        def add(d, raw):
            if d is None:
                return
            if d.dma or dma or d.eng != eng or (raw and eng != "pe"):
                _dput(deps, d)

        for r in reads:
            add(r.lw, True)
        for w in writes:
            add(w.lw, False)
            for x in w.rd.values():
                add(x, False)
        o.deps = list(deps.values())
        for d in o.deps:
            d.sig = True
        for r in reads:
            _dput(r.rd, o)
        for w in writes:
            w.lw = o
            w.rd = {}
        self.ops[eng].append(o)
        return o

    def dma(self, eng, out, in_, reads=(), writes=(), **kw):
        return self.op(eng, lambda e: e.dma_start(out=out, in_=in_, **kw), reads=reads, writes=writes, dma=True)

    def emit(self):
        nc = self.nc
        for e in ENGS:
            c = 0
            j = 0
            for o in self.ops[e]:
                if o.cc is not None:
                    continue
                if o.dma:
                    o.dj = j
                    j += 1
                elif o.sig:
                    c += 1
                    o.signo = c
        with ExitStack() as st:
            csem = {e: [st.enter_context(nc.semaphore(f"c_{e}_{i}")) for i in range(NSC)]
                    for e in ENGS if e != "sp"}
            dsem = {e: [st.enter_context(nc.semaphore(f"d_{e}_{i}")) for i in range(NSD)]
                    for e in ENGS if e != "pe"}
            ncc = sum(1 for e in ENGS for o in self.ops[e] if o.cc is not None)
            ccsem = [st.enter_context(nc.semaphore(f"cc_{i}")) for i in range(ncc)]
            block = st.enter_context(nc.Block())

            def body(eng, e):
                cw = {}
                dw = {}

                def wait_dma(q, dj):
                    slot, val = dj % NSD, 16 * (dj // NSD + 1)
                    if dw.get((q, slot), 0) < val:
                        eng.wait_ge(dsem[q][slot], val)
                        dw[(q, slot)] = val

                for o in self.ops[e]:
                    for d in o.deps:
                        if d.cc is not None:
                            if dw.get(("cc", d.cc), 0) < 1:
                                eng.wait_ge(ccsem[d.cc], 1)
                                dw[("cc", d.cc)] = 1
                        elif d.dma:
                            wait_dma(d.eng, d.dj)
                        elif cw.get(d.eng, 0) < d.signo:
                            s = d.signo - 1
                            eng.wait_ge(csem[d.eng][s % NSC], s // NSC + 1)
                            cw[d.eng] = d.signo
                    if o.cc is not None:
                        o.fn(eng).then_inc(ccsem[o.cc])
                        continue
                    if o.dma and o.dj >= NSD:
                        wait_dma(e, o.dj - NSD)
                    ins = o.fn(eng)
                    if o.dma:
                        ins.then_inc(dsem[e][o.dj % NSD], 16)
                    elif o.sig:
                        ins.then_inc(csem[e][(o.signo - 1) % NSC], 1)
                n = sum(1 for o in self.ops[e] if o.dma and o.cc is None)
                for dj in range(max(0, n - NSD), n):
                    wait_dma(e, dj)

            names = {"pe": "tensor", "act": "scalar", "dve": "vector", "pool": "gpsimd", "sp": "sync"}
            for e in ENGS:
                getattr(block, names[e])(lambda eng, e=e: body(eng, e))


class Stream:
    def __init__(self, S, slots, res, plan):
        self.S, self.slots, self.res, self.plan = S, slots, res, plan
        self.rec = []
        self.pos = 0
        self.issued = 0

    def next(self, src):
        R = len(self.slots)
        if self.plan is None:
            self.rec.append(src)
            return self.slots[0], self.res[0]
        while self.issued < min(len(self.plan), self.pos + R):
            k = self.issued % R
            self.S.dma("pool", self.slots[k][:], self.plan[self.issued], writes=[self.res[k]])
            self.issued += 1
        k = self.pos % R
        self.pos += 1
        return self.slots[k], self.res[k]


class Bank:
    def __init__(self, ap, name):
        self.ap = ap
        self.res = Res(name)


class Rot:
    def __init__(self, items):
        self.items = items
        self.i = 0

    def get(self):
        b = self.items[self.i % len(self.items)]
        self.i += 1
        return b


def build(NTOK, stages=3):
    NT = NTOK // TT
    nc = bass.Bass("TRN2", target_bir_lowering=False)

    def din(name, shape, dt=F32):
        return nc.dram_tensor(name, list(shape), dt, kind="ExternalInput").ap()

    xin = din("xin", [NTOK, D])
    wg = din("wg", [L, 2, NF, 128, 8, 128])
    wu = din("wu", [L, 2, NF, 128, 8, 128])
    wd2 = din("wd2", [L, 2, 2, NF, 128, 512])
    win = din("win", [L, NWT, 128, 8, 128])
    wbr = din("wbr", [L, 8, 128, 12, 128])
    wo2 = din("wo2", [L, 2, 8, 128, 512])
    gains = din("gains", [L, 3, 128, D])
    cdw = din("cdw", [L, 128, 2, 31])
    cvec = din("cvec", [L, 128, 2, 3])
    scw = din("scw", [L, 128, 2, 3])
    qkg = din("qkg", [L, 64, 4])
    bfg = din("bfg", [L, 4, 1])
    sink = din("sink", [L, 65, 4])
    bm = din("bm", [128, 4, 256])
    identin = din("identin", [128, 128])
    flagin = din("flagin", [128, 1])
    maskin = din("maskin", [1, 4 * TT])
    yout = nc.dram_tensor("yout", [NTOK, D], F32, kind="ExternalOutput").ap()

    def dscr(name, shape, dt):
        return nc.dram_tensor(name, list(shape), dt, kind="Internal")

    Xs = dscr("Xs", [NTOK, D], F32).ap()
    FU = dscr("FU", [NT, 128, 2, TT], F32).ap()
    FSX = dscr("FSX", [NT, 128, 2, TT], F32).ap()
    FSB = dscr("FSB", [NT, 128, 2, TT], F32).ap()
    FQS = dscr("FQS", [NT, 64, 4, TT], BF16).ap()
    FKS = dscr("FKS", [NT, 64, 2, TT], BF16).ap()
    FVS = dscr("FVS", [NT, 128, NS, 2 * 65], BF16).ap()
    FQF = dscr("FQF", [NT, 64, 4, TT], BF16).ap()
    FG = dscr("FG", [NT, 4, TT], F32).ap()
    QFs = dscr("QFs", [2, 3, 4, TT], BF16).ap()
    NVC = max(1, NTOK // 1024)
    VC = NTOK // NVC
    KFh = [[dscr(f"KF{l}_{hh}", [70, NTOK], BF16) for hh in range(4)] for l in range(L)]
    VFh = [[dscr(f"VF{l}_{q}", [VC, 4 * 65], BF16) for q in range(NVC)] for l in range(L)]
    HFh = [dscr(f"HF{l}", [128, 128], F32) for l in range(L)]
    HBh = [dscr(f"HB{l}", [128, 512], BF16) for l in range(L)]
    GKFh = [[dscr(f"GKF{l}_{hh}", [2 * 70, NTOK], BF16) for hh in range(4)] for l in range(L)]
    GVFh = [[dscr(f"GVF{l}_{q}", [2 * VC, 4 * 65], BF16) for q in range(NVC)] for l in range(L)]
    GHFh = [dscr(f"GHF{l}", [2 * 128, 128], F32) for l in range(L)]
    GHBh = [dscr(f"GHB{l}", [2 * 128, 512], BF16) for l in range(L)]
    KF = [[t.ap() for t in row] for row in KFh]
    VF = [[t.ap() for t in row] for row in VFh]
    HF = [t.ap() for t in HFh]
    HB = [t.ap() for t in HBh]
    GKF = [[t.ap().rearrange("(g r) t -> g r t", g=2) for t in row] for row in GKFh]
    GVF = [[t.ap().rearrange("(g n) e -> g n e", g=2) for t in row] for row in GVFh]
    GHF = [t.ap().rearrange("(g p) e -> g p e", g=2) for t in GHFh]
    GHB = [t.ap().rearrange("(g p) e -> g p e", g=2) for t in GHBh]

    with ExitStack() as st:
        def sb(name, shape, dt):
            return st.enter_context(nc.sbuf_tensor(name, list(shape), dt))

        x = sb("x", [128, NS, D], F32)
        r_x = [Res(f"x{s}") for s in range(NS)]
        h = sb("h", [128, NS, D], BF16)
        r_h = [Res(f"h{s}") for s in range(NS)]
        hT = sb("hT", [128, 8, TT], BF16)
        r_hT = Res("hT")
        act = sb("act", [128, NF, TT], BF16)
        r_act = [Res(f"act{f}") for f in range(NF)]
        RA, RD, RR = 6, 8, 2
        wA = [sb(f"wA{i}", [128, 8, 128], BF16) for i in range(RA)]
        rA = [Res(f"wA{i}") for i in range(RA)]
        wD = [sb(f"wD{i}", [128, 512], BF16) for i in range(RD)]
        rD = [Res(f"wD{i}") for i in range(RD)]
        wR = [sb(f"wR{i}", [128, 12, 128], BF16) for i in range(RR)]
        rR = [Res(f"wR{i}") for i in range(RR)]
        gb = sb("gb", [128, D], F32)
        r_gb = Res("gb")
        ss = sb("ss", [128, 8], F32)
        r_ssq = [Res(f"ss{i}") for i in range(NS)]
        sg = [sb(f"sg{i}", [128, TT], F32) for i in range(2)]
        r_sg = [Res(f"sg{i}") for i in range(2)]
        ident = sb("ident", [128, 128], BF16)
        r_ident = Res("ident")
        ones = sb("ones", [128, 128], F32)
        r_ones = Res("ones")
        o64 = sb("o64", [64, 64], F32)
        r_o64 = Res("o64")
        uext = sb("uext", [128, 2, 30 + TT], F32)
        r_uext = Res("uext")
        sxext = sb("sxext", [128, 2, 2 + TT], F32)
        r_sxext = Res("sxext")
        sbt = sb("sbt", [128, 2, TT], F32)
        r_sbt = Res("sbt")
        cacc = sb("cacc", [128, 2, TT], F32)
        r_cacc = [Res("cacc0"), Res("cacc1")]
        csq = sb("csq", [128, 2, TT], F32)
        r_csq = Res("csq")
        lnm = sb("lnm", [128, TT], F32)
        r_lnm = Res("lnm")
        lnv = sb("lnv", [128, TT], F32)
        r_lnv = Res("lnv")
        uT = sb("uT", [128, 2, TT], BF16)
        r_uT = Res("uT")
        scT = sb("scT", [128, 2, TT], BF16)
        r_scT = Res("scT")
        cdw_t = sb("cdw_t", [128, 2, 31], F32)
        cvec_t = sb("cvec_t", [128, 2, 3], F32)
        scw_t = sb("scw_t", [128, 2, 3], F32)
        qkg_t = sb("qkg_t", [64, 4], F32)
        bfg_t = sb("bfg_t", [4, 1], F32)
        sink_t = sb("sink_t", [65, 4], F32)
        r_small = Res("small")
        bm_t = sb("bm_t", [128, 4, 256], F32)
        r_bm = Res("bm")
        qs = sb("qs", [64, 4, TT], BF16)
        r_qs = Res("qs")
        ksx = sb("ksx", [64, 2, 128 + TT], BF16)
        r_ksx = Res("ksx")
        vsx = sb("vsx", [128, NS + 1, 2, 65], BF16)
        r_vsx = Res("vsx")
        pT = [sb(f"pT{i}", [128, TT], BF16) for i in range(4)]
        r_pT = [Res(f"pT{i}") for i in range(4)]
        oa = sb("oa", [65, TT], F32)
        r_oa = Res("oa")
        rden = sb("rden", [65, TT], F32)
        r_rden = Res("rden")
        onT = [sb(f"onT{i}", [64, 4, TT], BF16) for i in range(2)]
        r_onT = [Res("onT0"), Res("onT1")]
        qf = sb("qf", [128, 4, TT], BF16)
        r_qf = Res("qf")
        qfx = sb("qfx", [128, 4, TT], BF16)
        r_qfx = Res("qfx")
        flag_t = sb("flag_t", [128, 1], F32)
        r_flag = Res("flag")
        tot_t = sb("tot_t", [4, 1], F32)
        r_tot = Res("tot")
        kf = sb("kf", [64, 4, TT], BF16)
        r_kf = Res("kf")
        vf = sb("vf", [128, NS, 4, 65], BF16)
        r_vf = Res("vf")
        kblk = [sb(f"kblk{i}", [128, 4, TT], BF16) for i in range(2)]
        r_kblk = [Res(f"kblk{i}") for i in range(2)]
        vblk = [sb(f"vblk{i}", [128, NS, 324], BF16) for i in range(2)]
        r_vblk = [Res(f"vblk{i}") for i in range(2)]
        fe = sb("fe", [4, TT], F32)
        r_fe = Res("fe")
        fG = sb("fG", [4, TT], F32)
        r_fG = Res("fG")
        fones = sb("fones", [4, TT], F32)
        r_fones = Res("fones")
        gsp = sb("gsp", [4, 3, TT], BF16)
        r_gs = Res("gs")
        monesb = sb("monesb", [4, 3, TT], BF16)
        r_monesb = Res("monesb")
        fcar = [sb(f"fcar{l}", [4, 1], F32) for l in range(L)]
        r_fcar = [Res(f"fcar{l}") for l in range(L)]
        macc = sb("macc", [128, TT], F32)
        r_macc = Res("macc")
        mtmp = [sb(f"mtmp{i}", [128, TT], F32) for i in range(2)]
        r_mtmp = [Res(f"mtmp{i}") for i in range(2)]
        mT = sb("mT", [128, 8, TT], BF16)
        r_mT = Res("mT")
        banks = [Bank(st.enter_context(nc.psum_tensor(f"ps{i}", [128, 512], F32)), f"ps{i}") for i in range(7)]
        ptb = Bank(st.enter_context(nc.psum_tensor("ptb", [128, 1024], BF16)), "ptb")
        ptb2 = Bank(banks[6].ap[:, :].bitcast(BF16), "ptb2")
        ptb2.res = banks[6].res
        ptbs = [ptb, ptb2]

        def program(S, plans):
            WA = Stream(S, wA, rA, plans[0])
            WD = Stream(S, wD, rD, plans[1])
            WR = Stream(S, wR, rR, plans[2])
            rot = Rot(banks)
            rot_hi = Rot(banks[4:7])
            RKF = [[Res(f"KF{l}_{j}") for j in range(NT)] for l in range(L)]
            RX = [Res(f"X{j}") for j in range(NT)]
            RF = [Res(f"F{j}") for j in range(NT)]
            r_HAL = [Res(f"HAL{l}") for l in range(L)]
            r_G = [Res(f"G{l}") for l in range(L)]
            r_QFs = Res("QFs")
            ccn = [0]
            kbi = [0]
            pti = [0]
            sgi = [0]

            def ACT(fn, reads, writes):
                S.op("act", fn, reads, writes)

            def DVE(fn, reads, writes):
                S.op("dve", fn, reads, writes)

            def PE(fn, reads, writes):
                S.op("pe", fn, reads, writes)

            def POOL(fn, reads, writes):
                S.op("pool", fn, reads, writes)

            S.dma("pool", ident[:], identin, writes=[r_ident])
            DVE(lambda e: e.memset(ones[:], 1.0), [], [r_ones])
            DVE(lambda e: e.memset(o64[:], 1.0 / 64.0), [], [r_o64])
            DVE(lambda e: e.memset(fones[:], 1.0), [], [r_fones])
            DVE(lambda e: e.memset(monesb[:], -1.0), [], [r_monesb])
            DVE(lambda e: e.memset(qf[:], 0.0), [], [r_qf])
            DVE(lambda e: e.memset(qf[64:70, :, :], 1.0), [], [r_qf])
            DVE(lambda e: e.memset(qfx[:], 0.0), [], [r_qfx])
            DVE(lambda e: e.memset(qfx[64:70, :, :], 1.0), [], [r_qfx])
            DVE(lambda e: e.memset(qfx[64:71, :, :], 1.0), [], [r_qfx])
            for i in range(2):
                DVE(lambda e, i=i: e.memset(kblk[i][:], 0.0), [], [r_kblk[i]])
                DVE(lambda e, i=i: e.memset(vblk[i][:], 0.0), [], [r_vblk[i]])
                S.dma("pool", kblk[i][70:71, :, :], maskin.rearrange("o (h t) -> o h t", h=4), writes=[r_kblk[i]])
            S.dma("sp", flag_t[:], flagin, writes=[r_flag])
            S.dma("sp", bm_t[:], bm, writes=[r_bm])
            DVE(lambda e: e.memset(vsx[:], 1.0), [], [r_vsx])
            DVE(lambda e: e.memset(vf[:], 1.0), [], [r_vf])
            for l in range(L):
                DVE(lambda e, l=l: e.memset(fcar[l][:], 0.0), [], [r_fcar[l]])

            def rmsnorm_hT(gain_src):
                S.dma("sp", gb[:], gain_src, writes=[r_gb])

                def st1(s):
                    ACT(lambda e: e.activation(out=h[:, s, :], in_=x[:, s, :], func=AF.Square,
                                               accum_out=ss[:, s:s + 1]), [r_x[s]], [r_h[s], r_ssq[s]])
                    ACT(lambda e: e.activation(out=ss[:, 4 + s:5 + s], in_=ss[:, s:s + 1], func=AF.Ln,
                                               scale=1.0 / D, bias=EPS), [r_ssq[s]], [r_ssq[s]])
                    ACT(lambda e: e.activation(out=ss[:, 4 + s:5 + s], in_=ss[:, 4 + s:5 + s], func=AF.Exp,
                                               scale=-0.5), [r_ssq[s]], [r_ssq[s]])

                def st2(s):
                    DVE(lambda e: e.scalar_tensor_tensor(out=h[:, s, :], in0=x[:, s, :],
                                                         scalar=ss[:, 4 + s:5 + s], in1=gb[:],
                                                         op0=ALU.mult, op1=ALU.mult),
                        [r_x[s], r_ssq[s], r_gb], [r_h[s]])

                def st3(s):
                    pb = ptbs[s % 2]
                    for c in range(8):
                        PE(lambda e, c=c, pb=pb: e.transpose(out=pb.ap[:, c * 128:(c + 1) * 128],
                                                             in_=h[:, s, c * 128:(c + 1) * 128],
                                                             identity=ident[:]),
                           [r_h[s], r_ident], [pb.res])

                def st4(s):
                    pb = ptbs[s % 2]
                    fn = lambda e: e.copy(out=hT[:, :, s * 128:(s + 1) * 128],
                                          in_=pb.ap[:, :].rearrange("p (c t) -> p c t", c=8))
                    fn2 = lambda e: e.tensor_copy(out=hT[:, :, s * 128:(s + 1) * 128],
                                                  in_=pb.ap[:, :].rearrange("p (c t) -> p c t", c=8))
                    if s % 2 == 0:
                        ACT(fn, [pb.res], [r_hT])
                    else:
                        DVE(fn2, [pb.res], [r_hT])

                stages = [st1, st2, st3, st4]
                for step in range(NS + 3):
                    for k, st_fn in enumerate(stages):
                        sidx = step - k
                        if 0 <= sidx < NS:
                            st_fn(sidx)

            def resid_proj(src, src_res_of, nk, tile_src, scale):
                for n in range(2):
                    accs = [rot.get() for _ in range(NS)]
                    for kc in range(nk):
                        t, r = WD.next(tile_src(n, kc))
                        for s in range(NS):
                            PE(lambda e, s=s, kc=kc, t=t, a=accs[s]: e.matmul(
                                a.ap[:, :], lhsT=src[:, kc, s * 128:(s + 1) * 128], rhs=t[:],
                                start=(kc == 0), stop=(kc == nk - 1)), [src_res_of(kc), r], [accs[s].res])
                    for s in range(NS):
                        DVE(lambda e, s=s, n=n, a=accs[s]: e.scalar_tensor_tensor(
                            out=x[:, s, n * 512:(n + 1) * 512], in0=a.ap[:, :], scalar=scale,
                            in1=x[:, s, n * 512:(n + 1) * 512], op0=ALU.mult, op1=ALU.add),
                            [accs[s].res, r_x[s]], [r_x[s]])

            def ffn(l, k, mid=None):
                rmsnorm_hT(gains[l, 2 * k])
                for f in range(NF):
                    tg, rg = WA.next(wg[l, k, f])
                    bg = rot.get()
                    for c in range(8):
                        PE(lambda e, c=c, tg=tg, bg=bg: e.matmul(bg.ap[:, :], lhsT=tg[:, c, :], rhs=hT[:, c, :],
                                                                 start=(c == 0), stop=(c == 7)),
                           [rg, r_hT], [bg.res])
                    tu, ru = WA.next(wu[l, k, f])
                    bu = rot.get()
                    for c in range(8):
                        PE(lambda e, c=c, tu=tu, bu=bu: e.matmul(bu.ap[:, :], lhsT=tu[:, c, :], rhs=hT[:, c, :],
                                                                 start=(c == 0), stop=(c == 7)),
                           [ru, r_hT], [bu.res])
                    i = sgi[0] % 2
                    sgi[0] += 1
                    ACT(lambda e, i=i, bg=bg: e.activation(out=sg[i][:], in_=bg.ap[:, :], func=AF.Silu),
                        [bg.res], [r_sg[i]])
                    DVE(lambda e, i=i, f=f, bu=bu: e.tensor_tensor(out=act[:, f, :], in0=sg[i][:], in1=bu.ap[:, :],
                                                                   op=ALU.mult),
                        [r_sg[i], bu.res], [r_act[f]])
                if mid is not None:
                    mid()
                resid_proj(act, lambda f: r_act[f], NF, lambda n, f: wd2[l, k, n, f], 0.5)

            def proj128(l, wi):
                t, r = WA.next(win[l, wi])
                b = rot.get()
                for c in range(8):
                    PE(lambda e, c=c, t=t, b=b: e.matmul(b.ap[:, :], lhsT=t[:, c, :], rhs=hT[:, c, :],
                                                         start=(c == 0), stop=(c == 7)), [r, r_hT], [b.res])
                return b

            def head_norm(src_ap, src_res, gcol, out_ap, out_res):
                ACT(lambda e: e.activation(out=lnm[0:64, :], in_=src_ap, func=AF.Square), [src_res], [r_lnm])
                b = rot.get()
                PE(lambda e, b=b: e.matmul(b.ap[0:64, :], lhsT=o64[:], rhs=lnm[0:64, :], start=True, stop=True),
                   [r_lnm, r_o64], [b.res])
                ACT(lambda e, b=b: e.activation(out=lnv[0:64, :], in_=b.ap[0:64, :], func=AF.Ln, bias=EPS),
                    [b.res], [r_lnv])
                ACT(lambda e: e.activation(out=lnv[0:64, :], in_=lnv[0:64, :], func=AF.Exp, scale=-0.5),
                    [r_lnv], [r_lnv])
                DVE(lambda e: e.scalar_tensor_tensor(out=out_ap, in0=src_ap, scalar=qkg_t[:, gcol:gcol + 1],
                                                     in1=lnv[0:64, :], op0=ALU.mult, op1=ALU.mult),
                    [src_res, r_lnv, r_small], [out_res])

            def heads_tile(l, wi, specs):
                t, r = WA.next(win[l, wi])
                for hh in range(2):
                    b = rot.get()
                    for c in range(8):
                        PE(lambda e, c=c, t=t, b=b, hh=hh: e.matmul(
                            b.ap[0:64, :], lhsT=t[:, c, hh * 64:(hh + 1) * 64], rhs=hT[:, c, :],
                            start=(c == 0), stop=(c == 7)), [r, r_hT], [b.res])
                    gcol, out_ap, out_res = specs[hh]
                    head_norm(b.ap[0:64, :], b.res, gcol, out_ap, out_res)

            def v_tile(l, wi, dst, dst_res, chunk_off, h0):
                t, r = WA.next(win[l, wi])
                for s in range(NS):
                    b = rot.get()
                    for c in range(8):
                        PE(lambda e, c=c, t=t, b=b, s=s: e.matmul(
                            b.ap[:, 0:128], lhsT=hT[:, c, s * 128:(s + 1) * 128], rhs=t[:, c, :],
                            start=(c == 0), stop=(c == 7)), [r, r_hT], [b.res])
                    ACT(lambda e, b=b, s=s: e.copy(out=dst[:, chunk_off + s, h0:h0 + 2, 0:64],
                                                   in_=b.ap[:, 0:128].rearrange("p (h d) -> p h d", h=2)),
                        [b.res], [dst_res])

            def attn_norm(acc, extra_den, out_ap, out_res):
                ACT(lambda e: e.copy(out=oa[:], in_=acc.ap[0:65, :]), [acc.res], [r_oa])
                if extra_den is not None:
                    DVE(lambda e: e.tensor_scalar(out=oa[64:65, :], in0=oa[64:65, :], scalar1=extra_den,
                                                  scalar2=None, op0=ALU.add), [r_oa, r_small], [r_oa])
                ACT(lambda e: e.activation(out=rden[64:65, :], in_=oa[64:65, :], func=AF.Ln), [r_oa], [r_rden])
                ACT(lambda e: e.activation(out=rden[64:65, :], in_=rden[64:65, :], func=AF.Exp, scale=-1.0),
                    [r_rden], [r_rden])
                b = rot_hi.get()
                PE(lambda e, b=b: e.matmul(b.ap[0:64, :], lhsT=ones[64:65, 0:64], rhs=rden[64:65, :],
                                           start=True, stop=True), [r_rden, r_ones], [b.res])
                DVE(lambda e, b=b: e.tensor_tensor(out=out_ap, in0=oa[0:64, :], in1=b.ap[0:64, :], op=ALU.mult),
                    [r_oa, b.res], [out_res])

            def load_small(l):
                S.dma("sp", cdw_t[:], cdw[l], writes=[r_small])
                S.dma("sp", cvec_t[:], cvec[l], writes=[r_small])
                S.dma("sp", scw_t[:], scw[l], writes=[r_small])
                S.dma("sp", qkg_t[:], qkg[l], writes=[r_small])
                S.dma("sp", bfg_t[:], bfg[l], writes=[r_small])
                S.dma("sp", sink_t[:], sink[l], writes=[r_small])
                DVE(lambda e: e.tensor_scalar(out=qkg_t[:, 0:1], in0=qkg_t[:, 0:1], scalar1=0.125, scalar2=None,
                                              op0=ALU.mult), [r_small], [r_small])
                DVE(lambda e: e.tensor_scalar(out=qkg_t[:, 2:3], in0=qkg_t[:, 2:3], scalar1=0.125, scalar2=None,
                                              op0=ALU.mult), [r_small], [r_small])
                ACT(lambda e: e.activation(out=sink_t[64:65, :], in_=sink_t[64:65, :], func=AF.Exp),
                    [r_small], [r_small])
                DVE(lambda e: e.tensor_scalar(out=bfg_t[:], in0=bfg_t[:], scalar1=-1.0, scalar2=None,
                                              op0=ALU.mult), [r_small], [r_small])

            def split3(src, src_res):
                DVE(lambda e: e.tensor_copy(out=gsp[:, 0, :], in_=src[:]), [src_res], [r_gs])
                DVE(lambda e: e.tensor_tensor(out=fe[:], in0=src[:], in1=gsp[:, 0, :], op=ALU.subtract),
                    [src_res, r_gs], [r_fe])
                DVE(lambda e: e.tensor_copy(out=gsp[:, 1, :], in_=fe[:]), [r_fe], [r_gs])
                DVE(lambda e: e.tensor_tensor(out=fe[:], in0=fe[:], in1=gsp[:, 1, :], op=ALU.subtract),
                    [r_fe, r_gs], [r_fe])
                DVE(lambda e: e.tensor_copy(out=gsp[:, 2, :], in_=fe[:]), [r_fe], [r_gs])

            def front(l, j):
                t0 = j * TT
                last = (j == NT - 1)
                load_small(l)
                rmsnorm_hT(gains[l, 1])
                if j + 1 < NT:
                    src = xin if l == 0 else Xs
                    S.dma("act", x[:], src[t0 + TT:t0 + 2 * TT, :].rearrange("(s p) d -> p s d", p=128),
                          reads=([RX[j + 1]] if l > 0 else []), writes=r_x)
                for ch in range(2):
                    ba = proj128(l, 2 * ch)
                    bb = proj128(l, 2 * ch + 1)
                    i = sgi[0] % 2
                    sgi[0] += 1
                    ACT(lambda e, i=i, bb=bb: e.activation(out=sg[i][:], in_=bb.ap[:, :], func=AF.Sigmoid),
                        [bb.res], [r_sg[i]])
                    DVE(lambda e, i=i, ba=ba, ch=ch: e.tensor_tensor(out=uext[:, ch, 30:30 + TT], in0=sg[i][:],
                                                                     in1=ba.ap[:, :], op=ALU.mult),
                        [r_sg[i], ba.res], [r_uext])
                for ch in range(2):
                    b = proj128(l, 4 + ch)
                    ACT(lambda e, b=b, ch=ch: e.copy(out=sbt[:, ch, :], in_=b.ap[:, :]), [b.res], [r_sbt])
                for ch in range(2):
                    bc = proj128(l, 6 + 2 * ch)
                    bx = proj128(l, 7 + 2 * ch)
                    i = sgi[0] % 2
                    sgi[0] += 1
                    ACT(lambda e, i=i, bc=bc: e.copy(out=sg[i][:], in_=bc.ap[:, :]), [bc.res], [r_sg[i]])
                    DVE(lambda e, i=i, bx=bx, ch=ch: e.tensor_tensor(out=sxext[:, ch, 2:2 + TT], in0=sg[i][:],
                                                                     in1=bx.ap[:, :], op=ALU.mult),
                        [r_sg[i], bx.res], [r_sxext])
                S.dma("sp", FU[j], uext[:, :, 30:30 + TT], reads=[r_uext], writes=[RF[j]])
                S.dma("sp", FSX[j], sxext[:, :, 2:2 + TT], reads=[r_sxext], writes=[RF[j]])
                S.dma("sp", FSB[j], sbt[:], reads=[r_sbt], writes=[RF[j]])
                if last:
                    S.dma("sp", HF[l][:, 0:60].rearrange("p (c k) -> p c k", c=2), uext[:, :, TT:TT + 30],
                          reads=[r_uext], writes=[r_HAL[l]])
                    S.dma("sp", HF[l][:, 60:64].rearrange("p (c k) -> p c k", c=2), sxext[:, :, TT:TT + 2],
                          reads=[r_sxext], writes=[r_HAL[l]])
                heads_tile(l, 10, [(0, qs[:, 0, :], r_qs), (0, qs[:, 1, :], r_qs)])
                heads_tile(l, 11, [(0, qs[:, 2, :], r_qs), (0, qs[:, 3, :], r_qs)])
                heads_tile(l, 12, [(1, ksx[:, 0, 128:128 + TT], r_ksx), (1, ksx[:, 1, 128:128 + TT], r_ksx)])
                v_tile(l, 13, vsx, r_vsx, 1, 0)
                S.dma("sp", FQS[j], qs[:], reads=[r_qs], writes=[RF[j]])
                S.dma("sp", FKS[j], ksx[:, :, 128:128 + TT], reads=[r_ksx], writes=[RF[j]])
                S.dma("sp", FVS[j], vsx[:, 1:NS + 1, :, :].rearrange("p s h e -> p s (h e)"), reads=[r_vsx],
                      writes=[RF[j]])
                if last:
                    S.dma("sp", HB[l][0:64, 0:256].rearrange("p (c k) -> p c k", c=2), ksx[:, :, TT:TT + 128],
                          reads=[r_ksx], writes=[r_HAL[l]])
                    S.dma("sp", HB[l][:, 256:386], vsx[:, NS, :, :].rearrange("p h e -> p (h e)"),
                          reads=[r_vsx], writes=[r_HAL[l]])
                heads_tile(l, 14, [(2, qf[0:64, 0, :], r_qf), (2, qf[0:64, 1, :], r_qf)])
                heads_tile(l, 15, [(2, qf[0:64, 2, :], r_qf), (2, qf[0:64, 3, :], r_qf)])
                heads_tile(l, 16, [(3, kf[:, 0, :], r_kf), (3, kf[:, 1, :], r_kf)])
                heads_tile(l, 17, [(3, kf[:, 2, :], r_kf), (3, kf[:, 3, :], r_kf)])
                v_tile(l, 18, vf, r_vf, 0, 0)
                v_tile(l, 19, vf, r_vf, 0, 2)
                t, r = WA.next(win[l, 20])
                bF = rot.get()
                for c in range(8):
                    PE(lambda e, c=c, t=t, bF=bF: e.matmul(bF.ap[0:4, :], lhsT=t[:, c, 0:4], rhs=hT[:, c, :],
                                                           start=(c == 0), stop=(c == 7)), [r, r_hT], [bF.res])
                ACT(lambda e: e.activation(out=fe[:], in_=bF.ap[0:4, :], func=AF.Exp, scale=-1.0,
                                           bias=bfg_t[:, 0:1]), [bF.res, r_small], [r_fe])
                ACT(lambda e: e.activation(out=fe[:], in_=fe[:], func=AF.Ln, bias=1.0), [r_fe], [r_fe])
                if j == 0:
                    DVE(lambda e: e.memset(fcar[l][:], 0.0), [], [r_fcar[l]])
                DVE(lambda e: e.tensor_tensor_scan(out=fG[:], data0=fones[:], data1=fe[:], initial=fcar[l][:, 0:1],
                                                   op0=ALU.mult, op1=ALU.add), [r_fe, r_fones, r_fcar[l]], [r_fG])
                DVE(lambda e: e.tensor_copy(out=fcar[l][:], in_=fG[:, TT - 1:TT]), [r_fG], [r_fcar[l]])
                split3(fG, r_fG)
                S.dma("sp", FQF[j], qf[0:64, :, :], reads=[r_qf], writes=[RF[j]])
                S.dma("sp", FG[j], fG[:], reads=[r_fG], writes=[RF[j]])
                rKF = RKF[l][j]
                for hh in range(4):
                    S.dma("sp", KF[l][hh][0:64, t0:t0 + TT], kf[:, hh, :], reads=[r_kf], writes=[rKF])
                    S.dma("sp", KF[l][hh][64:67, t0:t0 + TT].rearrange("(o a) t -> o a t", o=1),
                          monesb[hh:hh + 1, :, :], reads=[r_monesb], writes=[rKF])
                    S.dma("sp", KF[l][hh][67:70, t0:t0 + TT].rearrange("(o a) t -> o a t", o=1),
                          gsp[hh:hh + 1, :, :], reads=[r_gs], writes=[rKF])
                vq, vo = t0 // VC, t0 % VC
                S.dma("sp", VF[l][vq][vo:vo + TT, :].rearrange("(s p) e -> p s e", p=128),
                      vf[:].rearrange("p s h e -> p s (h e)"), reads=[r_vf], writes=[rKF])
                if last:
                    S.dma("sp", HF[l][0:4, 64:65], fcar[l][:], reads=[r_fcar[l]], writes=[r_HAL[l]],
                          allow_slow_non_contiguous=True)

            def exchange(l):
                srcs = ([(KFh[l][hh], GKFh[l][hh]) for hh in range(4)]
                        + [(VFh[l][q], GVFh[l][q]) for q in range(NVC)]
                        + [(HFh[l], GHFh[l]), (HBh[l], GHBh[l])])
                for (a, g) in srcs:
                    k = ccn[0]
                    ccn[0] += 1
                    S.op("pool", lambda e, a=a, g=g: e.collective_compute(
                        "AllGather", ALU.bypass, replica_groups=[[0, 1], [2, 3], [4, 5], [6, 7]],
                        ins=[a.ap().opt()], outs=[g.ap().opt()]),
                        reads=RKF[l] + [r_HAL[l]], writes=[r_G[l]], dma=True, cc=k)

            def back_prep(l, j):
                t0 = j * TT
                load_small(l)
                S.dma("sp", uext[:, :, 30:30 + TT], FU[j], reads=[RF[j]], writes=[r_uext])
                S.dma("sp", sxext[:, :, 2:2 + TT], FSX[j], reads=[RF[j]], writes=[r_sxext])
                S.dma("sp", sbt[:], FSB[j], reads=[RF[j]], writes=[r_sbt])
                if j == 0:
                    S.dma("sp", uext[:, :, 0:30], GHF[l][0, :, 0:60].rearrange("p (c k) -> p c k", c=2),
                          reads=[r_G[l]], writes=[r_uext])
                    S.dma("sp", sxext[:, :, 0:2], GHF[l][0, :, 60:64].rearrange("p (c k) -> p c k", c=2),
                          reads=[r_G[l]], writes=[r_sxext])
                    S.dma("sp", tot_t[:], GHF[l][0, 0:4, 64:65], reads=[r_G[l]], writes=[r_tot],
                          allow_slow_non_contiguous=True)
                    DVE(lambda e: e.tensor_scalar(out=uext[:, :, 0:30], in0=uext[:, :, 0:30],
                                                  scalar1=flag_t[:, 0:1], scalar2=None, op0=ALU.mult),
                        [r_uext, r_flag], [r_uext])
                    DVE(lambda e: e.tensor_scalar(out=sxext[:, :, 0:2], in0=sxext[:, :, 0:2],
                                                  scalar1=flag_t[:, 0:1], scalar2=None, op0=ALU.mult),
                        [r_sxext, r_flag], [r_sxext])
                else:
                    S.dma("sp", uext[:, :, 0:30], FU[j - 1][:, :, TT - 30:TT], reads=[RF[j - 1]], writes=[r_uext])
                    S.dma("sp", sxext[:, :, 0:2], FSX[j - 1][:, :, TT - 2:TT], reads=[RF[j - 1]], writes=[r_sxext])
                S.dma("sp", qf[0:64, :, :], FQF[j], reads=[RF[j]], writes=[r_qf])
                S.dma("sp", qfx[0:64, :, :], FQF[j], reads=[RF[j]], writes=[r_qfx])
                S.dma("sp", fG[:], FG[j], reads=[RF[j]], writes=[r_fG])
                split3(fG, r_fG)
                for hh in range(4):
                    S.dma("sp", QFs[0, :, hh, :].rearrange("(o a) t -> o a t", o=1), gsp[hh:hh + 1, :, :],
                          reads=[r_gs], writes=[r_QFs])
                S.dma("sp", qf[64:67, :, :], QFs[0], reads=[r_QFs], writes=[r_qf])
                DVE(lambda e: e.tensor_scalar(out=fG[:], in0=fG[:], scalar1=tot_t[:, 0:1], scalar2=None,
                                              op0=ALU.add), [r_fG, r_tot], [r_fG])
                split3(fG, r_fG)
                for hh in range(4):
                    S.dma("sp", QFs[1, :, hh, :].rearrange("(o a) t -> o a t", o=1), gsp[hh:hh + 1, :, :],
                          reads=[r_gs], writes=[r_QFs])
                S.dma("sp", qfx[64:67, :, :], QFs[1], reads=[r_QFs], writes=[r_qfx])
            def back(l, j):
                t0 = j * TT
                for kk in range(31):
                    for ch in range(2):
                        if kk == 0:
                            DVE(lambda e, ch=ch: e.tensor_scalar(
                                out=cacc[:, ch, :], in0=uext[:, ch, 0:TT], scalar1=cdw_t[:, ch, 0:1],
                                scalar2=cvec_t[:, ch, 0:1], op0=ALU.mult, op1=ALU.add),
                                [r_uext, r_small], [r_cacc[ch]])
                        else:
                            DVE(lambda e, ch=ch, kk=kk: e.scalar_tensor_tensor(
                                out=cacc[:, ch, :], in0=uext[:, ch, kk:kk + TT], scalar=cdw_t[:, ch, kk:kk + 1],
                                in1=cacc[:, ch, :], op0=ALU.mult, op1=ALU.add),
                                [r_uext, r_small, r_cacc[ch]], [r_cacc[ch]])
                accs = banks[0:4]
                nkb = NT + j + 1
                jobs = [(kk, hh, c) for kk in range(nkb) for hh in range(4) for c in range(NS)]
                blk = {}

                def load_blk(kk):
                    cross = kk < NT
                    kb = kk if cross else kk - NT
                    i = kbi[0] % 2
                    kbi[0] += 1
                    vq, vo = (kb * TT) // VC, (kb * TT) % VC
                    if cross:
                        for hh in range(4):
                            S.dma("sp", kblk[i][0:70, hh, :], GKF[l][hh][0, :, kb * TT:(kb + 1) * TT],
                                  reads=[r_G[l]], writes=[r_kblk[i]])
                        S.dma("sp", vblk[i][:, :, 0:260],
                              GVF[l][vq][0, vo:vo + TT, :].rearrange("(s p) e -> p s e", p=128),
                              reads=[r_G[l]], writes=[r_vblk[i]])
                    else:
                        for hh in range(4):
                            S.dma("sp", kblk[i][0:70, hh, :], KF[l][hh][:, kb * TT:(kb + 1) * TT],
                                  reads=[RKF[l][kb]], writes=[r_kblk[i]])
                        S.dma("sp", vblk[i][:, :, 0:260],
                              VF[l][vq][vo:vo + TT, :].rearrange("(s p) e -> p s e", p=128),
                              reads=[RKF[l][kb]], writes=[r_vblk[i]])
                    blk[kk] = (i, cross, (not cross) and kb == j)

                pis = {}

                def emit_S(job):
                    kk, hh, c = job
                    if kk not in blk:
                        load_blk(kk)
                    i, cross, diag = blk[kk]
                    qsrc, rq = (qfx, r_qfx) if cross else (qf, r_qf)
                    bS = rot_hi.get()
                    PE(lambda e, i=i, hh=hh, c=c, bS=bS, qsrc=qsrc: e.matmul(
                        bS.ap[:, :], lhsT=kblk[i][:, hh, c * 128:(c + 1) * 128], rhs=qsrc[:, hh, :],
                        start=True, stop=True), [r_kblk[i], rq], [bS.res])
                    pi = pti[0] % 4
                    pti[0] += 1
                    pis[job] = pi
                    ACT(lambda e, pi=pi, bS=bS: e.activation(out=pT[pi][:], in_=bS.ap[:, :], func=AF.Exp),
                        [bS.res], [r_pT[pi]])
                    if diag:
                        POOL(lambda e, pi=pi, c=c: e.affine_select(
                            out=pT[pi][:], in_=pT[pi][:], pattern=[[1, TT]], compare_op=ALU.is_ge,
                            fill=0.0, base=-c * 128, channel_multiplier=-1), [r_pT[pi]], [r_pT[pi]])

                def emit_PV(job):
                    kk, hh, c = job
                    i = blk[kk][0]
                    pi = pis[job]
                    PE(lambda e, i=i, hh=hh, c=c, pi=pi, kk=kk: e.matmul(
                        accs[hh].ap[:, :], lhsT=vblk[i][:, c, hh * 65:hh * 65 + 128], rhs=pT[pi][:],
                        start=(kk == 0 and c == 0), stop=(kk == nkb - 1 and c == NS - 1)),
                       [r_vblk[i], r_pT[pi]], [accs[hh].res])

                LA = 2
                for idx, job in enumerate(jobs):
                    emit_S(job)
                    if idx >= LA:
                        emit_PV(jobs[idx - LA])
                for job in jobs[len(jobs) - LA:]:
                    emit_PV(job)
                for hh in range(4):
                    attn_norm(accs[hh], None, onT[1][:, hh, :], r_onT[1])

                rmsnorm_hT(gains[l, 1])
                for ch in range(2):
                    ACT(lambda e, ch=ch: e.activation(out=csq[:, ch, :], in_=cacc[:, ch, :], func=AF.Square),
                        [r_cacc[ch]], [r_csq])
                b1 = rot.get()
                for ch in range(2):
                    PE(lambda e, ch=ch, b1=b1: e.matmul(b1.ap[:, :], lhsT=ones[:], rhs=cacc[:, ch, :],
                                                        start=(ch == 0), stop=(ch == 1)),
                       [r_cacc[ch], r_ones], [b1.res])
                b2 = rot.get()
                for ch in range(2):
                    PE(lambda e, ch=ch, b2=b2: e.matmul(b2.ap[:, :], lhsT=ones[:], rhs=csq[:, ch, :],
                                                        start=(ch == 0), stop=(ch == 1)),
                       [r_csq, r_ones], [b2.res])
                DVE(lambda e: e.tensor_scalar(out=lnm[:], in0=b1.ap[:, :], scalar1=1.0 / 256.0, scalar2=None,
                                              op0=ALU.mult), [b1.res], [r_lnm])
                DVE(lambda e: e.tensor_tensor(out=lnv[:], in0=lnm[:], in1=lnm[:], op=ALU.mult), [r_lnm], [r_lnv])
                DVE(lambda e: e.scalar_tensor_tensor(out=lnv[:], in0=b2.ap[:, :], scalar=1.0 / 256.0, in1=lnv[:],
                                                     op0=ALU.mult, op1=ALU.subtract), [b2.res, r_lnv], [r_lnv])
                DVE(lambda e: e.tensor_scalar(out=lnv[:], in0=lnv[:], scalar1=0.0, scalar2=None, op0=ALU.max),
                    [r_lnv], [r_lnv])
                ACT(lambda e: e.activation(out=lnv[:], in_=lnv[:], func=AF.Ln, bias=EPS), [r_lnv], [r_lnv])
                ACT(lambda e: e.activation(out=lnv[:], in_=lnv[:], func=AF.Exp, scale=-0.5), [r_lnv], [r_lnv])
                for ch in range(2):
                    DVE(lambda e, ch=ch: e.tensor_tensor(out=cacc[:, ch, :], in0=cacc[:, ch, :], in1=lnm[:],
                                                         op=ALU.subtract), [r_cacc[ch], r_lnm], [r_cacc[ch]])
                    DVE(lambda e, ch=ch: e.tensor_tensor(out=cacc[:, ch, :], in0=cacc[:, ch, :], in1=lnv[:],
                                                         op=ALU.mult), [r_cacc[ch], r_lnv], [r_cacc[ch]])
                    ACT(lambda e, ch=ch: e.activation(out=uT[:, ch, :], in_=cacc[:, ch, :], func=AF.Silu,
                                                      scale=cvec_t[:, ch, 1:2], bias=cvec_t[:, ch, 2:3]),
                        [r_cacc[ch], r_small], [r_uT])
                for ch in range(2):
                    DVE(lambda e, ch=ch: e.tensor_scalar(out=csq[:, ch, :], in0=sxext[:, ch, 0:TT],
                                                         scalar1=scw_t[:, ch, 0:1], scalar2=None, op0=ALU.mult),
                        [r_sxext, r_small], [r_csq])
                    for kk in (1, 2):
                        DVE(lambda e, ch=ch, kk=kk: e.scalar_tensor_tensor(
                            out=csq[:, ch, :], in0=sxext[:, ch, kk:kk + TT], scalar=scw_t[:, ch, kk:kk + 1],
                            in1=csq[:, ch, :], op0=ALU.mult, op1=ALU.add), [r_sxext, r_small, r_csq], [r_csq])
                    DVE(lambda e, ch=ch: e.tensor_tensor(out=scT[:, ch, :], in0=csq[:, ch, :], in1=sbt[:, ch, :],
                                                         op=ALU.mult), [r_csq, r_sbt], [r_scT])

                S.dma("sp", qs[:], FQS[j], reads=[RF[j]], writes=[r_qs])
                S.dma("sp", ksx[:, :, 128:128 + TT], FKS[j], reads=[RF[j]], writes=[r_ksx])
                S.dma("sp", vsx[:, 1:NS + 1, :, :].rearrange("p s h e -> p s (h e)"), FVS[j], reads=[RF[j]],
                      writes=[r_vsx])
                if j == 0:
                    S.dma("sp", ksx[:, :, 0:128], GHB[l][0, 0:64, 0:256].rearrange("p (c k) -> p c k", c=2),
                          reads=[r_G[l]], writes=[r_ksx])
                    S.dma("sp", vsx[:, 0, :, :].rearrange("p h e -> p (h e)"), GHB[l][0, :, 256:386],
                          reads=[r_G[l]], writes=[r_vsx])
                    DVE(lambda e: e.tensor_scalar(out=vsx[:, 0, :, :], in0=vsx[:, 0, :, :],
                                                  scalar1=flag_t[:, 0:1], scalar2=None, op0=ALU.mult),
                        [r_vsx, r_flag], [r_vsx])
                else:
                    S.dma("sp", ksx[:, :, 0:128], FKS[j - 1][:, :, TT - 128:TT], reads=[RF[j - 1]], writes=[r_ksx])
                    S.dma("sp", vsx[:, 0, :, :].rearrange("p h e -> p (h e)"), FVS[j - 1][:, NS - 1, :],
                          reads=[RF[j - 1]], writes=[r_vsx])
                for hq in range(4):
                    hk = hq // 2
                    acc = banks[hq % 4]
                    for pr in range(2):
                        bS = rot_hi.get()
                        i = sgi[0] % 2
                        sgi[0] += 1
                        for s2 in range(2):
                            s = 2 * pr + s2
                            for part in range(2):
                                PE(lambda e, s=s, s2=s2, part=part, bS=bS, hk=hk, hq=hq: e.matmul(
                                    bS.ap[:, s2 * 256 + part * 128: s2 * 256 + part * 128 + 128],
                                    lhsT=ksx[:, hk, (s + part) * 128:(s + part + 1) * 128],
                                    rhs=qs[:, hq, s * 128:(s + 1) * 128], start=True, stop=True),
                                   [r_ksx, r_qs], [bS.res])
                            DVE(lambda e, bS=bS, i=i, hq=hq, s2=s2: e.tensor_tensor(
                                out=sg[i][:, s2 * 256:(s2 + 1) * 256], in0=bS.ap[:, s2 * 256:(s2 + 1) * 256],
                                in1=bm_t[:, hq, :], op=ALU.add), [bS.res, r_bm], [r_sg[i]])
                        pi = pti[0] % 4
                        pti[0] += 1
                        ACT(lambda e, i=i, pi=pi: e.activation(out=pT[pi][:], in_=sg[i][:], func=AF.Exp),
                            [r_sg[i]], [r_pT[pi]])
                        for s2 in range(2):
                            s = 2 * pr + s2
                            first = True
                            for part in range(2):
                                PE(lambda e, s=s, s2=s2, part=part, pi=pi, hk=hk, acc=acc, first=first: e.matmul(
                                    acc.ap[0:65, s * 128:(s + 1) * 128], lhsT=vsx[:, s + part, hk, :],
                                    rhs=pT[pi][:, s2 * 256 + part * 128: s2 * 256 + part * 128 + 128],
                                    start=first, stop=(part == 1)), [r_vsx, r_pT[pi]], [acc.res])
                                first = False
                    attn_norm(acc, sink_t[64:65, hq:hq + 1], onT[0][:, hq, :], r_onT[0])

                brsrc = [(uT, r_uT), (scT, r_scT)]
                for m in range(8):
                    tr, rr = WR.next(wbr[l, m])
                    for br in range(4):
                        bp = rot.get()
                        if br < 2:
                            src, rs = brsrc[br]
                            for ch in range(2):
                                PE(lambda e, br=br, ch=ch, bp=bp, src=src, tr=tr: e.matmul(
                                    bp.ap[:, :], lhsT=tr[:, 2 * br + ch, :], rhs=src[:, ch, :],
                                    start=(ch == 0), stop=(ch == 1)), [rr, rs], [bp.res])
                        else:
                            a = br - 2
                            for hh in range(4):
                                PE(lambda e, a=a, hh=hh, bp=bp, tr=tr: e.matmul(
                                    bp.ap[:, :], lhsT=tr[0:64, 4 + 4 * a + hh, :],
                                    rhs=onT[a][:, hh, :], start=(hh == 0), stop=(hh == 3)),
                                   [rr, r_onT[a]], [bp.res])
                        bgt = proj128(l, 21 + 4 * m + br)
                        i = sgi[0] % 2
                        sgi[0] += 1
                        ACT(lambda e, i=i, bgt=bgt: e.activation(out=sg[i][:], in_=bgt.ap[:, :], func=AF.Sigmoid),
                            [bgt.res], [r_sg[i]])
                        if br == 0:
                            DVE(lambda e, i=i, bp=bp: e.tensor_tensor(out=macc[:], in0=sg[i][:], in1=bp.ap[:, :],
                                                                      op=ALU.mult), [r_sg[i], bp.res], [r_macc])
                        else:
                            k2 = br % 2
                            DVE(lambda e, i=i, bp=bp, k2=k2: e.tensor_tensor(out=mtmp[k2][:], in0=sg[i][:],
                                                                             in1=bp.ap[:, :], op=ALU.mult),
                                [r_sg[i], bp.res], [r_mtmp[k2]])
                            if br < 3:
                                DVE(lambda e, k2=k2: e.tensor_tensor(out=macc[:], in0=macc[:], in1=mtmp[k2][:],
                                                                     op=ALU.add), [r_macc, r_mtmp[k2]], [r_macc])
                            else:
                                DVE(lambda e, k2=k2, m=m: e.tensor_tensor(out=mT[:, m, :], in0=macc[:],
                                                                          in1=mtmp[k2][:], op=ALU.add),
                                    [r_macc, r_mtmp[k2]], [r_mT])
                resid_proj(mT, lambda m: r_mT, 8, lambda n, m: wo2[l, n, m], 1.0)

            for l in range(L):
                for j in range(NT):
                    t0 = j * TT
                    src = xin if l == 0 else Xs
                    if j == 0:
                        S.dma("sp", x[:], src[t0:t0 + TT, :].rearrange("(s p) d -> p s d", p=128),
                              reads=([RX[j]] if l > 0 else []), writes=r_x)
                    ffn(l, 0)
                    S.dma("sp", Xs[t0:t0 + TT, :].rearrange("(s p) d -> p s d", p=128), x[:], reads=r_x,
                          writes=[RX[j]])
                    front(l, j)
                exchange(l)
                back_prep(l, 0)
                for j in range(NT):
                    t0 = j * TT
                    S.dma("sp", x[:], Xs[t0:t0 + TT, :].rearrange("(s p) d -> p s d", p=128), reads=[RX[j]],
                          writes=r_x)
                    back(l, j)
                    ffn(l, 1, mid=((lambda l=l, j=j: back_prep(l, j + 1)) if j + 1 < NT else None))
                    dst = Xs if l < L - 1 else yout
                    S.dma("sp", dst[t0:t0 + TT, :].rearrange("(s p) d -> p s d", p=128), x[:], reads=r_x,
                          writes=[RX[j]])
            return [WA.rec, WD.rec, WR.rec]

        plans = program(Sched(nc, dry=True), [None, None, None])
        S = Sched(nc)
        program(S, plans)
        S.emit()
    return nc


def _t5_bucket(dist):
    max_exact = 16
    d = np.maximum(dist, 1).astype(np.float32)
    large = max_exact + (np.log(d / np.float32(max_exact)) / np.float32(np.log(128 / max_exact))
                         * np.float32(32 - max_exact)).astype(np.int32)
    large = np.minimum(large, 31)
    return np.where(dist < max_exact, dist, large)


def host_prep(inp):
    f = lambda a: np.ascontiguousarray(np.asarray(a, dtype=np.float32))

    def wtile(w, col0, n):
        out = np.zeros((128, 8, 128), np.float32)
        out[:, :, :n] = w[:, col0:col0 + n].reshape(8, 128, n).transpose(1, 0, 2)
        return out

    shared = {}
    gates = [np.asarray(inp["ffn1_w_gate"]), np.asarray(inp["ffn2_w_gate"])]
    ups = [np.asarray(inp["ffn1_w_up"]), np.asarray(inp["ffn2_w_up"])]
    downs = [np.asarray(inp["ffn1_w_down"]), np.asarray(inp["ffn2_w_down"])]

    def ftile(w):
        return w.reshape(8, 128, NF, 128).transpose(2, 1, 0, 3)

    def dtile(w):
        return w.reshape(NF, 128, 2, 512).transpose(2, 0, 1, 3)

    shared["wg"] = f(np.stack([np.stack([ftile(gates[k][l]) for k in range(2)]) for l in range(L)]))
    shared["wu"] = f(np.stack([np.stack([ftile(ups[k][l]) for k in range(2)]) for l in range(L)]))
    shared["wd2"] = f(np.stack([np.stack([dtile(downs[k][l]) for k in range(2)]) for l in range(L)]))
    w_in = np.asarray(inp["w_in"])
    shared["win"] = f(np.stack([np.stack([wtile(w_in[l], c0, n) for (c0, n) in WIN_TILES]) for l in range(L)]))
    cwo, swo = np.asarray(inp["conf_w_out"]), np.asarray(inp["sc_w_out"])
    awo, fwo = np.asarray(inp["swa_w_o"]), np.asarray(inp["fox_w_o"])
    wbr = np.zeros((L, 8, 128, 12, 128), np.float32)
    for l in range(L):
        for m in range(8):
            cs = slice(m * 128, (m + 1) * 128)
            for ch in range(2):
                wbr[l, m, :, ch, :] = cwo[l][ch * 128:(ch + 1) * 128, cs]
                wbr[l, m, :, 2 + ch, :] = swo[l][ch * 128:(ch + 1) * 128, cs]
            for hh in range(4):
                wbr[l, m, 0:64, 4 + hh, :] = awo[l][hh * 64:(hh + 1) * 64, cs]
                wbr[l, m, 0:64, 8 + hh, :] = fwo[l][hh * 64:(hh + 1) * 64, cs]
    shared["wbr"] = wbr
    shared["wo2"] = f(np.stack([np.asarray(inp["w_out"])[l].reshape(8, 128, 2, 512).transpose(2, 0, 1, 3)
                                for l in range(L)]))
    gn = [np.asarray(inp["ffn1_norm"]), np.asarray(inp["mix_norm"]), np.asarray(inp["ffn2_norm"])]
    shared["gains"] = f(np.stack([np.stack([np.broadcast_to(gn[i][l][None, :], (128, D)) for i in range(3)])
                                  for l in range(L)]))
    shared["cdw"] = f(np.stack([np.asarray(inp["conf_dw"])[l].T.reshape(2, 128, 31).transpose(1, 0, 2)
                                for l in range(L)]))
    cv = [np.asarray(inp["conf_dw_b"]), np.asarray(inp["conf_ln_g"]), np.asarray(inp["conf_ln_b"])]
    shared["cvec"] = f(np.stack([np.stack([cv[i][l].reshape(2, 128).T for i in range(3)], axis=-1)
                                 for l in range(L)]))
    shared["scw"] = f(np.stack([np.asarray(inp["sc_conv"])[l].T.reshape(2, 128, 3).transpose(1, 0, 2)
                                for l in range(L)]))
    qk = [np.asarray(inp["swa_q_norm"]), np.asarray(inp["swa_k_norm"]),
          np.asarray(inp["fox_q_norm"]), np.asarray(inp["fox_k_norm"])]
    shared["qkg"] = f(np.stack([np.stack([qk[i][l] for i in range(4)], axis=-1) for l in range(L)]))
    shared["bfg"] = f(np.asarray(inp["b_forget"]).reshape(L, 4, 1))
    sk = np.zeros((L, 65, 4), np.float32)
    sk[:, 64, :] = np.asarray(inp["swa_sink"])
    shared["sink"] = sk
    rb = np.asarray(inp["rel_bias"], dtype=np.float32)
    i = np.arange(128)[:, None]
    jq = np.arange(128)[None, :]
    bmt = np.full((128, 4, 256), NEG, np.float32)
    d_prev = jq + 128 - i
    ok_prev = d_prev <= 127
    d_cur = jq - i
    ok_cur = d_cur >= 0
    bk_prev = _t5_bucket(np.clip(d_prev, 0, 127))
    bk_cur = _t5_bucket(np.clip(d_cur, 0, 127))
    for hq in range(4):
        bmt[:, hq, 0:128] = np.where(ok_prev, rb[bk_prev, hq], np.float32(NEG))
        bmt[:, hq, 128:256] = np.where(ok_cur, rb[bk_cur, hq], np.float32(NEG))
    shared["bm"] = bmt
    shared["identin"] = np.eye(128, dtype=np.float32)
    return shared


_NC_CACHE = {}


def kernel(**inputs):
    x = np.asarray(inputs["x"], dtype=np.float32)
    B, T, _ = x.shape
    ntok = T // 2
    shared = host_prep(inputs)
    if ntok not in _NC_CACHE:
        _NC_CACHE[ntok] = build(ntok)
    nc = _NC_CACHE[ntok]
    in_maps = []
    for c in range(N_CORES):
        b, half = c // 2, c % 2
        m = dict(shared)
        m["xin"] = np.ascontiguousarray(x[b, half * ntok:(half + 1) * ntok])
        m["flagin"] = np.full((128, 1), float(half), np.float32)
        m["maskin"] = np.full((1, 4 * TT), 0.0 if half else NEG, np.float32)
        in_maps.append(m)
    res = run_bass_kernel_spmd(nc, in_maps, core_ids=list(range(N_CORES)))
    out = np.empty((B, T, D), np.float32)
    for c in range(N_CORES):
        b, half = c // 2, c % 2
        out[b, half * ntok:(half + 1) * ntok] = np.asarray(res.results[c]["yout"], dtype=np.float32)
    return out
```

```python
from contextlib import ExitStack

import numpy as np

import concourse.bass as bass
import concourse.mybir as mybir
from concourse.bass_utils import run_bass_kernel_spmd

F32 = mybir.dt.float32
BF16 = mybir.dt.bfloat16
AF = mybir.ActivationFunctionType
ALU = mybir.AluOpType

D = 1024
DFF = 2816
NF = DFF // 128
L = 2
TT = 512
NS = TT // 128
EPS = 1e-6
NEG = -30000.0
N_CORES = 8

WIN_TILES = ([(0, 128), (256, 128), (128, 128), (384, 128), (512, 128), (640, 128),
              (768, 128), (1024, 128), (896, 128), (1152, 128)]
             + [(1280, 128), (1408, 128), (1536, 128), (1664, 128)]
             + [(1792, 128), (1920, 128), (2048, 128), (2176, 128), (2304, 128), (2432, 128), (2560, 4)]
             + [(2564 + i * 1024 + m * 128, 128) for m in range(8) for i in range(4)])
NWT = len(WIN_TILES)


class Res:
    __slots__ = ("name", "lw", "rd")

    def __init__(self, name=""):
        self.name = name
        self.lw = None
        self.rd = {}


class Op:
    __slots__ = ("eng", "fn", "deps", "sig", "signo", "dma", "dj", "idx", "cc")


ENGS = ["pe", "act", "dve", "pool", "sp"]
NSC = 4
NSD = 8


def _dkey(o):
    return ("d", id(o)) if o.dma else ("c", o.eng)


def _dput(d, o):
    k = _dkey(o)
    cur = d.get(k)
    if cur is None or o.dma or o.idx > cur.idx:
        d[k] = o


class Sched:
    def __init__(self, nc, dry=False):
        self.nc = nc
        self.dry = dry
        self.ops = {e: [] for e in ENGS}

    def op(self, eng, fn, reads=(), writes=(), dma=False, cc=None):
        if self.dry:
            return None
        o = Op()
        o.eng, o.fn, o.dma, o.sig, o.cc = eng, fn, dma, False, cc
        o.idx = len(self.ops[eng])
        deps = {}

        def add(d, raw):
            if d is None:
                return
            if d.dma or dma or d.eng != eng or (raw and eng != "pe"):
                _dput(deps, d)

        for r in reads:
            add(r.lw, True)
        for w in writes:
            add(w.lw, False)
            for x in w.rd.values():
                add(x, False)
        o.deps = list(deps.values())
        for d in o.deps:
            d.sig = True
        for r in reads:
            _dput(r.rd, o)
        for w in writes:
            w.lw = o
            w.rd = {}
        self.ops[eng].append(o)
        return o

    def dma(self, eng, out, in_, reads=(), writes=(), **kw):
        return self.op(eng, lambda e: e.dma_start(out=out, in_=in_, **kw), reads=reads, writes=writes, dma=True)

    def emit(self):
        nc = self.nc
        for e in ENGS:
            c = 0
            j = 0
            for o in self.ops[e]:
                if o.cc is not None:
                    continue
                if o.dma:
                    o.dj = j
                    j += 1
                elif o.sig:
                    c += 1
                    o.signo = c
        with ExitStack() as st:
            csem = {e: [st.enter_context(nc.semaphore(f"c_{e}_{i}")) for i in range(NSC)]
                    for e in ENGS if e != "sp"}
            dsem = {e: [st.enter_context(nc.semaphore(f"d_{e}_{i}")) for i in range(NSD)]
                    for e in ENGS if e != "pe"}
            ncc = sum(1 for e in ENGS for o in self.ops[e] if o.cc is not None)
            ccsem = [st.enter_context(nc.semaphore(f"cc_{i}")) for i in range(ncc)]
            block = st.enter_context(nc.Block())

            def body(eng, e):
                cw = {}
                dw = {}

                def wait_dma(q, dj):
                    slot, val = dj % NSD, 16 * (dj // NSD + 1)
                    if dw.get((q, slot), 0) < val:
                        eng.wait_ge(dsem[q][slot], val)
                        dw[(q, slot)] = val

                for o in self.ops[e]:
                    for d in o.deps:
                        if d.cc is not None:
                            if dw.get(("cc", d.cc), 0) < 1:
                                eng.wait_ge(ccsem[d.cc], 1)
                                dw[("cc", d.cc)] = 1
                        elif d.dma:
                            wait_dma(d.eng, d.dj)
                        elif cw.get(d.eng, 0) < d.signo:
                            s = d.signo - 1
                            eng.wait_ge(csem[d.eng][s % NSC], s // NSC + 1)
                            cw[d.eng] = d.signo
                    if o.cc is not None:
                        o.fn(eng).then_inc(ccsem[o.cc])
                        continue
                    if o.dma and o.dj >= NSD:
                        wait_dma(e, o.dj - NSD)
                    ins = o.fn(eng)
                    if o.dma:
                        ins.then_inc(dsem[e][o.dj % NSD], 16)
                    elif o.sig:
                        ins.then_inc(csem[e][(o.signo - 1) % NSC], 1)
                n = sum(1 for o in self.ops[e] if o.dma and o.cc is None)
                for dj in range(max(0, n - NSD), n):
                    wait_dma(e, dj)

            names = {"pe": "tensor", "act": "scalar", "dve": "vector", "pool": "gpsimd", "sp": "sync"}
            for e in ENGS:
                getattr(block, names[e])(lambda eng, e=e: body(eng, e))


class Stream:
    def __init__(self, S, slots, res, plan):
        self.S, self.slots, self.res, self.plan = S, slots, res, plan
        self.rec = []
        self.pos = 0
        self.issued = 0

    def next(self, src):
        R = len(self.slots)
        if self.plan is None:
            self.rec.append(src)
            return self.slots[0], self.res[0]
        while self.issued < min(len(self.plan), self.pos + R):
            k = self.issued % R
            self.S.dma("pool", self.slots[k][:], self.plan[self.issued], writes=[self.res[k]])
            self.issued += 1
        k = self.pos % R
        self.pos += 1
        return self.slots[k], self.res[k]


class Bank:
    def __init__(self, ap, name):
        self.ap = ap
        self.res = Res(name)


class Rot:
    def __init__(self, items):
        self.items = items
        self.i = 0

    def get(self):
        b = self.items[self.i % len(self.items)]
        self.i += 1
        return b


def build(NTOK, stages=3):
    NT = NTOK // TT
    nc = bass.Bass("TRN2", target_bir_lowering=False)

    def din(name, shape, dt=F32):
        return nc.dram_tensor(name, list(shape), dt, kind="ExternalInput").ap()

    xin = din("xin", [NTOK, D])
    wg = din("wg", [L, 2, NF, 128, 8, 128])
    wu = din("wu", [L, 2, NF, 128, 8, 128])
    wd2 = din("wd2", [L, 2, 2, NF, 128, 512])
    win = din("win", [L, NWT, 128, 8, 128])
    wbr = din("wbr", [L, 8, 128, 12, 128])
    wo2 = din("wo2", [L, 2, 8, 128, 512])
    gains = din("gains", [L, 3, 128, D])
    cdw = din("cdw", [L, 128, 2, 31])
    cvec = din("cvec", [L, 128, 2, 3])
    scw = din("scw", [L, 128, 2, 3])
    qkg = din("qkg", [L, 64, 4])
    bfg = din("bfg", [L, 4, 1])
    sink = din("sink", [L, 65, 4])
    bm = din("bm", [128, 4, 256])
    identin = din("identin", [128, 128])
    flagin = din("flagin", [128, 1])
    maskin = din("maskin", [1, 4 * TT])
    yout = nc.dram_tensor("yout", [NTOK, D], F32, kind="ExternalOutput").ap()

    def dscr(name, shape, dt):
        return nc.dram_tensor(name, list(shape), dt, kind="Internal")

    Xs = dscr("Xs", [NTOK, D], F32).ap()
    FU = dscr("FU", [NT, 128, 2, TT], F32).ap()
    FSX = dscr("FSX", [NT, 128, 2, TT], F32).ap()
    FSB = dscr("FSB", [NT, 128, 2, TT], F32).ap()
    FQS = dscr("FQS", [NT, 64, 4, TT], BF16).ap()
    FKS = dscr("FKS", [NT, 64, 2, TT], BF16).ap()
    FVS = dscr("FVS", [NT, 128, NS, 2 * 65], BF16).ap()
    FQF = dscr("FQF", [NT, 64, 4, TT], BF16).ap()
    FG = dscr("FG", [NT, 4, TT], F32).ap()
    QFs = dscr("QFs", [2, 3, 4, TT], BF16).ap()
    NVC = max(1, NTOK // 1024)
    VC = NTOK // NVC
    KFh = [[dscr(f"KF{l}_{hh}", [70, NTOK], BF16) for hh in range(4)] for l in range(L)]
    VFh = [[dscr(f"VF{l}_{q}", [VC, 4 * 65], BF16) for q in range(NVC)] for l in range(L)]
    HFh = [dscr(f"HF{l}", [128, 128], F32) for l in range(L)]
    HBh = [dscr(f"HB{l}", [128, 512], BF16) for l in range(L)]
    GKFh = [[dscr(f"GKF{l}_{hh}", [2 * 70, NTOK], BF16) for hh in range(4)] for l in range(L)]
    GVFh = [[dscr(f"GVF{l}_{q}", [2 * VC, 4 * 65], BF16) for q in range(NVC)] for l in range(L)]
    GHFh = [dscr(f"GHF{l}", [2 * 128, 128], F32) for l in range(L)]
    GHBh = [dscr(f"GHB{l}", [2 * 128, 512], BF16) for l in range(L)]
    KF = [[t.ap() for t in row] for row in KFh]
    VF = [[t.ap() for t in row] for row in VFh]
    HF = [t.ap() for t in HFh]
    HB = [t.ap() for t in HBh]
    GKF = [[t.ap().rearrange("(g r) t -> g r t", g=2) for t in row] for row in GKFh]
    GVF = [[t.ap().rearrange("(g n) e -> g n e", g=2) for t in row] for row in GVFh]
    GHF = [t.ap().rearrange("(g p) e -> g p e", g=2) for t in GHFh]
    GHB = [t.ap().rearrange("(g p) e -> g p e", g=2) for t in GHBh]

    with ExitStack() as st:
        def sb(name, shape, dt):
            return st.enter_context(nc.sbuf_tensor(name, list(shape), dt))

        x = sb("x", [128, NS, D], F32)
        r_x = [Res(f"x{s}") for s in range(NS)]
        h = sb("h", [128, NS, D], BF16)
        r_h = [Res(f"h{s}") for s in range(NS)]
        hT = sb("hT", [128, 8, TT], BF16)
        r_hT = Res("hT")
        act = sb("act", [128, NF, TT], BF16)
        r_act = [Res(f"act{f}") for f in range(NF)]
        RA, RD, RR = 6, 8, 2
        wA = [sb(f"wA{i}", [128, 8, 128], BF16) for i in range(RA)]
        rA = [Res(f"wA{i}") for i in range(RA)]
        wD = [sb(f"wD{i}", [128, 512], BF16) for i in range(RD)]
        rD = [Res(f"wD{i}") for i in range(RD)]
        wR = [sb(f"wR{i}", [128, 12, 128], BF16) for i in range(RR)]
        rR = [Res(f"wR{i}") for i in range(RR)]
        gb = sb("gb", [128, D], F32)
        r_gb = Res("gb")
        ss = sb("ss", [128, 8], F32)
        r_ssq = [Res(f"ss{i}") for i in range(NS)]
        sg = [sb(f"sg{i}", [128, TT], F32) for i in range(2)]
        r_sg = [Res(f"sg{i}") for i in range(2)]
        ident = sb("ident", [128, 128], BF16)
        r_ident = Res("ident")
        ones = sb("ones", [128, 128], F32)
        r_ones = Res("ones")
        o64 = sb("o64", [64, 64], F32)
        r_o64 = Res("o64")
        uext = sb("uext", [128, 2, 30 + TT], F32)
        r_uext = Res("uext")
        sxext = sb("sxext", [128, 2, 2 + TT], F32)
        r_sxext = Res("sxext")
        sbt = sb("sbt", [128, 2, TT], F32)
        r_sbt = Res("sbt")
        cacc = sb("cacc", [128, 2, TT], F32)
        r_cacc = [Res("cacc0"), Res("cacc1")]
        csq = sb("csq", [128, 2, TT], F32)
        r_csq = Res("csq")
        lnm = sb("lnm", [128, TT], F32)
        r_lnm = Res("lnm")
        lnv = sb("lnv", [128, TT], F32)
        r_lnv = Res("lnv")
        uT = sb("uT", [128, 2, TT], BF16)
        r_uT = Res("uT")
        scT = sb("scT", [128, 2, TT], BF16)
        r_scT = Res("scT")
        cdw_t = sb("cdw_t", [128, 2, 31], F32)
        cvec_t = sb("cvec_t", [128, 2, 3], F32)
        scw_t = sb("scw_t", [128, 2, 3], F32)
        qkg_t = sb("qkg_t", [64, 4], F32)
        bfg_t = sb("bfg_t", [4, 1], F32)
        sink_t = sb("sink_t", [65, 4], F32)
        r_small = Res("small")
        bm_t = sb("bm_t", [128, 4, 256], F32)
        r_bm = Res("bm")
        qs = sb("qs", [64, 4, TT], BF16)
        r_qs = Res("qs")
        ksx = sb("ksx", [64, 2, 128 + TT], BF16)
        r_ksx = Res("ksx")
        vsx = sb("vsx", [128, NS + 1, 2, 65], BF16)
        r_vsx = Res("vsx")
        pT = [sb(f"pT{i}", [128, TT], BF16) for i in range(4)]
        r_pT = [Res(f"pT{i}") for i in range(4)]
        oa4 = sb("oa4", [65, 4, TT], F32)
        oa = oa4[:, 0, :]
        r_oa = Res("oa")
        rden = sb("rden", [65, TT], F32)
        r_rden = Res("rden")
        onT = [sb(f"onT{i}", [64, 4, TT], BF16) for i in range(2)]
        r_onT = [Res("onT0"), Res("onT1")]
        qf = sb("qf", [128, 4, TT], BF16)
        r_qf = Res("qf")
        qfx = sb("qfx", [128, 4, TT], BF16)
        r_qfx = Res("qfx")
        flag_t = sb("flag_t", [128, 1], F32)
        r_flag = Res("flag")
        tot_t = sb("tot_t", [4, 1], F32)
        r_tot = Res("tot")
        kf = sb("kf", [64, 4, TT], BF16)
        r_kf = Res("kf")
        vf = sb("vf", [128, NS, 4, 65], BF16)
        r_vf = Res("vf")
        kblk = [sb(f"kblk{i}", [128, 4, TT], BF16) for i in range(2)]
        r_kblk = [Res(f"kblk{i}") for i in range(2)]
        vblk = [sb(f"vblk{i}", [128, NS, 324], BF16) for i in range(2)]
        r_vblk = [Res(f"vblk{i}") for i in range(2)]
        fe = sb("fe", [4, TT], F32)
        r_fe = Res("fe")
        fG = sb("fG", [4, TT], F32)
        r_fG = Res("fG")
        fones = sb("fones", [4, TT], F32)
        r_fones = Res("fones")
        gsp = sb("gsp", [4, 3, TT], BF16)
        r_gs = Res("gs")
        monesb = sb("monesb", [4, 3, TT], BF16)
        r_monesb = Res("monesb")
        fcar = [sb(f"fcar{l}", [4, 1], F32) for l in range(L)]
        r_fcar = [Res(f"fcar{l}") for l in range(L)]
        macc = sb("macc", [128, TT], F32)
        r_macc = Res("macc")
        mtmp = [sb(f"mtmp{i}", [128, TT], F32) for i in range(2)]
        r_mtmp = [Res(f"mtmp{i}") for i in range(2)]
        mT = sb("mT", [128, 8, TT], BF16)
        r_mT = Res("mT")
        banks = [Bank(st.enter_context(nc.psum_tensor(f"ps{i}", [128, 512], F32)), f"ps{i}") for i in range(7)]
        ptb = Bank(st.enter_context(nc.psum_tensor("ptb", [128, 1024], BF16)), "ptb")
        ptb2 = Bank(banks[6].ap[:, :].bitcast(BF16), "ptb2")
        ptb2.res = banks[6].res
        ptbs = [ptb, ptb2]

        def program(S, plans):
            WA = Stream(S, wA, rA, plans[0])
            WD = Stream(S, wD, rD, plans[1])
            WR = Stream(S, wR, rR, plans[2])
            rot = Rot(banks)
            rot_hi = Rot(banks[4:7])
            RKF = [[Res(f"KF{l}_{j}") for j in range(NT)] for l in range(L)]
            RX = [Res(f"X{j}") for j in range(NT)]
            RF = [Res(f"F{j}") for j in range(NT)]
            r_HAL = [Res(f"HAL{l}") for l in range(L)]
            r_G = [Res(f"G{l}") for l in range(L)]
            r_QFs = Res("QFs")
            ccn = [0]
            gb_cur = [None]
            preblk = {}
            kbi = [0]
            pti = [0]
            sgi = [0]

            def ACT(fn, reads, writes):
                S.op("act", fn, reads, writes)

            def DVE(fn, reads, writes):
                S.op("dve", fn, reads, writes)

            def PE(fn, reads, writes):
                S.op("pe", fn, reads, writes)

            def POOL(fn, reads, writes):
                S.op("pool", fn, reads, writes)

            S.dma("pool", ident[:], identin, writes=[r_ident])
            DVE(lambda e: e.memset(ones[:], 1.0), [], [r_ones])
            DVE(lambda e: e.memset(o64[:], 1.0 / 64.0), [], [r_o64])
            DVE(lambda e: e.memset(fones[:], 1.0), [], [r_fones])
            DVE(lambda e: e.memset(monesb[:], -1.0), [], [r_monesb])
            DVE(lambda e: e.memset(qf[:], 0.0), [], [r_qf])
            DVE(lambda e: e.memset(qf[64:70, :, :], 1.0), [], [r_qf])
            DVE(lambda e: e.memset(qfx[:], 0.0), [], [r_qfx])
            DVE(lambda e: e.memset(qfx[64:70, :, :], 1.0), [], [r_qfx])
            DVE(lambda e: e.memset(qfx[64:71, :, :], 1.0), [], [r_qfx])
            for i in range(2):
                DVE(lambda e, i=i: e.memset(kblk[i][:], 0.0), [], [r_kblk[i]])
                DVE(lambda e, i=i: e.memset(vblk[i][:], 0.0), [], [r_vblk[i]])
                S.dma("pool", kblk[i][70:71, :, :], maskin.rearrange("o (h t) -> o h t", h=4), writes=[r_kblk[i]])
            S.dma("sp", flag_t[:], flagin, writes=[r_flag])
            S.dma("sp", bm_t[:], bm, writes=[r_bm])
            DVE(lambda e: e.memset(vsx[:], 1.0), [], [r_vsx])
            DVE(lambda e: e.memset(vf[:], 1.0), [], [r_vf])
            for l in range(L):
                DVE(lambda e, l=l: e.memset(fcar[l][:], 0.0), [], [r_fcar[l]])

            def rmsnorm_hT(gkey, nxt=None):
                if gb_cur[0] != gkey:
                    S.dma("sp", gb[:], gains[gkey[0], gkey[1]], writes=[r_gb])
                    gb_cur[0] = gkey

                def st1(s):
                    ACT(lambda e: e.activation(out=h[:, s, :], in_=x[:, s, :], func=AF.Square,
                                               accum_out=ss[:, s:s + 1]), [r_x[s]], [r_h[s], r_ssq[s]])
                    ACT(lambda e: e.activation(out=ss[:, 4 + s:5 + s], in_=ss[:, s:s + 1], func=AF.Ln,
                                               scale=1.0 / D, bias=EPS), [r_ssq[s]], [r_ssq[s]])
                    ACT(lambda e: e.activation(out=ss[:, 4 + s:5 + s], in_=ss[:, 4 + s:5 + s], func=AF.Exp,
                                               scale=-0.5), [r_ssq[s]], [r_ssq[s]])

                def st2(s):
                    DVE(lambda e: e.scalar_tensor_tensor(out=h[:, s, :], in0=x[:, s, :],
                                                         scalar=ss[:, 4 + s:5 + s], in1=gb[:],
                                                         op0=ALU.mult, op1=ALU.mult),
                        [r_x[s], r_ssq[s], r_gb], [r_h[s]])

                def st3(s):
                    pb = ptbs[s % 2]
                    for c in range(8):
                        PE(lambda e, c=c, pb=pb: e.transpose(out=pb.ap[:, c * 128:(c + 1) * 128],
                                                             in_=h[:, s, c * 128:(c + 1) * 128],
                                                             identity=ident[:]),
                           [r_h[s], r_ident], [pb.res])

                def st4(s):
                    pb = ptbs[s % 2]
                    fn = lambda e: e.copy(out=hT[:, :, s * 128:(s + 1) * 128],
                                          in_=pb.ap[:, :].rearrange("p (c t) -> p c t", c=8))
                    fn2 = lambda e: e.tensor_copy(out=hT[:, :, s * 128:(s + 1) * 128],
                                                  in_=pb.ap[:, :].rearrange("p (c t) -> p c t", c=8))
                    if s % 2 == 0:
                        ACT(fn, [pb.res], [r_hT])
                    else:
                        DVE(fn2, [pb.res], [r_hT])

                stages = [st1, st2, st3, st4]
                for step in range(NS + 3):
                    for k, st_fn in enumerate(stages):
                        sidx = step - k
                        if 0 <= sidx < NS:
                            st_fn(sidx)
                    if step == NS and nxt is not None and nxt[0] < L:
                        S.dma("sp", gb[:], gains[nxt[0], nxt[1]], writes=[r_gb])
                        gb_cur[0] = nxt

            def resid_proj(src, src_res_of, nk, tile_src, scale):
                for n in range(2):
                    accs = [rot.get() for _ in range(NS)]
                    for kc in range(nk):
                        t, r = WD.next(tile_src(n, kc))
                        for s in range(NS):
                            PE(lambda e, s=s, kc=kc, t=t, a=accs[s]: e.matmul(
                                a.ap[:, :], lhsT=src[:, kc, s * 128:(s + 1) * 128], rhs=t[:],
                                start=(kc == 0), stop=(kc == nk - 1)), [src_res_of(kc), r], [accs[s].res])
                    for s in range(NS):
                        DVE(lambda e, s=s, n=n, a=accs[s]: e.scalar_tensor_tensor(
                            out=x[:, s, n * 512:(n + 1) * 512], in0=a.ap[:, :], scalar=scale,
                            in1=x[:, s, n * 512:(n + 1) * 512], op0=ALU.mult, op1=ALU.add),
                            [accs[s].res, r_x[s]], [r_x[s]])

            def ffn(l, k, mid=None, nxt=None):
                rmsnorm_hT((l, 2 * k), nxt=nxt)
                for f in range(NF):
                    tg, rg = WA.next(wg[l, k, f])
                    bg = rot.get()
                    for c in range(8):
                        PE(lambda e, c=c, tg=tg, bg=bg: e.matmul(bg.ap[:, :], lhsT=tg[:, c, :], rhs=hT[:, c, :],
                                                                 start=(c == 0), stop=(c == 7)),
                           [rg, r_hT], [bg.res])
                    tu, ru = WA.next(wu[l, k, f])
                    bu = rot.get()
                    for c in range(8):
                        PE(lambda e, c=c, tu=tu, bu=bu: e.matmul(bu.ap[:, :], lhsT=tu[:, c, :], rhs=hT[:, c, :],
                                                                 start=(c == 0), stop=(c == 7)),
                           [ru, r_hT], [bu.res])
                    i = sgi[0] % 2
                    sgi[0] += 1
                    ACT(lambda e, i=i, bg=bg: e.activation(out=sg[i][:], in_=bg.ap[:, :], func=AF.Silu),
                        [bg.res], [r_sg[i]])
                    DVE(lambda e, i=i, f=f, bu=bu: e.tensor_tensor(out=act[:, f, :], in0=sg[i][:], in1=bu.ap[:, :],
                                                                   op=ALU.mult),
                        [r_sg[i], bu.res], [r_act[f]])
                if mid is not None:
                    mid()
                resid_proj(act, lambda f: r_act[f], NF, lambda n, f: wd2[l, k, n, f], 0.5)

            def proj128(l, wi):
                t, r = WA.next(win[l, wi])
                b = rot.get()
                for c in range(8):
                    PE(lambda e, c=c, t=t, b=b: e.matmul(b.ap[:, :], lhsT=t[:, c, :], rhs=hT[:, c, :],
                                                         start=(c == 0), stop=(c == 7)), [r, r_hT], [b.res])
                return b

            def head_norm(src_ap, src_res, gcol, out_ap, out_res):
                ACT(lambda e: e.activation(out=lnm[0:64, :], in_=src_ap, func=AF.Square), [src_res], [r_lnm])
                b = rot.get()
                PE(lambda e, b=b: e.matmul(b.ap[0:64, :], lhsT=o64[:], rhs=lnm[0:64, :], start=True, stop=True),
                   [r_lnm, r_o64], [b.res])
                ACT(lambda e, b=b: e.activation(out=lnv[0:64, :], in_=b.ap[0:64, :], func=AF.Ln, bias=EPS),
                    [b.res], [r_lnv])
                ACT(lambda e: e.activation(out=lnv[0:64, :], in_=lnv[0:64, :], func=AF.Exp, scale=-0.5),
                    [r_lnv], [r_lnv])
                DVE(lambda e: e.scalar_tensor_tensor(out=out_ap, in0=src_ap, scalar=qkg_t[:, gcol:gcol + 1],
                                                     in1=lnv[0:64, :], op0=ALU.mult, op1=ALU.mult),
                    [src_res, r_lnv, r_small], [out_res])

            def heads_tile(l, wi, specs):
                t, r = WA.next(win[l, wi])
                for hh in range(2):
                    b = rot.get()
                    for c in range(8):
                        PE(lambda e, c=c, t=t, b=b, hh=hh: e.matmul(
                            b.ap[0:64, :], lhsT=t[:, c, hh * 64:(hh + 1) * 64], rhs=hT[:, c, :],
                            start=(c == 0), stop=(c == 7)), [r, r_hT], [b.res])
                    gcol, out_ap, out_res = specs[hh]
                    head_norm(b.ap[0:64, :], b.res, gcol, out_ap, out_res)

            def v_tile(l, wi, dst, dst_res, chunk_off, h0):
                t, r = WA.next(win[l, wi])
                for s in range(NS):
                    b = rot.get()
                    for c in range(8):
                        PE(lambda e, c=c, t=t, b=b, s=s: e.matmul(
                            b.ap[:, 0:128], lhsT=hT[:, c, s * 128:(s + 1) * 128], rhs=t[:, c, :],
                            start=(c == 0), stop=(c == 7)), [r, r_hT], [b.res])
                    ACT(lambda e, b=b, s=s: e.copy(out=dst[:, chunk_off + s, h0:h0 + 2, 0:64],
                                                   in_=b.ap[:, 0:128].rearrange("p (h d) -> p h d", h=2)),
                        [b.res], [dst_res])

            def attn_norm(acc, extra_den, out_ap, out_res):
                ACT(lambda e: e.copy(out=oa, in_=acc.ap[0:65, :]), [acc.res], [r_oa])
                if extra_den is not None:
                    DVE(lambda e: e.tensor_scalar(out=oa[64:65, :], in0=oa[64:65, :], scalar1=extra_den,
                                                  scalar2=None, op0=ALU.add), [r_oa, r_small], [r_oa])
                ACT(lambda e: e.activation(out=rden[64:65, :], in_=oa[64:65, :], func=AF.Ln), [r_oa], [r_rden])
                ACT(lambda e: e.activation(out=rden[64:65, :], in_=rden[64:65, :], func=AF.Exp, scale=-1.0),
                    [r_rden], [r_rden])
                b = rot_hi.get()
                PE(lambda e, b=b: e.matmul(b.ap[0:64, :], lhsT=ones[64:65, 0:64], rhs=rden[64:65, :],
                                           start=True, stop=True), [r_rden, r_ones], [b.res])
                DVE(lambda e, b=b: e.tensor_tensor(out=out_ap, in0=oa[0:64, :], in1=b.ap[0:64, :], op=ALU.mult),
                    [r_oa, b.res], [out_res])

            def attn_norm4(accs4, out_t, out_res):
                for hh in range(4):
                    if hh % 2 == 0:
                        ACT(lambda e, hh=hh: e.copy(out=oa4[:, hh, :], in_=accs4[hh].ap[0:65, :]),
                            [accs4[hh].res], [r_oa])
                    else:
                        DVE(lambda e, hh=hh: e.tensor_copy(out=oa4[:, hh, :], in_=accs4[hh].ap[0:65, :]),
                            [accs4[hh].res], [r_oa])
                ACT(lambda e: e.activation(out=oa4[64:65, :, :], in_=oa4[64:65, :, :], func=AF.Ln), [r_oa], [r_oa])
                ACT(lambda e: e.activation(out=oa4[64:65, :, :], in_=oa4[64:65, :, :], func=AF.Exp, scale=-1.0),
                    [r_oa], [r_oa])
                for hh in range(4):
                    b = rot_hi.get()
                    PE(lambda e, b=b, hh=hh: e.matmul(b.ap[0:64, :], lhsT=ones[64:65, 0:64],
                                                      rhs=oa4[64:65, hh, :], start=True, stop=True),
                       [r_oa, r_ones], [b.res])
                    DVE(lambda e, b=b, hh=hh: e.tensor_tensor(out=out_t[:, hh, :], in0=oa4[0:64, hh, :],
                                                              in1=b.ap[0:64, :], op=ALU.mult),
                        [r_oa, b.res], [out_res])

            def load_small(l):
                S.dma("sp", cdw_t[:], cdw[l], writes=[r_small])
                S.dma("sp", cvec_t[:], cvec[l], writes=[r_small])
                S.dma("sp", scw_t[:], scw[l], writes=[r_small])
                S.dma("sp", qkg_t[:], qkg[l], writes=[r_small])
                S.dma("sp", bfg_t[:], bfg[l], writes=[r_small])
                S.dma("sp", sink_t[:], sink[l], writes=[r_small])
                DVE(lambda e: e.tensor_scalar(out=qkg_t[:, 0:1], in0=qkg_t[:, 0:1], scalar1=0.125, scalar2=None,
                                              op0=ALU.mult), [r_small], [r_small])
                DVE(lambda e: e.tensor_scalar(out=qkg_t[:, 2:3], in0=qkg_t[:, 2:3], scalar1=0.125, scalar2=None,
                                              op0=ALU.mult), [r_small], [r_small])
                ACT(lambda e: e.activation(out=sink_t[64:65, :], in_=sink_t[64:65, :], func=AF.Exp),
                    [r_small], [r_small])
                DVE(lambda e: e.tensor_scalar(out=bfg_t[:], in0=bfg_t[:], scalar1=-1.0, scalar2=None,
                                              op0=ALU.mult), [r_small], [r_small])

            def split3(src, src_res):
                DVE(lambda e: e.tensor_copy(out=gsp[:, 0, :], in_=src[:]), [src_res], [r_gs])
                DVE(lambda e: e.tensor_tensor(out=fe[:], in0=src[:], in1=gsp[:, 0, :], op=ALU.subtract),
                    [src_res, r_gs], [r_fe])
                DVE(lambda e: e.tensor_copy(out=gsp[:, 1, :], in_=fe[:]), [r_fe], [r_gs])
                DVE(lambda e: e.tensor_tensor(out=fe[:], in0=fe[:], in1=gsp[:, 1, :], op=ALU.subtract),
                    [r_fe, r_gs], [r_fe])
                DVE(lambda e: e.tensor_copy(out=gsp[:, 2, :], in_=fe[:]), [r_fe], [r_gs])

            def front(l, j):
                t0 = j * TT
                last = (j == NT - 1)
                rmsnorm_hT((l, 1), nxt=((l, 0) if j + 1 < NT else (l, 1)))
                if j + 1 < NT:
                    src = xin if l == 0 else Xs
                    S.dma("act", x[:], src[t0 + TT:t0 + 2 * TT, :].rearrange("(s p) d -> p s d", p=128),
                          reads=([RX[j + 1]] if l > 0 else []), writes=r_x)
                for ch in range(2):
                    ba = proj128(l, 2 * ch)
                    bb = proj128(l, 2 * ch + 1)
                    i = sgi[0] % 2
                    sgi[0] += 1
                    ACT(lambda e, i=i, bb=bb: e.activation(out=sg[i][:], in_=bb.ap[:, :], func=AF.Sigmoid),
                        [bb.res], [r_sg[i]])
                    DVE(lambda e, i=i, ba=ba, ch=ch: e.tensor_tensor(out=uext[:, ch, 30:30 + TT], in0=sg[i][:],
                                                                     in1=ba.ap[:, :], op=ALU.mult),
                        [r_sg[i], ba.res], [r_uext])
                for ch in range(2):
                    b = proj128(l, 4 + ch)
                    ACT(lambda e, b=b, ch=ch: e.copy(out=sbt[:, ch, :], in_=b.ap[:, :]), [b.res], [r_sbt])
                for ch in range(2):
                    bc = proj128(l, 6 + 2 * ch)
                    bx = proj128(l, 7 + 2 * ch)
                    i = sgi[0] % 2
                    sgi[0] += 1
                    ACT(lambda e, i=i, bc=bc: e.copy(out=sg[i][:], in_=bc.ap[:, :]), [bc.res], [r_sg[i]])
                    DVE(lambda e, i=i, bx=bx, ch=ch: e.tensor_tensor(out=sxext[:, ch, 2:2 + TT], in0=sg[i][:],
                                                                     in1=bx.ap[:, :], op=ALU.mult),
                        [r_sg[i], bx.res], [r_sxext])
                S.dma("sp", FU[j], uext[:, :, 30:30 + TT], reads=[r_uext], writes=[RF[j]])
                S.dma("sp", FSX[j], sxext[:, :, 2:2 + TT], reads=[r_sxext], writes=[RF[j]])
                S.dma("sp", FSB[j], sbt[:], reads=[r_sbt], writes=[RF[j]])
                if last:
                    S.dma("sp", HF[l][:, 0:60].rearrange("p (c k) -> p c k", c=2), uext[:, :, TT:TT + 30],
                          reads=[r_uext], writes=[r_HAL[l]])
                    S.dma("sp", HF[l][:, 60:64].rearrange("p (c k) -> p c k", c=2), sxext[:, :, TT:TT + 2],
                          reads=[r_sxext], writes=[r_HAL[l]])
                heads_tile(l, 10, [(0, qs[:, 0, :], r_qs), (0, qs[:, 1, :], r_qs)])
                heads_tile(l, 11, [(0, qs[:, 2, :], r_qs), (0, qs[:, 3, :], r_qs)])
                heads_tile(l, 12, [(1, ksx[:, 0, 128:128 + TT], r_ksx), (1, ksx[:, 1, 128:128 + TT], r_ksx)])
                v_tile(l, 13, vsx, r_vsx, 1, 0)
                S.dma("sp", FQS[j], qs[:], reads=[r_qs], writes=[RF[j]])
                S.dma("sp", FKS[j], ksx[:, :, 128:128 + TT], reads=[r_ksx], writes=[RF[j]])
                S.dma("sp", FVS[j], vsx[:, 1:NS + 1, :, :].rearrange("p s h e -> p s (h e)"), reads=[r_vsx],
                      writes=[RF[j]])
                if last:
                    S.dma("sp", HB[l][0:64, 0:256].rearrange("p (c k) -> p c k", c=2), ksx[:, :, TT:TT + 128],
                          reads=[r_ksx], writes=[r_HAL[l]])
                    S.dma("sp", HB[l][:, 256:386], vsx[:, NS, :, :].rearrange("p h e -> p (h e)"),
                          reads=[r_vsx], writes=[r_HAL[l]])
                heads_tile(l, 14, [(2, qf[0:64, 0, :], r_qf), (2, qf[0:64, 1, :], r_qf)])
                heads_tile(l, 15, [(2, qf[0:64, 2, :], r_qf), (2, qf[0:64, 3, :], r_qf)])
                heads_tile(l, 16, [(3, kf[:, 0, :], r_kf), (3, kf[:, 1, :], r_kf)])
                heads_tile(l, 17, [(3, kf[:, 2, :], r_kf), (3, kf[:, 3, :], r_kf)])
                v_tile(l, 18, vf, r_vf, 0, 0)
                v_tile(l, 19, vf, r_vf, 0, 2)
                t, r = WA.next(win[l, 20])
                bF = rot.get()
                for c in range(8):
                    PE(lambda e, c=c, t=t, bF=bF: e.matmul(bF.ap[0:4, :], lhsT=t[:, c, 0:4], rhs=hT[:, c, :],
                                                           start=(c == 0), stop=(c == 7)), [r, r_hT], [bF.res])
                ACT(lambda e: e.activation(out=fe[:], in_=bF.ap[0:4, :], func=AF.Exp, scale=-1.0,
                                           bias=bfg_t[:, 0:1]), [bF.res, r_small], [r_fe])
                ACT(lambda e: e.activation(out=fe[:], in_=fe[:], func=AF.Ln, bias=1.0), [r_fe], [r_fe])
                if j == 0:
                    DVE(lambda e: e.memset(fcar[l][:], 0.0), [], [r_fcar[l]])
                DVE(lambda e: e.tensor_tensor_scan(out=fG[:], data0=fones[:], data1=fe[:], initial=fcar[l][:, 0:1],
                                                   op0=ALU.mult, op1=ALU.add), [r_fe, r_fones, r_fcar[l]], [r_fG])
                DVE(lambda e: e.tensor_copy(out=fcar[l][:], in_=fG[:, TT - 1:TT]), [r_fG], [r_fcar[l]])
                split3(fG, r_fG)
                S.dma("sp", FQF[j], qf[0:64, :, :], reads=[r_qf], writes=[RF[j]])
                S.dma("sp", FG[j], fG[:], reads=[r_fG], writes=[RF[j]])
                rKF = RKF[l][j]
                for hh in range(4):
                    S.dma("sp", KF[l][hh][0:64, t0:t0 + TT], kf[:, hh, :], reads=[r_kf], writes=[rKF])
                    S.dma("sp", KF[l][hh][64:67, t0:t0 + TT].rearrange("(o a) t -> o a t", o=1),
                          monesb[hh:hh + 1, :, :], reads=[r_monesb], writes=[rKF])
                    S.dma("sp", KF[l][hh][67:70, t0:t0 + TT].rearrange("(o a) t -> o a t", o=1),
                          gsp[hh:hh + 1, :, :], reads=[r_gs], writes=[rKF])
                vq, vo = t0 // VC, t0 % VC
                S.dma("sp", VF[l][vq][vo:vo + TT, :].rearrange("(s p) e -> p s e", p=128),
                      vf[:].rearrange("p s h e -> p s (h e)"), reads=[r_vf], writes=[rKF])
                if last:
                    S.dma("sp", HF[l][0:4, 64:65], fcar[l][:], reads=[r_fcar[l]], writes=[r_HAL[l]],
                          allow_slow_non_contiguous=True)

            def exchange(l):
                srcs = ([(KFh[l][hh], GKFh[l][hh]) for hh in range(4)]
                        + [(VFh[l][q], GVFh[l][q]) for q in range(NVC)]
                        + [(HFh[l], GHFh[l]), (HBh[l], GHBh[l])])
                for (a, g) in srcs:
                    k = ccn[0]
                    ccn[0] += 1
                    S.op("pool", lambda e, a=a, g=g: e.collective_compute(
                        "AllGather", ALU.bypass, replica_groups=[[0, 1], [2, 3], [4, 5], [6, 7]],
                        ins=[a.ap().opt()], outs=[g.ap().opt()]),
                        reads=RKF[l] + [r_HAL[l]], writes=[r_G[l]], dma=True, cc=k)

            def load_blk_into(blk, l, j, kk):
                cross = kk < NT
                kb = kk if cross else kk - NT
                i = kbi[0] % 2
                kbi[0] += 1
                vq, vo = (kb * TT) // VC, (kb * TT) % VC
                if cross:
                    for hh in range(4):
                        S.dma("sp", kblk[i][0:70, hh, :], GKF[l][hh][0, :, kb * TT:(kb + 1) * TT],
                              reads=[r_G[l]], writes=[r_kblk[i]])
                    S.dma("sp", vblk[i][:, :, 0:260],
                          GVF[l][vq][0, vo:vo + TT, :].rearrange("(s p) e -> p s e", p=128),
                          reads=[r_G[l]], writes=[r_vblk[i]])
                else:
                    for hh in range(4):
                        S.dma("sp", kblk[i][0:70, hh, :], KF[l][hh][:, kb * TT:(kb + 1) * TT],
                              reads=[RKF[l][kb]], writes=[r_kblk[i]])
                    S.dma("sp", vblk[i][:, :, 0:260],
                          VF[l][vq][vo:vo + TT, :].rearrange("(s p) e -> p s e", p=128),
                          reads=[RKF[l][kb]], writes=[r_vblk[i]])
                blk[kk] = (i, cross, (not cross) and kb == j)

            def back_prep(l, j):
                t0 = j * TT
                S.dma("sp", uext[:, :, 30:30 + TT], FU[j], reads=[RF[j]], writes=[r_uext])
                S.dma("sp", sxext[:, :, 2:2 + TT], FSX[j], reads=[RF[j]], writes=[r_sxext])
                S.dma("sp", sbt[:], FSB[j], reads=[RF[j]], writes=[r_sbt])
                if j == 0:
                    S.dma("sp", uext[:, :, 0:30], GHF[l][0, :, 0:60].rearrange("p (c k) -> p c k", c=2),
                          reads=[r_G[l]], writes=[r_uext])
                    S.dma("sp", sxext[:, :, 0:2], GHF[l][0, :, 60:64].rearrange("p (c k) -> p c k", c=2),
                          reads=[r_G[l]], writes=[r_sxext])
                    S.dma("sp", tot_t[:], GHF[l][0, 0:4, 64:65], reads=[r_G[l]], writes=[r_tot],
                          allow_slow_non_contiguous=True)
                    DVE(lambda e: e.tensor_scalar(out=uext[:, :, 0:30], in0=uext[:, :, 0:30],
                                                  scalar1=flag_t[:, 0:1], scalar2=None, op0=ALU.mult),
                        [r_uext, r_flag], [r_uext])
                    DVE(lambda e: e.tensor_scalar(out=sxext[:, :, 0:2], in0=sxext[:, :, 0:2],
                                                  scalar1=flag_t[:, 0:1], scalar2=None, op0=ALU.mult),
                        [r_sxext, r_flag], [r_sxext])
                else:
                    S.dma("sp", uext[:, :, 0:30], FU[j - 1][:, :, TT - 30:TT], reads=[RF[j - 1]], writes=[r_uext])
                    S.dma("sp", sxext[:, :, 0:2], FSX[j - 1][:, :, TT - 2:TT], reads=[RF[j - 1]], writes=[r_sxext])
                S.dma("sp", qf[0:64, :, :], FQF[j], reads=[RF[j]], writes=[r_qf])
                S.dma("sp", qfx[0:64, :, :], FQF[j], reads=[RF[j]], writes=[r_qfx])
                S.dma("sp", fG[:], FG[j], reads=[RF[j]], writes=[r_fG])
                split3(fG, r_fG)
                for hh in range(4):
                    S.dma("sp", QFs[0, :, hh, :].rearrange("(o a) t -> o a t", o=1), gsp[hh:hh + 1, :, :],
                          reads=[r_gs], writes=[r_QFs])
                S.dma("sp", qf[64:67, :, :], QFs[0], reads=[r_QFs], writes=[r_qf])
                DVE(lambda e: e.tensor_scalar(out=fG[:], in0=fG[:], scalar1=tot_t[:, 0:1], scalar2=None,
                                              op0=ALU.add), [r_fG, r_tot], [r_fG])
                split3(fG, r_fG)
                for hh in range(4):
                    S.dma("sp", QFs[1, :, hh, :].rearrange("(o a) t -> o a t", o=1), gsp[hh:hh + 1, :, :],
                          reads=[r_gs], writes=[r_QFs])
                S.dma("sp", qfx[64:67, :, :], QFs[1], reads=[r_QFs], writes=[r_qfx])
                pb = {}
                for kk in range(2):
                    load_blk_into(pb, l, j, kk)
                preblk[(l, j)] = pb
            def back(l, j):
                t0 = j * TT
                for kk in range(31):
                    for ch in range(2):
                        if kk == 0:
                            DVE(lambda e, ch=ch: e.tensor_scalar(
                                out=cacc[:, ch, :], in0=uext[:, ch, 0:TT], scalar1=cdw_t[:, ch, 0:1],
                                scalar2=cvec_t[:, ch, 0:1], op0=ALU.mult, op1=ALU.add),
                                [r_uext, r_small], [r_cacc[ch]])
                        else:
                            DVE(lambda e, ch=ch, kk=kk: e.scalar_tensor_tensor(
                                out=cacc[:, ch, :], in0=uext[:, ch, kk:kk + TT], scalar=cdw_t[:, ch, kk:kk + 1],
                                in1=cacc[:, ch, :], op0=ALU.mult, op1=ALU.add),
                                [r_uext, r_small, r_cacc[ch]], [r_cacc[ch]])
                accs = banks[0:4]
                nkb = NT + j + 1
                jobs = [(kk, hh, c) for kk in range(nkb) for hh in range(4) for c in range(NS)]
                blk = preblk.pop((l, j), {})

                def load_blk(kk):
                    load_blk_into(blk, l, j, kk)

                def _unused(kk):
                    cross = kk < NT
                    kb = kk if cross else kk - NT
                    i = kbi[0] % 2
                    kbi[0] += 1
                    vq, vo = (kb * TT) // VC, (kb * TT) % VC
                    if cross:
                        for hh in range(4):
                            S.dma("sp", kblk[i][0:70, hh, :], GKF[l][hh][0, :, kb * TT:(kb + 1) * TT],
                                  reads=[r_G[l]], writes=[r_kblk[i]])
                        S.dma("sp", vblk[i][:, :, 0:260],
                              GVF[l][vq][0, vo:vo + TT, :].rearrange("(s p) e -> p s e", p=128),
                              reads=[r_G[l]], writes=[r_vblk[i]])
                    else:
                        for hh in range(4):
                            S.dma("sp", kblk[i][0:70, hh, :], KF[l][hh][:, kb * TT:(kb + 1) * TT],
                                  reads=[RKF[l][kb]], writes=[r_kblk[i]])
                        S.dma("sp", vblk[i][:, :, 0:260],
                              VF[l][vq][vo:vo + TT, :].rearrange("(s p) e -> p s e", p=128),
                              reads=[RKF[l][kb]], writes=[r_vblk[i]])
                    blk[kk] = (i, cross, (not cross) and kb == j)

                pis = {}

                def emit_S(job):
                    kk, hh, c = job
                    if kk not in blk:
                        load_blk(kk)
                    i, cross, diag = blk[kk]
                    qsrc, rq = (qfx, r_qfx) if cross else (qf, r_qf)
                    bS = rot_hi.get()
                    PE(lambda e, i=i, hh=hh, c=c, bS=bS, qsrc=qsrc: e.matmul(
                        bS.ap[:, :], lhsT=kblk[i][:, hh, c * 128:(c + 1) * 128], rhs=qsrc[:, hh, :],
                        start=True, stop=True), [r_kblk[i], rq], [bS.res])
                    pi = pti[0] % 4
                    pti[0] += 1
                    pis[job] = pi
                    ACT(lambda e, pi=pi, bS=bS: e.activation(out=pT[pi][:], in_=bS.ap[:, :], func=AF.Exp),
                        [bS.res], [r_pT[pi]])
                    if diag:
                        POOL(lambda e, pi=pi, c=c: e.affine_select(
                            out=pT[pi][:], in_=pT[pi][:], pattern=[[1, TT]], compare_op=ALU.is_ge,
                            fill=0.0, base=-c * 128, channel_multiplier=-1), [r_pT[pi]], [r_pT[pi]])

                def emit_PV(job):
                    kk, hh, c = job
                    i = blk[kk][0]
                    pi = pis[job]
                    PE(lambda e, i=i, hh=hh, c=c, pi=pi, kk=kk: e.matmul(
                        accs[hh].ap[:, :], lhsT=vblk[i][:, c, hh * 65:hh * 65 + 128], rhs=pT[pi][:],
                        start=(kk == 0 and c == 0), stop=(kk == nkb - 1 and c == NS - 1)),
                       [r_vblk[i], r_pT[pi]], [accs[hh].res])

                LA = 2
                for idx, job in enumerate(jobs):
                    emit_S(job)
                    if idx >= LA:
                        emit_PV(jobs[idx - LA])
                for job in jobs[len(jobs) - LA:]:
                    emit_PV(job)
                attn_norm4(accs, onT[1], r_onT[1])

                rmsnorm_hT((l, 1), nxt=(l, 2))
                for ch in range(2):
                    ACT(lambda e, ch=ch: e.activation(out=csq[:, ch, :], in_=cacc[:, ch, :], func=AF.Square),
                        [r_cacc[ch]], [r_csq])
                b1 = rot.get()
                for ch in range(2):
                    PE(lambda e, ch=ch, b1=b1: e.matmul(b1.ap[:, :], lhsT=ones[:], rhs=cacc[:, ch, :],
                                                        start=(ch == 0), stop=(ch == 1)),
                       [r_cacc[ch], r_ones], [b1.res])
                b2 = rot.get()
                for ch in range(2):
                    PE(lambda e, ch=ch, b2=b2: e.matmul(b2.ap[:, :], lhsT=ones[:], rhs=csq[:, ch, :],
                                                        start=(ch == 0), stop=(ch == 1)),
                       [r_csq, r_ones], [b2.res])
                DVE(lambda e: e.tensor_scalar(out=lnm[:], in0=b1.ap[:, :], scalar1=1.0 / 256.0, scalar2=None,
                                              op0=ALU.mult), [b1.res], [r_lnm])
                DVE(lambda e: e.tensor_tensor(out=lnv[:], in0=lnm[:], in1=lnm[:], op=ALU.mult), [r_lnm], [r_lnv])
                DVE(lambda e: e.scalar_tensor_tensor(out=lnv[:], in0=b2.ap[:, :], scalar=1.0 / 256.0, in1=lnv[:],
                                                     op0=ALU.mult, op1=ALU.subtract), [b2.res, r_lnv], [r_lnv])
                DVE(lambda e: e.tensor_scalar(out=lnv[:], in0=lnv[:], scalar1=0.0, scalar2=None, op0=ALU.max),
                    [r_lnv], [r_lnv])
                ACT(lambda e: e.activation(out=lnv[:], in_=lnv[:], func=AF.Ln, bias=EPS), [r_lnv], [r_lnv])
                ACT(lambda e: e.activation(out=lnv[:], in_=lnv[:], func=AF.Exp, scale=-0.5), [r_lnv], [r_lnv])
                for ch in range(2):
                    DVE(lambda e, ch=ch: e.tensor_tensor(out=cacc[:, ch, :], in0=cacc[:, ch, :], in1=lnm[:],
                                                         op=ALU.subtract), [r_cacc[ch], r_lnm], [r_cacc[ch]])
                    DVE(lambda e, ch=ch: e.tensor_tensor(out=cacc[:, ch, :], in0=cacc[:, ch, :], in1=lnv[:],
                                                         op=ALU.mult), [r_cacc[ch], r_lnv], [r_cacc[ch]])
                    ACT(lambda e, ch=ch: e.activation(out=uT[:, ch, :], in_=cacc[:, ch, :], func=AF.Silu,
                                                      scale=cvec_t[:, ch, 1:2], bias=cvec_t[:, ch, 2:3]),
                        [r_cacc[ch], r_small], [r_uT])
                for ch in range(2):
                    DVE(lambda e, ch=ch: e.tensor_scalar(out=csq[:, ch, :], in0=sxext[:, ch, 0:TT],
                                                         scalar1=scw_t[:, ch, 0:1], scalar2=None, op0=ALU.mult),
                        [r_sxext, r_small], [r_csq])
                    for kk in (1, 2):
                        DVE(lambda e, ch=ch, kk=kk: e.scalar_tensor_tensor(
                            out=csq[:, ch, :], in0=sxext[:, ch, kk:kk + TT], scalar=scw_t[:, ch, kk:kk + 1],
                            in1=csq[:, ch, :], op0=ALU.mult, op1=ALU.add), [r_sxext, r_small, r_csq], [r_csq])
                    DVE(lambda e, ch=ch: e.tensor_tensor(out=scT[:, ch, :], in0=csq[:, ch, :], in1=sbt[:, ch, :],
                                                         op=ALU.mult), [r_csq, r_sbt], [r_scT])

                S.dma("sp", qs[:], FQS[j], reads=[RF[j]], writes=[r_qs])
                S.dma("sp", ksx[:, :, 128:128 + TT], FKS[j], reads=[RF[j]], writes=[r_ksx])
                S.dma("sp", vsx[:, 1:NS + 1, :, :].rearrange("p s h e -> p s (h e)"), FVS[j], reads=[RF[j]],
                      writes=[r_vsx])
                if j == 0:
                    S.dma("sp", ksx[:, :, 0:128], GHB[l][0, 0:64, 0:256].rearrange("p (c k) -> p c k", c=2),
                          reads=[r_G[l]], writes=[r_ksx])
                    S.dma("sp", vsx[:, 0, :, :].rearrange("p h e -> p (h e)"), GHB[l][0, :, 256:386],
                          reads=[r_G[l]], writes=[r_vsx])
                    DVE(lambda e: e.tensor_scalar(out=vsx[:, 0, :, :], in0=vsx[:, 0, :, :],
                                                  scalar1=flag_t[:, 0:1], scalar2=None, op0=ALU.mult),
                        [r_vsx, r_flag], [r_vsx])
                else:
                    S.dma("sp", ksx[:, :, 0:128], FKS[j - 1][:, :, TT - 128:TT], reads=[RF[j - 1]], writes=[r_ksx])
                    S.dma("sp", vsx[:, 0, :, :].rearrange("p h e -> p (h e)"), FVS[j - 1][:, NS - 1, :],
                          reads=[RF[j - 1]], writes=[r_vsx])
                for hq in range(4):
                    hk = hq // 2
                    acc = banks[hq % 4]
                    for pr in range(2):
                        bS = rot_hi.get()
                        i = sgi[0] % 2
                        sgi[0] += 1
                        for s2 in range(2):
                            s = 2 * pr + s2
                            for part in range(2):
                                PE(lambda e, s=s, s2=s2, part=part, bS=bS, hk=hk, hq=hq: e.matmul(
                                    bS.ap[:, s2 * 256 + part * 128: s2 * 256 + part * 128 + 128],
                                    lhsT=ksx[:, hk, (s + part) * 128:(s + part + 1) * 128],
                                    rhs=qs[:, hq, s * 128:(s + 1) * 128], start=True, stop=True),
                                   [r_ksx, r_qs], [bS.res])
                            DVE(lambda e, bS=bS, i=i, hq=hq, s2=s2: e.tensor_tensor(
                                out=sg[i][:, s2 * 256:(s2 + 1) * 256], in0=bS.ap[:, s2 * 256:(s2 + 1) * 256],
                                in1=bm_t[:, hq, :], op=ALU.add), [bS.res, r_bm], [r_sg[i]])
                        pi = pti[0] % 4
                        pti[0] += 1
                        ACT(lambda e, i=i, pi=pi: e.activation(out=pT[pi][:], in_=sg[i][:], func=AF.Exp),
                            [r_sg[i]], [r_pT[pi]])
                        for s2 in range(2):
                            s = 2 * pr + s2
                            first = True
                            for part in range(2):
                                PE(lambda e, s=s, s2=s2, part=part, pi=pi, hk=hk, acc=acc, first=first: e.matmul(
                                    acc.ap[0:65, s * 128:(s + 1) * 128], lhsT=vsx[:, s + part, hk, :],
                                    rhs=pT[pi][:, s2 * 256 + part * 128: s2 * 256 + part * 128 + 128],
                                    start=first, stop=(part == 1)), [r_vsx, r_pT[pi]], [acc.res])
                                first = False
                    attn_norm(acc, sink_t[64:65, hq:hq + 1], onT[0][:, hq, :], r_onT[0])

                brsrc = [(uT, r_uT), (scT, r_scT)]
                for m in range(8):
                    tr, rr = WR.next(wbr[l, m])
                    for br in range(4):
                        bp = rot.get()
                        if br < 2:
                            src, rs = brsrc[br]
                            for ch in range(2):
                                PE(lambda e, br=br, ch=ch, bp=bp, src=src, tr=tr: e.matmul(
                                    bp.ap[:, :], lhsT=tr[:, 2 * br + ch, :], rhs=src[:, ch, :],
                                    start=(ch == 0), stop=(ch == 1)), [rr, rs], [bp.res])
                        else:
                            a = br - 2
                            for hh in range(4):
                                PE(lambda e, a=a, hh=hh, bp=bp, tr=tr: e.matmul(
                                    bp.ap[:, :], lhsT=tr[0:64, 4 + 4 * a + hh, :],
                                    rhs=onT[a][:, hh, :], start=(hh == 0), stop=(hh == 3)),
                                   [rr, r_onT[a]], [bp.res])
                        bgt = proj128(l, 21 + 4 * m + br)
                        i = sgi[0] % 2
                        sgi[0] += 1
                        ACT(lambda e, i=i, bgt=bgt: e.activation(out=sg[i][:], in_=bgt.ap[:, :], func=AF.Sigmoid),
                            [bgt.res], [r_sg[i]])
                        if br == 0:
                            DVE(lambda e, i=i, bp=bp: e.tensor_tensor(out=macc[:], in0=sg[i][:], in1=bp.ap[:, :],
                                                                      op=ALU.mult), [r_sg[i], bp.res], [r_macc])
                        else:
                            k2 = br % 2
                            DVE(lambda e, i=i, bp=bp, k2=k2: e.tensor_tensor(out=mtmp[k2][:], in0=sg[i][:],
                                                                             in1=bp.ap[:, :], op=ALU.mult),
                                [r_sg[i], bp.res], [r_mtmp[k2]])
                            if br < 3:
                                DVE(lambda e, k2=k2: e.tensor_tensor(out=macc[:], in0=macc[:], in1=mtmp[k2][:],
                                                                     op=ALU.add), [r_macc, r_mtmp[k2]], [r_macc])
                            else:
                                DVE(lambda e, k2=k2, m=m: e.tensor_tensor(out=mT[:, m, :], in0=macc[:],
                                                                          in1=mtmp[k2][:], op=ALU.add),
                                    [r_macc, r_mtmp[k2]], [r_mT])
                resid_proj(mT, lambda m: r_mT, 8, lambda n, m: wo2[l, n, m], 1.0)

            for l in range(L):
                load_small(l)
                for j in range(NT):
                    t0 = j * TT
                    src = xin if l == 0 else Xs
                    if j == 0:
                        S.dma("sp", x[:], src[t0:t0 + TT, :].rearrange("(s p) d -> p s d", p=128),
                              reads=([RX[j]] if l > 0 else []), writes=r_x)
                    ffn(l, 0, nxt=(l, 1))
                    S.dma("sp", Xs[t0:t0 + TT, :].rearrange("(s p) d -> p s d", p=128), x[:], reads=r_x,
                          writes=[RX[j]])
                    front(l, j)
                exchange(l)
                back_prep(l, 0)
                for j in range(NT):
                    t0 = j * TT
                    S.dma("sp", x[:], Xs[t0:t0 + TT, :].rearrange("(s p) d -> p s d", p=128), reads=[RX[j]],
                          writes=r_x)
                    back(l, j)
                    ffn(l, 1, mid=((lambda l=l, j=j: back_prep(l, j + 1)) if j + 1 < NT else None),
                        nxt=((l, 1) if j + 1 < NT else (l + 1, 0)))
                    dst = Xs if l < L - 1 else yout
                    S.dma("sp", dst[t0:t0 + TT, :].rearrange("(s p) d -> p s d", p=128), x[:], reads=r_x,
                          writes=[RX[j]])
            return [WA.rec, WD.rec, WR.rec]

        plans = program(Sched(nc, dry=True), [None, None, None])
        S = Sched(nc)
        program(S, plans)
        S.emit()
    return nc


def _t5_bucket(dist):
    max_exact = 16
    d = np.maximum(dist, 1).astype(np.float32)
    large = max_exact + (np.log(d / np.float32(max_exact)) / np.float32(np.log(128 / max_exact))
                         * np.float32(32 - max_exact)).astype(np.int32)
    large = np.minimum(large, 31)
    return np.where(dist < max_exact, dist, large)


def host_prep(inp):
    f = lambda a: np.ascontiguousarray(np.asarray(a, dtype=np.float32))

    def wtile(w, col0, n):
        out = np.zeros((128, 8, 128), np.float32)
        out[:, :, :n] = w[:, col0:col0 + n].reshape(8, 128, n).transpose(1, 0, 2)
        return out

    shared = {}
    gates = [np.asarray(inp["ffn1_w_gate"]), np.asarray(inp["ffn2_w_gate"])]
    ups = [np.asarray(inp["ffn1_w_up"]), np.asarray(inp["ffn2_w_up"])]
    downs = [np.asarray(inp["ffn1_w_down"]), np.asarray(inp["ffn2_w_down"])]

    def ftile(w):
        return w.reshape(8, 128, NF, 128).transpose(2, 1, 0, 3)

    def dtile(w):
        return w.reshape(NF, 128, 2, 512).transpose(2, 0, 1, 3)

    shared["wg"] = f(np.stack([np.stack([ftile(gates[k][l]) for k in range(2)]) for l in range(L)]))
    shared["wu"] = f(np.stack([np.stack([ftile(ups[k][l]) for k in range(2)]) for l in range(L)]))
    shared["wd2"] = f(np.stack([np.stack([dtile(downs[k][l]) for k in range(2)]) for l in range(L)]))
    w_in = np.asarray(inp["w_in"])
    shared["win"] = f(np.stack([np.stack([wtile(w_in[l], c0, n) for (c0, n) in WIN_TILES]) for l in range(L)]))
    cwo, swo = np.asarray(inp["conf_w_out"]), np.asarray(inp["sc_w_out"])
    awo, fwo = np.asarray(inp["swa_w_o"]), np.asarray(inp["fox_w_o"])
    wbr = np.zeros((L, 8, 128, 12, 128), np.float32)
    for l in range(L):
        for m in range(8):
            cs = slice(m * 128, (m + 1) * 128)
            for ch in range(2):
                wbr[l, m, :, ch, :] = cwo[l][ch * 128:(ch + 1) * 128, cs]
                wbr[l, m, :, 2 + ch, :] = swo[l][ch * 128:(ch + 1) * 128, cs]
            for hh in range(4):
                wbr[l, m, 0:64, 4 + hh, :] = awo[l][hh * 64:(hh + 1) * 64, cs]
                wbr[l, m, 0:64, 8 + hh, :] = fwo[l][hh * 64:(hh + 1) * 64, cs]
    shared["wbr"] = wbr
    shared["wo2"] = f(np.stack([np.asarray(inp["w_out"])[l].reshape(8, 128, 2, 512).transpose(2, 0, 1, 3)
                                for l in range(L)]))
    gn = [np.asarray(inp["ffn1_norm"]), np.asarray(inp["mix_norm"]), np.asarray(inp["ffn2_norm"])]
    shared["gains"] = f(np.stack([np.stack([np.broadcast_to(gn[i][l][None, :], (128, D)) for i in range(3)])
                                  for l in range(L)]))
    shared["cdw"] = f(np.stack([np.asarray(inp["conf_dw"])[l].T.reshape(2, 128, 31).transpose(1, 0, 2)
                                for l in range(L)]))
    cv = [np.asarray(inp["conf_dw_b"]), np.asarray(inp["conf_ln_g"]), np.asarray(inp["conf_ln_b"])]
    shared["cvec"] = f(np.stack([np.stack([cv[i][l].reshape(2, 128).T for i in range(3)], axis=-1)
                                 for l in range(L)]))
    shared["scw"] = f(np.stack([np.asarray(inp["sc_conv"])[l].T.reshape(2, 128, 3).transpose(1, 0, 2)
                                for l in range(L)]))
    qk = [np.asarray(inp["swa_q_norm"]), np.asarray(inp["swa_k_norm"]),
          np.asarray(inp["fox_q_norm"]), np.asarray(inp["fox_k_norm"])]
    shared["qkg"] = f(np.stack([np.stack([qk[i][l] for i in range(4)], axis=-1) for l in range(L)]))
    shared["bfg"] = f(np.asarray(inp["b_forget"]).reshape(L, 4, 1))
    sk = np.zeros((L, 65, 4), np.float32)
    sk[:, 64, :] = np.asarray(inp["swa_sink"])
    shared["sink"] = sk
    rb = np.asarray(inp["rel_bias"], dtype=np.float32)
    i = np.arange(128)[:, None]
    jq = np.arange(128)[None, :]
    bmt = np.full((128, 4, 256), NEG, np.float32)
    d_prev = jq + 128 - i
    ok_prev = d_prev <= 127
    d_cur = jq - i
    ok_cur = d_cur >= 0
    bk_prev = _t5_bucket(np.clip(d_prev, 0, 127))
    bk_cur = _t5_bucket(np.clip(d_cur, 0, 127))
    for hq in range(4):
        bmt[:, hq, 0:128] = np.where(ok_prev, rb[bk_prev, hq], np.float32(NEG))
        bmt[:, hq, 128:256] = np.where(ok_cur, rb[bk_cur, hq], np.float32(NEG))
    shared["bm"] = bmt
    shared["identin"] = np.eye(128, dtype=np.float32)
    return shared


_NC_CACHE = {}


def kernel(**inputs):
    x = np.asarray(inputs["x"], dtype=np.float32)
    B, T, _ = x.shape
    ntok = T // 2
    shared = host_prep(inputs)
    if ntok not in _NC_CACHE:
        _NC_CACHE[ntok] = build(ntok)
    nc = _NC_CACHE[ntok]
    in_maps = []
    for c in range(N_CORES):
        b, half = c // 2, c % 2
        m = dict(shared)
        m["xin"] = np.ascontiguousarray(x[b, half * ntok:(half + 1) * ntok])
        m["flagin"] = np.full((128, 1), float(half), np.float32)
        m["maskin"] = np.full((1, 4 * TT), 0.0 if half else NEG, np.float32)
        in_maps.append(m)
    res = run_bass_kernel_spmd(nc, in_maps, core_ids=list(range(N_CORES)))
    out = np.empty((B, T, D), np.float32)
    for c in range(N_CORES):
        b, half = c // 2, c % 2
        out[b, half * ntok:(half + 1) * ntok] = np.asarray(res.results[c]["yout"], dtype=np.float32)
    return out
```

```python
from contextlib import ExitStack

import numpy as np

import concourse.bass as bass
import concourse.mybir as mybir
from concourse.bass_utils import run_bass_kernel_spmd

F32 = mybir.dt.float32
BF16 = mybir.dt.bfloat16
AF = mybir.ActivationFunctionType
ALU = mybir.AluOpType

D = 1024
DFF = 2816
NF = DFF // 128
L = 2
TT = 512
NS = TT // 128
EPS = 1e-6
NEG = -30000.0
N_CORES = 8

WIN_TILES = ([(0, 128), (256, 128), (128, 128), (384, 128), (512, 128), (640, 128),
              (768, 128), (1024, 128), (896, 128), (1152, 128)]
             + [(1280, 128), (1408, 128), (1536, 128), (1664, 128)]
             + [(1792, 128), (1920, 128), (2048, 128), (2176, 128), (2304, 128), (2432, 128), (2560, 4)]
             + [(2564 + i * 1024 + m * 128, 128) for m in range(8) for i in range(4)])
NWT = len(WIN_TILES)


class Res:
    __slots__ = ("name", "lw", "rd")

    def __init__(self, name=""):
        self.name = name
        self.lw = None
        self.rd = {}


class Op:
    __slots__ = ("eng", "fn", "deps", "sig", "signo", "dma", "dj", "idx", "cc")


ENGS = ["pe", "act", "dve", "pool", "sp"]
NSC = 4
NSD = 8


def _dkey(o):
    return ("d", id(o)) if o.dma else ("c", o.eng)


def _dput(d, o):
    k = _dkey(o)
    cur = d.get(k)
    if cur is None or o.dma or o.idx > cur.idx:
        d[k] = o


class Sched:
    def __init__(self, nc, dry=False):
        self.nc = nc
        self.dry = dry
        self.ops = {e: [] for e in ENGS}

    def op(self, eng, fn, reads=(), writes=(), dma=False, cc=None):
        if self.dry:
            return None
        o = Op()
        o.eng, o.fn, o.dma, o.sig, o.cc = eng, fn, dma, False, cc
        o.idx = len(self.ops[eng])
        deps = {}

        def add(d, raw):
            if d is None:
                return
            if d.dma or dma or d.eng != eng or (raw and eng != "pe"):
                _dput(deps, d)

        for r in reads:
            add(r.lw, True)
        for w in writes:
            add(w.lw, False)
            for x in w.rd.values():
                add(x, False)
        o.deps = list(deps.values())
        for d in o.deps:
            d.sig = True
        for r in reads:
            _dput(r.rd, o)
        for w in writes:
            w.lw = o
            w.rd = {}
        self.ops[eng].append(o)
        return o

    def dma(self, eng, out, in_, reads=(), writes=(), **kw):
        return self.op(eng, lambda e: e.dma_start(out=out, in_=in_, **kw), reads=reads, writes=writes, dma=True)

    def emit(self):
        nc = self.nc
        for e in ENGS:
            c = 0
            j = 0
            for o in self.ops[e]:
                if o.cc is not None:
                    continue
                if o.dma:
                    o.dj = j
                    j += 1
                elif o.sig:
                    c += 1
                    o.signo = c
        with ExitStack() as st:
            csem = {e: [st.enter_context(nc.semaphore(f"c_{e}_{i}")) for i in range(NSC)]
                    for e in ENGS if e != "sp"}
            dsem = {e: [st.enter_context(nc.semaphore(f"d_{e}_{i}")) for i in range(NSD)]
                    for e in ENGS if e != "pe"}
            ncc = sum(1 for e in ENGS for o in self.ops[e] if o.cc is not None)
            ccsem = [st.enter_context(nc.semaphore(f"cc_{i}")) for i in range(ncc)]
            block = st.enter_context(nc.Block())

            def body(eng, e):
                cw = {}
                dw = {}

                def wait_dma(q, dj):
                    slot, val = dj % NSD, 16 * (dj // NSD + 1)
                    if dw.get((q, slot), 0) < val:
                        eng.wait_ge(dsem[q][slot], val)
                        dw[(q, slot)] = val

                for o in self.ops[e]:
                    for d in o.deps:
                        if d.cc is not None:
                            if dw.get(("cc", d.cc), 0) < 1:
                                eng.wait_ge(ccsem[d.cc], 1)
                                dw[("cc", d.cc)] = 1
                        elif d.dma:
                            wait_dma(d.eng, d.dj)
                        elif cw.get(d.eng, 0) < d.signo:
                            s = d.signo - 1
                            eng.wait_ge(csem[d.eng][s % NSC], s // NSC + 1)
                            cw[d.eng] = d.signo
                    if o.cc is not None:
                        o.fn(eng).then_inc(ccsem[o.cc])
                        continue
                    if o.dma and o.dj >= NSD:
                        wait_dma(e, o.dj - NSD)
                    ins = o.fn(eng)
                    if o.dma:
                        ins.then_inc(dsem[e][o.dj % NSD], 16)
                    elif o.sig:
                        ins.then_inc(csem[e][(o.signo - 1) % NSC], 1)
                n = sum(1 for o in self.ops[e] if o.dma and o.cc is None)
                for dj in range(max(0, n - NSD), n):
                    wait_dma(e, dj)

            names = {"pe": "tensor", "act": "scalar", "dve": "vector", "pool": "gpsimd", "sp": "sync"}
            for e in ENGS:
                getattr(block, names[e])(lambda eng, e=e: body(eng, e))


class Stream:
    def __init__(self, S, slots, res, plan):
        self.S, self.slots, self.res, self.plan = S, slots, res, plan
        self.rec = []
        self.pos = 0
        self.issued = 0

    def next(self, src):
        R = len(self.slots)
        if self.plan is None:
            self.rec.append(src)
            return self.slots[0], self.res[0]
        while self.issued < min(len(self.plan), self.pos + R):
            k = self.issued % R
            self.S.dma("pool", self.slots[k][:], self.plan[self.issued], writes=[self.res[k]])
            self.issued += 1
        k = self.pos % R
        self.pos += 1
        return self.slots[k], self.res[k]


class Bank:
    def __init__(self, ap, name):
        self.ap = ap
        self.res = Res(name)


class Rot:
    def __init__(self, items):
        self.items = items
        self.i = 0

    def get(self):
        b = self.items[self.i % len(self.items)]
        self.i += 1
        return b


def build(NTOK, stages=3):
    NT = NTOK // TT
    nc = bass.Bass("TRN2", target_bir_lowering=False)

    def din(name, shape, dt=F32):
        return nc.dram_tensor(name, list(shape), dt, kind="ExternalInput").ap()

    xin = din("xin", [NTOK, D])
    wg = din("wg", [L, 2, NF, 128, 8, 128])
    wu = din("wu", [L, 2, NF, 128, 8, 128])
    wd2 = din("wd2", [L, 2, 2, NF, 128, 512])
    win = din("win", [L, NWT, 128, 8, 128])
    wbr = din("wbr", [L, 8, 128, 12, 128])
    wo2 = din("wo2", [L, 2, 8, 128, 512])
    gains = din("gains", [L, 3, 128, D])
    cdw = din("cdw", [L, 128, 2, 31])
    cvec = din("cvec", [L, 128, 2, 3])
    scw = din("scw", [L, 128, 2, 3])
    qkg = din("qkg", [L, 64, 4])
    bfg = din("bfg", [L, 4, 1])
    sink = din("sink", [L, 65, 4])
    bm = din("bm", [128, 4, 256])
    identin = din("identin", [128, 128])
    flagin = din("flagin", [128, 1])
    maskin = din("maskin", [1, 4 * TT])
    yout = nc.dram_tensor("yout", [NTOK, D], F32, kind="ExternalOutput").ap()

    def dscr(name, shape, dt):
        return nc.dram_tensor(name, list(shape), dt, kind="Internal")

    Xs = dscr("Xs", [NTOK, D], F32).ap()
    FU = dscr("FU", [NT, 128, 2, TT], F32).ap()
    FSX = dscr("FSX", [NT, 128, 2, TT], F32).ap()
    FSB = dscr("FSB", [NT, 128, 2, TT], F32).ap()
    FQS = dscr("FQS", [NT, 64, 4, TT], BF16).ap()
    FKS = dscr("FKS", [NT, 64, 2, TT], BF16).ap()
    FVS = dscr("FVS", [NT, 128, NS, 2 * 65], BF16).ap()
    FQF = dscr("FQF", [NT, 64, 4, TT], BF16).ap()
    FG = dscr("FG", [NT, 4, TT], F32).ap()
    QFs = dscr("QFs", [2, 3, 4, TT], BF16).ap()
    NVC = max(1, NTOK // 1024)
    VC = NTOK // NVC
    KFh = [[dscr(f"KF{l}_{hh}", [70, NTOK], BF16) for hh in range(4)] for l in range(L)]
    VFh = [[dscr(f"VF{l}_{q}", [VC, 4 * 65], BF16) for q in range(NVC)] for l in range(L)]
    HFh = [dscr(f"HF{l}", [128, 128], F32) for l in range(L)]
    HBh = [dscr(f"HB{l}", [128, 512], BF16) for l in range(L)]
    GKFh = [[dscr(f"GKF{l}_{hh}", [2 * 70, NTOK], BF16) for hh in range(4)] for l in range(L)]
    GVFh = [[dscr(f"GVF{l}_{q}", [2 * VC, 4 * 65], BF16) for q in range(NVC)] for l in range(L)]
    GHFh = [dscr(f"GHF{l}", [2 * 128, 128], F32) for l in range(L)]
    GHBh = [dscr(f"GHB{l}", [2 * 128, 512], BF16) for l in range(L)]
    KF = [[t.ap() for t in row] for row in KFh]
    VF = [[t.ap() for t in row] for row in VFh]
    HF = [t.ap() for t in HFh]
    HB = [t.ap() for t in HBh]
    GKF = [[t.ap().rearrange("(g r) t -> g r t", g=2) for t in row] for row in GKFh]
    GVF = [[t.ap().rearrange("(g n) e -> g n e", g=2) for t in row] for row in GVFh]
    GHF = [t.ap().rearrange("(g p) e -> g p e", g=2) for t in GHFh]
    GHB = [t.ap().rearrange("(g p) e -> g p e", g=2) for t in GHBh]

    with ExitStack() as st:
        def sb(name, shape, dt):
            return st.enter_context(nc.sbuf_tensor(name, list(shape), dt))

        x = sb("x", [128, NS, D], F32)
        r_x = [Res(f"x{s}") for s in range(NS)]
        h = sb("h", [128, NS, D], BF16)
        r_h = [Res(f"h{s}") for s in range(NS)]
        hT = sb("hT", [128, 8, TT], BF16)
        r_hT = Res("hT")
        act = sb("act", [128, NF, TT], BF16)
        r_act = [Res(f"act{f}") for f in range(NF)]
        RA, RD, RR = 6, 8, 2
        wA = [sb(f"wA{i}", [128, 8, 128], BF16) for i in range(RA)]
        rA = [Res(f"wA{i}") for i in range(RA)]
        wD = [sb(f"wD{i}", [128, 512], BF16) for i in range(RD)]
        rD = [Res(f"wD{i}") for i in range(RD)]
        wR = [sb(f"wR{i}", [128, 12, 128], BF16) for i in range(RR)]
        rR = [Res(f"wR{i}") for i in range(RR)]
        gb = sb("gb", [128, D], F32)
        r_gb = Res("gb")
        ss = sb("ss", [128, 8], F32)
        r_ssq = [Res(f"ss{i}") for i in range(NS)]
        sg = [sb(f"sg{i}", [128, TT], F32) for i in range(2)]
        r_sg = [Res(f"sg{i}") for i in range(2)]
        ident = sb("ident", [128, 128], BF16)
        r_ident = Res("ident")
        ones = sb("ones", [128, 128], F32)
        r_ones = Res("ones")
        o64 = sb("o64", [64, 64], F32)
        r_o64 = Res("o64")
        uext = sb("uext", [128, 2, 30 + TT], F32)
        r_uext = Res("uext")
        sxext = sb("sxext", [128, 2, 2 + TT], F32)
        r_sxext = Res("sxext")
        sbt = sb("sbt", [128, 2, TT], F32)
        r_sbt = Res("sbt")
        cacc = sb("cacc", [128, 2, TT], F32)
        r_cacc = [Res("cacc0"), Res("cacc1")]
        csq = sb("csq", [128, 2, TT], F32)
        r_csq = Res("csq")
        lnm = sb("lnm", [128, TT], F32)
        r_lnm = Res("lnm")
        lnv = sb("lnv", [128, TT], F32)
        r_lnv = Res("lnv")
        uT = sb("uT", [128, 2, TT], BF16)
        r_uT = Res("uT")
        scT = sb("scT", [128, 2, TT], BF16)
        r_scT = Res("scT")
        cdw_t = sb("cdw_t", [128, 2, 31], F32)
        cvec_t = sb("cvec_t", [128, 2, 3], F32)
        scw_t = sb("scw_t", [128, 2, 3], F32)
        qkg_t = sb("qkg_t", [64, 4], F32)
        bfg_t = sb("bfg_t", [4, 1], F32)
        sink_t = sb("sink_t", [65, 4], F32)
        r_small = Res("small")
        bm_t = sb("bm_t", [128, 4, 256], F32)
        r_bm = Res("bm")
        qs = sb("qs", [64, 4, TT], BF16)
        r_qs = Res("qs")
        ksx = sb("ksx", [64, 2, 128 + TT], BF16)
        r_ksx = Res("ksx")
        vsx = sb("vsx", [128, NS + 1, 2, 65], BF16)
        r_vsx = Res("vsx")
        pT = [sb(f"pT{i}", [128, TT], BF16) for i in range(4)]
        r_pT = [Res(f"pT{i}") for i in range(4)]
        pTs = [sb(f"pTs{i}", [128, TT], BF16) for i in range(2)]
        r_pTs = [Res(f"pTs{i}") for i in range(2)]
        oa4 = sb("oa4", [65, 4, TT], F32)
        oa = oa4[:, 0, :]
        r_oa = Res("oa")
        rden = sb("rden", [65, TT], F32)
        r_rden = Res("rden")
        onT = [sb(f"onT{i}", [64, 4, TT], BF16) for i in range(2)]
        r_onT = [Res("onT0"), Res("onT1")]
        qf = sb("qf", [128, 4, TT], BF16)
        r_qf = Res("qf")
        qfx = sb("qfx", [128, 4, TT], BF16)
        r_qfx = Res("qfx")
        flag_t = sb("flag_t", [128, 1], F32)
        r_flag = Res("flag")
        tot_t = sb("tot_t", [4, 1], F32)
        r_tot = Res("tot")
        kf = sb("kf", [64, 4, TT], BF16)
        r_kf = Res("kf")
        vf = sb("vf", [128, NS, 4, 65], BF16)
        r_vf = Res("vf")
        kblk = [sb(f"kblk{i}", [128, 4, TT], BF16) for i in range(2)]
        r_kblk = [Res(f"kblk{i}") for i in range(2)]
        vblk = [sb(f"vblk{i}", [128, NS, 324], BF16) for i in range(2)]
        r_vblk = [Res(f"vblk{i}") for i in range(2)]
        fe = sb("fe", [4, TT], F32)
        r_fe = Res("fe")
        fG = sb("fG", [4, TT], F32)
        r_fG = Res("fG")
        fones = sb("fones", [4, TT], F32)
        r_fones = Res("fones")
        gsp = sb("gsp", [4, 3, TT], BF16)
        r_gs = Res("gs")
        monesb = sb("monesb", [4, 3, TT], BF16)
        r_monesb = Res("monesb")
        fcar = [sb(f"fcar{l}", [4, 1], F32) for l in range(L)]
        r_fcar = [Res(f"fcar{l}") for l in range(L)]
        macc = sb("macc", [128, TT], F32)
        r_macc = Res("macc")
        mtmp = [sb(f"mtmp{i}", [128, TT], F32) for i in range(2)]
        r_mtmp = [Res(f"mtmp{i}") for i in range(2)]
        mT = sb("mT", [128, 8, TT], BF16)
        r_mT = Res("mT")
        banks = [Bank(st.enter_context(nc.psum_tensor(f"ps{i}", [128, 512], F32)), f"ps{i}") for i in range(7)]
        ptb = Bank(st.enter_context(nc.psum_tensor("ptb", [128, 1024], BF16)), "ptb")
        ptb2 = Bank(banks[6].ap[:, :].bitcast(BF16), "ptb2")
        ptb2.res = banks[6].res
        ptbs = [ptb, ptb2]
        swacc = Bank(ptb.ap[:, :].bitcast(F32), "swacc")
        swacc.res = ptb.res

        def program(S, plans):
            WA = Stream(S, wA, rA, plans[0])
            WD = Stream(S, wD, rD, plans[1])
            WR = Stream(S, wR, rR, plans[2])
            rot = Rot(banks)
            rot_hi = Rot(banks[4:7])
            RKF = [[Res(f"KF{l}_{j}") for j in range(NT)] for l in range(L)]
            RX = [Res(f"X{j}") for j in range(NT)]
            RF = [Res(f"F{j}") for j in range(NT)]
            r_HAL = [Res(f"HAL{l}") for l in range(L)]
            r_G = [Res(f"G{l}") for l in range(L)]
            r_QFs = Res("QFs")
            ccn = [0]
            gb_cur = [None]
            spi = [0]
            preblk = {}
            kbi = [0]
            pti = [0]
            sgi = [0]

            def ACT(fn, reads, writes):
                S.op("act", fn, reads, writes)

            def DVE(fn, reads, writes):
                S.op("dve", fn, reads, writes)

            def PE(fn, reads, writes):
                S.op("pe", fn, reads, writes)

            def POOL(fn, reads, writes):
                S.op("pool", fn, reads, writes)

            S.dma("pool", ident[:], identin, writes=[r_ident])
            DVE(lambda e: e.memset(ones[:], 1.0), [], [r_ones])
            DVE(lambda e: e.memset(o64[:], 1.0 / 64.0), [], [r_o64])
            DVE(lambda e: e.memset(fones[:], 1.0), [], [r_fones])
            DVE(lambda e: e.memset(monesb[:], -1.0), [], [r_monesb])
            DVE(lambda e: e.memset(qf[:], 0.0), [], [r_qf])
            DVE(lambda e: e.memset(qf[64:70, :, :], 1.0), [r_qf], [r_qf])
            DVE(lambda e: e.memset(qfx[:], 0.0), [], [r_qfx])
            DVE(lambda e: e.memset(qfx[64:70, :, :], 1.0), [r_qfx], [r_qfx])
            DVE(lambda e: e.memset(qfx[64:71, :, :], 1.0), [r_qfx], [r_qfx])
            for i in range(2):
                DVE(lambda e, i=i: e.memset(kblk[i][:], 0.0), [], [r_kblk[i]])
                DVE(lambda e, i=i: e.memset(vblk[i][:], 0.0), [], [r_vblk[i]])
                S.dma("pool", kblk[i][70:71, :, :], maskin.rearrange("o (h t) -> o h t", h=4), writes=[r_kblk[i]])
            S.dma("sp", flag_t[:], flagin, writes=[r_flag])
            S.dma("sp", bm_t[:], bm, writes=[r_bm])
            DVE(lambda e: e.memset(vsx[:], 1.0), [], [r_vsx])
            DVE(lambda e: e.memset(vf[:], 1.0), [], [r_vf])
            for l in range(L):
                DVE(lambda e, l=l: e.memset(fcar[l][:], 0.0), [], [r_fcar[l]])

            def rmsnorm_hT(gkey, nxt=None):
                if gb_cur[0] != gkey:
                    S.dma("sp", gb[:], gains[gkey[0], gkey[1]], writes=[r_gb])
                    gb_cur[0] = gkey

                def st1(s):
                    ACT(lambda e: e.activation(out=h[:, s, :], in_=x[:, s, :], func=AF.Square,
                                               accum_out=ss[:, s:s + 1]), [r_x[s]], [r_h[s], r_ssq[s]])
                    ACT(lambda e: e.activation(out=ss[:, 4 + s:5 + s], in_=ss[:, s:s + 1], func=AF.Ln,
                                               scale=1.0 / D, bias=EPS), [r_ssq[s]], [r_ssq[s]])
                    ACT(lambda e: e.activation(out=ss[:, 4 + s:5 + s], in_=ss[:, 4 + s:5 + s], func=AF.Exp,
                                               scale=-0.5), [r_ssq[s]], [r_ssq[s]])

                def st2(s):
                    DVE(lambda e: e.scalar_tensor_tensor(out=h[:, s, :], in0=x[:, s, :],
                                                         scalar=ss[:, 4 + s:5 + s], in1=gb[:],
                                                         op0=ALU.mult, op1=ALU.mult),
                        [r_x[s], r_ssq[s], r_gb], [r_h[s]])

                def st3(s):
                    pb = ptbs[s % 2]
                    for c in range(8):
                        PE(lambda e, c=c, pb=pb: e.transpose(out=pb.ap[:, c * 128:(c + 1) * 128],
                                                             in_=h[:, s, c * 128:(c + 1) * 128],
                                                             identity=ident[:]),
                           [r_h[s], r_ident], [pb.res])

                def st4(s):
                    pb = ptbs[s % 2]
                    fn = lambda e: e.copy(out=hT[:, :, s * 128:(s + 1) * 128],
                                          in_=pb.ap[:, :].rearrange("p (c t) -> p c t", c=8))
                    fn2 = lambda e: e.tensor_copy(out=hT[:, :, s * 128:(s + 1) * 128],
                                                  in_=pb.ap[:, :].rearrange("p (c t) -> p c t", c=8))
                    if s % 2 == 0:
                        ACT(fn, [pb.res], [r_hT])
                    else:
                        DVE(fn2, [pb.res], [r_hT])

                stages = [st1, st2, st3, st4]
                for step in range(NS + 3):
                    for k, st_fn in enumerate(stages):
                        sidx = step - k
                        if 0 <= sidx < NS:
                            st_fn(sidx)
                    if step == NS and nxt is not None and nxt[0] < L:
                        S.dma("sp", gb[:], gains[nxt[0], nxt[1]], writes=[r_gb])
                        gb_cur[0] = nxt

            def resid_proj(src, src_res_of, nk, tile_src, scale):
                for n in range(2):
                    accs = [rot.get() for _ in range(NS)]
                    for kc in range(nk):
                        t, r = WD.next(tile_src(n, kc))
                        for s in range(NS):
                            PE(lambda e, s=s, kc=kc, t=t, a=accs[s]: e.matmul(
                                a.ap[:, :], lhsT=src[:, kc, s * 128:(s + 1) * 128], rhs=t[:],
                                start=(kc == 0), stop=(kc == nk - 1)), [src_res_of(kc), r], [accs[s].res])
                    for s in range(NS):
                        DVE(lambda e, s=s, n=n, a=accs[s]: e.scalar_tensor_tensor(
                            out=x[:, s, n * 512:(n + 1) * 512], in0=a.ap[:, :], scalar=scale,
                            in1=x[:, s, n * 512:(n + 1) * 512], op0=ALU.mult, op1=ALU.add),
                            [accs[s].res, r_x[s]], [r_x[s]])

            def ffn(l, k, mid=None, nxt=None):
                rmsnorm_hT((l, 2 * k), nxt=nxt)
                for f in range(NF):
                    tg, rg = WA.next(wg[l, k, f])
                    bg = rot.get()
                    for c in range(8):
                        PE(lambda e, c=c, tg=tg, bg=bg: e.matmul(bg.ap[:, :], lhsT=tg[:, c, :], rhs=hT[:, c, :],
                                                                 start=(c == 0), stop=(c == 7)),
                           [rg, r_hT], [bg.res])
                    tu, ru = WA.next(wu[l, k, f])
                    bu = rot.get()
                    for c in range(8):
                        PE(lambda e, c=c, tu=tu, bu=bu: e.matmul(bu.ap[:, :], lhsT=tu[:, c, :], rhs=hT[:, c, :],
                                                                 start=(c == 0), stop=(c == 7)),
                           [ru, r_hT], [bu.res])
                    i = sgi[0] % 2
                    sgi[0] += 1
                    ACT(lambda e, i=i, bg=bg: e.activation(out=sg[i][:], in_=bg.ap[:, :], func=AF.Silu),
                        [bg.res], [r_sg[i]])
                    DVE(lambda e, i=i, f=f, bu=bu: e.tensor_tensor(out=act[:, f, :], in0=sg[i][:], in1=bu.ap[:, :],
                                                                   op=ALU.mult),
                        [r_sg[i], bu.res], [r_act[f]])
                if mid is not None:
                    mid()
                resid_proj(act, lambda f: r_act[f], NF, lambda n, f: wd2[l, k, n, f], 0.5)

            def proj128(l, wi):
                t, r = WA.next(win[l, wi])
                b = rot.get()
                for c in range(8):
                    PE(lambda e, c=c, t=t, b=b: e.matmul(b.ap[:, :], lhsT=t[:, c, :], rhs=hT[:, c, :],
                                                         start=(c == 0), stop=(c == 7)), [r, r_hT], [b.res])
                return b

            def head_norm(src_ap, src_res, gcol, out_ap, out_res):
                ACT(lambda e: e.activation(out=lnm[0:64, :], in_=src_ap, func=AF.Square), [src_res], [r_lnm])
                b = rot.get()
                PE(lambda e, b=b: e.matmul(b.ap[0:64, :], lhsT=o64[:], rhs=lnm[0:64, :], start=True, stop=True),
                   [r_lnm, r_o64], [b.res])
                ACT(lambda e, b=b: e.activation(out=lnv[0:64, :], in_=b.ap[0:64, :], func=AF.Ln, bias=EPS),
                    [b.res], [r_lnv])
                ACT(lambda e: e.activation(out=lnv[0:64, :], in_=lnv[0:64, :], func=AF.Exp, scale=-0.5),
                    [r_lnv], [r_lnv])
                DVE(lambda e: e.scalar_tensor_tensor(out=out_ap, in0=src_ap, scalar=qkg_t[:, gcol:gcol + 1],
                                                     in1=lnv[0:64, :], op0=ALU.mult, op1=ALU.mult),
                    [src_res, r_lnv, r_small], [out_res])

            def heads_tile(l, wi, specs):
                t, r = WA.next(win[l, wi])
                for hh in range(2):
                    b = rot.get()
                    for c in range(8):
                        PE(lambda e, c=c, t=t, b=b, hh=hh: e.matmul(
                            b.ap[0:64, :], lhsT=t[:, c, hh * 64:(hh + 1) * 64], rhs=hT[:, c, :],
                            start=(c == 0), stop=(c == 7)), [r, r_hT], [b.res])
                    gcol, out_ap, out_res = specs[hh]
                    head_norm(b.ap[0:64, :], b.res, gcol, out_ap, out_res)

            def v_tile(l, wi, dst, dst_res, chunk_off, h0):
                t, r = WA.next(win[l, wi])
                for s in range(NS):
                    b = rot.get()
                    for c in range(8):
                        PE(lambda e, c=c, t=t, b=b, s=s: e.matmul(
                            b.ap[:, 0:128], lhsT=hT[:, c, s * 128:(s + 1) * 128], rhs=t[:, c, :],
                            start=(c == 0), stop=(c == 7)), [r, r_hT], [b.res])
                    ACT(lambda e, b=b, s=s: e.copy(out=dst[:, chunk_off + s, h0:h0 + 2, 0:64],
                                                   in_=b.ap[:, 0:128].rearrange("p (h d) -> p h d", h=2)),
                        [b.res], [dst_res])

            def attn_norm(acc, extra_den, out_ap, out_res):
                ACT(lambda e: e.copy(out=oa, in_=acc.ap[0:65, :]), [acc.res], [r_oa])
                if extra_den is not None:
                    DVE(lambda e: e.tensor_scalar(out=oa[64:65, :], in0=oa[64:65, :], scalar1=extra_den,
                                                  scalar2=None, op0=ALU.add), [r_oa, r_small], [r_oa])
                ACT(lambda e: e.activation(out=rden[64:65, :], in_=oa[64:65, :], func=AF.Ln), [r_oa], [r_rden])
                ACT(lambda e: e.activation(out=rden[64:65, :], in_=rden[64:65, :], func=AF.Exp, scale=-1.0),
                    [r_rden], [r_rden])
                b = rot_hi.get()
                PE(lambda e, b=b: e.matmul(b.ap[0:64, :], lhsT=ones[64:65, 0:64], rhs=rden[64:65, :],
                                           start=True, stop=True), [r_rden, r_ones], [b.res])
                DVE(lambda e, b=b: e.tensor_tensor(out=out_ap, in0=oa[0:64, :], in1=b.ap[0:64, :], op=ALU.mult),
                    [r_oa, b.res], [out_res])

            def attn_norm4(accs4, out_t, out_res):
                for hh in range(4):
                    if hh % 2 == 0:
                        ACT(lambda e, hh=hh: e.copy(out=oa4[:, hh, :], in_=accs4[hh].ap[0:65, :]),
                            [accs4[hh].res], [r_oa])
                    else:
                        DVE(lambda e, hh=hh: e.tensor_copy(out=oa4[:, hh, :], in_=accs4[hh].ap[0:65, :]),
                            [accs4[hh].res], [r_oa])
                ACT(lambda e: e.activation(out=oa4[64:65, :, :], in_=oa4[64:65, :, :], func=AF.Ln), [r_oa], [r_oa])
                ACT(lambda e: e.activation(out=oa4[64:65, :, :], in_=oa4[64:65, :, :], func=AF.Exp, scale=-1.0),
                    [r_oa], [r_oa])
                for hh in range(4):
                    b = rot_hi.get()
                    PE(lambda e, b=b, hh=hh: e.matmul(b.ap[0:64, :], lhsT=ones[64:65, 0:64],
                                                      rhs=oa4[64:65, hh, :], start=True, stop=True),
                       [r_oa, r_ones], [b.res])
                    DVE(lambda e, b=b, hh=hh: e.tensor_tensor(out=out_t[:, hh, :], in0=oa4[0:64, hh, :],
                                                              in1=b.ap[0:64, :], op=ALU.mult),
                        [r_oa, b.res], [out_res])

            def load_small(l):
                S.dma("sp", cdw_t[:], cdw[l], writes=[r_small])
                S.dma("sp", cvec_t[:], cvec[l], writes=[r_small])
                S.dma("sp", scw_t[:], scw[l], writes=[r_small])
                S.dma("sp", qkg_t[:], qkg[l], writes=[r_small])
                S.dma("sp", bfg_t[:], bfg[l], writes=[r_small])
                S.dma("sp", sink_t[:], sink[l], writes=[r_small])
                DVE(lambda e: e.tensor_scalar(out=qkg_t[:, 0:1], in0=qkg_t[:, 0:1], scalar1=0.125, scalar2=None,
                                              op0=ALU.mult), [r_small], [r_small])
                DVE(lambda e: e.tensor_scalar(out=qkg_t[:, 2:3], in0=qkg_t[:, 2:3], scalar1=0.125, scalar2=None,
                                              op0=ALU.mult), [r_small], [r_small])
                ACT(lambda e: e.activation(out=sink_t[64:65, :], in_=sink_t[64:65, :], func=AF.Exp),
                    [r_small], [r_small])
                DVE(lambda e: e.tensor_scalar(out=bfg_t[:], in0=bfg_t[:], scalar1=-1.0, scalar2=None,
                                              op0=ALU.mult), [r_small], [r_small])

            def split3(src, src_res):
                DVE(lambda e: e.tensor_copy(out=gsp[:, 0, :], in_=src[:]), [src_res], [r_gs])
                DVE(lambda e: e.tensor_tensor(out=fe[:], in0=src[:], in1=gsp[:, 0, :], op=ALU.subtract),
                    [src_res, r_gs], [r_fe])
                DVE(lambda e: e.tensor_copy(out=gsp[:, 1, :], in_=fe[:]), [r_fe], [r_gs])
                DVE(lambda e: e.tensor_tensor(out=fe[:], in0=fe[:], in1=gsp[:, 1, :], op=ALU.subtract),
                    [r_fe, r_gs], [r_fe])
                DVE(lambda e: e.tensor_copy(out=gsp[:, 2, :], in_=fe[:]), [r_fe], [r_gs])

            def front(l, j):
                t0 = j * TT
                last = (j == NT - 1)
                rmsnorm_hT((l, 1), nxt=((l, 0) if j + 1 < NT else (l, 1)))
                if j + 1 < NT:
                    src = xin if l == 0 else Xs
                    S.dma("act", x[:], src[t0 + TT:t0 + 2 * TT, :].rearrange("(s p) d -> p s d", p=128),
                          reads=([RX[j + 1]] if l > 0 else []), writes=r_x)
                for ch in range(2):
                    ba = proj128(l, 2 * ch)
                    bb = proj128(l, 2 * ch + 1)
                    i = sgi[0] % 2
                    sgi[0] += 1
                    ACT(lambda e, i=i, bb=bb: e.activation(out=sg[i][:], in_=bb.ap[:, :], func=AF.Sigmoid),
                        [bb.res], [r_sg[i]])
                    DVE(lambda e, i=i, ba=ba, ch=ch: e.tensor_tensor(out=uext[:, ch, 30:30 + TT], in0=sg[i][:],
                                                                     in1=ba.ap[:, :], op=ALU.mult),
                        [r_sg[i], ba.res], [r_uext])
                for ch in range(2):
                    b = proj128(l, 4 + ch)
                    ACT(lambda e, b=b, ch=ch: e.copy(out=sbt[:, ch, :], in_=b.ap[:, :]), [b.res], [r_sbt])
                for ch in range(2):
                    bc = proj128(l, 6 + 2 * ch)
                    bx = proj128(l, 7 + 2 * ch)
                    i = sgi[0] % 2
                    sgi[0] += 1
                    ACT(lambda e, i=i, bc=bc: e.copy(out=sg[i][:], in_=bc.ap[:, :]), [bc.res], [r_sg[i]])
                    DVE(lambda e, i=i, bx=bx, ch=ch: e.tensor_tensor(out=sxext[:, ch, 2:2 + TT], in0=sg[i][:],
                                                                     in1=bx.ap[:, :], op=ALU.mult),
                        [r_sg[i], bx.res], [r_sxext])
                S.dma("sp", FU[j], uext[:, :, 30:30 + TT], reads=[r_uext], writes=[RF[j]])
                S.dma("sp", FSX[j], sxext[:, :, 2:2 + TT], reads=[r_sxext], writes=[RF[j]])
                S.dma("sp", FSB[j], sbt[:], reads=[r_sbt], writes=[RF[j]])
                if last:
                    S.dma("sp", HF[l][:, 0:60].rearrange("p (c k) -> p c k", c=2), uext[:, :, TT:TT + 30],
                          reads=[r_uext], writes=[r_HAL[l]])
                    S.dma("sp", HF[l][:, 60:64].rearrange("p (c k) -> p c k", c=2), sxext[:, :, TT:TT + 2],
                          reads=[r_sxext], writes=[r_HAL[l]])
                heads_tile(l, 10, [(0, qs[:, 0, :], r_qs), (0, qs[:, 1, :], r_qs)])
                heads_tile(l, 11, [(0, qs[:, 2, :], r_qs), (0, qs[:, 3, :], r_qs)])
                heads_tile(l, 12, [(1, ksx[:, 0, 128:128 + TT], r_ksx), (1, ksx[:, 1, 128:128 + TT], r_ksx)])
                v_tile(l, 13, vsx, r_vsx, 1, 0)
                S.dma("sp", FQS[j], qs[:], reads=[r_qs], writes=[RF[j]])
                S.dma("sp", FKS[j], ksx[:, :, 128:128 + TT], reads=[r_ksx], writes=[RF[j]])
                S.dma("sp", FVS[j], vsx[:, 1:NS + 1, :, :].rearrange("p s h e -> p s (h e)"), reads=[r_vsx],
                      writes=[RF[j]])
                if last:
                    S.dma("sp", HB[l][0:64, 0:256].rearrange("p (c k) -> p c k", c=2), ksx[:, :, TT:TT + 128],
                          reads=[r_ksx], writes=[r_HAL[l]])
                    S.dma("sp", HB[l][:, 256:386], vsx[:, NS, :, :].rearrange("p h e -> p (h e)"),
                          reads=[r_vsx], writes=[r_HAL[l]])
                heads_tile(l, 14, [(2, qf[0:64, 0, :], r_qf), (2, qf[0:64, 1, :], r_qf)])
                heads_tile(l, 15, [(2, qf[0:64, 2, :], r_qf), (2, qf[0:64, 3, :], r_qf)])
                heads_tile(l, 16, [(3, kf[:, 0, :], r_kf), (3, kf[:, 1, :], r_kf)])
                heads_tile(l, 17, [(3, kf[:, 2, :], r_kf), (3, kf[:, 3, :], r_kf)])
                v_tile(l, 18, vf, r_vf, 0, 0)
                v_tile(l, 19, vf, r_vf, 0, 2)
                t, r = WA.next(win[l, 20])
                bF = rot.get()
                for c in range(8):
                    PE(lambda e, c=c, t=t, bF=bF: e.matmul(bF.ap[0:4, :], lhsT=t[:, c, 0:4], rhs=hT[:, c, :],
                                                           start=(c == 0), stop=(c == 7)), [r, r_hT], [bF.res])
                ACT(lambda e: e.activation(out=fe[:], in_=bF.ap[0:4, :], func=AF.Exp, scale=-1.0,
                                           bias=bfg_t[:, 0:1]), [bF.res, r_small], [r_fe])
                ACT(lambda e: e.activation(out=fe[:], in_=fe[:], func=AF.Ln, bias=1.0), [r_fe], [r_fe])
                if j == 0:
                    DVE(lambda e: e.memset(fcar[l][:], 0.0), [], [r_fcar[l]])
                DVE(lambda e: e.tensor_tensor_scan(out=fG[:], data0=fones[:], data1=fe[:], initial=fcar[l][:, 0:1],
                                                   op0=ALU.mult, op1=ALU.add), [r_fe, r_fones, r_fcar[l]], [r_fG])
                DVE(lambda e: e.tensor_copy(out=fcar[l][:], in_=fG[:, TT - 1:TT]), [r_fG], [r_fcar[l]])
                split3(fG, r_fG)
                S.dma("sp", FQF[j], qf[0:64, :, :], reads=[r_qf], writes=[RF[j]])
                S.dma("sp", FG[j], fG[:], reads=[r_fG], writes=[RF[j]])
                rKF = RKF[l][j]
                for hh in range(4):
                    S.dma("sp", KF[l][hh][0:64, t0:t0 + TT], kf[:, hh, :], reads=[r_kf], writes=[rKF])
                    S.dma("sp", KF[l][hh][64:67, t0:t0 + TT].rearrange("(o a) t -> o a t", o=1),
                          monesb[hh:hh + 1, :, :], reads=[r_monesb], writes=[rKF])
                    S.dma("sp", KF[l][hh][67:70, t0:t0 + TT].rearrange("(o a) t -> o a t", o=1),
                          gsp[hh:hh + 1, :, :], reads=[r_gs], writes=[rKF])
                vq, vo = t0 // VC, t0 % VC
                S.dma("sp", VF[l][vq][vo:vo + TT, :].rearrange("(s p) e -> p s e", p=128),
                      vf[:].rearrange("p s h e -> p s (h e)"), reads=[r_vf], writes=[rKF])
                if last:
                    S.dma("sp", HF[l][0:4, 64:65], fcar[l][:], reads=[r_fcar[l]], writes=[r_HAL[l]],
                          allow_slow_non_contiguous=True)

            def exchange(l):
                srcs = ([(KFh[l][hh], GKFh[l][hh]) for hh in range(4)]
                        + [(VFh[l][q], GVFh[l][q]) for q in range(NVC)]
                        + [(HFh[l], GHFh[l]), (HBh[l], GHBh[l])])
                for (a, g) in srcs:
                    k = ccn[0]
                    ccn[0] += 1
                    S.op("pool", lambda e, a=a, g=g: e.collective_compute(
                        "AllGather", ALU.bypass, replica_groups=[[0, 1], [2, 3], [4, 5], [6, 7]],
                        ins=[a.ap().opt()], outs=[g.ap().opt()]),
                        reads=RKF[l] + [r_HAL[l]], writes=[r_G[l]], dma=True, cc=k)

            def load_blk_into(blk, l, j, kk):
                cross = kk < NT
                kb = kk if cross else kk - NT
                i = kbi[0] % 2
                kbi[0] += 1
                vq, vo = (kb * TT) // VC, (kb * TT) % VC
                if cross:
                    for hh in range(4):
                        S.dma("sp", kblk[i][0:70, hh, :], GKF[l][hh][0, :, kb * TT:(kb + 1) * TT],
                              reads=[r_G[l]], writes=[r_kblk[i]])
                    S.dma("sp", vblk[i][:, :, 0:260],
                          GVF[l][vq][0, vo:vo + TT, :].rearrange("(s p) e -> p s e", p=128),
                          reads=[r_G[l]], writes=[r_vblk[i]])
                else:
                    for hh in range(4):
                        S.dma("sp", kblk[i][0:70, hh, :], KF[l][hh][:, kb * TT:(kb + 1) * TT],
                              reads=[RKF[l][kb]], writes=[r_kblk[i]])
                    S.dma("sp", vblk[i][:, :, 0:260],
                          VF[l][vq][vo:vo + TT, :].rearrange("(s p) e -> p s e", p=128),
                          reads=[RKF[l][kb]], writes=[r_vblk[i]])
                blk[kk] = (i, cross, (not cross) and kb == j)

            def back_prep(l, j):
                t0 = j * TT
                S.dma("sp", uext[:, :, 30:30 + TT], FU[j], reads=[RF[j]], writes=[r_uext])
                S.dma("sp", sxext[:, :, 2:2 + TT], FSX[j], reads=[RF[j]], writes=[r_sxext])
                S.dma("sp", sbt[:], FSB[j], reads=[RF[j]], writes=[r_sbt])
                if j == 0:
                    S.dma("sp", uext[:, :, 0:30], GHF[l][0, :, 0:60].rearrange("p (c k) -> p c k", c=2),
                          reads=[r_G[l]], writes=[r_uext])
                    S.dma("sp", sxext[:, :, 0:2], GHF[l][0, :, 60:64].rearrange("p (c k) -> p c k", c=2),
                          reads=[r_G[l]], writes=[r_sxext])
                    S.dma("sp", tot_t[:], GHF[l][0, 0:4, 64:65], reads=[r_G[l]], writes=[r_tot],
                          allow_slow_non_contiguous=True)
                    DVE(lambda e: e.tensor_scalar(out=uext[:, :, 0:30], in0=uext[:, :, 0:30],
                                                  scalar1=flag_t[:, 0:1], scalar2=None, op0=ALU.mult),
                        [r_uext, r_flag], [r_uext])
                    DVE(lambda e: e.tensor_scalar(out=sxext[:, :, 0:2], in0=sxext[:, :, 0:2],
                                                  scalar1=flag_t[:, 0:1], scalar2=None, op0=ALU.mult),
                        [r_sxext, r_flag], [r_sxext])
                else:
                    S.dma("sp", uext[:, :, 0:30], FU[j - 1][:, :, TT - 30:TT], reads=[RF[j - 1]], writes=[r_uext])
                    S.dma("sp", sxext[:, :, 0:2], FSX[j - 1][:, :, TT - 2:TT], reads=[RF[j - 1]], writes=[r_sxext])
                S.dma("sp", qf[0:64, :, :], FQF[j], reads=[RF[j]], writes=[r_qf])
                S.dma("sp", qfx[0:64, :, :], FQF[j], reads=[RF[j]], writes=[r_qfx])
                S.dma("sp", fG[:], FG[j], reads=[RF[j]], writes=[r_fG])
                split3(fG, r_fG)
                for hh in range(4):
                    S.dma("sp", QFs[0, :, hh, :].rearrange("(o a) t -> o a t", o=1), gsp[hh:hh + 1, :, :],
                          reads=[r_gs], writes=[r_QFs])
                S.dma("sp", qf[64:67, :, :], QFs[0], reads=[r_QFs], writes=[r_qf])
                DVE(lambda e: e.tensor_scalar(out=fG[:], in0=fG[:], scalar1=tot_t[:, 0:1], scalar2=None,
                                              op0=ALU.add), [r_fG, r_tot], [r_fG])
                split3(fG, r_fG)
                for hh in range(4):
                    S.dma("sp", QFs[1, :, hh, :].rearrange("(o a) t -> o a t", o=1), gsp[hh:hh + 1, :, :],
                          reads=[r_gs], writes=[r_QFs])
                S.dma("sp", qfx[64:67, :, :], QFs[1], reads=[r_QFs], writes=[r_qfx])
                pb = {}
                for kk in range(2):
                    load_blk_into(pb, l, j, kk)
                preblk[(l, j)] = pb
            def back(l, j):
                t0 = j * TT
                for kk in range(31):
                    for ch in range(2):
                        if kk == 0:
                            DVE(lambda e, ch=ch: e.tensor_scalar(
                                out=cacc[:, ch, :], in0=uext[:, ch, 0:TT], scalar1=cdw_t[:, ch, 0:1],
                                scalar2=cvec_t[:, ch, 0:1], op0=ALU.mult, op1=ALU.add),
                                [r_uext, r_small], [r_cacc[ch]])
                        else:
                            DVE(lambda e, ch=ch, kk=kk: e.scalar_tensor_tensor(
                                out=cacc[:, ch, :], in0=uext[:, ch, kk:kk + TT], scalar=cdw_t[:, ch, kk:kk + 1],
                                in1=cacc[:, ch, :], op0=ALU.mult, op1=ALU.add),
                                [r_uext, r_small, r_cacc[ch]], [r_cacc[ch]])
                S.dma("sp", qs[:], FQS[j], reads=[RF[j]], writes=[r_qs])
                S.dma("sp", ksx[:, :, 128:128 + TT], FKS[j], reads=[RF[j]], writes=[r_ksx])
                S.dma("sp", vsx[:, 1:NS + 1, :, :].rearrange("p s h e -> p s (h e)"), FVS[j], reads=[RF[j]],
                      writes=[r_vsx])
                if j == 0:
                    S.dma("sp", ksx[:, :, 0:128], GHB[l][0, 0:64, 0:256].rearrange("p (c k) -> p c k", c=2),
                          reads=[r_G[l]], writes=[r_ksx])
                    S.dma("sp", vsx[:, 0, :, :].rearrange("p h e -> p (h e)"), GHB[l][0, :, 256:386],
                          reads=[r_G[l]], writes=[r_vsx])
                    DVE(lambda e: e.tensor_scalar(out=vsx[:, 0, :, :], in0=vsx[:, 0, :, :],
                                                  scalar1=flag_t[:, 0:1], scalar2=None, op0=ALU.mult),
                        [r_vsx, r_flag], [r_vsx])
                else:
                    S.dma("sp", ksx[:, :, 0:128], FKS[j - 1][:, :, TT - 128:TT], reads=[RF[j - 1]], writes=[r_ksx])
                    S.dma("sp", vsx[:, 0, :, :].rearrange("p h e -> p (h e)"), FVS[j - 1][:, NS - 1, :],
                          reads=[RF[j - 1]], writes=[r_vsx])
                def step_ln():
                    for ch in range(2):
                        ACT(lambda e, ch=ch: e.activation(out=csq[:, ch, :], in_=cacc[:, ch, :], func=AF.Square),
                            [r_cacc[ch]], [r_csq])
                    b1 = rot_hi.get()
                    for ch in range(2):
                        PE(lambda e, ch=ch, b1=b1: e.matmul(b1.ap[:, :], lhsT=ones[:], rhs=cacc[:, ch, :],
                                                            start=(ch == 0), stop=(ch == 1)),
                           [r_cacc[ch], r_ones], [b1.res])
                    b2 = rot_hi.get()
                    for ch in range(2):
                        PE(lambda e, ch=ch, b2=b2: e.matmul(b2.ap[:, :], lhsT=ones[:], rhs=csq[:, ch, :],
                                                            start=(ch == 0), stop=(ch == 1)),
                           [r_csq, r_ones], [b2.res])
                    DVE(lambda e: e.tensor_scalar(out=lnm[:], in0=b1.ap[:, :], scalar1=1.0 / 256.0, scalar2=None,
                                                  op0=ALU.mult), [b1.res], [r_lnm])
                    DVE(lambda e: e.tensor_tensor(out=lnv[:], in0=lnm[:], in1=lnm[:], op=ALU.mult), [r_lnm], [r_lnv])
                    DVE(lambda e: e.scalar_tensor_tensor(out=lnv[:], in0=b2.ap[:, :], scalar=1.0 / 256.0, in1=lnv[:],
                                                         op0=ALU.mult, op1=ALU.subtract), [b2.res, r_lnv], [r_lnv])
                    DVE(lambda e: e.tensor_scalar(out=lnv[:], in0=lnv[:], scalar1=0.0, scalar2=None, op0=ALU.max),
                        [r_lnv], [r_lnv])
                    ACT(lambda e: e.activation(out=lnv[:], in_=lnv[:], func=AF.Ln, bias=EPS), [r_lnv], [r_lnv])
                    ACT(lambda e: e.activation(out=lnv[:], in_=lnv[:], func=AF.Exp, scale=-0.5), [r_lnv], [r_lnv])
                    for ch in range(2):
                        DVE(lambda e, ch=ch: e.tensor_tensor(out=cacc[:, ch, :], in0=cacc[:, ch, :], in1=lnm[:],
                                                             op=ALU.subtract), [r_cacc[ch], r_lnm], [r_cacc[ch]])
                        DVE(lambda e, ch=ch: e.tensor_tensor(out=cacc[:, ch, :], in0=cacc[:, ch, :], in1=lnv[:],
                                                             op=ALU.mult), [r_cacc[ch], r_lnv], [r_cacc[ch]])
                        ACT(lambda e, ch=ch: e.activation(out=uT[:, ch, :], in_=cacc[:, ch, :], func=AF.Silu,
                                                          scale=cvec_t[:, ch, 1:2], bias=cvec_t[:, ch, 2:3]),
                            [r_cacc[ch], r_small], [r_uT])
                    for ch in range(2):
                        DVE(lambda e, ch=ch: e.tensor_scalar(out=csq[:, ch, :], in0=sxext[:, ch, 0:TT],
                                                             scalar1=scw_t[:, ch, 0:1], scalar2=None, op0=ALU.mult),
                            [r_sxext, r_small], [r_csq])
                        for kk in (1, 2):
                            DVE(lambda e, ch=ch, kk=kk: e.scalar_tensor_tensor(
                                out=csq[:, ch, :], in0=sxext[:, ch, kk:kk + TT], scalar=scw_t[:, ch, kk:kk + 1],
                                in1=csq[:, ch, :], op0=ALU.mult, op1=ALU.add), [r_sxext, r_small, r_csq], [r_csq])
                        DVE(lambda e, ch=ch: e.tensor_tensor(out=scT[:, ch, :], in0=csq[:, ch, :], in1=sbt[:, ch, :],
                                                             op=ALU.mult), [r_csq, r_sbt], [r_scT])


                side = [step_ln]

                def mk_swa(hq, pr):
                    hk = hq // 2
                    st = {}

                    def step_S():
                        bS = rot_hi.get()
                        i = sgi[0] % 2
                        sgi[0] += 1
                        for s2 in range(2):
                            s = 2 * pr + s2
                            for part in range(2):
                                PE(lambda e, s=s, s2=s2, part=part, bS=bS: e.matmul(
                                    bS.ap[:, s2 * 256 + part * 128: s2 * 256 + part * 128 + 128],
                                    lhsT=ksx[:, hk, (s + part) * 128:(s + part + 1) * 128],
                                    rhs=qs[:, hq, s * 128:(s + 1) * 128], start=True, stop=True),
                                   [r_ksx, r_qs], [bS.res])
                            DVE(lambda e, bS=bS, i=i, s2=s2: e.tensor_tensor(
                                out=sg[i][:, s2 * 256:(s2 + 1) * 256], in0=bS.ap[:, s2 * 256:(s2 + 1) * 256],
                                in1=bm_t[:, hq, :], op=ALU.add), [bS.res, r_bm], [r_sg[i]])
                        pi = spi[0] % 2
                        spi[0] += 1
                        st["pi"] = pi
                        ACT(lambda e, i=i, pi=pi: e.activation(out=pTs[pi][:], in_=sg[i][:], func=AF.Exp),
                            [r_sg[i]], [r_pTs[pi]])

                    def step_PV():
                        pi = st["pi"]
                        for s2 in range(2):
                            s = 2 * pr + s2
                            for part in range(2):
                                PE(lambda e, s=s, s2=s2, part=part, pi=pi: e.matmul(
                                    swacc.ap[0:65, s * 128:(s + 1) * 128], lhsT=vsx[:, s + part, hk, :],
                                    rhs=pTs[pi][:, s2 * 256 + part * 128: s2 * 256 + part * 128 + 128],
                                    start=(part == 0), stop=(part == 1)), [r_vsx, r_pTs[pi]], [swacc.res])
                    return step_S, step_PV

                for hq in range(4):
                    for pr in range(2):
                        a, b_ = mk_swa(hq, pr)
                        side.append(a)
                        side.append(b_)
                    side.append(lambda hq=hq: attn_norm(swacc, sink_t[64:65, hq:hq + 1], onT[0][:, hq, :],
                                                        r_onT[0]))
                accs = banks[0:4]
                nkb = NT + j + 1
                jobs = [(kk, hh, c) for kk in range(nkb) for hh in range(4) for c in range(NS)]
                blk = preblk.pop((l, j), {})

                def load_blk(kk):
                    load_blk_into(blk, l, j, kk)

                def _unused(kk):
                    cross = kk < NT
                    kb = kk if cross else kk - NT
                    i = kbi[0] % 2
                    kbi[0] += 1
                    vq, vo = (kb * TT) // VC, (kb * TT) % VC
                    if cross:
                        for hh in range(4):
                            S.dma("sp", kblk[i][0:70, hh, :], GKF[l][hh][0, :, kb * TT:(kb + 1) * TT],
                                  reads=[r_G[l]], writes=[r_kblk[i]])
                        S.dma("sp", vblk[i][:, :, 0:260],
                              GVF[l][vq][0, vo:vo + TT, :].rearrange("(s p) e -> p s e", p=128),
                              reads=[r_G[l]], writes=[r_vblk[i]])
                    else:
                        for hh in range(4):
                            S.dma("sp", kblk[i][0:70, hh, :], KF[l][hh][:, kb * TT:(kb + 1) * TT],
                                  reads=[RKF[l][kb]], writes=[r_kblk[i]])
                        S.dma("sp", vblk[i][:, :, 0:260],
                              VF[l][vq][vo:vo + TT, :].rearrange("(s p) e -> p s e", p=128),
                              reads=[RKF[l][kb]], writes=[r_vblk[i]])
                    blk[kk] = (i, cross, (not cross) and kb == j)

                pis = {}

                def emit_S(job):
                    kk, hh, c = job
                    if kk not in blk:
                        load_blk(kk)
                    i, cross, diag = blk[kk]
                    qsrc, rq = (qfx, r_qfx) if cross else (qf, r_qf)
                    bS = rot_hi.get()
                    PE(lambda e, i=i, hh=hh, c=c, bS=bS, qsrc=qsrc: e.matmul(
                        bS.ap[:, :], lhsT=kblk[i][:, hh, c * 128:(c + 1) * 128], rhs=qsrc[:, hh, :],
                        start=True, stop=True), [r_kblk[i], rq], [bS.res])
                    pi = pti[0] % 4
                    pti[0] += 1
                    pis[job] = pi
                    ACT(lambda e, pi=pi, bS=bS: e.activation(out=pT[pi][:], in_=bS.ap[:, :], func=AF.Exp),
                        [bS.res], [r_pT[pi]])
                    if diag:
                        POOL(lambda e, pi=pi, c=c: e.affine_select(
                            out=pT[pi][:], in_=pT[pi][:], pattern=[[1, TT]], compare_op=ALU.is_ge,
                            fill=0.0, base=-c * 128, channel_multiplier=-1), [r_pT[pi]], [r_pT[pi]])

                def emit_PV(job):
                    kk, hh, c = job
                    i = blk[kk][0]
                    pi = pis[job]
                    PE(lambda e, i=i, hh=hh, c=c, pi=pi, kk=kk: e.matmul(
                        accs[hh].ap[:, :], lhsT=vblk[i][:, c, hh * 65:hh * 65 + 128], rhs=pT[pi][:],
                        start=(kk == 0 and c == 0), stop=(kk == nkb - 1 and c == NS - 1)),
                       [r_vblk[i], r_pT[pi]], [accs[hh].res])

                LA = 2
                stride = max(1, (len(jobs) - 8) // (len(side) + 1))
                for idx, job in enumerate(jobs):
                    emit_S(job)
                    if idx >= LA:
                        emit_PV(jobs[idx - LA])
                    if side and idx >= 8 and (idx - 8) % stride == stride - 1:
                        side.pop(0)()
                for job in jobs[len(jobs) - LA:]:
                    emit_PV(job)
                while side:
                    side.pop(0)()
                attn_norm4(accs, onT[1], r_onT[1])

                rmsnorm_hT((l, 1), nxt=(l, 2))
                brsrc = [(uT, r_uT), (scT, r_scT)]
                for m in range(8):
                    tr, rr = WR.next(wbr[l, m])
                    for br in range(4):
                        bp = rot.get()
                        if br < 2:
                            src, rs = brsrc[br]
                            for ch in range(2):
                                PE(lambda e, br=br, ch=ch, bp=bp, src=src, tr=tr: e.matmul(
                                    bp.ap[:, :], lhsT=tr[:, 2 * br + ch, :], rhs=src[:, ch, :],
                                    start=(ch == 0), stop=(ch == 1)), [rr, rs], [bp.res])
                        else:
                            a = br - 2
                            for hh in range(4):
                                PE(lambda e, a=a, hh=hh, bp=bp, tr=tr: e.matmul(
                                    bp.ap[:, :], lhsT=tr[0:64, 4 + 4 * a + hh, :],
                                    rhs=onT[a][:, hh, :], start=(hh == 0), stop=(hh == 3)),
                                   [rr, r_onT[a]], [bp.res])
                        bgt = proj128(l, 21 + 4 * m + br)
                        i = sgi[0] % 2
                        sgi[0] += 1
                        ACT(lambda e, i=i, bgt=bgt: e.activation(out=sg[i][:], in_=bgt.ap[:, :], func=AF.Sigmoid),
                            [bgt.res], [r_sg[i]])
                        if br == 0:
                            DVE(lambda e, i=i, bp=bp: e.tensor_tensor(out=macc[:], in0=sg[i][:], in1=bp.ap[:, :],
                                                                      op=ALU.mult), [r_sg[i], bp.res], [r_macc])
                        else:
                            k2 = br % 2
                            DVE(lambda e, i=i, bp=bp, k2=k2: e.tensor_tensor(out=mtmp[k2][:], in0=sg[i][:],
                                                                             in1=bp.ap[:, :], op=ALU.mult),
                                [r_sg[i], bp.res], [r_mtmp[k2]])
                            if br < 3:
                                DVE(lambda e, k2=k2: e.tensor_tensor(out=macc[:], in0=macc[:], in1=mtmp[k2][:],
                                                                     op=ALU.add), [r_macc, r_mtmp[k2]], [r_macc])
                            else:
                                DVE(lambda e, k2=k2, m=m: e.tensor_tensor(out=mT[:, m, :], in0=macc[:],
                                                                          in1=mtmp[k2][:], op=ALU.add),
                                    [r_macc, r_mtmp[k2]], [r_mT])
                resid_proj(mT, lambda m: r_mT, 8, lambda n, m: wo2[l, n, m], 1.0)

            for l in range(L):
                load_small(l)
                for j in range(NT):
                    t0 = j * TT
                    src = xin if l == 0 else Xs
                    if j == 0:
                        S.dma("sp", x[:], src[t0:t0 + TT, :].rearrange("(s p) d -> p s d", p=128),
                              reads=([RX[j]] if l > 0 else []), writes=r_x)
                    ffn(l, 0, nxt=(l, 1))
                    S.dma("sp", Xs[t0:t0 + TT, :].rearrange("(s p) d -> p s d", p=128), x[:], reads=r_x,
                          writes=[RX[j]])
                    front(l, j)
                exchange(l)
                back_prep(l, 0)
                for j in range(NT):
                    t0 = j * TT
                    S.dma("sp", x[:], Xs[t0:t0 + TT, :].rearrange("(s p) d -> p s d", p=128), reads=[RX[j]],
                          writes=r_x)
                    back(l, j)
                    ffn(l, 1, mid=((lambda l=l, j=j: back_prep(l, j + 1)) if j + 1 < NT else None),
                        nxt=((l, 1) if j + 1 < NT else (l + 1, 0)))
                    dst = Xs if l < L - 1 else yout
                    S.dma("sp", dst[t0:t0 + TT, :].rearrange("(s p) d -> p s d", p=128), x[:], reads=r_x,
                          writes=[RX[j]])
            return [WA.rec, WD.rec, WR.rec]

        plans = program(Sched(nc, dry=True), [None, None, None])
        S = Sched(nc)
        program(S, plans)
        S.emit()
    return nc


def _t5_bucket(dist):
    max_exact = 16
    d = np.maximum(dist, 1).astype(np.float32)
    large = max_exact + (np.log(d / np.float32(max_exact)) / np.float32(np.log(128 / max_exact))
                         * np.float32(32 - max_exact)).astype(np.int32)
    large = np.minimum(large, 31)
    return np.where(dist < max_exact, dist, large)


def host_prep(inp):
    f = lambda a: np.ascontiguousarray(np.asarray(a, dtype=np.float32))

    def wtile(w, col0, n):
        out = np.zeros((128, 8, 128), np.float32)
        out[:, :, :n] = w[:, col0:col0 + n].reshape(8, 128, n).transpose(1, 0, 2)
        return out

    shared = {}
    gates = [np.asarray(inp["ffn1_w_gate"]), np.asarray(inp["ffn2_w_gate"])]
    ups = [np.asarray(inp["ffn1_w_up"]), np.asarray(inp["ffn2_w_up"])]
    downs = [np.asarray(inp["ffn1_w_down"]), np.asarray(inp["ffn2_w_down"])]

    def ftile(w):
        return w.reshape(8, 128, NF, 128).transpose(2, 1, 0, 3)

    def dtile(w):
        return w.reshape(NF, 128, 2, 512).transpose(2, 0, 1, 3)

    shared["wg"] = f(np.stack([np.stack([ftile(gates[k][l]) for k in range(2)]) for l in range(L)]))
    shared["wu"] = f(np.stack([np.stack([ftile(ups[k][l]) for k in range(2)]) for l in range(L)]))
    shared["wd2"] = f(np.stack([np.stack([dtile(downs[k][l]) for k in range(2)]) for l in range(L)]))
    w_in = np.asarray(inp["w_in"])
    shared["win"] = f(np.stack([np.stack([wtile(w_in[l], c0, n) for (c0, n) in WIN_TILES]) for l in range(L)]))
    cwo, swo = np.asarray(inp["conf_w_out"]), np.asarray(inp["sc_w_out"])
    awo, fwo = np.asarray(inp["swa_w_o"]), np.asarray(inp["fox_w_o"])
    wbr = np.zeros((L, 8, 128, 12, 128), np.float32)
    for l in range(L):
        for m in range(8):
            cs = slice(m * 128, (m + 1) * 128)
            for ch in range(2):
                wbr[l, m, :, ch, :] = cwo[l][ch * 128:(ch + 1) * 128, cs]
                wbr[l, m, :, 2 + ch, :] = swo[l][ch * 128:(ch + 1) * 128, cs]
            for hh in range(4):
                wbr[l, m, 0:64, 4 + hh, :] = awo[l][hh * 64:(hh + 1) * 64, cs]
                wbr[l, m, 0:64, 8 + hh, :] = fwo[l][hh * 64:(hh + 1) * 64, cs]
    shared["wbr"] = wbr
    shared["wo2"] = f(np.stack([np.asarray(inp["w_out"])[l].reshape(8, 128, 2, 512).transpose(2, 0, 1, 3)
                                for l in range(L)]))
    gn = [np.asarray(inp["ffn1_norm"]), np.asarray(inp["mix_norm"]), np.asarray(inp["ffn2_norm"])]
    shared["gains"] = f(np.stack([np.stack([np.broadcast_to(gn[i][l][None, :], (128, D)) for i in range(3)])
                                  for l in range(L)]))
    shared["cdw"] = f(np.stack([np.asarray(inp["conf_dw"])[l].T.reshape(2, 128, 31).transpose(1, 0, 2)
                                for l in range(L)]))
    cv = [np.asarray(inp["conf_dw_b"]), np.asarray(inp["conf_ln_g"]), np.asarray(inp["conf_ln_b"])]
    shared["cvec"] = f(np.stack([np.stack([cv[i][l].reshape(2, 128).T for i in range(3)], axis=-1)
                                 for l in range(L)]))
    shared["scw"] = f(np.stack([np.asarray(inp["sc_conv"])[l].T.reshape(2, 128, 3).transpose(1, 0, 2)
                                for l in range(L)]))
    qk = [np.asarray(inp["swa_q_norm"]), np.asarray(inp["swa_k_norm"]),
          np.asarray(inp["fox_q_norm"]), np.asarray(inp["fox_k_norm"])]
    shared["qkg"] = f(np.stack([np.stack([qk[i][l] for i in range(4)], axis=-1) for l in range(L)]))
    shared["bfg"] = f(np.asarray(inp["b_forget"]).reshape(L, 4, 1))
    sk = np.zeros((L, 65, 4), np.float32)
    sk[:, 64, :] = np.asarray(inp["swa_sink"])
    shared["sink"] = sk
    rb = np.asarray(inp["rel_bias"], dtype=np.float32)
    i = np.arange(128)[:, None]
    jq = np.arange(128)[None, :]
    bmt = np.full((128, 4, 256), NEG, np.float32)
    d_prev = jq + 128 - i
    ok_prev = d_prev <= 127
    d_cur = jq - i
    ok_cur = d_cur >= 0
    bk_prev = _t5_bucket(np.clip(d_prev, 0, 127))
    bk_cur = _t5_bucket(np.clip(d_cur, 0, 127))
    for hq in range(4):
        bmt[:, hq, 0:128] = np.where(ok_prev, rb[bk_prev, hq], np.float32(NEG))
        bmt[:, hq, 128:256] = np.where(ok_cur, rb[bk_cur, hq], np.float32(NEG))
    shared["bm"] = bmt
    shared["identin"] = np.eye(128, dtype=np.float32)
    return shared


_NC_CACHE = {}


def kernel(**inputs):
    x = np.asarray(inputs["x"], dtype=np.float32)
    B, T, _ = x.shape
    ntok = T // 2
    shared = host_prep(inputs)
    if ntok not in _NC_CACHE:
        _NC_CACHE[ntok] = build(ntok)
    nc = _NC_CACHE[ntok]
    in_maps = []
    for c in range(N_CORES):
        b, half = c // 2, c % 2
        m = dict(shared)
        m["xin"] = np.ascontiguousarray(x[b, half * ntok:(half + 1) * ntok])
        m["flagin"] = np.full((128, 1), float(half), np.float32)
        m["maskin"] = np.full((1, 4 * TT), 0.0 if half else NEG, np.float32)
        in_maps.append(m)
    res = run_bass_kernel_spmd(nc, in_maps, core_ids=list(range(N_CORES)))
    out = np.empty((B, T, D), np.float32)
    for c in range(N_CORES):
        b, half = c // 2, c % 2
        out[b, half * ntok:(half + 1) * ntok] = np.asarray(res.results[c]["yout"], dtype=np.float32)
    return out
```
